# Optimizing a Trainium2 kernel written in Bass

```python
import math
import numpy as np
import jax
import jax.numpy as jnp
from jax import lax

D_MODEL = 2048
BATCH = 16
SEQ = 2048
DEPTH = 2

GRID_W = 64
CTX_LEN = 256
N_BRANCH = 4
BR_W = D_MODEL // N_BRANCH

NA_HEADS = 8
NA_DH = BR_W // NA_HEADS
WIN_R = 8
WIN_C = 16
QBLK_C = 16
KBLK_C = WIN_C + QBLK_C
N_CBLK = GRID_W // QBLK_C

GLA_HEADS = 4
GLA_DV = BR_W // GLA_HEADS
GLA_DK = GLA_DV // 2
GLA_RANK = 16
GLA_TAU = 16.0
CHUNK = 64

GDN_HEADS = 4
GDN_DH = BR_W // GDN_HEADS
GDN_CONV = 5

HY_CONV = 3
HY_EMB = 33
HY_FFN = 64
HY_INNER = 2
HY_TARGET = 1e-2
HY_FAST = 0.3
HY_SLOW = 1.5

ROPE_THETA = 10000.0
EPS = 1e-6
F32 = jnp.float32

SPLITS = (
    ("na_qkv", 3 * BR_W), ("na_z", BR_W),
    ("gla_q", GLA_HEADS * GLA_DK), ("gla_k", GLA_HEADS * GLA_DK), ("gla_v", BR_W), ("gla_z", BR_W),
    ("gla_g", 2 * GLA_RANK),
    ("gdn_qkv", 3 * BR_W), ("gdn_z", BR_W), ("gdn_a", 2 * GDN_HEADS), ("gdn_b", 2 * GDN_HEADS),
    ("hy_xv", 3 * BR_W), ("hy_z", BR_W),
)
N_IN = sum(w for _, w in SPLITS)

kernel_name = "hybrid_na_gla_gdn_hyena_prefix_dit"


def rmsnorm(x, g):
    xf = x.astype(F32)
    y = xf * lax.rsqrt(jnp.mean(xf * xf, axis=-1, keepdims=True) + EPS)
    return (y * g.astype(F32)).astype(x.dtype)


def l2norm(x):
    return x * lax.rsqrt(jnp.sum(x * x, axis=-1, keepdims=True) + EPS)


def split_cols(p):
    out, o = {}, 0
    for name, w in SPLITS:
        out[name] = p[..., o:o + w]
        o += w
    return out


def to_heads(x, h):
    b, t, _ = x.shape
    return x.reshape(b, t, h, -1).transpose(0, 2, 1, 3)


def from_heads(x):
    b, h, t, d = x.shape
    return x.transpose(0, 2, 1, 3).reshape(b, t, h * d)


def dwconv(x, w):
    ch, k = w.shape
    return lax.conv_general_dilated(x, w.T[:, None, :].astype(x.dtype), window_strides=(1,),
                                    padding=[(k // 2, k // 2)], dimension_numbers=("NWC", "WIO", "NWC"),
                                    feature_group_count=ch)


def axial_rope(t_len):
    pos = jnp.arange(t_len)
    row = (pos // GRID_W).astype(F32)
    col = (pos % GRID_W).astype(F32)
    n = GLA_DK // 4
    freqs = ROPE_THETA ** (-jnp.arange(n, dtype=F32) / n)
    ang = jnp.concatenate([row[:, None] * freqs, col[:, None] * freqs], axis=-1)
    return jnp.cos(ang), jnp.sin(ang)


def apply_rope(x, cos, sin):
    x1, x2 = jnp.split(x, 2, axis=-1)
    return jnp.concatenate([x1 * cos - x2 * sin, x2 * cos + x1 * sin], axis=-1)


def to_chunks(x):
    b, h, t = x.shape[:3]
    return jnp.moveaxis(x.reshape(b, h, t // CHUNK, CHUNK, *x.shape[3:]), 2, 0)


def from_chunks(x):
    n, b, h, cl = x.shape[:4]
    return jnp.moveaxis(x, 0, 2).reshape(b, h, n * cl, *x.shape[4:])


def na_col_tables():
    qcol = np.arange(N_CBLK)[:, None] * QBLK_C + np.arange(QBLK_C)[None]
    kstart = np.clip(np.arange(N_CBLK) * QBLK_C - WIN_C // 2, 0, GRID_W - KBLK_C)
    kcol = kstart[:, None] + np.arange(KBLK_C)[None]
    cstart = np.clip(qcol - WIN_C // 2, 0, GRID_W - WIN_C)
    kc = kcol[:, None, :]
    cmask = (kc >= cstart[..., None]) & (kc < cstart[..., None] + WIN_C)
    dc_idx = np.clip(kc - qcol[..., None] + WIN_C - 1, 0, 2 * WIN_C - 2)
    return kcol, cmask, dc_idx


def neighbourhood_attention(q, k, v, k_ctx, v_ctx, rpb):
    b, h, t, dh = q.shape
    rows = t // GRID_W
    wr = min(WIN_R, rows)
    kcol, cmask, dc_idx = na_col_tables()
    scale = dh ** -0.5
    qg = q.reshape(b, h, rows, N_CBLK, QBLK_C, dh).transpose(2, 0, 1, 3, 4, 5)
    kg = k.reshape(b, h, rows, GRID_W, dh)
    vg = v.reshape(b, h, rows, GRID_W, dh)

    def row_block(args):
        r, q_r = args
        rs = jnp.clip(r - wr // 2, 0, rows - wr)
        k_r = lax.dynamic_slice_in_dim(kg, rs, wr, axis=2)[:, :, :, kcol]
        v_r = lax.dynamic_slice_in_dim(vg, rs, wr, axis=2)[:, :, :, kcol]
        dr_idx = rs + jnp.arange(wr) - r + WIN_R - 1
        bias = rpb[:, dr_idx[None, None, :, None], dc_idx[:, :, None, :]]
        s_win = jnp.einsum("bhcqd,bhrckd->bhcqrk", q_r, k_r) * scale + bias
        s_win = jnp.where(cmask[:, :, None, :], s_win, -jnp.inf).reshape(b, h, N_CBLK, QBLK_C, wr * KBLK_C)
        s_ctx = jnp.einsum("bhcqd,bhkd->bhcqk", q_r, k_ctx) * scale
        p = jax.nn.softmax(jnp.concatenate([s_win, s_ctx], axis=-1), axis=-1)
        p_win = p[..., :wr * KBLK_C].reshape(b, h, N_CBLK, QBLK_C, wr, KBLK_C)
        return (jnp.einsum("bhcqrk,bhrckd->bhcqd", p_win, v_r)
                + jnp.einsum("bhcqk,bhkd->bhcqd", p[..., wr * KBLK_C:], v_ctx))

    o = lax.map(row_block, (jnp.arange(rows), qg))
    return o.transpose(1, 2, 0, 3, 4, 5).reshape(b, h, t, dh)


def dense_attention(q, k, v):
    s = jnp.einsum("bhqd,bhkd->bhqk", q, k) * (q.shape[-1] ** -0.5)
    return jnp.einsum("bhqk,bhkd->bhqd", jax.nn.softmax(s, axis=-1), v)


def gla_scan(q, k, v, g, s0):
    qc, kc, vc = to_chunks(q), to_chunks(k), to_chunks(v)
    bc = jnp.cumsum(to_chunks(g), axis=3)
    incl = jnp.tril(jnp.ones((CHUNK, CHUNK), bool))

    def step(s, inp):
        q_, k_, v_, b_ = inp
        diff = jnp.where(incl[:, :, None], b_[..., :, None, :] - b_[..., None, :, :], -jnp.inf)
        att = jnp.einsum("bhtd,bhsd,bhtsd->bhts", q_, k_, jnp.exp(diff))
        b_end = b_[..., -1:, :]
        o = att @ v_ + (q_ * jnp.exp(b_)) @ s
        s = jnp.exp(b_end[..., 0, :])[..., None] * s + jnp.einsum("bhsd,bhse->bhde", k_ * jnp.exp(b_end - b_), v_)
        return s, o

    s_fin, o = lax.scan(step, s0, (qc, kc, vc, bc))
    return from_chunks(o), s_fin


def gla_inputs(p, wg2, bg, rope):
    q = to_heads(p["gla_q"], GLA_HEADS).astype(F32) * GLA_DK ** -0.5
    k = to_heads(p["gla_k"], GLA_HEADS).astype(F32)
    v = to_heads(p["gla_v"], GLA_HEADS).astype(F32)
    if rope is not None:
        q, k = apply_rope(q, *rope), apply_rope(k, *rope)
    lr = p["gla_g"].astype(F32)
    dirs = []
    for d in range(2):
        gl = jax.nn.log_sigmoid(lr[..., d * GLA_RANK:(d + 1) * GLA_RANK] @ wg2[d] + bg[d]) / GLA_TAU
        gl = to_heads(gl, GLA_HEADS)
        dirs.append((q, k, v, jnp.concatenate([gl, gl], axis=-1)))
    return dirs


def gdn_scan(q, k, v, beta, g, s0):
    qc, kc, vc = to_chunks(q), to_chunks(k), to_chunks(v)
    bt = to_chunks(beta)
    gam = jnp.cumsum(to_chunks(g), axis=-1)
    dv = v.shape[-1]
    strict = jnp.tril(jnp.ones((CHUNK, CHUNK), bool), -1)
    incl = jnp.tril(jnp.ones((CHUNK, CHUNK), bool))
    diff = gam[..., :, None] - gam[..., None, :]
    a_kk = jnp.einsum("nbhtd,nbhsd->nbhts", kc * bt[..., None], kc) * jnp.exp(jnp.where(strict, diff, -jnp.inf))
    rhs = jnp.concatenate([vc * bt[..., None], kc * (bt * jnp.exp(gam))[..., None]], axis=-1)
    sol = lax.linalg.triangular_solve(a_kk + jnp.eye(CHUNK, dtype=F32), rhs, left_side=True, lower=True,
                                      unit_diagonal=True)
    u, w = sol[..., :dv], sol[..., dv:]
    a_qk = jnp.einsum("nbhtd,nbhsd->nbhts", qc, kc) * jnp.exp(jnp.where(incl, diff, -jnp.inf))
    q_dec = qc * jnp.exp(gam)[..., None]
    k_dec = kc * jnp.exp(gam[..., -1:] - gam)[..., None]
    d_end = jnp.exp(gam[..., -1])

    def step(s, inp):
        u_, w_, aqk_, qd_, kd_, de_ = inp
        e = u_ - w_ @ s
        o = qd_ @ s + aqk_ @ e
        s = de_[..., None, None] * s + jnp.einsum("bhsd,bhse->bhde", kd_, e)
        return s, o

    s_fin, o = lax.scan(step, s0, (u, w, a_qk, q_dec, k_dec, d_end))
    return from_chunks(o), s_fin


def gdn_inputs(p, conv_w, a_log, dt_bias):
    qkv = jax.nn.silu(dwconv(p["gdn_qkv"], conv_w).astype(F32))
    q, k, v = [to_heads(a, GDN_HEADS) for a in jnp.split(qkv, 3, axis=-1)]
    q = l2norm(q) * GDN_DH ** -0.5
    k = l2norm(k)
    a = p["gdn_a"].astype(F32)
    bb = p["gdn_b"].astype(F32)
    dirs = []
    for d in range(2):
        sl = slice(d * GDN_HEADS, (d + 1) * GDN_HEADS)
        beta = jax.nn.sigmoid(bb[..., sl]).transpose(0, 2, 1)
        g = (-jnp.exp(a_log[d]) * jax.nn.softplus(a[..., sl] + dt_bias[d])).transpose(0, 2, 1)
        dirs.append((q, k, v, beta, g))
    return dirs


def bidir_scan(scan_fn, ctx_dirs, lat_dirs, s0):
    def flip(arrs):
        return [jnp.flip(a, axis=2) for a in arrs]
    oc_f, sc_f = scan_fn(*ctx_dirs[0], s0)
    ol_f, _ = scan_fn(*lat_dirs[0], sc_f)
    oc_b, sc_b = scan_fn(*flip(ctx_dirs[1]), s0)
    ol_b, _ = scan_fn(*flip(lat_dirs[1]), sc_b)
    return oc_f + jnp.flip(oc_b, axis=2), ol_f + jnp.flip(ol_b, axis=2)


def hyena_filters(length, w_in, b_in, w_mid, b_mid, freq, w_out):
    t = jnp.linspace(0.0, 1.0, length, dtype=F32)[:, None]
    bands = (HY_EMB - 1) // 2
    wpos = 2.0 * math.pi * jnp.arange(length, dtype=F32)[:, None] / length
    f = jnp.linspace(1e-4, bands - 1, bands, dtype=F32)[None]
    z = jnp.concatenate([t, jnp.cos(f * wpos), -jnp.sin(f * wpos)], axis=-1)
    h = jnp.sin(freq[0] * (z @ w_in + b_in))
    for i in range(HY_INNER):
        h = jnp.sin(freq[i + 1] * (h @ w_mid[i] + b_mid[i]))
    h = h @ w_out
    deltas = jnp.abs(jnp.linspace(math.log(HY_TARGET) / HY_SLOW, math.log(HY_TARGET) / HY_FAST, BR_W, dtype=F32))
    decay = jnp.exp(-t * deltas)
    return h[:, :BR_W] * decay, h[:, BR_W:] * decay


def two_sided_long_conv(u, h_f, h_b):
    length = u.shape[1]
    n = 2 * length
    kc = jnp.concatenate([h_f[:1] + h_b[:1], h_f[1:], jnp.zeros_like(h_f[:1]), h_b[:0:-1]], axis=0)
    y = jnp.fft.irfft(jnp.fft.rfft(u, n=n, axis=1) * jnp.fft.rfft(kc, n=n, axis=0)[None], n=n, axis=1)
    return y[:, :length]


def hyena_stream(p_xv, conv_w, conv_b, filt, skip):
    xv = (dwconv(p_xv, conv_w) + conv_b).astype(F32)
    x0, x1, v = jnp.split(xv, 3, axis=-1)
    h_f, h_b = hyena_filters(xv.shape[1], *filt)
    v = v * x1
    v = two_sided_long_conv(v, h_f, h_b) + skip * v
    return v * x0


def merge_branches(u, ys, w_gate, b_gate, w_branch, w_out):
    merged = None
    for i, y in enumerate(ys):
        term = jax.nn.sigmoid(u @ w_gate[i] + b_gate[i]) * (y @ w_branch[i])
        merged = term if merged is None else merged + term
    return merged @ w_out


def hybrid_mixer(u, uc, w_in, w_gate, b_gate, w_branch, w_out, na_rpb, gla_wg2, gla_bg, gla_norm,
                 gdn_conv, gdn_a_log, gdn_dt_bias, gdn_norm, hy_conv, hy_conv_b, hy_filt, hy_skip, with_ctx):
    dt = u.dtype
    b, t, _ = u.shape
    pl = split_cols(u @ w_in)
    pc = split_cols(uc @ w_in)

    qa, ka, va = [to_heads(a, NA_HEADS).astype(F32) for a in jnp.split(pl["na_qkv"], 3, axis=-1)]
    qac, kac, vac = [to_heads(a, NA_HEADS).astype(F32) for a in jnp.split(pc["na_qkv"], 3, axis=-1)]
    oa = neighbourhood_attention(qa, ka, va, kac, vac, na_rpb.astype(F32))

    wg2, bg = gla_wg2.astype(F32), gla_bg.astype(F32)
    s0_b = jnp.zeros((b, GLA_HEADS, GLA_DK, GLA_DV), F32)
    ob_c, ob = bidir_scan(gla_scan, gla_inputs(pc, wg2, bg, None), gla_inputs(pl, wg2, bg, axial_rope(t)), s0_b)

    a_log, dtb = gdn_a_log.astype(F32), gdn_dt_bias.astype(F32)
    s0_c = jnp.zeros((b, GDN_HEADS, GDN_DH, GDN_DH), F32)
    oc_c, oc = bidir_scan(gdn_scan, gdn_inputs(pc, gdn_conv, a_log, dtb), gdn_inputs(pl, gdn_conv, a_log, dtb), s0_c)

    filt = tuple(a.astype(F32) for a in hy_filt)
    skip = hy_skip.astype(F32)
    od = hyena_stream(pl["hy_xv"], hy_conv, hy_conv_b, filt, skip)

    def finish(p, oa_, ob_, oc_, od_):
        return [from_heads(oa_).astype(dt) * jax.nn.silu(p["na_z"]),
                from_heads(rmsnorm(ob_, gla_norm)).astype(dt) * jax.nn.silu(p["gla_z"]),
                from_heads(rmsnorm(oc_, gdn_norm)).astype(dt) * jax.nn.silu(p["gdn_z"]),
                od_.astype(dt) * jax.nn.silu(p["hy_z"])]

    y = merge_branches(u, finish(pl, oa, ob, oc, od), w_gate, b_gate, w_branch, w_out)
    if not with_ctx:
        return y, None
    oa_c = dense_attention(qac, kac, vac)
    od_c = hyena_stream(pc["hy_xv"], hy_conv, hy_conv_b, filt, skip)
    yc = merge_branches(uc, finish(pc, oa_c, ob_c, oc_c, od_c), w_gate, b_gate, w_branch, w_out)
    return y, yc


def setup_inputs(seed: int = 0) -> dict:
    key = jax.random.key(seed)
    ks = iter(jax.random.split(key, 40))

    def nrm(shape, s):
        return jax.random.normal(next(ks), shape, F32) * s

    d = D_MODEL
    dt = jnp.exp(jax.random.uniform(next(ks), (DEPTH, 2, GDN_HEADS), F32, math.log(1e-3), math.log(1e-1)))
    a_log = jnp.log(jax.random.uniform(next(ks), (DEPTH, 2, GDN_HEADS), F32, 1.0, 16.0))
    return {
        "x": nrm((BATCH, SEQ, d), 1.0),
        "c": nrm((BATCH, d), 1.0),
        "ctx": nrm((BATCH, CTX_LEN, d), 1.0),
        "c_ctx": nrm((d,), 1.0),
        "w_mod": nrm((DEPTH, d, 3 * d), 0.5 * d ** -0.5),
        "b_mod": nrm((DEPTH, 3 * d), 0.01),
        "g_pre": 1.0 + nrm((DEPTH, d), 0.01),
        "g_post": 1.0 + nrm((DEPTH, d), 0.01),
        "w_in": nrm((DEPTH, d, N_IN), d ** -0.5),
        "w_gate": nrm((DEPTH, N_BRANCH, d, d), d ** -0.5),
        "b_gate": nrm((DEPTH, N_BRANCH, d), 0.01),
        "w_branch": nrm((DEPTH, N_BRANCH, BR_W, d), BR_W ** -0.5),
        "w_out": nrm((DEPTH, d, d), d ** -0.5),
        "na_rpb": nrm((DEPTH, NA_HEADS, 2 * WIN_R - 1, 2 * WIN_C - 1), 0.1),
        "gla_wg2": nrm((DEPTH, 2, GLA_RANK, GLA_HEADS * GLA_DK // 2), GLA_RANK ** -0.5),
        "gla_bg": nrm((DEPTH, 2, GLA_HEADS * GLA_DK // 2), 0.01),
        "gla_norm": 1.0 + nrm((DEPTH, GLA_DV), 0.01),
        "gdn_conv": nrm((DEPTH, 3 * BR_W, GDN_CONV), GDN_CONV ** -0.5),
        "gdn_a_log": a_log,
        "gdn_dt_bias": jnp.log(jnp.expm1(dt)),
        "gdn_norm": 1.0 + nrm((DEPTH, GDN_DH), 0.01),
        "hy_conv": nrm((DEPTH, 3 * BR_W, HY_CONV), HY_CONV ** -0.5),
        "hy_conv_b": nrm((DEPTH, 3 * BR_W), 0.01),
        "hy_w_in": nrm((DEPTH, HY_EMB, HY_FFN), HY_EMB ** -0.5),
        "hy_b_in": nrm((DEPTH, HY_FFN), 0.01),
        "hy_w_mid": nrm((DEPTH, HY_INNER, HY_FFN, HY_FFN), HY_FFN ** -0.5),
        "hy_b_mid": nrm((DEPTH, HY_INNER, HY_FFN), 0.01),
        "hy_freq": 1.0 + nrm((DEPTH, HY_INNER + 1, HY_FFN), 0.01),
        "hy_w_out": nrm((DEPTH, HY_FFN, 2 * BR_W), 0.05 * HY_FFN ** -0.5),
        "hy_skip": nrm((DEPTH, BR_W), 1.0),
    }


def reference(x, c, ctx, c_ctx, w_mod, b_mod, g_pre, g_post, w_in, w_gate, b_gate, w_branch, w_out,
              na_rpb, gla_wg2, gla_bg, gla_norm, gdn_conv, gdn_a_log, gdn_dt_bias, gdn_norm,
              hy_conv, hy_conv_b, hy_w_in, hy_b_in, hy_w_mid, hy_b_mid, hy_freq, hy_w_out, hy_skip):
    h, hc = x, ctx
    for l in range(DEPTH):
        with_ctx = l < DEPTH - 1
        shift, scale, gate = jnp.split((jax.nn.silu(c) @ w_mod[l] + b_mod[l])[:, None, :], 3, axis=-1)
        shift_c, scale_c, gate_c = jnp.split(jax.nn.silu(c_ctx) @ w_mod[l] + b_mod[l], 3, axis=-1)
        u = rmsnorm(h, g_pre[l]) * (1 + scale) + shift
        uc = rmsnorm(hc, g_pre[l]) * (1 + scale_c) + shift_c
        y, yc = hybrid_mixer(u, uc, w_in[l], w_gate[l], b_gate[l], w_branch[l], w_out[l], na_rpb[l],
                             gla_wg2[l], gla_bg[l], gla_norm[l], gdn_conv[l], gdn_a_log[l], gdn_dt_bias[l],
                             gdn_norm[l], hy_conv[l], hy_conv_b[l],
                             (hy_w_in[l], hy_b_in[l], hy_w_mid[l], hy_b_mid[l], hy_freq[l], hy_w_out[l]),
                             hy_skip[l], with_ctx)
        h = h + gate * rmsnorm(y, g_post[l])
        if with_ctx:
            hc = hc + gate_c * rmsnorm(yc, g_post[l])
    return h
```

```python
import numpy as np
import concourse.bass as bass
import concourse.mybir as mybir
from concourse.bass_utils import run_bass_kernel_spmd

F32 = mybir.dt.float32
BF16 = mybir.dt.bfloat16
AF = mybir.ActivationFunctionType
ALU = mybir.AluOpType
AX = mybir.AxisListType


class Prog:
    import os as _os
    NHW = int(_os.environ.get("NHW", "16"))
    NDMA = NHW + 8

    def __init__(self, nc, same_engine_sync=True):
        self.nc = nc
        self.E = {'pe': nc.tensor, 'act': nc.scalar, 'dve': nc.vector, 'pool': nc.gpsimd, 'sp': nc.sync}
        self.sem = {e: nc.alloc_semaphore(name="sem_" + e) for e in ('pe', 'act', 'dve', 'pool')}
        self.cnt = {e: 0 for e in self.sem}
        self.dsem = [nc.alloc_semaphore(name="dsem%d" % i) for i in range(self.NDMA)]
        self.dval = [0] * self.NDMA
        self.dnext = 0
        self.dnext_sw = 0
        self.known = {e: {} for e in self.E}
        self.evclock = {}
        self.lastw = {}
        self.readers = {}
        self.same = same_engine_sync
        self.nwaits = 0
        self.ninst = 0
        self.pending = {}
        self._n = 0

    def sb(self, name, shape, dtype):
        return self.nc.alloc_sbuf_tensor(name, list(shape), dtype)

    def ps(self, name, shape, dtype=F32):
        return self.nc.alloc_psum_tensor(name, list(shape), dtype)

    def dram(self, name, shape, dtype, kind="Internal"):
        return self.nc.dram_tensor(name, list(shape), dtype, kind=kind)

    def _semof(self, k):
        return self.sem[k] if isinstance(k, str) else self.dsem[k[1]]

    def _wait(self, X, ev):
        k, v = ev
        kn = self.known[X]
        if kn.get(k, 0) >= v:
            return
        self.E[X].wait_ge(self._semof(k), v)
        self.nwaits += 1
        clk = self.evclock.get(ev)
        if clk:
            for kk, vv in clk.items():
                if kn.get(kk, 0) < vv:
                    kn[kk] = vv
        if kn.get(k, 0) < v:
            kn[k] = v

    @staticmethod
    def _keys(ks):
        return [k if isinstance(k, (str, tuple)) else k.name for k in ks]

    def _deps(self, X, r, w, wa=()):
        for key in r:
            for ev in self.lastw.get(key, ()):
                yield ev
        for key in w:
            for ev in self.lastw.get(key, ()):
                yield ev
            for ev in self.readers.get(key, ()):
                yield ev
        for key in wa:
            for ev in self.readers.get(key, ()):
                yield ev

    def _record(self, ev, r, w, wa=()):
        for key in r:
            lst = self.readers.setdefault(key, [])
            lst[:] = [e for e in lst if e[0] != ev[0]]
            lst.append(ev)
        for key in w:
            self.lastw[key] = [ev]
            self.readers[key] = []
        for key in wa:
            lst = self.lastw.setdefault(key, [])
            lst[:] = [e for e in lst if e[0] != ev[0]]
            lst.append(ev)

    def op(self, X, method, r=(), w=(), wa=(), defer=False, **kw):
        r = self._keys(r)
        w = self._keys(w)
        wa = self._keys(wa)
        pend = self.pending.setdefault(X, [[], [], []])
        if defer:
            assert X == 'pe'
            for ev in list(self._deps(X, r, w, wa)):
                if ev[0] == X:
                    continue
                self._wait(X, ev)
            getattr(self.E[X], method)(**kw)
            pend[0] += r
            pend[1] += w
            pend[2] += wa
            self.ninst += 1
            return None
        if pend[0] or pend[1] or pend[2]:
            r = list(dict.fromkeys(r + pend[0]))
            w = list(dict.fromkeys(w + pend[1]))
            wa = list(dict.fromkeys(wa + pend[2]))
            self.pending[X] = [[], [], []]
        for ev in list(self._deps(X, r, w, wa)):
            if ev[0] == X and (X == 'pe' or not self.same):
                continue
            self._wait(X, ev)
        inst = getattr(self.E[X], method)(**kw)
        self.cnt[X] += 1
        c = self.cnt[X]
        inst.then_inc(self.sem[X], 1)
        ev = (X, c)
        clk = dict(self.known[X])
        clk[X] = c
        self.evclock[ev] = clk
        self._record(ev, r, w, wa)
        self.ninst += 1
        return ev

    def pe(self, method, **kw):
        return self.op('pe', method, **kw)

    def act(self, method, **kw):
        return self.op('act', method, **kw)

    def dve(self, method, **kw):
        return self.op('dve', method, **kw)

    def pool(self, method, **kw):
        return self.op('pool', method, **kw)

    def dma(self, Q, out, in_, r=(), w=(), wa=(), **kw):
        r = self._keys(r)
        w = self._keys(w)
        wa = self._keys(wa)
        if Q == 'pool':
            i = self.NHW + self.dnext_sw
            self.dnext_sw = (self.dnext_sw + 1) % (self.NDMA - self.NHW)
        else:
            i = self.dnext
            self.dnext = (i + 1) % self.NHW
        k = ('d', i)
        if self.dval[i] > 0:
            self._wait(Q, (k, self.dval[i]))
        for ev in list(self._deps(Q, r, w, wa)):
            self._wait(Q, ev)
        inst = self.E[Q].dma_start(out=out, in_=in_, **kw)
        self.dval[i] += 16
        inst.then_inc(self.dsem[i], 16)
        ev = (k, self.dval[i])
        clk = dict(self.known[Q])
        clk[k] = self.dval[i]
        self.evclock[ev] = clk
        self._record(ev, r, w, wa)
        self.ninst += 1
        return ev

    def barrier(self):
        for X in ('pe', 'act', 'dve', 'pool', 'sp'):
            for e in self.sem:
                if self.cnt[e] > 0 and not (e == X and X == 'pe'):
                    self._wait(X, (e, self.cnt[e]))
            for i in range(self.NDMA):
                if self.dval[i] > 0:
                    self._wait(X, (('d', i), self.dval[i]))

    def scope(self):
        return Scope(self)

    def finish(self):
        for i in range(self.NDMA):
            if self.dval[i] > 0:
                self._wait('sp', (('d', i), self.dval[i]))
        for e in self.sem:
            if self.cnt[e] > 0:
                self._wait('sp', (e, self.cnt[e]))


D = 2048
TCTX = 256
TLAT = 2048
TT = TCTX + TLAT
NTT = TT // 128
N_TM = 6672
N_FM = 2080
PROWS = 2312
EPS = 1e-6

TM_NAV, TM_NAZ, TM_GLK, TM_GLKS, TM_GLV, TM_GLZ = 0, 512, 1024, 1280, 1536, 2048
TM_GDQKV, TM_GDZ, TM_HYXV, TM_HYZ, TM_GDAB = 2560, 4096, 4608, 6144, 6656
FM_NAQ, FM_NAK, FM_GLQ, FM_GLQS, FM_GLK, FM_GLKS, FM_GLG = 0, 512, 1024, 1280, 1536, 1792, 2048


def prow(tt):
    return 2 + tt * 128 if tt < 2 else 262 + (tt - 2) * 128


def w_in_perm():
    o = {}
    off = 0
    for name, w in (("na_q", 512), ("na_k", 512), ("na_v", 512), ("na_z", 512), ("gla_q", 256), ("gla_k", 256),
                    ("gla_v", 512), ("gla_z", 512), ("gla_g", 32), ("gdn_qkv", 1536), ("gdn_z", 512),
                    ("gdn_a", 8), ("gdn_b", 8), ("hy_xv", 1536), ("hy_z", 512)):
        o[name] = np.arange(off, off + w)
        off += w
    assert off == 7728

    def sw(ix):
        return ix.reshape(-1, 2, 32)[:, ::-1, :].reshape(-1)
    tm = np.concatenate([o["na_v"], o["na_z"], o["gla_k"], sw(o["gla_k"]), o["gla_v"], o["gla_z"], o["gdn_qkv"],
                         o["gdn_z"], o["hy_xv"], o["hy_z"], o["gdn_a"], o["gdn_b"]])
    fm = np.concatenate([o["na_q"], o["na_k"], o["gla_q"], sw(o["gla_q"]), o["gla_k"], sw(o["gla_k"]), o["gla_g"]])
    assert tm.size == N_TM and fm.size == N_FM
    return tm, fm


class Ctx:
    pass


class Scope:
    _uid = [0]

    def __init__(self, P):
        self.P = P
        self.cms = []

    def __enter__(self):
        return self

    def sb(self, name, shape, dtype):
        Scope._uid[0] += 1
        cm = self.P.nc.sbuf_tensor("%s_%d" % (name, Scope._uid[0]), list(shape), dtype)
        t = cm.__enter__()
        self.cms.append(cm)
        return t

    def ps(self, name, shape, dtype=F32):
        Scope._uid[0] += 1
        cm = self.P.nc.psum_tensor("%s_%d" % (name, Scope._uid[0]), list(shape), dtype)
        t = cm.__enter__()
        self.cms.append(cm)
        return t

    def __exit__(self, *a):
        self.P.barrier()
        for cm in reversed(self.cms):
            cm.__exit__(None, None, None)
        return False


def build_program(NB=2, NL=2, dbg=(), stop_after=None, branches="ABCDM"):
    nc = bass.Bass("TRN2", target_bir_lowering=False)
    P = Prog(nc)
    g = Ctx()
    g.nc, g.P, g.NB, g.NL = nc, P, NB, NL
    g.NL_total = 2
    g.branches = branches

    def din(name, shape, dt=F32):
        return nc.dram_tensor(name, list(shape), dt, kind="ExternalInput").ap()

    def dscr(name, shape, dt=F32):
        kind = "ExternalOutput" if name in dbg else "Internal"
        return nc.dram_tensor(name, list(shape), dt, kind=kind).ap()

    g.xin = din("xin", [NB, TT, D])
    g.cT = din("cT", [128, 16, 3])
    g.w_mod_t = din("w_mod_t", [2, 32, 128, 16, 128])
    g.w_mod_g = din("w_mod_g", [2, D, D])
    g.b_mod_col = din("b_mod_col", [2, 128, 32])
    g.b_mod_gate = din("b_mod_gate", [2, 1, D])
    g.g_pre_col = din("g_pre_col", [2, 128, 16])
    g.g_post_row = din("g_post_row", [2, 1, D])
    g.w_tm = din("w_tm", [2, D, N_TM])
    g.w_fm_t = din("w_fm_t", [2, 17, 128, 16, 128])
    g.w_gate_t = din("w_gate_t", [2, 4, 16, 128, 16, 128])
    g.b_gate_col = din("b_gate_col", [2, 128, 4, 16])
    g.w_branch_t = din("w_branch_t", [2, 4, 16, 128, 4, 128])
    g.w_out = din("w_out", [2, D, D])
    g.ident_in = din("ident", [128, 128])
    g.sel_in = din("sel", [3, 3, 128])
    g.na_tab = din("na_tab", [2, 64, 8, 15, 64])
    g.gla_Lm = din("gla_Lm", [2, 128, 128])
    g.rope_c_tm = din("rope_c_tm", [128, 16, 64])
    g.rope_s_tm = din("rope_s_tm", [128, 16, 64])
    g.rope_c_fm = din("rope_c_fm", [128, TLAT])
    g.rope_s_fm = din("rope_s_fm", [128, TLAT])
    g.gla_wg2 = din("gla_wg2", [2, 2, 16, 128])
    g.gla_bg_row = din("gla_bg_row", [2, 1, 2, 128])
    g.gla_norm_row = din("gla_norm_row", [2, 1, 128])
    g.gdn_mbs = din("gdn_mbs", [2, 128, 128])
    g.gdn_mbi = din("gdn_mbi", [2, 128, 128])
    g.ones128 = din("ones128", [128, 128])
    g.gdn_conv_row = din("gdn_conv_row", [2, 5, 1536])
    g.gdn_dtb_row = din("gdn_dtb_row", [2, 1, 8])
    g.gdn_alog_row = din("gdn_alog_row", [2, 1, 8])
    g.gdn_norm_row = din("gdn_norm_row", [2, 1, 128])
    g.hy_w_in = din("hy_w_in", [2, 33, 64])
    g.hy_w_mid = din("hy_w_mid", [2, 2, 64, 64])
    g.hy_w_out = din("hy_w_out", [2, 64, 1024])
    g.hy_freq_col = din("hy_freq_col", [2, 64, 3])
    g.hy_b_col = din("hy_b_col", [2, 64, 3])
    g.hy_conv_row = din("hy_conv_row", [2, 3, 1536])
    g.hy_conv_b_row = din("hy_conv_b_row", [2, 1, 1536])
    g.hy_skip_row = din("hy_skip_row", [2, 1, 512])
    g.hy_zT = [din("hy_zT0", [33, TLAT]), din("hy_zT1", [33, TCTX])]
    g.hy_decay = [din("hy_decay0", [128, 16, 512]), din("hy_decay1", [128, 2, 512])]
    g.hy_FmT = [din("hy_FmT0", [32, 128, 16, 128], BF16), din("hy_FmT1", [4, 128, 2, 128], BF16)]
    g.hy_GT = [din("hy_GT0", [16, 128, 32, 128], BF16), din("hy_GT1", [2, 128, 4, 128], BF16)]

    g.ptm = dscr("ptm", [NB, PROWS, N_TM])
    g.pfm = dscr("pfm", [NB, N_FM, TT])
    g.hbuf = dscr("hbuf", [NB, TT, D])
    g.yfm = dscr("yfm", [NB, 4 * 512, TT], BF16)
    g.khat = [dscr("khat0", [2 * TLAT, 512]), dscr("khat1", [2 * TCTX, 512])]
    g.hy_g0s = dscr("hy_g0s", [TT, 512])
    g.gq = dscr("gq", [TT, GQW])
    g.ybuf = dscr("ybuf", [TT, D])
    g.hy_vss = dscr("hy_vss", [TT, 512])
    g.out = nc.dram_tensor("out", [NB, TLAT, D], F32, kind="ExternalOutput").ap()

    g.ident = P.sb("ident_sb", [128, 128], F32)
    g.sel = P.sb("sel_sb", [3, 3, 128], F32)
    g.uT = P.sb("uT", [128, 16, TT], BF16)
    P.dma('sp', g.ident[:], g.ident_in, w=[g.ident])
    P.dma('sp', g.sel[:], g.sel_in, w=[g.sel])
    g.psA = [P.ps("psA%d" % i, [128, 512], F32) for i in range(4)]
    g.psB = [P.ps("psB%d" % i, [128, 512], F32) for i in range(4)]
    g.modcol = P.sb("modcol", [128, 32, 3], F32)
    g.Acol = P.sb("Acol", [128, 16, 3], F32)
    g.grow = P.sb("grow", [3, D], F32)
    with P.scope() as S:
        zero = S.sb("zero", [8, N_TM], F32)
        P.dve('memset', ap=zero[:], constant=0.0, w=[zero])
        for b in range(NB):
            for r0, n in ((0, 2), (258, 4), (2310, 2)):
                P.dma('sp', g.ptm[b, r0:r0 + n, :], zero[0:n, :], r=[zero], w=[('ptm', b)])

    for l in range(NL):
        phase_mod(g, l)
        if 'D' in g.branches:
            phase_hyfilt(g, l, 0)
            if l < g.NL_total - 1:
                phase_hyfilt(g, l, 1)
        for b in range(NB):
            phase_norm(g, l, b)
            if stop_after == 'norm':
                continue
            phase_proj(g, l, b)
            if stop_after == 'proj':
                continue
            if 'A' in g.branches:
                phase_na(g, l, b)
            if 'B' in g.branches:
                phase_gla(g, l, b)
            if 'C' in g.branches:
                phase_gdn_pre(g, l, b)
                phase_gdn(g, l, b)
            if 'D' in g.branches:
                phase_hy(g, l, b, 0)
                if l < g.NL_total - 1:
                    phase_hy(g, l, b, 1)
            if 'M' in g.branches:
                phase_merge(g, l, b)
    P.finish()
    return nc, g


def phase_mod(g, l):
    P, NB = g.P, g.NB
    with P.scope() as S:
        _phase_mod(g, l, S)


def _phase_mod(g, l, S):
    P = g.P
    g.cT_sb = S.sb("cT_sb", [128, 16, 3], F32)
    g.scT = S.sb("scT", [128, 16, 3], F32)
    g.mslab = [S.sb("mslab%d" % i, [128, 16, 128], F32) for i in range(2)]
    g.gslab = [S.sb("gslab%d" % i, [128, 4, 512], F32) for i in range(2)]
    g.bmc = S.sb("bmc", [128, 32], F32)
    g.gpc = S.sb("gpc", [128, 16], F32)
    g.brow = S.sb("brow", [3, D], F32)
    P.dma('sp', g.cT_sb[:], g.cT, w=[g.cT_sb])
    P.act('activation', out=g.scT[:], in_=g.cT_sb[:], func=AF.Silu, r=[g.cT_sb], w=[g.scT])
    ps = g.psA[0]
    for ft in range(32):
        slab = g.mslab[ft % 2]
        P.dma('sp', slab[:], g.w_mod_t[l, ft], w=[slab])
        for kt in range(16):
            P.pe('matmul', out=ps[:, ft * 3:(ft + 1) * 3], lhsT=slab[:, kt, :], rhs=g.scT[:, kt, :],
                 start=(kt == 0), stop=(kt == 15), r=[slab, g.scT], w=[ps])
    P.dma('sp', g.bmc[:], g.b_mod_col[l], w=[g.bmc])
    P.dma('sp', g.gpc[:], g.g_pre_col[l], w=[g.gpc])
    for j in range(3):
        P.dve('tensor_tensor', out=g.modcol[:, :, j], in0=ps[:, 0:96].rearrange("p (f j) -> p f j", j=3)[:, :, j],
              in1=g.bmc[:], op=ALU.add, r=[ps, g.bmc], w=[g.modcol])
        P.dve('scalar_tensor_tensor', out=g.Acol[:, :, j], in0=g.modcol[:, 16:32, j], scalar=1.0, in1=g.gpc[:],
              op0=ALU.add, op1=ALU.mult, r=[g.modcol, g.gpc], w=[g.Acol])
    psg = g.psA[1:3]
    for jc in range(4):
        for kq in range(4):
            slab = g.gslab[(jc * 4 + kq) % 2]
            P.dma('sp', slab[:], g.w_mod_g[l, kq * 512:(kq + 1) * 512, jc * 512:(jc + 1) * 512]
                  .rearrange("(kt p) f -> p kt f", p=128), w=[slab])
            for k4 in range(4):
                kt = kq * 4 + k4
                P.pe('matmul', out=psg[jc % 2][0:3, :], lhsT=g.scT[:, kt, :], rhs=slab[:, k4, :],
                     start=(kt == 0), stop=(kt == 15), r=[slab, g.scT], w=[psg[jc % 2]])
        P.act('activation', out=g.grow[:, jc * 512:(jc + 1) * 512], in_=psg[jc % 2][0:3, :], func=AF.Copy,
              r=[psg[jc % 2]], w=[g.grow])
    P.dma('sp', g.brow[:], g.b_mod_gate[l].partition_broadcast(3), w=[g.brow])
    P.dve('tensor_tensor', out=g.grow[:], in0=g.grow[:], in1=g.brow[:], op=ALU.add, r=[g.brow, g.grow], w=[g.grow])
    P.dma('sp', g.brow[:], g.g_post_row[l].partition_broadcast(3), r=[], w=[g.brow])
    P.dve('tensor_tensor', out=g.grow[:], in0=g.grow[:], in1=g.brow[:], op=ALU.mult, r=[g.brow, g.grow], w=[g.grow])


def phase_norm(g, l, b):
    with g.P.scope() as S:
        _phase_norm(g, l, b, S)


def _phase_norm(g, l, b, S):
    P = g.P
    g.hx = [S.sb("hx%d" % i, [128, D], F32) for i in range(2)]
    g.xr = [S.sb("xr%d" % i, [128, D], F32) for i in range(2)]
    g.ss = [S.sb("ss%d" % i, [128, 2], F32) for i in range(2)]
    src = g.xin if l == 0 else g.hbuf
    srckey = () if l == 0 else [('hbuf', b)]
    for tt in range(NTT):
        hx, xr, ss = g.hx[tt % 2], g.xr[tt % 2], g.ss[tt % 2]
        j = 2 if tt < 2 else b
        P.dma('sp', hx[:], src[b, tt * 128:(tt + 1) * 128, :], r=srckey, w=[hx])
        P.act('activation', out=xr[:], in_=hx[:], func=AF.Square, accum_out=ss[:, 0:1], r=[hx], w=[xr, ss])
        P.dve('tensor_scalar', out=ss[:, 1:2], in0=ss[:, 0:1], scalar1=1.0 / D, scalar2=EPS, op0=ALU.mult,
              op1=ALU.add, r=[ss], w=[ss])
        P.act('activation', out=ss[:, 1:2], in_=ss[:, 1:2], func=AF.Sqrt, r=[ss], w=[ss])
        P.dve('reciprocal', out=ss[:, 1:2], in_=ss[:, 1:2], r=[ss], w=[ss])
        P.dve('tensor_scalar', out=xr[:], in0=hx[:], scalar1=ss[:, 1:2], scalar2=None, op0=ALU.mult,
              r=[hx, ss], w=[xr])
        for kt in range(16):
            pt = g.psA[kt % 4]
            P.pe('transpose', out=pt[:, 0:128], in_=xr[:, kt * 128:(kt + 1) * 128], identity=g.ident[:],
                 r=[xr, g.ident], w=[pt])
            P.act('activation', out=g.uT[:, kt, tt * 128:(tt + 1) * 128], in_=pt[:, 0:128], func=AF.Identity,
                  scale=g.Acol[:, kt, j:j + 1], bias=g.modcol[:, kt, j:j + 1], r=[pt, g.Acol, g.modcol],
                  wa=[g.uT])


def phase_proj(g, l, b):
    with g.P.scope() as S:
        _phase_proj(g, l, b, S)


def _phase_proj(g, l, b, S):
    P = g.P
    g.wst = [S.sb("wst%d" % i, [128, 16, 512], F32) for i in range(2)]
    g.wbf = [S.sb("wbf%d" % i, [128, 16, 512], BF16) for i in range(2)]
    g.stg = [S.sb("stg%d" % i, [128, 512], F32) for i in range(3)]
    nchunk = (N_TM + 511) // 512
    it = 0
    for cc in range(nchunk):
        c0 = cc * 512
        cw = min(512, N_TM - c0)
        wst, wbf = g.wst[cc % 2], g.wbf[cc % 2]
        for h4 in range(4):
            P.dma('sp', wst[:, h4 * 4:(h4 + 1) * 4, 0:cw],
                  g.w_tm[l, h4 * 512:(h4 + 1) * 512, c0:c0 + cw].rearrange("(kt p) f -> p kt f", p=128),
                  w=[(wst.name, h4)])
            if h4 % 2 == 0:
                P.dve('tensor_copy', out=wbf[:, h4 * 4:(h4 + 1) * 4, 0:cw], in_=wst[:, h4 * 4:(h4 + 1) * 4, 0:cw],
                      r=[(wst.name, h4)], w=[(wbf.name, h4)])
            else:
                P.act('activation', out=wbf[:, h4 * 4:(h4 + 1) * 4, 0:cw], in_=wst[:, h4 * 4:(h4 + 1) * 4, 0:cw],
                      func=AF.Copy, r=[(wst.name, h4)], w=[(wbf.name, h4)])
        for tt in range(NTT):
            ps = g.psA[it % 4]
            stg = g.stg[it % 3]
            it += 1
            for kt in range(16):
                P.pe('matmul', out=ps[:, 0:cw], lhsT=g.uT[:, kt, tt * 128:(tt + 1) * 128], rhs=wbf[:, kt, 0:cw],
                     start=(kt == 0), stop=(kt == 15), r=[g.uT, (wbf.name, kt // 4)], w=[ps], defer=(kt < 15))
            P.act('activation', out=stg[:, 0:cw], in_=ps[:, 0:cw], func=AF.Copy, r=[ps], w=[stg])
            P.dma('pool', g.ptm[b, prow(tt):prow(tt) + 128, c0:c0 + cw], stg[:, 0:cw], r=[stg], wa=[('ptm', b)])
    nft = (N_FM + 127) // 128
    chunks = [(0, 256)] + [(256 + i * 512, 512) for i in range(4)]
    for ft in range(nft):
        f0 = ft * 128
        fw = min(128, N_FM - f0)
        wst, wbf = g.wst[ft % 2], g.wbf[ft % 2]
        P.dma('sp', wst[:, :, 0:128], g.w_fm_t[l, ft], w=[(wst.name, i) for i in range(4)])
        P.dve('tensor_copy', out=wbf[:, :, 0:fw], in_=wst[:, :, 0:fw], r=[(wst.name, i) for i in range(4)],
              w=[(wbf.name, i) for i in range(4)])
        for (t0, n) in chunks:
            ps = g.psA[it % 4]
            stg = g.stg[it % 3]
            it += 1
            for kt in range(16):
                P.pe('matmul', out=ps[0:fw, 0:n], lhsT=wbf[:, kt, 0:fw], rhs=g.uT[:, kt, t0:t0 + n],
                     start=(kt == 0), stop=(kt == 15), r=[g.uT, (wbf.name, kt // 4)], w=[ps], defer=(kt < 15))
            P.act('activation', out=stg[0:fw, 0:n], in_=ps[0:fw, 0:n], func=AF.Copy, r=[ps], w=[stg])
            P.dma('pool', g.pfm[b, f0:f0 + fw, t0:t0 + n], stg[0:fw, 0:n], r=[stg], wa=[('pfm', b)])


def prep_shared(inp):
    f = lambda a: np.ascontiguousarray(np.asarray(a, dtype=np.float32))
    tm, fm = w_in_perm()
    sh = {}
    w_mod = np.asarray(inp["w_mod"], dtype=np.float32)
    sh["w_mod_t"] = f(w_mod[:, :, :2 * D].reshape(2, 16, 128, 32, 128).transpose(0, 3, 2, 1, 4))
    sh["w_mod_g"] = f(w_mod[:, :, 2 * D:])
    b_mod = f(inp["b_mod"])
    sh["b_mod_col"] = f(b_mod[:, :2 * D].reshape(2, 32, 128).transpose(0, 2, 1))
    sh["b_mod_gate"] = f(b_mod[:, 2 * D:].reshape(2, 1, D))
    sh["g_pre_col"] = f(f(inp["g_pre"]).reshape(2, 16, 128).transpose(0, 2, 1))
    sh["g_post_row"] = f(f(inp["g_post"]).reshape(2, 1, D))
    sh["w_gate_t"] = f(np.asarray(inp["w_gate"], dtype=np.float32).reshape(2, 4, 16, 128, 16, 128)
                       .transpose(0, 1, 4, 3, 2, 5))
    sh["b_gate_col"] = f(f(inp["b_gate"]).reshape(2, 4, 16, 128).transpose(0, 3, 1, 2))
    sh["w_branch_t"] = f(np.asarray(inp["w_branch"], dtype=np.float32).reshape(2, 4, 4, 128, 16, 128)
                         .transpose(0, 1, 4, 3, 2, 5))
    sh["w_out"] = f(inp["w_out"])
    w_in = np.asarray(inp["w_in"], dtype=np.float32)
    sh["w_tm"] = f(w_in[:, :, tm])
    wfm = np.zeros((2, D, 17 * 128), np.float32)
    wfm[:, :, :N_FM] = w_in[:, :, fm]
    sh["w_fm_t"] = f(wfm.reshape(2, 16, 128, 17, 128).transpose(0, 3, 2, 1, 4))
    sh["ident"] = np.eye(128, dtype=np.float32)
    sel = np.zeros((3, 3, 128), np.float32)
    for j in range(3):
        sel[j, j, :] = 1.0
    sh["sel"] = sel
    sh["na_tab"] = na_table(inp["na_rpb"])
    sh.update(gla_consts())
    sh["gla_wg2"] = f(inp["gla_wg2"])
    sh["gla_bg_row"] = f(f(inp["gla_bg"]).reshape(2, 1, 2, 128))
    sh["gla_norm_row"] = f(f(inp["gla_norm"]).reshape(2, 1, 128))
    sh.update(gdn_consts())
    sh["gdn_conv_row"] = f(f(inp["gdn_conv"]).transpose(0, 2, 1))
    sh["gdn_dtb_row"] = f(f(inp["gdn_dt_bias"]).reshape(2, 1, 8))
    sh["gdn_alog_row"] = f(f(inp["gdn_a_log"]).reshape(2, 1, 8))
    sh["gdn_norm_row"] = f(f(inp["gdn_norm"]).reshape(2, 1, 128))
    sh["hy_w_in"] = f(inp["hy_w_in"])
    sh["hy_w_mid"] = f(inp["hy_w_mid"])
    sh["hy_w_out"] = f(inp["hy_w_out"])
    sh["hy_freq_col"] = f(f(inp["hy_freq"]).transpose(0, 2, 1))
    sh["hy_b_col"] = f(np.concatenate([f(inp["hy_b_in"])[:, None, :], f(inp["hy_b_mid"])], axis=1).transpose(0, 2, 1))
    sh["hy_conv_row"] = f(f(inp["hy_conv"]).transpose(0, 2, 1))
    sh["hy_conv_b_row"] = f(f(inp["hy_conv_b"]).reshape(2, 1, 1536))
    sh["hy_skip_row"] = f(f(inp["hy_skip"]).reshape(2, 1, 512))
    for li, L in enumerate((TLAT, TCTX)):
        hc = hy_consts(L)
        sh["hy_zT%d" % li] = hc["zT"]
        sh["hy_decay%d" % li] = hc["decay"]
        sh["hy_FmT%d" % li] = hc["FmT"]
        sh["hy_GT%d" % li] = hc["GT"]
    return sh


def prep_core(inp, bs):
    x = np.asarray(inp["x"], dtype=np.float32)
    ctx = np.asarray(inp["ctx"], dtype=np.float32)
    c = np.asarray(inp["c"], dtype=np.float32)
    c_ctx = np.asarray(inp["c_ctx"], dtype=np.float32)
    d = {}
    d["xin"] = np.ascontiguousarray(np.stack([np.concatenate([ctx[b], x[b]], axis=0) for b in bs]))
    cb = [c[b] for b in bs]
    while len(cb) < 2:
        cb.append(cb[0])
    cm = np.stack(cb[:2] + [c_ctx], axis=1)
    d["cT"] = np.ascontiguousarray(cm.reshape(16, 128, 3).transpose(1, 0, 2))
    return d


def na_table(rpb):
    rpb = np.asarray(rpb, dtype=np.float32)
    kc = np.arange(64)[:, None]
    qc = np.arange(64)[None, :]
    cstart = np.clip(qc - 8, 0, 48)
    valid = (kc >= cstart) & (kc < cstart + 16)
    dc = np.clip(kc - qc + 15, 0, 30)
    tab = rpb[:, :, :, dc]
    tab = np.where(valid[None, None, None], tab, np.float32(-30000.0))
    return np.ascontiguousarray(tab.transpose(0, 3, 1, 2, 4).astype(np.float32))


def rowtok(row):
    return 64 * row


def phase_na(g, l, b):
    with g.P.scope() as S:
        _phase_na(g, l, b, S)


def _phase_na(g, l, b, S):
    P = g.P
    with_ctx = l < g.NL_total - 1
    stage = S.sb("na_stage", [128, TT], F32)
    qT = S.sb("na_qT", [128, TT], BF16)
    kTh = [S.sb("na_kT%d" % i, [128, TT], BF16) for i in range(2)]
    for i in range(2):
        P.pool('memset', ap=kTh[i][:], constant=0.0, w=[kTh[i]])
    st2 = S.sb("na_st2", [64, 36, 128], F32)
    vb = S.sb("na_vb", [64, 36, 2, 65], BF16)
    oA = S.sb("na_oA", [64, 36, 128], F32)
    Tb = S.sb("na_Tb", [64, 2, 15, 64], F32)
    sw = [S.sb("na_sw%d" % i, [64, 512], F32) for i in range(2)]
    pw = [S.sb("na_pw%d" % i, [64, 512], BF16) for i in range(2)]
    pcx = [S.sb("na_pc%d" % i, [64, 256], BF16) for i in range(2)]
    rec = [S.sb("na_rec%d" % i, [64, 1], F32) for i in range(2)]
    yst = S.sb("na_yst", [128, TT], BF16)
    psW, psC, psO, psT = g.psA[0:2], g.psA[2:4], g.psB[0:2], g.psB[2]
    row0 = 0 if with_ctx else 4
    for hp in range(4):
        P.dma('sp', stage[:], g.pfm[b, FM_NAQ + hp * 128:FM_NAQ + (hp + 1) * 128, :], r=[('pfm', b)], w=[stage])
        P.dve('tensor_copy', out=qT[:], in_=stage[:], r=[stage], w=[qT])
        P.dma('sp', stage[:], g.pfm[b, FM_NAK + hp * 128:FM_NAK + (hp + 1) * 128, :], r=[('pfm', b)], w=[stage])
        for i in range(2):
            P.dve('tensor_copy', out=kTh[i][64 * i:64 * i + 64, :], in_=stage[64 * i:64 * i + 64, :], r=[stage],
                  w=[kTh[i]])
        c0 = TM_NAV + hp * 128
        P.dma('sp', st2[:, 0:4, :], g.ptm[b, 2:258, c0:c0 + 128].rearrange("(r p) c -> p r c", p=64),
              r=[('ptm', b)], w=[st2])
        P.dma('sp', st2[:, 4:36, :], g.ptm[b, 262:2310, c0:c0 + 128].rearrange("(r p) c -> p r c", p=64),
              r=[('ptm', b)], wa=[st2])
        P.pool('memset', ap=vb[:], constant=1.0, w=[vb])
        P.dve('tensor_copy', out=vb[:, :, :, 0:64], in_=st2[:].rearrange("p r (h d) -> p r h d", h=2),
              r=[st2], w=[vb])
        c0 = TM_NAZ + hp * 128
        P.dma('sp', st2[:, 0:4, :], g.ptm[b, 2:258, c0:c0 + 128].rearrange("(r p) c -> p r c", p=64),
              r=[('ptm', b)], w=[st2])
        P.dma('sp', st2[:, 4:36, :], g.ptm[b, 262:2310, c0:c0 + 128].rearrange("(r p) c -> p r c", p=64),
              r=[('ptm', b)], wa=[st2])
        P.act('activation', out=st2[:], in_=st2[:], func=AF.Silu, r=[st2], w=[st2])
        P.dma('sp', Tb[:], g.na_tab[l, :, 2 * hp:2 * hp + 2, :, :], w=[Tb])
        it = 0
        for h2 in range(2):
            hb = 64 * h2
            for row in range(row0, 36):
                i2 = it % 2
                it += 1
                tq = rowtok(row)
                qv = qT[:, tq:tq + 64]
                kT = kTh[h2]
                lat = row >= 4
                if lat:
                    r_ = row - 4
                    rs = min(max(r_ - 4, 0), 24)
                    dr0 = rs - r_ + 7
                    for j in range(8):
                        tk = rowtok(4 + rs + j)
                        P.pe('matmul', out=psW[i2][0:64, j * 64:(j + 1) * 64], lhsT=kT[:, tk:tk + 64],
                             rhs=qv, start=True, stop=True, r=[kT, qT], w=[psW[i2]])
                for i in range(4):
                    P.pe('matmul', out=psC[i2][0:64, i * 64:(i + 1) * 64], lhsT=kT[:, 64 * i:64 * i + 64],
                         rhs=qv, start=True, stop=True, r=[kT, qT], w=[psC[i2]])
                if lat:
                    P.dve('scalar_tensor_tensor', out=sw[i2][:], in0=psW[i2][0:64, :], scalar=0.125,
                          in1=Tb[:, h2, dr0:dr0 + 8, :].rearrange("p a b -> p (a b)"), op0=ALU.mult, op1=ALU.add,
                          r=[psW[i2], Tb], w=[sw[i2]])
                    P.act('activation', out=pw[i2][:], in_=sw[i2][:], func=AF.Exp, r=[sw[i2]], w=[pw[i2]])
                P.act('activation', out=pcx[i2][:], in_=psC[i2][0:64, 0:256], func=AF.Exp, scale=0.125,
                      r=[psC[i2]], w=[pcx[i2]])
                if lat:
                    for j in range(8):
                        P.pe('matmul', out=psO[i2][0:64, 0:65], lhsT=pw[i2][:, j * 64:(j + 1) * 64],
                             rhs=vb[:, 4 + rs + j, h2, :], start=(j == 0), stop=False, r=[pw[i2], vb], w=[psO[i2]])
                for i in range(4):
                    P.pe('matmul', out=psO[i2][0:64, 0:65], lhsT=pcx[i2][:, i * 64:(i + 1) * 64],
                         rhs=vb[:, i, h2, :], start=(i == 0 and not lat), stop=(i == 3), r=[pcx[i2], vb],
                         w=[psO[i2]])
                P.dve('reciprocal', out=rec[i2][:], in_=psO[i2][0:64, 64:65], r=[psO[i2]], w=[rec[i2]])
                P.dve('tensor_scalar', out=oA[:, row, hb:hb + 64], in0=psO[i2][0:64, 0:64], scalar1=rec[i2][:, 0:1],
                      scalar2=None, op0=ALU.mult, r=[psO[i2], rec[i2]], wa=[oA])
        P.dve('tensor_tensor', out=oA[:], in0=oA[:], in1=st2[:], op=ALU.mult, r=[st2, oA], w=[oA])
        for row in range(row0, 36):
            P.pe('transpose', out=psT[:, 0:64], in_=oA[:, row, :], identity=g.ident[0:64, 0:64],
                 r=[oA, g.ident], w=[psT])
            P.act('activation', out=yst[:, 64 * row:64 * row + 64], in_=psT[:, 0:64], func=AF.Copy,
                  r=[psT], wa=[yst])
        t0 = 64 * row0
        P.dma('sp', g.yfm[b, hp * 128:(hp + 1) * 128, t0:TT], yst[:, t0:TT], r=[yst], wa=[('yfm', b)])


TWO_PI = 2.0 * np.pi


def hy_consts(L):
    import ml_dtypes
    N = 2 * L
    ntt = L // 128
    t = np.arange(L, dtype=np.float64)
    f = np.arange(L, dtype=np.float64)
    ang = 2.0 * np.pi * np.outer(t, f) / N
    Fm = np.empty((L, N), np.float64)
    Fm[:, :L] = np.cos(ang)
    Fm[:, L:] = -np.sin(ang)
    Fm[:, L] = np.cos(np.pi * t)
    G = np.empty((N, L), np.float64)
    G[:L, :] = 2.0 * np.cos(ang.T) / N
    G[0, :] = 1.0 / N
    G[L:, :] = -2.0 * np.sin(ang.T) / N
    G[L, :] = np.cos(np.pi * t) / N
    FmT = Fm.reshape(ntt, 128, 2 * ntt, 128).transpose(2, 1, 0, 3)
    GT = G.reshape(2 * ntt, 128, ntt, 128).transpose(2, 1, 0, 3)
    tt_ = np.linspace(0.0, 1.0, L, dtype=np.float32)[:, None]
    bands = 16
    wpos = (np.float32(2.0 * np.pi) * np.arange(L, dtype=np.float32)[:, None] / np.float32(L)).astype(np.float32)
    fb = np.linspace(1e-4, bands - 1, bands, dtype=np.float32)[None]
    z = np.concatenate([tt_, np.cos(fb * wpos), -np.sin(fb * wpos)], axis=-1).astype(np.float32)
    deltas = np.abs(np.linspace(np.log(1e-2) / 1.5, np.log(1e-2) / 0.3, 512, dtype=np.float32))
    decay = np.exp(-tt_ * deltas).astype(np.float32)
    return {
        "FmT": np.ascontiguousarray(FmT).astype(ml_dtypes.bfloat16),
        "GT": np.ascontiguousarray(GT).astype(ml_dtypes.bfloat16),
        "zT": np.ascontiguousarray(z.T),
        "decay": np.ascontiguousarray(decay.reshape(ntt, 128, 512).transpose(1, 0, 2)),
    }


def phase_hyfilt(g, l, li):
    with g.P.scope() as S:
        _phase_hyfilt(g, l, li, S)


def _phase_hyfilt(g, l, li, S):
    P = g.P
    L = (TLAT, TCTX)[li]
    ntt = L // 128
    wi = S.sb("hf_wi", [33, 64], F32)
    wm = S.sb("hf_wm", [64, 2, 64], F32)
    wo = S.sb("hf_wo", [64, 1024], F32)
    fcol = S.sb("hf_fcol", [64, 3], F32)
    bcol = S.sb("hf_bcol", [64, 3], F32)
    fs = S.sb("hf_fs", [64, 3], F32)
    fb = S.sb("hf_fb", [64, 3], F32)
    zT = S.sb("hf_zT", [33, L], F32)
    hT = [S.sb("hf_hT%d" % i, [64, L], F32) for i in range(2)]
    ua = [S.sb("hf_ua%d" % i, [64, 512], F32) for i in range(2)]
    ub = [S.sb("hf_ub%d" % i, [64, 512], F32) for i in range(2)]
    dec = S.sb("hf_dec", [128, ntt, 512], F32)
    hfb = [S.sb("hf_hfb%d" % i, [128, 512], F32) for i in range(2)]
    hsb = S.sb("hf_hsb", [128, ntt, 512], BF16)
    hdb = S.sb("hf_hdb", [128, ntt, 512], BF16)
    fmt = [S.sb("hf_fmt%d" % i, [128, ntt, 128], BF16) for i in range(2)]
    kst = [S.sb("hf_kst%d" % i, [128, 512], F32) for i in range(2)]
    P.dma('sp', wi[:], g.hy_w_in[l], w=[wi])
    P.dma('sp', wm[:], g.hy_w_mid[l].rearrange("i k n -> k i n"), w=[wm])
    P.dma('sp', wo[:], g.hy_w_out[l], w=[wo])
    P.dma('sp', fcol[:], g.hy_freq_col[l], w=[fcol])
    P.dma('sp', bcol[:], g.hy_b_col[l], w=[bcol])
    P.dma('sp', zT[:], g.hy_zT[li], w=[zT])
    P.dma('sp', dec[:], g.hy_decay[li], w=[dec])
    P.dve('tensor_scalar', out=fs[:], in0=fcol[:], scalar1=1.0 / TWO_PI, scalar2=None, op0=ALU.mult,
          r=[fcol], w=[fs])
    P.dve('tensor_tensor', out=fb[:], in0=fs[:], in1=bcol[:], op=ALU.mult, r=[fs, bcol], w=[fb])
    nch = (L + 511) // 512
    cwid = min(512, L)
    it = 0
    for i in range(3):
        dst = hT[i % 2]
        for c in range(nch):
            ps = g.psA[it % 4]
            i2 = it % 2
            it += 1
            if i == 0:
                P.pe('matmul', out=ps[0:64, 0:cwid], lhsT=wi[:], rhs=zT[:, c * cwid:(c + 1) * cwid], start=True,
                     stop=True, r=[wi, zT], w=[ps])
            else:
                src = hT[(i - 1) % 2]
                P.pe('matmul', out=ps[0:64, 0:cwid], lhsT=wm[:, i - 1, :], rhs=src[:, c * cwid:(c + 1) * cwid],
                     start=True, stop=True, r=[wm, src], w=[ps])
            P.act('activation', out=ua[i2][:, 0:cwid], in_=ps[0:64, 0:cwid], func=AF.Identity,
                  scale=fs[:, i:i + 1], bias=fb[:, i:i + 1], r=[ps, fs, fb], w=[ua[i2]])
            P.dve('scalar_tensor_tensor', out=ub[i2][:, 0:cwid], in0=ua[i2][:, 0:cwid], scalar=0.5,
                  in1=ua[i2][:, 0:cwid], op0=ALU.is_gt, op1=ALU.subtract, r=[ua[i2]], w=[ub[i2]])
            P.dve('scalar_tensor_tensor', out=ub[i2][:, 0:cwid], in0=ua[i2][:, 0:cwid], scalar=-0.5,
                  in1=ub[i2][:, 0:cwid], op0=ALU.is_lt, op1=ALU.subtract, r=[ua[i2], ub[i2]], w=[ub[i2]])
            P.act('activation', out=dst[:, c * cwid:(c + 1) * cwid], in_=ub[i2][:, 0:cwid], func=AF.Sin,
                  scale=TWO_PI, r=[ub[i2]], wa=[dst])
    h3 = hT[0]
    for tt in range(ntt):
        for half in range(2):
            ps = g.psA[it % 4]
            it += 1
            P.pe('matmul', out=ps[:, :], lhsT=h3[:, tt * 128:(tt + 1) * 128], rhs=wo[:, half * 512:(half + 1) * 512],
                 start=True, stop=True, r=[h3, wo], w=[ps])
            P.dve('tensor_tensor', out=hfb[half][:], in0=ps[:, :], in1=dec[:, tt, :], op=ALU.mult,
                  r=[ps, dec], w=[hfb[half]])
        P.dve('tensor_tensor', out=hsb[:, tt, :], in0=hfb[0][:], in1=hfb[1][:], op=ALU.add,
              r=[hfb[0], hfb[1]], wa=[hsb])
        P.pool('tensor_tensor', out=hdb[:, tt, :], in0=hfb[0][:], in1=hfb[1][:], op=ALU.subtract,
               r=[hfb[0], hfb[1]], wa=[hdb])
    for rt in range(2 * ntt):
        fm = fmt[rt % 2]
        ks = kst[rt % 2]
        P.dma('sp', fm[:], g.hy_FmT[li][rt], w=[fm])
        ps = g.psA[it % 4]
        it += 1
        src = hsb if rt < ntt else hdb
        for tt in range(ntt):
            P.pe('matmul', out=ps[:, :], lhsT=fm[:, tt, :], rhs=src[:, tt, :], start=(tt == 0), stop=(tt == ntt - 1),
                 r=[fm, src], w=[ps])
        P.act('activation', out=ks[:], in_=ps[:, :], func=AF.Copy, r=[ps], w=[ks])
        if rt == ntt:
            ps2 = g.psA[it % 4]
            it += 1
            for tt in range(ntt):
                P.pe('matmul', out=ps2[:, :], lhsT=fm[:, tt, :], rhs=hsb[:, tt, :], start=(tt == 0),
                     stop=(tt == ntt - 1), r=[fm, hsb], w=[ps2])
            P.act('activation', out=ks[0:1, :], in_=ps2[0:1, :], func=AF.Copy, r=[ps2, ks], w=[ks])
        P.dma('sp', g.khat[li][rt * 128:(rt + 1) * 128, :], ks[:], r=[ks], wa=[('khat', li)])


def phase_hy(g, l, b, li):
    with g.P.scope() as S:
        _phase_hy(g, l, b, li, S)


def _phase_hy(g, l, b, li, S):
    P = g.P
    L = (TLAT, TCTX)[li]
    ntt = L // 128
    tile0 = 2 if li == 0 else 0
    tok0 = 256 if li == 0 else 0
    vvb = S.sb("hy_vvb", [128, ntt, 512], BF16)
    with P.scope() as S1:
        wrow = S1.sb("hy_wrow", [128, 3, 1536], F32)
        brow = S1.sb("hy_brow", [128, 1536], F32)
        srow = S1.sb("hy_srow", [128, 512], F32)
        xs = [S1.sb("hy_xs%d" % i, [128, 1536], F32) for i in range(3)]
        acc = S1.sb("hy_acc", [128, 1536], F32)
        tmp = S1.sb("hy_tmp", [128, 1536], F32)
        zt = S1.sb("hy_zt", [128, 512], F32)
        vv = S1.sb("hy_vv", [128, 512], F32)
        g0 = S1.sb("hy_g0", [128, 512], F32)
        vs = S1.sb("hy_vs", [128, 512], F32)
        for k in range(3):
            P.dma('sp', wrow[:, k, :], g.hy_conv_row[l, k:k + 1, :].partition_broadcast(128), wa=[wrow])
        P.dma('sp', brow[:], g.hy_conv_b_row[l].partition_broadcast(128), w=[brow])
        P.dma('sp', srow[:], g.hy_skip_row[l].partition_broadcast(128), w=[srow])
        for tt in range(ntt):
            base = prow(tile0 + tt)
            for k in range(3):
                P.dma('sp', xs[k][:], g.ptm[b, base + k - 1:base + k - 1 + 128, TM_HYXV:TM_HYXV + 1536],
                      r=[('ptm', b)], w=[xs[k]])
            P.dma('sp', zt[:], g.ptm[b, base:base + 128, TM_HYZ:TM_HYZ + 512], r=[('ptm', b)], w=[zt])
            P.dve('tensor_tensor', out=acc[:], in0=xs[0][:], in1=wrow[:, 0, :], op=ALU.mult, r=[xs[0], wrow], w=[acc])
            P.pool('tensor_tensor', out=tmp[:], in0=xs[1][:], in1=wrow[:, 1, :], op=ALU.mult, r=[xs[1], wrow],
                   w=[tmp])
            P.dve('tensor_tensor', out=acc[:], in0=acc[:], in1=tmp[:], op=ALU.add, r=[tmp, acc], w=[acc])
            P.pool('tensor_tensor', out=tmp[:], in0=xs[2][:], in1=wrow[:, 2, :], op=ALU.mult, r=[xs[2], wrow],
                   w=[tmp])
            P.dve('tensor_tensor', out=acc[:], in0=acc[:], in1=tmp[:], op=ALU.add, r=[tmp, acc], w=[acc])
            P.dve('tensor_tensor', out=acc[:], in0=acc[:], in1=brow[:], op=ALU.add, r=[brow, acc], w=[acc])
            P.dve('tensor_tensor', out=vv[:], in0=acc[:, 1024:1536], in1=acc[:, 512:1024], op=ALU.mult,
                  r=[acc], w=[vv])
            P.pool('tensor_copy', out=vvb[:, tt, :], in_=vv[:], r=[vv], wa=[vvb])
            P.act('activation', out=zt[:], in_=zt[:], func=AF.Silu, r=[zt], w=[zt])
            P.dve('tensor_tensor', out=g0[:], in0=acc[:, 0:512], in1=zt[:], op=ALU.mult, r=[acc, zt], w=[g0])
            P.pool('tensor_tensor', out=vs[:], in0=vv[:], in1=srow[:], op=ALU.mult, r=[vv, srow], w=[vs])
            P.dve('tensor_tensor', out=vs[:], in0=vs[:], in1=g0[:], op=ALU.mult, r=[vs, g0], w=[vs])
            P.dma('pool', g.hy_g0s[tok0 + tt * 128:tok0 + (tt + 1) * 128, :], g0[:], r=[g0], wa=['hy_g0s'])
            P.dma('pool', g.hy_vss[tok0 + tt * 128:tok0 + (tt + 1) * 128, :], vs[:], r=[vs], wa=['hy_vss'])
    yhat = S.sb("hy_yhat", [128, 2 * ntt, 512], BF16)
    with P.scope() as S2:
        fmt = [S2.sb("hy_fmt%d" % i, [128, ntt, 128], BF16) for i in range(4)]
        kk = [S2.sb("hy_kk%d" % i, [128, 512], F32) for i in range(4)]
        vh = [S2.sb("hy_vh%d" % i, [128, 512], F32) for i in range(4)]
        tq = [S2.sb("hy_tq%d" % i, [128, 512], F32) for i in range(4)]
        it = 0
        for i in range(ntt):
            i2 = (i % 2) * 2
            fre, fim, kre, kim, vre, vim = fmt[i2], fmt[i2 + 1], kk[i2], kk[i2 + 1], vh[i2], vh[i2 + 1]
            P.dma('sp', fre[:], g.hy_FmT[li][i], w=[fre])
            P.dma('sp', fim[:], g.hy_FmT[li][i + ntt], w=[fim])
            P.dma('sp', kre[:], g.khat[li][i * 128:(i + 1) * 128, :], r=[('khat', li)], w=[kre])
            P.dma('sp', kim[:], g.khat[li][(i + ntt) * 128:(i + ntt + 1) * 128, :], r=[('khat', li)], w=[kim])
            for (fm, vdst) in ((fre, vre), (fim, vim)):
                ps = g.psA[it % 4]
                it += 1
                for tt in range(ntt):
                    P.pe('matmul', out=ps[:, :], lhsT=fm[:, tt, :], rhs=vvb[:, tt, :], start=(tt == 0),
                         stop=(tt == ntt - 1), r=[fm, vvb], w=[ps])
                P.act('activation', out=vdst[:], in_=ps[:, :], func=AF.Copy, r=[ps], w=[vdst])
            t1, t2, t3, t4 = tq
            P.dve('tensor_tensor', out=t1[:], in0=vre[:], in1=kre[:], op=ALU.mult, r=[vre, kre], w=[t1])
            P.pool('tensor_tensor', out=t2[:], in0=vim[:], in1=kim[:], op=ALU.mult, r=[vim, kim], w=[t2])
            P.dve('tensor_tensor', out=t3[:], in0=vre[:], in1=kim[:], op=ALU.mult, r=[vre, kim], w=[t3])
            P.pool('tensor_tensor', out=t4[:], in0=vim[:], in1=kre[:], op=ALU.mult, r=[vim, kre], w=[t4])
            P.dve('tensor_tensor', out=yhat[:, i, :], in0=t1[:], in1=t2[:], op=ALU.subtract, r=[t1, t2], wa=[yhat])
            P.pool('tensor_tensor', out=yhat[:, i + ntt, :], in0=t3[:], in1=t4[:], op=ALU.add, r=[t3, t4], wa=[yhat])
            if i == 0:
                P.dve('tensor_tensor', out=yhat[0:1, 0, :], in0=vre[0:1, :], in1=kre[0:1, :], op=ALU.mult,
                      r=[vre, kre], w=[yhat])
                P.dve('tensor_tensor', out=yhat[0:1, ntt, :], in0=vim[0:1, :], in1=kim[0:1, :], op=ALU.mult,
                      r=[vim, kim], w=[yhat])
    yst = S.sb("hy_yst", [128, 4, L], BF16)
    with P.scope() as S3:
        gt = [S3.sb("hy_gt%d" % i, [128, 2 * ntt, 128], BF16) for i in range(2)]
        g0 = [S3.sb("hy_g0b%d" % i, [128, 512], F32) for i in range(2)]
        vs = [S3.sb("hy_vsb%d" % i, [128, 512], F32) for i in range(2)]
        o = [S3.sb("hy_o%d" % i, [128, 512], F32) for i in range(2)]
        for j in range(ntt):
            j2 = j % 2
            P.dma('sp', gt[j2][:], g.hy_GT[li][j], w=[gt[j2]])
            P.dma('sp', g0[j2][:], g.hy_g0s[tok0 + j * 128:tok0 + (j + 1) * 128, :], r=['hy_g0s'], w=[g0[j2]])
            P.dma('sp', vs[j2][:], g.hy_vss[tok0 + j * 128:tok0 + (j + 1) * 128, :], r=['hy_vss'], w=[vs[j2]])
            ps = g.psA[j2]
            for rt in range(2 * ntt):
                P.pe('matmul', out=ps[:, :], lhsT=gt[j2][:, rt, :], rhs=yhat[:, rt, :], start=(rt == 0),
                     stop=(rt == 2 * ntt - 1), r=[gt[j2], yhat], w=[ps])
            P.dve('tensor_tensor', out=o[j2][:], in0=ps[:, :], in1=g0[j2][:], op=ALU.mult, r=[ps, g0[j2]], w=[o[j2]])
            P.pool('tensor_tensor', out=o[j2][:], in0=o[j2][:], in1=vs[j2][:], op=ALU.add, r=[vs[j2], o[j2]],
                   w=[o[j2]])
            pt = g.psB[j2]
            for k in range(4):
                P.pe('transpose', out=pt[:, k * 128:(k + 1) * 128], in_=o[j2][:, k * 128:(k + 1) * 128],
                     identity=g.ident[:], r=[o[j2], g.ident], w=[pt])
            P.act('activation', out=yst[:, :, j * 128:(j + 1) * 128], in_=pt[:, :].rearrange("p (k t) -> p k t", k=4),
                  func=AF.Copy, r=[pt], wa=[yst])
        for k in range(4):
            P.dma('sp', g.yfm[b, 1536 + k * 128:1536 + (k + 1) * 128, tok0:tok0 + L], yst[:, k, :], r=[yst],
                  wa=[('yfm', b)])


def gla_consts():
    s = np.arange(128)[:, None]
    t = np.arange(128)[None, :]
    Lm = np.stack([(s <= t), (s >= t)]).astype(np.float32)
    pos = np.arange(TLAT)
    row = (pos // 64).astype(np.float32)
    col = (pos % 64).astype(np.float32)
    n = 16
    freqs = (np.float32(10000.0) ** (-np.arange(n, dtype=np.float32) / np.float32(n))).astype(np.float32)
    ang = np.concatenate([row[:, None] * freqs, col[:, None] * freqs], axis=-1).astype(np.float32)
    cos, sin = np.cos(ang).astype(np.float32), np.sin(ang).astype(np.float32)
    c64 = np.concatenate([cos, cos], axis=-1)
    s64 = np.concatenate([-sin, sin], axis=-1)
    return {
        "gla_Lm": Lm,
        "rope_c_tm": np.ascontiguousarray(c64.reshape(16, 128, 64).transpose(1, 0, 2)),
        "rope_s_tm": np.ascontiguousarray(s64.reshape(16, 128, 64).transpose(1, 0, 2)),
        "rope_c_fm": np.ascontiguousarray(np.concatenate([c64.T, c64.T], axis=0)),
        "rope_s_fm": np.ascontiguousarray(np.concatenate([s64.T, s64.T], axis=0)),
    }


def phase_gla(g, l, b):
    with g.P.scope() as S:
        _phase_gla(g, l, b, S)


def _phase_gla(g, l, b, S):
    P = g.P
    with_ctx = l < g.NL_total - 1
    oB = S.sb("gl_oB", [128, NTT, 512], F32)
    S0, S = S, Scope(P)
    cfm = S.sb("gl_cfm", [128, TLAT], F32)
    sfm = S.sb("gl_sfm", [128, TLAT], F32)
    ctm = S.sb("gl_ctm", [128, 16, 64], F32)
    stm = S.sb("gl_stm", [128, 16, 64], F32)
    Lm = S.sb("gl_Lm", [128, 2, 128], F32)
    wg2 = S.sb("gl_wg2", [16, 2, 128], F32)
    bg = S.sb("gl_bg", [1, 2, 128], F32)
    ones = S.sb("gl_ones", [1, 128], F32)
    Sp = [S.sb("gl_S%d" % i, [128, 128], F32) for i in range(2)]
    Sbf = [S.sb("gl_Sbf%d" % i, [128, 128], BF16) for i in range(2)]
    NR = 2
    fmq = [[S.sb("gl_fm%d_%d" % (k, r), [128, 128], F32) for k in range(8)] for r in range(NR)]
    lrt = [S.sb("gl_lrt%d" % r, [16, 128], F32) for r in range(NR)]
    ktm = [S.sb("gl_ktm%d" % r, [128, 512], F32) for r in range(NR)]
    vtm = [S.sb("gl_vtm%d" % r, [128, 512], F32) for r in range(NR)]
    vbf = [S.sb("gl_vbf%d" % r, [128, 512], BF16) for r in range(NR)]
    ee = [S.sb("gl_ee%d" % r, [128, 128], F32) for r in range(NR)]
    gdup = [S.sb("gl_gdup%d" % r, [128, 256], F32) for r in range(NR)]
    enb_tm = [S.sb("gl_enbtm%d" % r, [128, 256], F32) for r in range(NR)]
    eb_fm = [[S.sb("gl_ebfm%d_%d" % (i, r), [128, 128], F32) for i in range(2)] for r in range(NR)]
    enb_fm = [[S.sb("gl_enbfm%d_%d" % (i, r), [128, 128], F32) for i in range(2)] for r in range(NR)]
    t1 = [S.sb("gl_t1_%d" % r, [128, 256], F32) for r in range(NR)]
    t2 = [S.sb("gl_t2_%d" % r, [128, 256], F32) for r in range(NR)]
    qeT = [[S.sb("gl_qeT%d_%d" % (h, r), [128, 128], BF16) for h in range(4)] for r in range(NR)]
    keT = [[S.sb("gl_keT%d_%d" % (h, r), [128, 128], BF16) for h in range(4)] for r in range(NR)]
    for r in range(NR):
        for h in range(4):
            P.pool('memset', ap=qeT[r][h][:], constant=0.0, w=[qeT[r][h]])
            P.pool('memset', ap=keT[r][h][:], constant=0.0, w=[keT[r][h]])
    ke_tm = [S.sb("gl_ketm%d" % r, [128, 256], BF16) for r in range(NR)]
    attT = [S.sb("gl_attT%d" % r, [128, 512], BF16) for r in range(NR)]
    stmp = S.sb("gl_stmp", [128, 128], F32)
    P.dma('sp', cfm[:], g.rope_c_fm, w=[cfm])
    P.dma('sp', sfm[:], g.rope_s_fm, w=[sfm])
    P.dma('sp', ctm[:], g.rope_c_tm, w=[ctm])
    P.dma('sp', stm[:], g.rope_s_tm, w=[stm])
    P.dma('sp', Lm[:], g.gla_Lm.rearrange("d s t -> s d t"), w=[Lm])
    P.dma('sp', wg2[:], g.gla_wg2[l].rearrange("d k n -> k d n"), w=[wg2])
    P.dma('sp', bg[:], g.gla_bg_row[l], w=[bg])
    P.dve('memset', ap=ones[:], constant=1.0, w=[ones])
    pg, pbt, pbf, patt, po, pss = g.psA[0], g.psA[1], g.psA[2:4], g.psB[0], g.psB[1], g.psB[2:4]
    it = 0
    for d in range(2):
        order = list(range(NTT)) if d == 0 else [1, 0] + list(range(NTT - 1, 1, -1))
        tend = 127 if d == 0 else 0
        for i in range(2):
            P.dve('memset', ap=Sp[i][:], constant=0.0, w=[Sp[i]])
            P.pool('memset', ap=Sbf[i][:], constant=0.0, w=[Sbf[i]])
        import os as _os
        LVL = int(_os.environ.get("GLA_LVL", "99"))
        if _os.environ.get("GLA_SHORT"):
            order = [int(v) for v in _os.environ["GLA_SHORT"].split(",")] if d == 0 else [int(v) for v in _os.environ.get("GLA_SHORT1", "").split(",") if v]
        for tt in order:
            r = it % NR
            it += 1
            tok = tt * 128
            lat = tt >= 2
            lt = tt - 2
            P.dma('sp', lrt[r][:], g.pfm[b, FM_GLG + 16 * d:FM_GLG + 16 * d + 16, tok:tok + 128], r=[('pfm', b)],
                  w=[lrt[r]])
            for k, f0 in enumerate((FM_GLQ, FM_GLQ + 128, FM_GLQS, FM_GLQS + 128, FM_GLK, FM_GLK + 128, FM_GLKS,
                                    FM_GLKS + 128)):
                if not lat and k in (2, 3, 6, 7):
                    continue
                P.dma('sp', fmq[r][k][:], g.pfm[b, f0:f0 + 128, tok:tok + 128], r=[('pfm', b)], w=[fmq[r][k]])
            P.dma('sp', ktm[r][:], g.ptm[b, prow(tt):prow(tt) + 128, TM_GLK:TM_GLK + 512], r=[('ptm', b)], w=[ktm[r]])
            P.dma('sp', vtm[r][:], g.ptm[b, prow(tt):prow(tt) + 128, TM_GLV:TM_GLV + 512], r=[('ptm', b)], w=[vtm[r]])
            P.pool('tensor_copy', out=vbf[r][:], in_=vtm[r][:], r=[vtm[r]], w=[vbf[r]])
            if LVL <= 1:
                continue
            P.pe('matmul', out=pg[:, 0:128], lhsT=lrt[r][:], rhs=wg2[:, d, :], start=True, stop=False,
                 r=[lrt[r], wg2], w=[pg])
            P.pe('matmul', out=pg[:, 0:128], lhsT=ones[:], rhs=bg[:, d, :], start=False, stop=True,
                 r=[ones, bg], w=[pg])
            if LVL <= 2:
                continue
            P.act('activation', out=ee[r][:], in_=pg[:, 0:128], func=AF.Exp, scale=-1.0, r=[pg], w=[ee[r]])
            P.act('activation', out=ee[r][:], in_=ee[r][:], func=AF.Ln, bias=1.0, r=[ee[r]], w=[ee[r]])
            gv = gdup[r][:].rearrange("p (h two c) -> p h two c", h=4, two=2)
            ev = ee[r][:].rearrange("p (h c) -> p h c", h=4)
            for two in range(2):
                P.dve('tensor_scalar', out=gv[:, :, two, :], in0=ev, scalar1=-1.0 / 16.0, scalar2=None, op0=ALU.mult,
                      r=[ee[r]], wa=[gdup[r]])
            if LVL <= 3:
                continue
            P.pe('matmul', out=pbt[:, 0:256], lhsT=Lm[:, d, :], rhs=gdup[r][:], start=True, stop=True,
                 r=[Lm, gdup[r]], w=[pbt])
            P.act('activation', out=enb_tm[r][:], in_=pbt[:, 0:256], func=AF.Exp, scale=-1.0, r=[pbt], w=[enb_tm[r]])
            for i in range(2):
                P.pe('matmul', out=pbf[i][:, 0:128], lhsT=gdup[r][:, 128 * i:128 * (i + 1)], rhs=Lm[:, d, :],
                     start=True, stop=True, r=[Lm, gdup[r]], w=[pbf[i]])
                P.act('activation', out=eb_fm[r][i][:], in_=pbf[i][:, 0:128], func=AF.Exp, r=[pbf[i]],
                      w=[eb_fm[r][i]])
                P.act('activation', out=enb_fm[r][i][:], in_=pbf[i][:, 0:128], func=AF.Exp, scale=-1.0, r=[pbf[i]],
                      w=[enb_fm[r][i]])
            if LVL <= 4:
                continue
            for i in range(2):
                qf, qs, kf, ks = fmq[r][i], fmq[r][2 + i], fmq[r][4 + i], fmq[r][6 + i]
                if lat:
                    cs = cfm[:, lt * 128:(lt + 1) * 128]
                    sn = sfm[:, lt * 128:(lt + 1) * 128]
                    P.dve('tensor_tensor', out=qf[:], in0=qf[:], in1=cs, op=ALU.mult, r=[qf, cfm], w=[qf])
                    P.pool('tensor_tensor', out=qs[:], in0=qs[:], in1=sn, op=ALU.mult, r=[qs, sfm], w=[qs])
                    P.dve('tensor_tensor', out=qf[:], in0=qf[:], in1=qs[:], op=ALU.add, r=[qf, qs], w=[qf])
                    P.dve('tensor_tensor', out=kf[:], in0=kf[:], in1=cs, op=ALU.mult, r=[kf, cfm], w=[kf])
                    P.pool('tensor_tensor', out=ks[:], in0=ks[:], in1=sn, op=ALU.mult, r=[ks, sfm], w=[ks])
                    P.dve('tensor_tensor', out=kf[:], in0=kf[:], in1=ks[:], op=ALU.add, r=[kf, ks], w=[kf])
                for hh in range(2):
                    h = 2 * i + hh
                    ps_ = slice(64 * hh, 64 * hh + 64)
                    P.dve('scalar_tensor_tensor', out=qeT[r][h][ps_, :], in0=qf[ps_, :], scalar=0.125,
                          in1=eb_fm[r][i][ps_, :], op0=ALU.mult, op1=ALU.mult, r=[qf, eb_fm[r][i]], w=[qeT[r][h]])
                    P.dve('tensor_tensor', out=keT[r][h][ps_, :], in0=kf[ps_, :], in1=enb_fm[r][i][ps_, :], op=ALU.mult,
                          r=[kf, enb_fm[r][i]], w=[keT[r][h]])
            if lat:
                for h in range(4):
                    P.dve('tensor_tensor', out=t1[r][:, 64 * h:64 * h + 64], in0=ktm[r][:, 64 * h:64 * h + 64],
                          in1=ctm[:, lt, :], op=ALU.mult, r=[ktm[r], ctm], wa=[t1[r]])
                    P.pool('tensor_tensor', out=t2[r][:, 64 * h:64 * h + 64], in0=ktm[r][:, 256 + 64 * h:256 + 64 * h + 64],
                           in1=stm[:, lt, :], op=ALU.mult, r=[ktm[r], stm], wa=[t2[r]])
                P.dve('tensor_tensor', out=t1[r][:], in0=t1[r][:], in1=t2[r][:], op=ALU.add, r=[t1[r], t2[r]], w=[t1[r]])
                ksrc, kkey = t1[r][:], t1[r]
            else:
                ksrc, kkey = ktm[r][:, 0:256], ktm[r]
            P.dve('tensor_tensor', out=ke_tm[r][:], in0=ksrc, in1=enb_tm[r][:], op=ALU.mult, r=[kkey, enb_tm[r]],
                  w=[ke_tm[r]])
            if LVL <= 5:
                continue
            for h in range(4):
                i, hb = h // 2, 64 * (h % 2)
                P.pe('matmul', out=patt[:, 128 * h:128 * (h + 1)], lhsT=keT[r][h][:],
                     rhs=qeT[r][h][:], start=True, stop=True, r=[keT[r][h], qeT[r][h]], w=[patt])
            for h in range(4):
                P.dve('tensor_tensor', out=attT[r][:, 128 * h:128 * (h + 1)], in0=patt[:, 128 * h:128 * (h + 1)],
                      in1=Lm[:, d, :], op=ALU.mult, r=[patt, Lm], wa=[attT[r]])
            if LVL <= 6:
                continue
            if lat or with_ctx:
                for h in range(4):
                    i, hb = h // 2, 64 * (h % 2)
                    P.pe('matmul', out=po[:, 128 * h:128 * (h + 1)], lhsT=attT[r][:, 128 * h:128 * (h + 1)],
                         rhs=vbf[r][:, 128 * h:128 * (h + 1)], start=True, stop=False, r=[attT[r], vbf[r]], w=[po])
                    P.pe('matmul', out=po[:, 128 * h:128 * (h + 1)], lhsT=qeT[r][h][:],
                         rhs=Sbf[i][:], start=False, stop=True, r=[qeT[r][h], Sbf[i]], w=[po])
                if d == 0:
                    P.act('activation', out=oB[:, tt, :], in_=po[:, :], func=AF.Copy, r=[po], w=[('oB', tt)])
                else:
                    P.dve('tensor_tensor', out=oB[:, tt, :], in0=po[:, :], in1=oB[:, tt, :], op=ALU.add,
                          r=[po, ('oB', tt)], w=[('oB', tt)])
            if LVL <= 7:
                continue
            for i in range(2):
                P.pe('matmul', out=pss[i][:, 0:256], lhsT=ke_tm[r][:, 128 * i:128 * (i + 1)],
                     rhs=vbf[r][:, 256 * i:256 * (i + 1)], start=True, stop=True, r=[ke_tm[r], vbf[r]], w=[pss[i]])
                for hh in range(2):
                    ps_ = slice(64 * hh, 64 * hh + 64)
                    P.dve('tensor_tensor', out=stmp[ps_, :], in0=pss[i][ps_, 128 * hh:128 * (hh + 1)], in1=Sp[i][ps_, :],
                          op=ALU.add, r=[pss[i], Sp[i]], w=[(stmp.name, hh)])
                    P.dve('tensor_scalar', out=Sp[i][ps_, :], in0=stmp[ps_, :], scalar1=eb_fm[r][i][ps_, tend:tend + 1],
                          scalar2=None, op0=ALU.mult, r=[(stmp.name, hh), eb_fm[r][i]], wa=[Sp[i]])
                P.pool('tensor_copy', out=Sbf[i][:], in_=Sp[i][:], r=[Sp[i]], w=[Sbf[i]])
    S.__exit__(None, None, None)
    S = S0
    gn = S.sb("gl_gn", [128, 128], F32)
    zt = [S.sb("gl_zt%d" % i, [128, 512], F32) for i in range(2)]
    sq = [S.sb("gl_sq%d" % i, [128, 512], F32) for i in range(2)]
    ssq = [S.sb("gl_ssq%d" % i, [128, 4], F32) for i in range(2)]
    yst = S.sb("gl_yst", [128, 4, TT], BF16)
    P.dma('sp', gn[:], g.gla_norm_row[l].partition_broadcast(128), w=[gn])
    tt0 = 0 if with_ctx else 2
    if LVL <= 8:
        tt0 = NTT
    for tt in range(tt0, NTT):
        r = tt % 2
        o = oB[:, tt, :]
        P.dma('sp', zt[r][:], g.ptm[b, prow(tt):prow(tt) + 128, TM_GLZ:TM_GLZ + 512], r=[('ptm', b)], w=[zt[r]])
        P.act('activation', out=zt[r][:], in_=zt[r][:], func=AF.Silu, r=[zt[r]], w=[zt[r]])
        P.pool('tensor_tensor', out=sq[r][:], in0=o, in1=o, op=ALU.mult, r=[('oB', tt)], w=[sq[r]])
        P.dve('tensor_reduce', out=ssq[r][:], in_=sq[r][:].rearrange("p (h c) -> p h c", h=4), axis=AX.X, op=ALU.add,
              r=[sq[r]], w=[ssq[r]])
        P.dve('tensor_scalar', out=ssq[r][:], in0=ssq[r][:], scalar1=1.0 / 128.0, scalar2=EPS, op0=ALU.mult,
              op1=ALU.add, r=[ssq[r]], w=[ssq[r]])
        P.act('activation', out=ssq[r][:], in_=ssq[r][:], func=AF.Sqrt, r=[ssq[r]], w=[ssq[r]])
        P.dve('reciprocal', out=ssq[r][:], in_=ssq[r][:], r=[ssq[r]], w=[ssq[r]])
        for h in range(4):
            P.dve('scalar_tensor_tensor', out=sq[r][:, 128 * h:128 * (h + 1)], in0=oB[:, tt, 128 * h:128 * (h + 1)],
                  scalar=ssq[r][:, h:h + 1], in1=gn[:], op0=ALU.mult, op1=ALU.mult, r=[('oB', tt), ssq[r], gn],
                  wa=[sq[r]])
        P.dve('tensor_tensor', out=sq[r][:], in0=sq[r][:], in1=zt[r][:], op=ALU.mult, r=[sq[r], zt[r]], w=[sq[r]])
        pt = g.psB[r]
        for k in range(4):
            P.pe('transpose', out=pt[:, k * 128:(k + 1) * 128], in_=sq[r][:, k * 128:(k + 1) * 128],
                 identity=g.ident[:], r=[sq[r], g.ident], w=[pt])
        P.act('activation', out=yst[:, :, tt * 128:(tt + 1) * 128], in_=pt[:, :].rearrange("p (k t) -> p k t", k=4),
              func=AF.Copy, r=[pt], wa=[yst])
    for k in range(4):
        if tt0 >= NTT:
            break
        P.dma('sp', g.yfm[b, 512 + k * 128:512 + (k + 1) * 128, tt0 * 128:TT], yst[:, k, tt0 * 128:TT], r=[yst],
              wa=[('yfm', b)])


GQW = 1536 + 16


def gdn_consts():
    s = np.arange(128)[:, None]
    t = np.arange(128)[None, :]
    big = np.float32(1e5)
    mbs = np.stack([np.where(t < s, 0.0, big), np.where(t > s, 0.0, big)]).astype(np.float32)
    mbi = np.stack([np.where(s <= t, 0.0, -big), np.where(s >= t, 0.0, -big)]).astype(np.float32)
    return {"gdn_mbs": np.ascontiguousarray(mbs), "gdn_mbi": np.ascontiguousarray(mbi),
            "ones128": np.ones((128, 128), np.float32)}


def phase_gdn_pre(g, l, b):
    with g.P.scope() as S:
        _phase_gdn_pre(g, l, b, S)


def _phase_gdn_pre(g, l, b, S):
    P = g.P
    wrow = S.sb("gd_wrow", [128, 5, 1536], F32)
    xs = [S.sb("gd_xs%d" % i, [128, 1536], F32) for i in range(5)]
    acc = S.sb("gd_acc", [128, GQW], F32)
    tmp = S.sb("gd_tmp", [128, 1536], F32)
    tmp2 = S.sb("gd_tmp2", [128, 1536], F32)
    ab = S.sb("gd_ab", [128, 16], F32)
    dtb = S.sb("gd_dtb", [128, 8], F32)
    eal = S.sb("gd_eal", [128, 8], F32)
    ssq = S.sb("gd_ssq", [128, 8], F32)
    sp = S.sb("gd_sp", [128, 8], F32)
    for k in range(5):
        P.dma('sp', wrow[:, k, :], g.gdn_conv_row[l, k:k + 1, :].partition_broadcast(128), wa=[wrow])
    P.dma('sp', dtb[:], g.gdn_dtb_row[l].partition_broadcast(128), w=[dtb])
    P.dma('sp', eal[:], g.gdn_alog_row[l].partition_broadcast(128), w=[eal])
    P.act('activation', out=eal[:], in_=eal[:], func=AF.Exp, r=[eal], w=[eal])
    for tt in range(NTT):
        base = prow(tt)
        for k in range(5):
            P.dma('sp', xs[k][:], g.ptm[b, base + k - 2:base + k - 2 + 128, TM_GDQKV:TM_GDQKV + 1536],
                  r=[('ptm', b)], w=[xs[k]])
        P.dma('sp', ab[:], g.ptm[b, base:base + 128, TM_GDAB:TM_GDAB + 16], r=[('ptm', b)], w=[ab])
        A = acc[:, 0:1536]
        P.dve('tensor_tensor', out=A, in0=xs[0][:], in1=wrow[:, 0, :], op=ALU.mult, r=[xs[0], wrow], w=[acc])
        for k in range(1, 5):
            eng = 'pool' if k % 2 == 1 else 'dve'
            tk = tmp if k % 2 == 1 else tmp2
            P.op(eng, 'tensor_tensor', out=tk[:], in0=xs[k][:], in1=wrow[:, k, :], op=ALU.mult, r=[xs[k], wrow],
                 w=[tk])
            P.dve('tensor_tensor', out=A, in0=A, in1=tk[:], op=ALU.add, r=[tk, acc], w=[acc])
        P.act('activation', out=A, in_=A, func=AF.Silu, r=[acc], w=[acc])
        P.pool('tensor_tensor', out=tmp[:, 0:1024], in0=acc[:, 0:1024], in1=acc[:, 0:1024], op=ALU.mult, r=[acc],
               w=[tmp])
        P.dve('tensor_reduce', out=ssq[:], in_=tmp[:, 0:1024].rearrange("p (h c) -> p h c", h=8), axis=AX.X,
              op=ALU.add, r=[tmp], w=[ssq])
        P.dve('tensor_scalar', out=ssq[:], in0=ssq[:], scalar1=EPS, scalar2=None, op0=ALU.add, r=[ssq], w=[ssq])
        P.act('activation', out=ssq[:], in_=ssq[:], func=AF.Sqrt, r=[ssq], w=[ssq])
        P.dve('reciprocal', out=ssq[:], in_=ssq[:], r=[ssq], w=[ssq])
        P.dve('tensor_scalar', out=ssq[:, 0:4], in0=ssq[:, 0:4], scalar1=128.0 ** -0.5, scalar2=None, op0=ALU.mult,
              r=[ssq], w=[ssq])
        for h8 in range(8):
            eng = 'dve' if h8 % 2 == 0 else 'pool'
            P.op(eng, 'tensor_scalar', out=acc[:, 128 * h8:128 * (h8 + 1)], in0=acc[:, 128 * h8:128 * (h8 + 1)],
                 scalar1=ssq[:, h8:h8 + 1], scalar2=None, op0=ALU.mult, r=[acc, ssq], w=[acc])
        P.act('activation', out=acc[:, 1536:1544], in_=ab[:, 8:16], func=AF.Sigmoid, r=[ab], w=[acc])
        P.dve('tensor_tensor', out=sp[:], in0=ab[:, 0:8], in1=dtb[:], op=ALU.add, r=[ab, dtb], w=[sp])
        P.act('activation', out=sp[:], in_=sp[:], func=AF.Exp, r=[sp], w=[sp])
        P.act('activation', out=sp[:], in_=sp[:], func=AF.Ln, bias=1.0, r=[sp], w=[sp])
        P.dve('scalar_tensor_tensor', out=acc[:, 1544:1552], in0=sp[:], scalar=-1.0, in1=eal[:], op0=ALU.mult,
              op1=ALU.mult, r=[sp, eal, acc], w=[acc])
        P.dma('sp', g.gq[tt * 128:(tt + 1) * 128, :], acc[:], r=[acc], wa=['gq'])


def phase_gdn(g, l, b):
    with g.P.scope() as S:
        _phase_gdn(g, l, b, S)


def _phase_gdn(g, l, b, S):
    P = g.P
    with_ctx = l < g.NL_total - 1
    oC = S.sb("gd_oC", [128, NTT, 512], F32)
    S0, S = S, Scope(P)
    Lm = S.sb("gd_Lm", [128, 2, 128], F32)
    mbs = S.sb("gd_mbs", [128, 2, 128], F32)
    mbi = S.sb("gd_mbi", [128, 2, 128], F32)
    ones = S.sb("gd_ones", [128, 128], F32)
    St = [S.sb("gd_S%d" % h, [128, 128], F32) for h in range(4)]
    qkv = [S.sb("gd_qkv%d" % i, [128, GQW], F32) for i in range(2)]
    sm = lambda n, w: S.sb("gd_" + n, [128, w], F32)
    gam, egam, bexp, nbeta, gend, kdsc, dend = (sm("gam", 4), sm("egam", 4), sm("bexp", 4), sm("nbeta", 4),
                                                sm("gend", 4), sm("kdsc", 4), sm("dend", 4))
    Lg = [sm("Lg%d" % i, 128) for i in range(2)]
    def smb(n, w):
        solve = n[:2] in ("PQ", "Y0", "Y1", "Y2", "Y3", "RH")
        return S.sb("gd_" + n, [128, w], F32 if solve else BF16)
    KQ = [smb("KQ%d" % h, 256) for h in range(4)]
    Nf = [sm("Nf%d" % i, 128) for i in range(2)]
    Sbf = [smb("Sbf%d" % h, 128) for h in range(4)]
    xx, EE, x2, E2 = ([sm("xx%d" % i, 128) for i in range(2)], [sm("EE%d" % i, 128) for i in range(2)],
                      [sm("x2%d" % i, 128) for i in range(2)], [sm("E2%d" % i, 128) for i in range(2)])
    aqk = [smb("aqk%d" % h, 128) for h in range(4)]
    PQh = [[smb("PQ%d_%d" % (h, i), 256) for i in range(2)] for h in range(4)]
    Yh = [[smb("Y%d_%d" % (h, i), 128) for i in range(2)] for h in range(4)]
    RHSu = [smb("RHSu%d" % h, 128) for h in range(4)]
    RHSw = [smb("RHSw%d" % h, 128) for h in range(4)]
    kdh = [smb("kd%d" % h, 128) for h in range(4)]
    wTn = [smb("wTn%d" % h, 128) for h in range(4)]
    esb = [smb("esb%d" % h, 128) for h in range(4)]
    o1 = [sm("o1%d" % h, 128) for h in range(4)]
    P.dma('sp', Lm[:], g.gla_Lm.rearrange("d s t -> s d t"), w=[Lm])
    P.dma('sp', mbs[:], g.gdn_mbs.rearrange("d s t -> s d t"), w=[mbs])
    P.dma('sp', mbi[:], g.gdn_mbi.rearrange("d s t -> s d t"), w=[mbi])
    P.dma('sp', ones[:], g.ones128, w=[ones])
    pA0, pG = g.psA[0], g.psA[1]
    it = 0
    import os as _os
    SHORT = _os.environ.get("GDN_SHORT")
    for d in range(2):
        order = list(range(NTT)) if d == 0 else [1, 0] + list(range(NTT - 1, 1, -1))
        if SHORT:
            order = order[:int(SHORT)]
        send = 127 if d == 0 else 0
        for h in range(4):
            P.dve('memset', ap=St[h][:], constant=0.0, w=[St[h]])
            P.pool('memset', ap=Sbf[h][:], constant=0.0, w=[Sbf[h]])
        for tt in order:
            r = it % 2
            it += 1
            X = qkv[r]
            P.dma('sp', X[:], g.gq[tt * 128:(tt + 1) * 128, :], r=['gq'], w=[X])
            bet = X[:, 1536 + 4 * d:1536 + 4 * d + 4]
            gg = X[:, 1544 + 4 * d:1544 + 4 * d + 4]
            P.pe('matmul', out=pA0[:, 0:4], lhsT=Lm[:, d, :], rhs=gg, start=True, stop=True, r=[Lm, X], w=[pA0])
            P.act('activation', out=gam[:], in_=pA0[:, 0:4], func=AF.Copy, r=[pA0], w=[gam])
            P.act('activation', out=egam[:], in_=pA0[:, 0:4], func=AF.Exp, r=[pA0], w=[egam])
            P.dve('tensor_tensor', out=bexp[:], in0=egam[:], in1=bet, op=ALU.mult, r=[egam, X], w=[bexp])
            P.dve('tensor_scalar', out=nbeta[:], in0=bet, scalar1=-1.0, scalar2=None, op0=ALU.mult, r=[X], w=[nbeta])
            for h in range(4):
                P.dve('tensor_scalar', out=Lg[h % 2][:], in0=Lm[:, d, :], scalar1=gg[:, h:h + 1], scalar2=None,
                      op0=ALU.mult, r=[Lm, X], w=[Lg[h % 2]])
                P.pe('matmul', out=pG[:, 128 * h:128 * (h + 1)], lhsT=ones[:], rhs=Lg[h % 2][:], start=True, stop=True,
                     r=[ones, Lg[h % 2]], w=[pG])
            gendv = pG[:, :].rearrange("p (h s) -> p h s", h=4)[:, :, send]
            P.act('activation', out=gend[:], in_=gendv, func=AF.Copy, r=[pG], w=[gend])
            P.act('activation', out=dend[:], in_=gend[:], func=AF.Exp, r=[gend], w=[dend])
            P.dve('tensor_tensor', out=kdsc[:], in0=gend[:], in1=gam[:], op=ALU.subtract, r=[gend, gam], w=[kdsc])
            P.act('activation', out=kdsc[:], in_=kdsc[:], func=AF.Exp, r=[kdsc], w=[kdsc])
            qs_ = lambda h: X[:, 128 * h:128 * (h + 1)]
            ks_ = lambda h: X[:, 512 + 128 * h:512 + 128 * (h + 1)]
            vs_ = lambda h: X[:, 1024 + 128 * h:1024 + 128 * (h + 1)]
            for h in range(4):
                i2 = h % 2
                pb = g.psA[2 + i2]
                P.pe('transpose', out=pb[:, 0:128], in_=ks_(h), identity=g.ident[:], r=[X, g.ident], w=[pb])
                P.pe('transpose', out=pb[:, 128:256], in_=qs_(h), identity=g.ident[:], r=[X, g.ident], w=[pb])
                P.act('activation', out=KQ[h][:], in_=pb[:, 0:256], func=AF.Copy, r=[pb], w=[KQ[h]])
                kTh, qTh = KQ[h][:, 0:128], KQ[h][:, 128:256]
                P.pe('matmul', out=pb[:, 256:384], lhsT=kTh, rhs=kTh, start=True, stop=True, r=[KQ[h]], w=[pb])
                P.pe('matmul', out=pb[:, 384:512], lhsT=kTh, rhs=qTh, start=True, stop=True, r=[KQ[h]], w=[pb])
                Gam = pG[:, 128 * h:128 * (h + 1)]
                P.dve('scalar_tensor_tensor', out=xx[i2][:], in0=Gam, scalar=gam[:, h:h + 1], in1=mbs[:, d, :],
                      op0=ALU.subtract, op1=ALU.max, r=[pG, gam, mbs], w=[xx[i2]])
                P.act('activation', out=EE[i2][:], in_=xx[i2][:], func=AF.Exp, scale=-1.0, r=[xx[i2]], w=[EE[i2]])
                P.dve('scalar_tensor_tensor', out=Nf[i2][:], in0=pb[:, 256:384], scalar=nbeta[:, h:h + 1],
                      in1=EE[i2][:], op0=ALU.mult, op1=ALU.mult, r=[pb, nbeta, EE[i2]], w=[Nf[i2]])
                P.pool('tensor_copy', out=PQh[h][0][:, 0:128], in_=Nf[i2][:], r=[Nf[i2]], w=[PQh[h][0]])
                P.pe('transpose', out=g.psB[h][:, 128:256], in_=Nf[i2][:], identity=g.ident[:],
                     r=[Nf[i2], g.ident], w=[g.psB[h]])
                P.dve('scalar_tensor_tensor', out=x2[i2][:], in0=Gam, scalar=gam[:, h:h + 1], in1=mbi[:, d, :],
                      op0=ALU.subtract, op1=ALU.min, r=[pG, gam, mbi], w=[x2[i2]])
                P.act('activation', out=E2[i2][:], in_=x2[i2][:], func=AF.Exp, r=[x2[i2]], w=[E2[i2]])
                P.dve('tensor_tensor', out=aqk[h][:], in0=pb[:, 384:512], in1=E2[i2][:], op=ALU.mult,
                      r=[pb, E2[i2]], w=[aqk[h]])
            for h in range(4):
                P.act('activation', out=PQh[h][0][:, 128:256], in_=g.psB[h][:, 128:256], func=AF.Copy,
                      r=[g.psB[h]], w=[PQh[h][0]])
            for h in range(4):
                P.op('dve' if h % 2 == 0 else 'pool', 'tensor_tensor', out=Yh[h][0][:], in0=PQh[h][0][:, 128:256],
                     in1=g.ident[:], op=ALU.add, r=[PQh[h][0], g.ident], w=[Yh[h][0]])
            yi = 0
            for j in range(6):
                for h in range(4):
                    cur = PQh[h][j % 2]
                    P.pe('matmul', out=g.psB[h][:, 0:128], lhsT=cur[:, 128:256], rhs=cur[:, 0:128], start=True,
                         stop=True, r=[cur], w=[g.psB[h]])
                    if j < 5:
                        P.pe('matmul', out=g.psB[h][:, 128:256], lhsT=cur[:, 0:128], rhs=cur[:, 128:256], start=True,
                             stop=True, r=[cur], w=[g.psB[h]])
                for h in range(4):
                    nxt = PQh[h][(j + 1) % 2]
                    wid = 256 if j < 5 else 128
                    P.act('activation', out=nxt[:, 0:wid], in_=g.psB[h][:, 0:wid], func=AF.Copy, r=[g.psB[h]],
                          w=[nxt])
                for h in range(4):
                    nxt = PQh[h][(j + 1) % 2]
                    P.pe('matmul', out=g.psA[h][:, 0:128], lhsT=nxt[:, 0:128], rhs=Yh[h][yi][:], start=True, stop=True,
                         r=[nxt, Yh[h][yi]], w=[g.psA[h]])
                for h in range(4):
                    P.dve('tensor_tensor', out=Yh[h][1 - yi][:], in0=g.psA[h][:, 0:128], in1=Yh[h][yi][:], op=ALU.add,
                          r=[g.psA[h], Yh[h][yi]], w=[Yh[h][1 - yi]])
                yi = 1 - yi
            need_o = (tt >= 2 or with_ctx)
            for h in range(4):
                P.dve('tensor_scalar', out=RHSu[h][:], in0=vs_(h), scalar1=bet[:, h:h + 1], scalar2=None,
                      op0=ALU.mult, r=[X], w=[RHSu[h]])
                P.pool('tensor_scalar', out=RHSw[h][:], in0=ks_(h), scalar1=bexp[:, h:h + 1], scalar2=None,
                       op0=ALU.mult, r=[X, bexp], w=[RHSw[h]])
                P.pool('tensor_scalar', out=kdh[h][:], in0=ks_(h), scalar1=kdsc[:, h:h + 1], scalar2=None,
                       op0=ALU.mult, r=[X, kdsc], w=[kdh[h]])
            for h in range(4):
                P.pe('matmul', out=g.psA[h][:, 128:256], lhsT=RHSw[h][:], rhs=Yh[h][yi][:], start=True, stop=True,
                     r=[RHSw[h], Yh[h][yi]], w=[g.psA[h]])
            for h in range(4):
                P.act('activation', out=wTn[h][:], in_=g.psA[h][:, 128:256], func=AF.Copy, scale=-1.0,
                      r=[g.psA[h]], w=[wTn[h]])
            for h in range(4):
                P.pe('matmul', out=g.psA[h][:, 0:128], lhsT=Yh[h][yi][:], rhs=RHSu[h][:], start=True, stop=False,
                     r=[Yh[h][yi], RHSu[h]], w=[g.psA[h]])
                P.pe('matmul', out=g.psA[h][:, 0:128], lhsT=wTn[h][:], rhs=Sbf[h][:], start=False, stop=True,
                     r=[wTn[h], Sbf[h]], w=[g.psA[h]])
            for h in range(4):
                P.act('activation', out=esb[h][:], in_=g.psA[h][:, 0:128], func=AF.Copy, r=[g.psA[h]], w=[esb[h]])
            for h in range(4):
                P.pe('matmul', out=g.psB[h][:, 256:384], lhsT=kdh[h][:], rhs=esb[h][:], start=True, stop=True,
                     r=[kdh[h], esb[h]], w=[g.psB[h]])
                if need_o:
                    P.pe('matmul', out=g.psB[h][:, 0:128], lhsT=KQ[h][:, 128:256], rhs=Sbf[h][:], start=True, stop=True,
                         r=[KQ[h], Sbf[h]], w=[g.psB[h]])
                    P.pe('matmul', out=g.psB[h][:, 128:256], lhsT=aqk[h][:], rhs=esb[h][:], start=True, stop=True,
                         r=[aqk[h], esb[h]], w=[g.psB[h]])
            if need_o:
                for h in range(4):
                    P.act('activation', out=o1[h][:], in_=g.psB[h][:, 0:128], func=AF.Copy, scale=egam[:, h:h + 1],
                          r=[g.psB[h], egam], w=[o1[h]])
                for h in range(4):
                    oslc = oC[:, tt, 128 * h:128 * (h + 1)]
                    if d == 0:
                        P.dve('tensor_tensor', out=oslc, in0=g.psB[h][:, 128:256], in1=o1[h][:], op=ALU.add,
                              r=[g.psB[h], o1[h]], w=[('oC', tt, h)])
                    else:
                        P.dve('tensor_tensor', out=o1[h][:], in0=g.psB[h][:, 128:256], in1=o1[h][:], op=ALU.add,
                              r=[g.psB[h], o1[h]], w=[o1[h]])
                        P.pool('tensor_tensor', out=oslc, in0=oslc, in1=o1[h][:], op=ALU.add,
                               r=[o1[h], ('oC', tt, h)], w=[('oC', tt, h)])
            for h in range(4):
                P.dve('scalar_tensor_tensor', out=St[h][:], in0=St[h][:], scalar=dend[:, h:h + 1],
                      in1=g.psB[h][:, 256:384], op0=ALU.mult, op1=ALU.add, r=[St[h], dend, g.psB[h]], w=[St[h]])
                P.pool('tensor_copy', out=Sbf[h][:], in_=St[h][:], r=[St[h]], w=[Sbf[h]])
    S.__exit__(None, None, None)
    S = S0
    okeys = lambda tt: [('oC', tt, h) for h in range(4)]
    branch_finish(g, l, b, S, oC, okeys, g.gdn_norm_row[l], TM_GDZ, 1024, with_ctx)


def branch_finish(g, l, b, S, oB, okeys, norm_row, zcol, yrow0, with_ctx):
    P = g.P
    gn = S.sb("bf_gn", [128, 128], F32)
    zt = [S.sb("bf_zt%d" % i, [128, 512], F32) for i in range(2)]
    sq = [S.sb("bf_sq%d" % i, [128, 512], F32) for i in range(2)]
    ssq = [S.sb("bf_ssq%d" % i, [128, 4], F32) for i in range(2)]
    yst = S.sb("bf_yst", [128, 4, TT], BF16)
    P.dma('sp', gn[:], norm_row.partition_broadcast(128), w=[gn])
    tt0 = 0 if with_ctx else 2
    for tt in range(tt0, NTT):
        r = tt % 2
        o = oB[:, tt, :]
        P.dma('sp', zt[r][:], g.ptm[b, prow(tt):prow(tt) + 128, zcol:zcol + 512], r=[('ptm', b)], w=[zt[r]])
        P.act('activation', out=zt[r][:], in_=zt[r][:], func=AF.Silu, r=[zt[r]], w=[zt[r]])
        P.pool('tensor_tensor', out=sq[r][:], in0=o, in1=o, op=ALU.mult, r=okeys(tt), w=[sq[r]])
        P.dve('tensor_reduce', out=ssq[r][:], in_=sq[r][:].rearrange("p (h c) -> p h c", h=4), axis=AX.X, op=ALU.add,
              r=[sq[r]], w=[ssq[r]])
        P.dve('tensor_scalar', out=ssq[r][:], in0=ssq[r][:], scalar1=1.0 / 128.0, scalar2=EPS, op0=ALU.mult,
              op1=ALU.add, r=[ssq[r]], w=[ssq[r]])
        P.act('activation', out=ssq[r][:], in_=ssq[r][:], func=AF.Sqrt, r=[ssq[r]], w=[ssq[r]])
        P.dve('reciprocal', out=ssq[r][:], in_=ssq[r][:], r=[ssq[r]], w=[ssq[r]])
        for h in range(4):
            P.dve('scalar_tensor_tensor', out=sq[r][:, 128 * h:128 * (h + 1)], in0=oB[:, tt, 128 * h:128 * (h + 1)],
                  scalar=ssq[r][:, h:h + 1], in1=gn[:], op0=ALU.mult, op1=ALU.mult, r=okeys(tt) + [ssq[r], gn],
                  wa=[sq[r]])
        P.dve('tensor_tensor', out=sq[r][:], in0=sq[r][:], in1=zt[r][:], op=ALU.mult, r=[sq[r], zt[r]], w=[sq[r]])
        pt = g.psB[r]
        for k in range(4):
            P.pe('transpose', out=pt[:, k * 128:(k + 1) * 128], in_=sq[r][:, k * 128:(k + 1) * 128],
                 identity=g.ident[:], r=[sq[r], g.ident], w=[pt])
        P.act('activation', out=yst[:, :, tt * 128:(tt + 1) * 128], in_=pt[:, :].rearrange("p (k t) -> p k t", k=4),
              func=AF.Copy, r=[pt], wa=[yst])
    for k in range(4):
        P.dma('sp', g.yfm[b, yrow0 + k * 128:yrow0 + (k + 1) * 128, tt0 * 128:TT], yst[:, k, tt0 * 128:TT], r=[yst],
              wa=[('yfm', b)])


def phase_merge(g, l, b):
    with g.P.scope() as S:
        _phase_merge(g, l, b, S)


def _phase_merge(g, l, b, S):
    P = g.P
    with_ctx = l < g.NL_total - 1
    last = not with_ctx
    if with_ctx:
        halves = [(0, 1280), (1280, 1024)]
    else:
        halves = [(256, 1024), (1280, 1024)]
    ss = S.sb("mg_ss", [128, NTT, 4], F32)
    bgc = S.sb("mg_bgc", [128, 4, 16], F32)
    mT = S.sb("mg_mT", [128, 16, 1280], BF16)
    P.dma('sp', bgc[:], g.b_gate_col[l], w=[bgc])
    for (t0, n) in halves:
        chunks = [(c0, min(512, n - c0)) for c0 in range(0, n, 512)]
        nch = len(chunks)
        with P.scope() as SA:
            yT = SA.sb("mg_yT", [128, 16, 1280], BF16)
            wgs = [SA.sb("mg_wgs%d" % i, [128, 16, 128], F32) for i in range(2)]
            wgb = [SA.sb("mg_wgb%d" % i, [128, 16, 128], BF16) for i in range(2)]
            wbs = [SA.sb("mg_wbs%d" % i, [128, 4, 128], F32) for i in range(2)]
            wbb = [SA.sb("mg_wbb%d" % i, [128, 4, 128], BF16) for i in range(2)]
            acc = SA.sb("mg_acc", [128, 1280], F32)
            sig = [SA.sb("mg_sig%d" % i, [128, 512], F32) for i in range(2)]
            tmp = [SA.sb("mg_tmp%d" % i, [128, 512], F32) for i in range(2)]
            for kt in range(16):
                P.dma('sp', yT[:, kt, 0:n], g.yfm[b, kt * 128:(kt + 1) * 128, t0:t0 + n], r=[('yfm', b)], wa=[yT])
            it = 0
            for ft in range(16):
                for i in range(4):
                    w2 = (ft * 4 + i) % 2
                    P.dma('sp', wgs[w2][:], g.w_gate_t[l, i, ft], w=[wgs[w2]])
                    P.act('activation', out=wgb[w2][:], in_=wgs[w2][:], func=AF.Copy, r=[wgs[w2]], w=[wgb[w2]])
                    P.dma('sp', wbs[w2][:], g.w_branch_t[l, i, ft], w=[wbs[w2]])
                    P.act('activation', out=wbb[w2][:], in_=wbs[w2][:], func=AF.Copy, r=[wbs[w2]], w=[wbb[w2]])
                    for c, (c0, cw) in enumerate(chunks):
                        pg_, pb_ = g.psA[it % 4], g.psB[it % 4]
                        i2 = it % 2
                        it += 1
                        for kt in range(16):
                            P.pe('matmul', out=pg_[:, 0:cw], lhsT=wgb[w2][:, kt, :],
                                 rhs=g.uT[:, kt, t0 + c0:t0 + c0 + cw], start=(kt == 0), stop=(kt == 15),
                                 r=[wgb[w2], g.uT], w=[pg_])
                        for kt in range(4):
                            P.pe('matmul', out=pb_[:, 0:cw], lhsT=wbb[w2][:, kt, :],
                                 rhs=yT[:, 4 * i + kt, c0:c0 + cw], start=(kt == 0), stop=(kt == 3),
                                 r=[wbb[w2], yT], w=[pb_])
                        P.act('activation', out=sig[i2][:, 0:cw], in_=pg_[:, 0:cw], func=AF.Sigmoid,
                              bias=bgc[:, i, ft:ft + 1], r=[pg_, bgc], w=[sig[i2]])
                        asl = acc[:, c0:c0 + cw]
                        if i == 0:
                            P.dve('tensor_tensor', out=asl, in0=pb_[:, 0:cw], in1=sig[i2][:, 0:cw], op=ALU.mult,
                                  r=[pb_, sig[i2]], w=[('mg_acc', c)])
                        else:
                            P.dve('tensor_tensor', out=tmp[i2][:, 0:cw], in0=pb_[:, 0:cw], in1=sig[i2][:, 0:cw],
                                  op=ALU.mult, r=[pb_, sig[i2]], w=[tmp[i2]])
                            P.pool('tensor_tensor', out=asl, in0=asl, in1=tmp[i2][:, 0:cw], op=ALU.add,
                                   r=[tmp[i2], ('mg_acc', c)], w=[('mg_acc', c)])
                P.act('activation', out=mT[:, ft, 0:n], in_=acc[:, 0:n], func=AF.Copy,
                      r=[('mg_acc', c) for c in range(nch)], wa=[mT])
        with P.scope() as SB:
            wos = [SB.sb("mg_wos%d" % i, [128, 16, 512], F32) for i in range(1)]
            wob = [SB.sb("mg_wob%d" % i, [128, 16, 512], BF16) for i in range(2)]
            yst = [SB.sb("mg_yst%d" % i, [128, 512], F32) for i in range(3)]
            junk = SB.sb("mg_junk", [128, 512], F32)
            it = 0
            for cc in range(4):
                P.dma('sp', wos[0][:], g.w_out[l, :, cc * 512:(cc + 1) * 512].rearrange("(kt p) f -> p kt f", p=128),
                      w=[wos[0]])
                for q4 in range(4):
                    if q4 % 2 == 0:
                        P.dve('tensor_copy', out=wob[cc % 2][:, 4 * q4:4 * q4 + 4, :],
                              in_=wos[0][:, 4 * q4:4 * q4 + 4, :], r=[wos[0]], w=[(wob[cc % 2].name, q4)])
                    else:
                        P.act('activation', out=wob[cc % 2][:, 4 * q4:4 * q4 + 4, :],
                              in_=wos[0][:, 4 * q4:4 * q4 + 4, :], func=AF.Copy, r=[wos[0]],
                              w=[(wob[cc % 2].name, q4)])
                for ti in range(n // 128):
                    tt = (t0 + ti * 128) // 128
                    ps = g.psA[it % 4]
                    st = yst[it % 3]
                    it += 1
                    for kt in range(16):
                        P.pe('matmul', out=ps[:, :], lhsT=mT[:, kt, ti * 128:(ti + 1) * 128], rhs=wob[cc % 2][:, kt, :],
                             start=(kt == 0), stop=(kt == 15), r=[mT, (wob[cc % 2].name, kt // 4)], w=[ps])
                    P.act('activation', out=st[:], in_=ps[:, :], func=AF.Copy, r=[ps], w=[st])
                    P.dve('tensor_tensor', out=junk[:], in0=st[:], in1=st[:], op=ALU.mult, r=[st], w=[junk])
                    P.dve('tensor_reduce', out=ss[:, tt, cc:cc + 1], in_=junk[:], axis=AX.X, op=ALU.add, r=[junk],
                          wa=[('mg_ss', tt)])
                    P.dma('pool', g.ybuf[tt * 128:(tt + 1) * 128, cc * 512:(cc + 1) * 512], st[:], r=[st],
                          wa=['ybuf'])
    with P.scope() as SC:
        GG = SC.sb("mg_GG", [128, 2, D], F32)
        yt = [SC.sb("mg_yt%d" % i, [128, D], F32) for i in range(2)]
        ht = [SC.sb("mg_ht%d" % i, [128, D], F32) for i in range(2)]
        rs = [SC.sb("mg_rs%d" % i, [128, 2], F32) for i in range(2)]
        for vi, j in enumerate((b, 2)):
            for jc in range(4):
                pb = g.psA[(vi * 4 + jc) % 4]
                P.pe('matmul', out=pb[:, :], lhsT=g.sel[:, j, :], rhs=g.grow[:, jc * 512:(jc + 1) * 512],
                     start=True, stop=True, r=[g.sel, g.grow], w=[pb])
                P.act('activation', out=GG[:, vi, jc * 512:(jc + 1) * 512], in_=pb[:, :], func=AF.Copy,
                      r=[pb], wa=[GG])
        src = g.xin if l == 0 else g.hbuf
        srckey = [] if l == 0 else [('hbuf', b)]
        tt0 = 0 if with_ctx else 2
        for tt in range(tt0, NTT):
            r = tt % 2
            vi = 1 if tt < 2 else 0
            P.dma('sp', yt[r][:], g.ybuf[tt * 128:(tt + 1) * 128, :], r=['ybuf'], w=[yt[r]])
            P.dma('sp', ht[r][:], src[b, tt * 128:(tt + 1) * 128, :], r=srckey, w=[ht[r]])
            P.dve('tensor_reduce', out=rs[r][:, 0:1], in_=ss[:, tt, :], axis=AX.X, op=ALU.add, r=[('mg_ss', tt)],
                  w=[rs[r]])
            P.dve('tensor_scalar', out=rs[r][:, 1:2], in0=rs[r][:, 0:1], scalar1=1.0 / D, scalar2=EPS, op0=ALU.mult,
                  op1=ALU.add, r=[rs[r]], w=[rs[r]])
            P.act('activation', out=rs[r][:, 1:2], in_=rs[r][:, 1:2], func=AF.Sqrt, r=[rs[r]], w=[rs[r]])
            P.dve('reciprocal', out=rs[r][:, 1:2], in_=rs[r][:, 1:2], r=[rs[r]], w=[rs[r]])
            P.dve('scalar_tensor_tensor', out=yt[r][:], in0=yt[r][:], scalar=rs[r][:, 1:2], in1=GG[:, vi, :],
                  op0=ALU.mult, op1=ALU.mult, r=[yt[r], rs[r], GG], w=[yt[r]])
            P.pool('tensor_tensor', out=ht[r][:], in0=ht[r][:], in1=yt[r][:], op=ALU.add, r=[yt[r], ht[r]], w=[ht[r]])
            if last:
                P.dma('sp', g.out[b, (tt - 2) * 128:(tt - 1) * 128, :], ht[r][:], r=[ht[r]], wa=['out'])
            else:
                P.dma('sp', g.hbuf[b, tt * 128:(tt + 1) * 128, :], ht[r][:], r=[ht[r]], wa=[('hbuf', b)])


N_CORES = 8
_PROGRAM = {}


def kernel(**inputs):
    sh = prep_shared(inputs)
    if "nc" not in _PROGRAM:
        _PROGRAM["nc"] = build_program(NB=2, NL=2)[0]
    nc = _PROGRAM["nc"]
    in_maps = []
    for c in range(N_CORES):
        m = dict(sh)
        m.update(prep_core(inputs, [2 * c, 2 * c + 1]))
        in_maps.append(m)
    res = run_bass_kernel_spmd(nc, in_maps, core_ids=list(range(N_CORES)))
    out = np.concatenate([np.asarray(res.results[c]["out"]) for c in range(N_CORES)], axis=0)
    return np.ascontiguousarray(out.astype(np.float32))
```

```python
import numpy as np
import concourse.bass as bass
import concourse.mybir as mybir
from concourse.bass_utils import run_bass_kernel_spmd

F32 = mybir.dt.float32
BF16 = mybir.dt.bfloat16
AF = mybir.ActivationFunctionType
ALU = mybir.AluOpType
AX = mybir.AxisListType


class Prog:
    import os as _os
    NHW = int(_os.environ.get("NHW", "16"))
    NDMA = NHW + 8

    def __init__(self, nc, same_engine_sync=True):
        self.nc = nc
        self.E = {'pe': nc.tensor, 'act': nc.scalar, 'dve': nc.vector, 'pool': nc.gpsimd, 'sp': nc.sync}
        self.sem = {e: nc.alloc_semaphore(name="sem_" + e) for e in ('pe', 'act', 'dve', 'pool')}
        self.cnt = {e: 0 for e in self.sem}
        self.dsem = [nc.alloc_semaphore(name="dsem%d" % i) for i in range(self.NDMA)]
        self.dval = [0] * self.NDMA
        self.dnext = 0
        self.dnext_sw = 0
        self.known = {e: {} for e in self.E}
        self.evclock = {}
        self.lastw = {}
        self.readers = {}
        self.same = same_engine_sync
        self.nwaits = 0
        self.ninst = 0
        self.pending = {}
        self._n = 0

    def sb(self, name, shape, dtype):
        return self.nc.alloc_sbuf_tensor(name, list(shape), dtype)

    def ps(self, name, shape, dtype=F32):
        return self.nc.alloc_psum_tensor(name, list(shape), dtype)

    def dram(self, name, shape, dtype, kind="Internal"):
        return self.nc.dram_tensor(name, list(shape), dtype, kind=kind)

    def _semof(self, k):
        return self.sem[k] if isinstance(k, str) else self.dsem[k[1]]

    def _wait(self, X, ev):
        k, v = ev
        kn = self.known[X]
        if kn.get(k, 0) >= v:
            return
        self.E[X].wait_ge(self._semof(k), v)
        self.nwaits += 1
        clk = self.evclock.get(ev)
        if clk:
            for kk, vv in clk.items():
                if kn.get(kk, 0) < vv:
                    kn[kk] = vv
        if kn.get(k, 0) < v:
            kn[k] = v

    @staticmethod
    def _keys(ks):
        return [k if isinstance(k, (str, tuple)) else k.name for k in ks]

    def _deps(self, X, r, w, wa=()):
        for key in r:
            for ev in self.lastw.get(key, ()):
                yield ev
        for key in w:
            for ev in self.lastw.get(key, ()):
                yield ev
            for ev in self.readers.get(key, ()):
                yield ev
        for key in wa:
            for ev in self.readers.get(key, ()):
                yield ev

    def _record(self, ev, r, w, wa=()):
        for key in r:
            lst = self.readers.setdefault(key, [])
            lst[:] = [e for e in lst if e[0] != ev[0]]
            lst.append(ev)
        for key in w:
            self.lastw[key] = [ev]
            self.readers[key] = []
        for key in wa:
            lst = self.lastw.setdefault(key, [])
            lst[:] = [e for e in lst if e[0] != ev[0]]
            lst.append(ev)

    def op(self, X, method, r=(), w=(), wa=(), defer=False, **kw):
        r = self._keys(r)
        w = self._keys(w)
        wa = self._keys(wa)
        pend = self.pending.setdefault(X, [[], [], []])
        if defer:
            assert X == 'pe'
            for ev in list(self._deps(X, r, w, wa)):
                if ev[0] == X:
                    continue
                self._wait(X, ev)
            getattr(self.E[X], method)(**kw)
            pend[0] += r
            pend[1] += w
            pend[2] += wa
            self.ninst += 1
            return None
        if pend[0] or pend[1] or pend[2]:
            r = list(dict.fromkeys(r + pend[0]))
            w = list(dict.fromkeys(w + pend[1]))
            wa = list(dict.fromkeys(wa + pend[2]))
            self.pending[X] = [[], [], []]
        for ev in list(self._deps(X, r, w, wa)):
            if ev[0] == X and (X == 'pe' or not self.same):
                continue
            self._wait(X, ev)
        inst = getattr(self.E[X], method)(**kw)
        self.cnt[X] += 1
        c = self.cnt[X]
        inst.then_inc(self.sem[X], 1)
        ev = (X, c)
        clk = dict(self.known[X])
        clk[X] = c
        self.evclock[ev] = clk
        self._record(ev, r, w, wa)
        self.ninst += 1
        return ev

    def pe(self, method, **kw):
        return self.op('pe', method, **kw)

    def act(self, method, **kw):
        return self.op('act', method, **kw)

    def dve(self, method, **kw):
        return self.op('dve', method, **kw)

    def pool(self, method, **kw):
        return self.op('pool', method, **kw)

    def dma(self, Q, out, in_, r=(), w=(), wa=(), **kw):
        r = self._keys(r)
        w = self._keys(w)
        wa = self._keys(wa)
        if Q == 'pool':
            i = self.NHW + self.dnext_sw
            self.dnext_sw = (self.dnext_sw + 1) % (self.NDMA - self.NHW)
        else:
            i = self.dnext
            self.dnext = (i + 1) % self.NHW
        k = ('d', i)
        if self.dval[i] > 0:
            self._wait(Q, (k, self.dval[i]))
        for ev in list(self._deps(Q, r, w, wa)):
            self._wait(Q, ev)
        inst = self.E[Q].dma_start(out=out, in_=in_, **kw)
        self.dval[i] += 16
        inst.then_inc(self.dsem[i], 16)
        ev = (k, self.dval[i])
        clk = dict(self.known[Q])
        clk[k] = self.dval[i]
        self.evclock[ev] = clk
        self._record(ev, r, w, wa)
        self.ninst += 1
        return ev

    def barrier(self):
        for X in ('pe', 'act', 'dve', 'pool', 'sp'):
            for e in self.sem:
                if self.cnt[e] > 0 and not (e == X and X == 'pe'):
                    self._wait(X, (e, self.cnt[e]))
            for i in range(self.NDMA):
                if self.dval[i] > 0:
                    self._wait(X, (('d', i), self.dval[i]))

    def scope(self):
        return Scope(self)

    def finish(self):
        for i in range(self.NDMA):
            if self.dval[i] > 0:
                self._wait('sp', (('d', i), self.dval[i]))
        for e in self.sem:
            if self.cnt[e] > 0:
                self._wait('sp', (e, self.cnt[e]))


D = 2048
TCTX = 256
TLAT = 2048
TT = TCTX + TLAT
NTT = TT // 128
N_TM = 6672
N_FM = 2080
PROWS = 2312
EPS = 1e-6

TM_NAV, TM_NAZ, TM_GLK, TM_GLKS, TM_GLV, TM_GLZ = 0, 512, 1024, 1280, 1536, 2048
TM_GDQKV, TM_GDZ, TM_HYXV, TM_HYZ, TM_GDAB = 2560, 4096, 4608, 6144, 6656
FM_NAQ, FM_NAK, FM_GLQ, FM_GLQS, FM_GLK, FM_GLKS, FM_GLG = 0, 512, 1024, 1280, 1536, 1792, 2048


def prow(tt):
    return 2 + tt * 128 if tt < 2 else 262 + (tt - 2) * 128


def w_in_perm():
    o = {}
    off = 0
    for name, w in (("na_q", 512), ("na_k", 512), ("na_v", 512), ("na_z", 512), ("gla_q", 256), ("gla_k", 256),
                    ("gla_v", 512), ("gla_z", 512), ("gla_g", 32), ("gdn_qkv", 1536), ("gdn_z", 512),
                    ("gdn_a", 8), ("gdn_b", 8), ("hy_xv", 1536), ("hy_z", 512)):
        o[name] = np.arange(off, off + w)
        off += w
    assert off == 7728

    def sw(ix):
        return ix.reshape(-1, 2, 32)[:, ::-1, :].reshape(-1)
    tm = np.concatenate([o["na_v"], o["na_z"], o["gla_k"], sw(o["gla_k"]), o["gla_v"], o["gla_z"], o["gdn_qkv"],
                         o["gdn_z"], o["hy_xv"], o["hy_z"], o["gdn_a"], o["gdn_b"]])
    fm = np.concatenate([o["na_q"], o["na_k"], o["gla_q"], sw(o["gla_q"]), o["gla_k"], sw(o["gla_k"]), o["gla_g"]])
    assert tm.size == N_TM and fm.size == N_FM
    return tm, fm


class Ctx:
    pass


class Scope:
    _uid = [0]

    def __init__(self, P):
        self.P = P
        self.cms = []

    def __enter__(self):
        return self

    def sb(self, name, shape, dtype):
        Scope._uid[0] += 1
        cm = self.P.nc.sbuf_tensor("%s_%d" % (name, Scope._uid[0]), list(shape), dtype)
        t = cm.__enter__()
        self.cms.append(cm)
        return t

    def ps(self, name, shape, dtype=F32):
        Scope._uid[0] += 1
        cm = self.P.nc.psum_tensor("%s_%d" % (name, Scope._uid[0]), list(shape), dtype)
        t = cm.__enter__()
        self.cms.append(cm)
        return t

    def __exit__(self, *a):
        self.P.barrier()
        for cm in reversed(self.cms):
            cm.__exit__(None, None, None)
        return False


def build_program(NB=2, NL=2, dbg=(), stop_after=None, branches="ABCDM"):
    nc = bass.Bass("TRN2", target_bir_lowering=False)
    P = Prog(nc)
    g = Ctx()
    g.nc, g.P, g.NB, g.NL = nc, P, NB, NL
    g.NL_total = 2
    g.branches = branches

    def din(name, shape, dt=F32):
        return nc.dram_tensor(name, list(shape), dt, kind="ExternalInput").ap()

    def dscr(name, shape, dt=F32):
        kind = "ExternalOutput" if name in dbg else "Internal"
        return nc.dram_tensor(name, list(shape), dt, kind=kind).ap()

    g.xin = din("xin", [NB, TT, D])
    g.cT = din("cT", [128, 16, 3])
    g.w_mod_t = din("w_mod_t", [2, 32, 128, 16, 128])
    g.w_mod_g = din("w_mod_g", [2, D, D])
    g.b_mod_col = din("b_mod_col", [2, 128, 32])
    g.b_mod_gate = din("b_mod_gate", [2, 1, D])
    g.g_pre_col = din("g_pre_col", [2, 128, 16])
    g.g_post_row = din("g_post_row", [2, 1, D])
    g.w_tm = din("w_tm", [2, D, N_TM])
    g.w_fm_t = din("w_fm_t", [2, 17, 128, 16, 128])
    g.w_gate_t = din("w_gate_t", [2, 4, 16, 128, 16, 128])
    g.b_gate_col = din("b_gate_col", [2, 128, 4, 16])
    g.w_branch_t = din("w_branch_t", [2, 4, 16, 128, 4, 128])
    g.w_out = din("w_out", [2, D, D])
    g.ident_in = din("ident", [128, 128])
    g.sel_in = din("sel", [3, 3, 128])
    g.na_tab2 = din("na_tab2", [2, 128, 8, 2, 7, 64])
    g.gla_Lm = din("gla_Lm", [2, 128, 128])
    g.rope_c_tm = din("rope_c_tm", [128, 16, 64])
    g.rope_s_tm = din("rope_s_tm", [128, 16, 64])
    g.rope_c_fm = din("rope_c_fm", [128, TLAT])
    g.rope_s_fm = din("rope_s_fm", [128, TLAT])
    g.gla_wg2 = din("gla_wg2", [2, 2, 16, 128])
    g.gla_bg_row = din("gla_bg_row", [2, 1, 2, 128])
    g.gla_norm_row = din("gla_norm_row", [2, 1, 128])
    g.gdn_mbs = din("gdn_mbs", [2, 128, 128])
    g.gdn_mbi = din("gdn_mbi", [2, 128, 128])
    g.ones128 = din("ones128", [128, 128])
    g.gdn_conv_row = din("gdn_conv_row", [2, 5, 1536])
    g.gdn_dtb_row = din("gdn_dtb_row", [2, 1, 8])
    g.gdn_alog_row = din("gdn_alog_row", [2, 1, 8])
    g.gdn_norm_row = din("gdn_norm_row", [2, 1, 128])
    g.hy_w_in = din("hy_w_in", [2, 33, 64])
    g.hy_w_mid = din("hy_w_mid", [2, 2, 64, 64])
    g.hy_w_out = din("hy_w_out", [2, 64, 1024])
    g.hy_freq_col = din("hy_freq_col", [2, 64, 3])
    g.hy_b_col = din("hy_b_col", [2, 64, 3])
    g.hy_conv_row = din("hy_conv_row", [2, 3, 1536])
    g.hy_conv_b_row = din("hy_conv_b_row", [2, 1, 1536])
    g.hy_skip_row = din("hy_skip_row", [2, 1, 512])
    g.hy_zT = [din("hy_zT0", [33, TLAT]), din("hy_zT1", [33, TCTX])]
    g.hy_decay = [din("hy_decay0", [128, 16, 512]), din("hy_decay1", [128, 2, 512])]
    g.hy_FmT = [din("hy_FmT0", [32, 128, 16, 128], BF16), din("hy_FmT1", [4, 128, 2, 128], BF16)]
    g.hy_GT = [din("hy_GT0", [16, 128, 32, 128], BF16), din("hy_GT1", [2, 128, 4, 128], BF16)]

    g.ptm = dscr("ptm", [NB, PROWS, N_TM])
    g.pfm = dscr("pfm", [NB, N_FM, TT])
    g.hbuf = dscr("hbuf", [NB, TT, D])
    g.yfm = dscr("yfm", [NB, 4 * 512, TT], BF16)
    g.khat = [dscr("khat0", [2 * TLAT, 512]), dscr("khat1", [2 * TCTX, 512])]
    g.hy_g0s = dscr("hy_g0s", [TT, 512])
    g.gq = dscr("gq", [TT, GQW])
    g.ybuf = dscr("ybuf", [TT, D])
    g.hy_vss = dscr("hy_vss", [TT, 512])
    g.out = nc.dram_tensor("out", [NB, TLAT, D], F32, kind="ExternalOutput").ap()

    g.ident = P.sb("ident_sb", [128, 128], F32)
    g.sel = P.sb("sel_sb", [3, 3, 128], F32)
    g.uT = P.sb("uT", [128, 16, TT], BF16)
    P.dma('sp', g.ident[:], g.ident_in, w=[g.ident])
    P.dma('sp', g.sel[:], g.sel_in, w=[g.sel])
    g.psA = [P.ps("psA%d" % i, [128, 512], F32) for i in range(4)]
    g.psB = [P.ps("psB%d" % i, [128, 512], F32) for i in range(4)]
    g.modcol = P.sb("modcol", [128, 32, 3], F32)
    g.Acol = P.sb("Acol", [128, 16, 3], F32)
    g.grow = P.sb("grow", [3, D], F32)
    with P.scope() as S:
        zero = S.sb("zero", [8, N_TM], F32)
        P.dve('memset', ap=zero[:], constant=0.0, w=[zero])
        for b in range(NB):
            for r0, n in ((0, 2), (258, 4), (2310, 2)):
                P.dma('sp', g.ptm[b, r0:r0 + n, :], zero[0:n, :], r=[zero], w=[('ptm', b)])

    for l in range(NL):
        phase_mod(g, l)
        if 'D' in g.branches:
            phase_hyfilt(g, l, 0)
            if l < g.NL_total - 1:
                phase_hyfilt(g, l, 1)
        for b in range(NB):
            phase_norm(g, l, b)
            if stop_after == 'norm':
                continue
            phase_proj(g, l, b)
            if stop_after == 'proj':
                continue
            if 'A' in g.branches:
                phase_na(g, l, b)
            if 'B' in g.branches:
                phase_gla(g, l, b)
            if 'C' in g.branches:
                phase_gdn_pre(g, l, b)
                phase_gdn(g, l, b)
            if 'D' in g.branches:
                phase_hy(g, l, b, 0)
                if l < g.NL_total - 1:
                    phase_hy(g, l, b, 1)
            if 'M' in g.branches:
                phase_merge(g, l, b)
    P.finish()
    return nc, g


def phase_mod(g, l):
    P, NB = g.P, g.NB
    with P.scope() as S:
        _phase_mod(g, l, S)


def _phase_mod(g, l, S):
    P = g.P
    g.cT_sb = S.sb("cT_sb", [128, 16, 3], F32)
    g.scT = S.sb("scT", [128, 16, 3], F32)
    g.mslab = [S.sb("mslab%d" % i, [128, 16, 128], F32) for i in range(2)]
    g.gslab = [S.sb("gslab%d" % i, [128, 4, 512], F32) for i in range(2)]
    g.bmc = S.sb("bmc", [128, 32], F32)
    g.gpc = S.sb("gpc", [128, 16], F32)
    g.brow = S.sb("brow", [3, D], F32)
    P.dma('sp', g.cT_sb[:], g.cT, w=[g.cT_sb])
    P.act('activation', out=g.scT[:], in_=g.cT_sb[:], func=AF.Silu, r=[g.cT_sb], w=[g.scT])
    ps = g.psA[0]
    for ft in range(32):
        slab = g.mslab[ft % 2]
        P.dma('sp', slab[:], g.w_mod_t[l, ft], w=[slab])
        for kt in range(16):
            P.pe('matmul', out=ps[:, ft * 3:(ft + 1) * 3], lhsT=slab[:, kt, :], rhs=g.scT[:, kt, :],
                 start=(kt == 0), stop=(kt == 15), r=[slab, g.scT], w=[ps])
    P.dma('sp', g.bmc[:], g.b_mod_col[l], w=[g.bmc])
    P.dma('sp', g.gpc[:], g.g_pre_col[l], w=[g.gpc])
    for j in range(3):
        P.dve('tensor_tensor', out=g.modcol[:, :, j], in0=ps[:, 0:96].rearrange("p (f j) -> p f j", j=3)[:, :, j],
              in1=g.bmc[:], op=ALU.add, r=[ps, g.bmc], w=[g.modcol])
        P.dve('scalar_tensor_tensor', out=g.Acol[:, :, j], in0=g.modcol[:, 16:32, j], scalar=1.0, in1=g.gpc[:],
              op0=ALU.add, op1=ALU.mult, r=[g.modcol, g.gpc], w=[g.Acol])
    psg = g.psA[1:3]
    for jc in range(4):
        for kq in range(4):
            slab = g.gslab[(jc * 4 + kq) % 2]
            P.dma('sp', slab[:], g.w_mod_g[l, kq * 512:(kq + 1) * 512, jc * 512:(jc + 1) * 512]
                  .rearrange("(kt p) f -> p kt f", p=128), w=[slab])
            for k4 in range(4):
                kt = kq * 4 + k4
                P.pe('matmul', out=psg[jc % 2][0:3, :], lhsT=g.scT[:, kt, :], rhs=slab[:, k4, :],
                     start=(kt == 0), stop=(kt == 15), r=[slab, g.scT], w=[psg[jc % 2]])
        P.act('activation', out=g.grow[:, jc * 512:(jc + 1) * 512], in_=psg[jc % 2][0:3, :], func=AF.Copy,
              r=[psg[jc % 2]], w=[g.grow])
    P.dma('sp', g.brow[:], g.b_mod_gate[l].partition_broadcast(3), w=[g.brow])
    P.dve('tensor_tensor', out=g.grow[:], in0=g.grow[:], in1=g.brow[:], op=ALU.add, r=[g.brow, g.grow], w=[g.grow])
    P.dma('sp', g.brow[:], g.g_post_row[l].partition_broadcast(3), r=[], w=[g.brow])
    P.dve('tensor_tensor', out=g.grow[:], in0=g.grow[:], in1=g.brow[:], op=ALU.mult, r=[g.brow, g.grow], w=[g.grow])


def phase_norm(g, l, b):
    with g.P.scope() as S:
        _phase_norm(g, l, b, S)


def _phase_norm(g, l, b, S):
    P = g.P
    g.hx = [S.sb("hx%d" % i, [128, D], F32) for i in range(2)]
    g.xr = [S.sb("xr%d" % i, [128, D], F32) for i in range(2)]
    g.ss = [S.sb("ss%d" % i, [128, 2], F32) for i in range(2)]
    src = g.xin if l == 0 else g.hbuf
    srckey = () if l == 0 else [('hbuf', b)]
    for tt in range(NTT):
        hx, xr, ss = g.hx[tt % 2], g.xr[tt % 2], g.ss[tt % 2]
        j = 2 if tt < 2 else b
        P.dma('sp', hx[:], src[b, tt * 128:(tt + 1) * 128, :], r=srckey, w=[hx])
        P.act('activation', out=xr[:], in_=hx[:], func=AF.Square, accum_out=ss[:, 0:1], r=[hx], w=[xr, ss])
        P.dve('tensor_scalar', out=ss[:, 1:2], in0=ss[:, 0:1], scalar1=1.0 / D, scalar2=EPS, op0=ALU.mult,
              op1=ALU.add, r=[ss], w=[ss])
        P.act('activation', out=ss[:, 1:2], in_=ss[:, 1:2], func=AF.Sqrt, r=[ss], w=[ss])
        P.dve('reciprocal', out=ss[:, 1:2], in_=ss[:, 1:2], r=[ss], w=[ss])
        P.dve('tensor_scalar', out=xr[:], in0=hx[:], scalar1=ss[:, 1:2], scalar2=None, op0=ALU.mult,
              r=[hx, ss], w=[xr])
        for kt in range(16):
            pt = g.psA[kt % 4]
            P.pe('transpose', out=pt[:, 0:128], in_=xr[:, kt * 128:(kt + 1) * 128], identity=g.ident[:],
                 r=[xr, g.ident], w=[pt])
            P.act('activation', out=g.uT[:, kt, tt * 128:(tt + 1) * 128], in_=pt[:, 0:128], func=AF.Identity,
                  scale=g.Acol[:, kt, j:j + 1], bias=g.modcol[:, kt, j:j + 1], r=[pt, g.Acol, g.modcol],
                  wa=[g.uT])


def phase_proj(g, l, b):
    with g.P.scope() as S:
        _phase_proj(g, l, b, S)


def _phase_proj(g, l, b, S):
    P = g.P
    g.wst = [S.sb("wst%d" % i, [128, 16, 512], F32) for i in range(2)]
    g.wbf = [S.sb("wbf%d" % i, [128, 16, 512], BF16) for i in range(2)]
    g.stg = [S.sb("stg%d" % i, [128, 512], F32) for i in range(3)]
    nchunk = (N_TM + 511) // 512
    it = 0
    for cc in range(nchunk):
        c0 = cc * 512
        cw = min(512, N_TM - c0)
        wst, wbf = g.wst[cc % 2], g.wbf[cc % 2]
        for h4 in range(4):
            P.dma('sp', wst[:, h4 * 4:(h4 + 1) * 4, 0:cw],
                  g.w_tm[l, h4 * 512:(h4 + 1) * 512, c0:c0 + cw].rearrange("(kt p) f -> p kt f", p=128),
                  w=[(wst.name, h4)])
            if h4 % 2 == 0:
                P.dve('tensor_copy', out=wbf[:, h4 * 4:(h4 + 1) * 4, 0:cw], in_=wst[:, h4 * 4:(h4 + 1) * 4, 0:cw],
                      r=[(wst.name, h4)], w=[(wbf.name, h4)])
            else:
                P.act('activation', out=wbf[:, h4 * 4:(h4 + 1) * 4, 0:cw], in_=wst[:, h4 * 4:(h4 + 1) * 4, 0:cw],
                      func=AF.Copy, r=[(wst.name, h4)], w=[(wbf.name, h4)])
        for tt in range(NTT):
            ps = g.psA[it % 4]
            stg = g.stg[it % 3]
            it += 1
            for kt in range(16):
                P.pe('matmul', out=ps[:, 0:cw], lhsT=g.uT[:, kt, tt * 128:(tt + 1) * 128], rhs=wbf[:, kt, 0:cw],
                     start=(kt == 0), stop=(kt == 15), r=[g.uT, (wbf.name, kt // 4)], w=[ps], defer=(kt < 15))
            P.act('activation', out=stg[:, 0:cw], in_=ps[:, 0:cw], func=AF.Copy, r=[ps], w=[stg])
            P.dma('pool', g.ptm[b, prow(tt):prow(tt) + 128, c0:c0 + cw], stg[:, 0:cw], r=[stg], wa=[('ptm', b)])
    nft = (N_FM + 127) // 128
    chunks = [(0, 256)] + [(256 + i * 512, 512) for i in range(4)]
    for ft in range(nft):
        f0 = ft * 128
        fw = min(128, N_FM - f0)
        wst, wbf = g.wst[ft % 2], g.wbf[ft % 2]
        P.dma('sp', wst[:, :, 0:128], g.w_fm_t[l, ft], w=[(wst.name, i) for i in range(4)])
        P.dve('tensor_copy', out=wbf[:, :, 0:fw], in_=wst[:, :, 0:fw], r=[(wst.name, i) for i in range(4)],
              w=[(wbf.name, i) for i in range(4)])
        for (t0, n) in chunks:
            ps = g.psA[it % 4]
            stg = g.stg[it % 3]
            it += 1
            for kt in range(16):
                P.pe('matmul', out=ps[0:fw, 0:n], lhsT=wbf[:, kt, 0:fw], rhs=g.uT[:, kt, t0:t0 + n],
                     start=(kt == 0), stop=(kt == 15), r=[g.uT, (wbf.name, kt // 4)], w=[ps], defer=(kt < 15))
            P.act('activation', out=stg[0:fw, 0:n], in_=ps[0:fw, 0:n], func=AF.Copy, r=[ps], w=[stg])
            P.dma('pool', g.pfm[b, f0:f0 + fw, t0:t0 + n], stg[0:fw, 0:n], r=[stg], wa=[('pfm', b)])


def prep_shared(inp):
    f = lambda a: np.ascontiguousarray(np.asarray(a, dtype=np.float32))
    tm, fm = w_in_perm()
    sh = {}
    w_mod = np.asarray(inp["w_mod"], dtype=np.float32)
    sh["w_mod_t"] = f(w_mod[:, :, :2 * D].reshape(2, 16, 128, 32, 128).transpose(0, 3, 2, 1, 4))
    sh["w_mod_g"] = f(w_mod[:, :, 2 * D:])
    b_mod = f(inp["b_mod"])
    sh["b_mod_col"] = f(b_mod[:, :2 * D].reshape(2, 32, 128).transpose(0, 2, 1))
    sh["b_mod_gate"] = f(b_mod[:, 2 * D:].reshape(2, 1, D))
    sh["g_pre_col"] = f(f(inp["g_pre"]).reshape(2, 16, 128).transpose(0, 2, 1))
    sh["g_post_row"] = f(f(inp["g_post"]).reshape(2, 1, D))
    sh["w_gate_t"] = f(np.asarray(inp["w_gate"], dtype=np.float32).reshape(2, 4, 16, 128, 16, 128)
                       .transpose(0, 1, 4, 3, 2, 5))
    sh["b_gate_col"] = f(f(inp["b_gate"]).reshape(2, 4, 16, 128).transpose(0, 3, 1, 2))
    sh["w_branch_t"] = f(np.asarray(inp["w_branch"], dtype=np.float32).reshape(2, 4, 4, 128, 16, 128)
                         .transpose(0, 1, 4, 3, 2, 5))
    sh["w_out"] = f(inp["w_out"])
    w_in = np.asarray(inp["w_in"], dtype=np.float32)
    sh["w_tm"] = f(w_in[:, :, tm])
    wfm = np.zeros((2, D, 17 * 128), np.float32)
    wfm[:, :, :N_FM] = w_in[:, :, fm]
    sh["w_fm_t"] = f(wfm.reshape(2, 16, 128, 17, 128).transpose(0, 3, 2, 1, 4))
    sh["ident"] = np.eye(128, dtype=np.float32)
    sel = np.zeros((3, 3, 128), np.float32)
    for j in range(3):
        sel[j, j, :] = 1.0
    sh["sel"] = sel
    tab5 = na_table(inp["na_rpb"])
    tab2 = np.empty((2, 2, 64, 8, 2, 7, 64), np.float32)
    for jj in range(2):
        for par in range(2):
            for m_ in range(7):
                tab2[:, jj, :, :, par, m_, :] = tab5[:, :, :, 2 * m_ + par + jj, :]
    sh["na_tab2"] = np.ascontiguousarray(tab2.reshape(2, 128, 8, 2, 7, 64))
    sh.update(gla_consts())
    sh["gla_wg2"] = f(inp["gla_wg2"])
    sh["gla_bg_row"] = f(f(inp["gla_bg"]).reshape(2, 1, 2, 128))
    sh["gla_norm_row"] = f(f(inp["gla_norm"]).reshape(2, 1, 128))
    sh.update(gdn_consts())
    sh["gdn_conv_row"] = f(f(inp["gdn_conv"]).transpose(0, 2, 1))
    sh["gdn_dtb_row"] = f(f(inp["gdn_dt_bias"]).reshape(2, 1, 8))
    sh["gdn_alog_row"] = f(f(inp["gdn_a_log"]).reshape(2, 1, 8))
    sh["gdn_norm_row"] = f(f(inp["gdn_norm"]).reshape(2, 1, 128))
    sh["hy_w_in"] = f(inp["hy_w_in"])
    sh["hy_w_mid"] = f(inp["hy_w_mid"])
    sh["hy_w_out"] = f(inp["hy_w_out"])
    sh["hy_freq_col"] = f(f(inp["hy_freq"]).transpose(0, 2, 1))
    sh["hy_b_col"] = f(np.concatenate([f(inp["hy_b_in"])[:, None, :], f(inp["hy_b_mid"])], axis=1).transpose(0, 2, 1))
    sh["hy_conv_row"] = f(f(inp["hy_conv"]).transpose(0, 2, 1))
    sh["hy_conv_b_row"] = f(f(inp["hy_conv_b"]).reshape(2, 1, 1536))
    sh["hy_skip_row"] = f(f(inp["hy_skip"]).reshape(2, 1, 512))
    for li, L in enumerate((TLAT, TCTX)):
        hc = hy_consts(L)
        sh["hy_zT%d" % li] = hc["zT"]
        sh["hy_decay%d" % li] = hc["decay"]
        sh["hy_FmT%d" % li] = hc["FmT"]
        sh["hy_GT%d" % li] = hc["GT"]
    return sh


def prep_core(inp, bs):
    x = np.asarray(inp["x"], dtype=np.float32)
    ctx = np.asarray(inp["ctx"], dtype=np.float32)
    c = np.asarray(inp["c"], dtype=np.float32)
    c_ctx = np.asarray(inp["c_ctx"], dtype=np.float32)
    d = {}
    d["xin"] = np.ascontiguousarray(np.stack([np.concatenate([ctx[b], x[b]], axis=0) for b in bs]))
    cb = [c[b] for b in bs]
    while len(cb) < 2:
        cb.append(cb[0])
    cm = np.stack(cb[:2] + [c_ctx], axis=1)
    d["cT"] = np.ascontiguousarray(cm.reshape(16, 128, 3).transpose(1, 0, 2))
    return d


def na_table(rpb):
    rpb = np.asarray(rpb, dtype=np.float32)
    kc = np.arange(64)[:, None]
    qc = np.arange(64)[None, :]
    cstart = np.clip(qc - 8, 0, 48)
    valid = (kc >= cstart) & (kc < cstart + 16)
    dc = np.clip(kc - qc + 15, 0, 30)
    tab = rpb[:, :, :, dc]
    tab = np.where(valid[None, None, None], tab, np.float32(-30000.0))
    return np.ascontiguousarray(tab.transpose(0, 3, 1, 2, 4).astype(np.float32))


def rowtok(row):
    return 64 * row


def phase_na(g, l, b):
    with g.P.scope() as S:
        _phase_na(g, l, b, S)


def _phase_na(g, l, b, S):
    P = g.P
    with_ctx = l < g.NL_total - 1
    stage = S.sb("na_stage", [128, TT], F32)
    qT = S.sb("na_qT", [128, TT], BF16)
    kTh = [S.sb("na_kT%d" % i, [128, TT], BF16) for i in range(2)]
    for i in range(2):
        P.pool('memset', ap=kTh[i][:], constant=0.0, w=[kTh[i]])
    stE = S.sb("na_stE", [128, 18, 128], F32)
    stO = S.sb("na_stO", [128, 15, 128], F32)
    vE = S.sb("na_vE", [128, 18, 2, 65], BF16)
    vO = S.sb("na_vO", [128, 15, 2, 65], BF16)
    st2 = S.sb("na_st2", [64, 36, 128], F32)
    oA = S.sb("na_oA", [64, 36, 128], F32)
    Tb = S.sb("na_Tb", [128, 2, 2, 7, 64], F32)
    sw = [S.sb("na_sw%d" % i, [128, 256], F32) for i in range(2)]
    pall = [S.sb("na_pall%d" % i, [128, 384], BF16) for i in range(2)]
    rec = [S.sb("na_rec%d" % i, [64, 1], F32) for i in range(2)]
    yst = S.sb("na_yst", [128, TT], BF16)
    psS, psO, psT = g.psA[0:2], g.psB[0:2], g.psB[2]
    row0 = 0 if with_ctx else 4
    for hp in range(4):
        P.dma('sp', stage[:], g.pfm[b, FM_NAQ + hp * 128:FM_NAQ + (hp + 1) * 128, :], r=[('pfm', b)], w=[stage])
        P.dve('tensor_copy', out=qT[:], in_=stage[:], r=[stage], w=[qT])
        P.dma('sp', stage[:], g.pfm[b, FM_NAK + hp * 128:FM_NAK + (hp + 1) * 128, :], r=[('pfm', b)], w=[stage])
        for i in range(2):
            P.dve('tensor_copy', out=kTh[i][64 * i:64 * i + 64, :], in_=stage[64 * i:64 * i + 64, :], r=[stage],
                  w=[kTh[i]])
        c0 = TM_NAV + hp * 128
        P.dma('sp', stE[:, 0:2, :], g.ptm[b, 2:258, c0:c0 + 128].rearrange("(m p) c -> p m c", p=128),
              r=[('ptm', b)], w=[stE])
        P.dma('sp', stE[:, 2:18, :], g.ptm[b, 262:2310, c0:c0 + 128].rearrange("(m p) c -> p m c", p=128),
              r=[('ptm', b)], wa=[stE])
        P.dma('sp', stO[:], g.ptm[b, 326:326 + 15 * 128, c0:c0 + 128].rearrange("(m p) c -> p m c", p=128),
              r=[('ptm', b)], w=[stO])
        P.pool('memset', ap=vE[:], constant=1.0, w=[vE])
        P.pool('memset', ap=vO[:], constant=1.0, w=[vO])
        P.dve('tensor_copy', out=vE[:, :, :, 0:64], in_=stE[:].rearrange("p r (h d) -> p r h d", h=2), r=[stE], w=[vE])
        P.dve('tensor_copy', out=vO[:, :, :, 0:64], in_=stO[:].rearrange("p r (h d) -> p r h d", h=2), r=[stO], w=[vO])
        c0 = TM_NAZ + hp * 128
        P.dma('sp', st2[:, 0:4, :], g.ptm[b, 2:258, c0:c0 + 128].rearrange("(r p) c -> p r c", p=64),
              r=[('ptm', b)], w=[st2])
        P.dma('sp', st2[:, 4:36, :], g.ptm[b, 262:2310, c0:c0 + 128].rearrange("(r p) c -> p r c", p=64),
              r=[('ptm', b)], wa=[st2])
        P.act('activation', out=st2[:], in_=st2[:], func=AF.Silu, r=[st2], w=[st2])
        P.dma('sp', Tb[:], g.na_tab2[l, :, 2 * hp:2 * hp + 2], w=[Tb])
        it = 0
        for h2 in range(2):
            hb = 64 * h2
            kT = kTh[h2]
            for row in range(row0, 36):
                i2 = it % 2
                it += 1
                tq = 64 * row
                qv = qT[:, tq:tq + 64]
                lat = row >= 4
                ps = psS[i2]
                vts = []
                if lat:
                    r_ = row - 4
                    rs = min(max(r_ - 4, 0), 24)
                    dr0 = rs - r_ + 7
                    for j in range(4):
                        tk = 256 + 64 * (rs + 2 * j)
                        P.pe('matmul', out=ps[:, j * 64:(j + 1) * 64], lhsT=kT[:, tk:tk + 128], rhs=qv, start=True,
                             stop=True, r=[kT, qT], w=[ps])
                        vts.append(vE[:, tk // 128, h2, :] if tk % 128 == 0 else vO[:, (tk - 64) // 128 - 2, h2, :])
                for i in range(2):
                    P.pe('matmul', out=ps[:, 256 + i * 64:256 + (i + 1) * 64], lhsT=kT[:, 128 * i:128 * i + 128],
                         rhs=qv, start=True, stop=True, r=[kT, qT], w=[ps])
                if lat:
                    P.dve('scalar_tensor_tensor', out=sw[i2][:], in0=ps[:, 0:256], scalar=0.125,
                          in1=Tb[:, h2, dr0 % 2, dr0 // 2:dr0 // 2 + 4, :].rearrange("p a b -> p (a b)"),
                          op0=ALU.mult, op1=ALU.add, r=[ps, Tb], w=[sw[i2]])
                    P.act('activation', out=pall[i2][:, 0:256], in_=sw[i2][:], func=AF.Exp, r=[sw[i2]],
                          w=[(pall[i2].name, 0)])
                P.act('activation', out=pall[i2][:, 256:384], in_=ps[:, 256:384], func=AF.Exp, scale=0.125,
                      r=[ps], w=[(pall[i2].name, 1)])
                if lat:
                    for j in range(4):
                        P.pe('matmul', out=psO[i2][0:64, 0:65], lhsT=pall[i2][:, j * 64:(j + 1) * 64], rhs=vts[j],
                             start=(j == 0), stop=False, r=[(pall[i2].name, 0), vE, vO], w=[psO[i2]])
                for i in range(2):
                    P.pe('matmul', out=psO[i2][0:64, 0:65], lhsT=pall[i2][:, 256 + i * 64:256 + (i + 1) * 64],
                         rhs=vE[:, i, h2, :], start=(i == 0 and not lat), stop=(i == 1),
                         r=[(pall[i2].name, 1), vE], w=[psO[i2]])
                P.dve('reciprocal', out=rec[i2][:], in_=psO[i2][0:64, 64:65], r=[psO[i2]], w=[rec[i2]])
                P.dve('tensor_scalar', out=oA[:, row, hb:hb + 64], in0=psO[i2][0:64, 0:64], scalar1=rec[i2][:, 0:1],
                      scalar2=None, op0=ALU.mult, r=[psO[i2], rec[i2]], wa=[oA])
        P.dve('tensor_tensor', out=oA[:], in0=oA[:], in1=st2[:], op=ALU.mult, r=[st2, oA], w=[oA])
        for row in range(row0, 36):
            P.pe('transpose', out=psT[:, 0:64], in_=oA[:, row, :], identity=g.ident[0:64, 0:64],
                 r=[oA, g.ident], w=[psT])
            P.act('activation', out=yst[:, 64 * row:64 * row + 64], in_=psT[:, 0:64], func=AF.Copy,
                  r=[psT], wa=[yst])
        t0 = 64 * row0
        P.dma('sp', g.yfm[b, hp * 128:(hp + 1) * 128, t0:TT], yst[:, t0:TT], r=[yst], wa=[('yfm', b)])


TWO_PI = 2.0 * np.pi


def hy_consts(L):
    import ml_dtypes
    N = 2 * L
    ntt = L // 128
    t = np.arange(L, dtype=np.float64)
    f = np.arange(L, dtype=np.float64)
    ang = 2.0 * np.pi * np.outer(t, f) / N
    Fm = np.empty((L, N), np.float64)
    Fm[:, :L] = np.cos(ang)
    Fm[:, L:] = -np.sin(ang)
    Fm[:, L] = np.cos(np.pi * t)
    G = np.empty((N, L), np.float64)
    G[:L, :] = 2.0 * np.cos(ang.T) / N
    G[0, :] = 1.0 / N
    G[L:, :] = -2.0 * np.sin(ang.T) / N
    G[L, :] = np.cos(np.pi * t) / N
    FmT = Fm.reshape(ntt, 128, 2 * ntt, 128).transpose(2, 1, 0, 3)
    GT = G.reshape(2 * ntt, 128, ntt, 128).transpose(2, 1, 0, 3)
    tt_ = np.linspace(0.0, 1.0, L, dtype=np.float32)[:, None]
    bands = 16
    wpos = (np.float32(2.0 * np.pi) * np.arange(L, dtype=np.float32)[:, None] / np.float32(L)).astype(np.float32)
    fb = np.linspace(1e-4, bands - 1, bands, dtype=np.float32)[None]
    z = np.concatenate([tt_, np.cos(fb * wpos), -np.sin(fb * wpos)], axis=-1).astype(np.float32)
    deltas = np.abs(np.linspace(np.log(1e-2) / 1.5, np.log(1e-2) / 0.3, 512, dtype=np.float32))
    decay = np.exp(-tt_ * deltas).astype(np.float32)
    return {
        "FmT": np.ascontiguousarray(FmT).astype(ml_dtypes.bfloat16),
        "GT": np.ascontiguousarray(GT).astype(ml_dtypes.bfloat16),
        "zT": np.ascontiguousarray(z.T),
        "decay": np.ascontiguousarray(decay.reshape(ntt, 128, 512).transpose(1, 0, 2)),
    }


def phase_hyfilt(g, l, li):
    with g.P.scope() as S:
        _phase_hyfilt(g, l, li, S)


def _phase_hyfilt(g, l, li, S):
    P = g.P
    L = (TLAT, TCTX)[li]
    ntt = L // 128
    wi = S.sb("hf_wi", [33, 64], F32)
    wm = S.sb("hf_wm", [64, 2, 64], F32)
    wo = S.sb("hf_wo", [64, 1024], F32)
    fcol = S.sb("hf_fcol", [64, 3], F32)
    bcol = S.sb("hf_bcol", [64, 3], F32)
    fs = S.sb("hf_fs", [64, 3], F32)
    fb = S.sb("hf_fb", [64, 3], F32)
    zT = S.sb("hf_zT", [33, L], F32)
    hT = [S.sb("hf_hT%d" % i, [64, L], F32) for i in range(2)]
    ua = [S.sb("hf_ua%d" % i, [64, 512], F32) for i in range(2)]
    ub = [S.sb("hf_ub%d" % i, [64, 512], F32) for i in range(2)]
    dec = S.sb("hf_dec", [128, ntt, 512], F32)
    hfb = [S.sb("hf_hfb%d" % i, [128, 512], F32) for i in range(2)]
    hsb = S.sb("hf_hsb", [128, ntt, 512], BF16)
    hdb = S.sb("hf_hdb", [128, ntt, 512], BF16)
    fmt = [S.sb("hf_fmt%d" % i, [128, ntt, 128], BF16) for i in range(2)]
    kst = [S.sb("hf_kst%d" % i, [128, 512], F32) for i in range(2)]
    P.dma('sp', wi[:], g.hy_w_in[l], w=[wi])
    P.dma('sp', wm[:], g.hy_w_mid[l].rearrange("i k n -> k i n"), w=[wm])
    P.dma('sp', wo[:], g.hy_w_out[l], w=[wo])
    P.dma('sp', fcol[:], g.hy_freq_col[l], w=[fcol])
    P.dma('sp', bcol[:], g.hy_b_col[l], w=[bcol])
    P.dma('sp', zT[:], g.hy_zT[li], w=[zT])
    P.dma('sp', dec[:], g.hy_decay[li], w=[dec])
    P.dve('tensor_scalar', out=fs[:], in0=fcol[:], scalar1=1.0 / TWO_PI, scalar2=None, op0=ALU.mult,
          r=[fcol], w=[fs])
    P.dve('tensor_tensor', out=fb[:], in0=fs[:], in1=bcol[:], op=ALU.mult, r=[fs, bcol], w=[fb])
    nch = (L + 511) // 512
    cwid = min(512, L)
    it = 0
    for i in range(3):
        dst = hT[i % 2]
        for c in range(nch):
            ps = g.psA[it % 4]
            i2 = it % 2
            it += 1
            if i == 0:
                P.pe('matmul', out=ps[0:64, 0:cwid], lhsT=wi[:], rhs=zT[:, c * cwid:(c + 1) * cwid], start=True,
                     stop=True, r=[wi, zT], w=[ps])
            else:
                src = hT[(i - 1) % 2]
                P.pe('matmul', out=ps[0:64, 0:cwid], lhsT=wm[:, i - 1, :], rhs=src[:, c * cwid:(c + 1) * cwid],
                     start=True, stop=True, r=[wm, src], w=[ps])
            P.act('activation', out=ua[i2][:, 0:cwid], in_=ps[0:64, 0:cwid], func=AF.Identity,
                  scale=fs[:, i:i + 1], bias=fb[:, i:i + 1], r=[ps, fs, fb], w=[ua[i2]])
            P.dve('scalar_tensor_tensor', out=ub[i2][:, 0:cwid], in0=ua[i2][:, 0:cwid], scalar=0.5,
                  in1=ua[i2][:, 0:cwid], op0=ALU.is_gt, op1=ALU.subtract, r=[ua[i2]], w=[ub[i2]])
            P.dve('scalar_tensor_tensor', out=ub[i2][:, 0:cwid], in0=ua[i2][:, 0:cwid], scalar=-0.5,
                  in1=ub[i2][:, 0:cwid], op0=ALU.is_lt, op1=ALU.subtract, r=[ua[i2], ub[i2]], w=[ub[i2]])
            P.act('activation', out=dst[:, c * cwid:(c + 1) * cwid], in_=ub[i2][:, 0:cwid], func=AF.Sin,
                  scale=TWO_PI, r=[ub[i2]], wa=[dst])
    h3 = hT[0]
    for tt in range(ntt):
        for half in range(2):
            ps = g.psA[it % 4]
            it += 1
            P.pe('matmul', out=ps[:, :], lhsT=h3[:, tt * 128:(tt + 1) * 128], rhs=wo[:, half * 512:(half + 1) * 512],
                 start=True, stop=True, r=[h3, wo], w=[ps])
            P.dve('tensor_tensor', out=hfb[half][:], in0=ps[:, :], in1=dec[:, tt, :], op=ALU.mult,
                  r=[ps, dec], w=[hfb[half]])
        P.dve('tensor_tensor', out=hsb[:, tt, :], in0=hfb[0][:], in1=hfb[1][:], op=ALU.add,
              r=[hfb[0], hfb[1]], wa=[hsb])
        P.pool('tensor_tensor', out=hdb[:, tt, :], in0=hfb[0][:], in1=hfb[1][:], op=ALU.subtract,
               r=[hfb[0], hfb[1]], wa=[hdb])
    for rt in range(2 * ntt):
        fm = fmt[rt % 2]
        ks = kst[rt % 2]
        P.dma('sp', fm[:], g.hy_FmT[li][rt], w=[fm])
        ps = g.psA[it % 4]
        it += 1
        src = hsb if rt < ntt else hdb
        for tt in range(ntt):
            P.pe('matmul', out=ps[:, :], lhsT=fm[:, tt, :], rhs=src[:, tt, :], start=(tt == 0), stop=(tt == ntt - 1),
                 r=[fm, src], w=[ps])
        P.act('activation', out=ks[:], in_=ps[:, :], func=AF.Copy, r=[ps], w=[ks])
        if rt == ntt:
            ps2 = g.psA[it % 4]
            it += 1
            for tt in range(ntt):
                P.pe('matmul', out=ps2[:, :], lhsT=fm[:, tt, :], rhs=hsb[:, tt, :], start=(tt == 0),
                     stop=(tt == ntt - 1), r=[fm, hsb], w=[ps2])
            P.act('activation', out=ks[0:1, :], in_=ps2[0:1, :], func=AF.Copy, r=[ps2, ks], w=[ks])
        P.dma('sp', g.khat[li][rt * 128:(rt + 1) * 128, :], ks[:], r=[ks], wa=[('khat', li)])


def phase_hy(g, l, b, li):
    with g.P.scope() as S:
        _phase_hy(g, l, b, li, S)


def _phase_hy(g, l, b, li, S):
    P = g.P
    L = (TLAT, TCTX)[li]
    ntt = L // 128
    tile0 = 2 if li == 0 else 0
    tok0 = 256 if li == 0 else 0
    vvb = S.sb("hy_vvb", [128, ntt, 512], BF16)
    with P.scope() as S1:
        wrow = S1.sb("hy_wrow", [128, 3, 1536], F32)
        brow = S1.sb("hy_brow", [128, 1536], F32)
        srow = S1.sb("hy_srow", [128, 512], F32)
        xs = [S1.sb("hy_xs%d" % i, [128, 1536], F32) for i in range(3)]
        acc = S1.sb("hy_acc", [128, 1536], F32)
        tmp = S1.sb("hy_tmp", [128, 1536], F32)
        zt = S1.sb("hy_zt", [128, 512], F32)
        vv = S1.sb("hy_vv", [128, 512], F32)
        g0 = S1.sb("hy_g0", [128, 512], F32)
        vs = S1.sb("hy_vs", [128, 512], F32)
        for k in range(3):
            P.dma('sp', wrow[:, k, :], g.hy_conv_row[l, k:k + 1, :].partition_broadcast(128), wa=[wrow])
        P.dma('sp', brow[:], g.hy_conv_b_row[l].partition_broadcast(128), w=[brow])
        P.dma('sp', srow[:], g.hy_skip_row[l].partition_broadcast(128), w=[srow])
        for tt in range(ntt):
            base = prow(tile0 + tt)
            for k in range(3):
                P.dma('sp', xs[k][:], g.ptm[b, base + k - 1:base + k - 1 + 128, TM_HYXV:TM_HYXV + 1536],
                      r=[('ptm', b)], w=[xs[k]])
            P.dma('sp', zt[:], g.ptm[b, base:base + 128, TM_HYZ:TM_HYZ + 512], r=[('ptm', b)], w=[zt])
            P.dve('tensor_tensor', out=acc[:], in0=xs[0][:], in1=wrow[:, 0, :], op=ALU.mult, r=[xs[0], wrow], w=[acc])
            P.pool('tensor_tensor', out=tmp[:], in0=xs[1][:], in1=wrow[:, 1, :], op=ALU.mult, r=[xs[1], wrow],
                   w=[tmp])
            P.dve('tensor_tensor', out=acc[:], in0=acc[:], in1=tmp[:], op=ALU.add, r=[tmp, acc], w=[acc])
            P.pool('tensor_tensor', out=tmp[:], in0=xs[2][:], in1=wrow[:, 2, :], op=ALU.mult, r=[xs[2], wrow],
                   w=[tmp])
            P.dve('tensor_tensor', out=acc[:], in0=acc[:], in1=tmp[:], op=ALU.add, r=[tmp, acc], w=[acc])
            P.dve('tensor_tensor', out=acc[:], in0=acc[:], in1=brow[:], op=ALU.add, r=[brow, acc], w=[acc])
            P.dve('tensor_tensor', out=vv[:], in0=acc[:, 1024:1536], in1=acc[:, 512:1024], op=ALU.mult,
                  r=[acc], w=[vv])
            P.pool('tensor_copy', out=vvb[:, tt, :], in_=vv[:], r=[vv], wa=[vvb])
            P.act('activation', out=zt[:], in_=zt[:], func=AF.Silu, r=[zt], w=[zt])
            P.dve('tensor_tensor', out=g0[:], in0=acc[:, 0:512], in1=zt[:], op=ALU.mult, r=[acc, zt], w=[g0])
            P.pool('tensor_tensor', out=vs[:], in0=vv[:], in1=srow[:], op=ALU.mult, r=[vv, srow], w=[vs])
            P.dve('tensor_tensor', out=vs[:], in0=vs[:], in1=g0[:], op=ALU.mult, r=[vs, g0], w=[vs])
            P.dma('pool', g.hy_g0s[tok0 + tt * 128:tok0 + (tt + 1) * 128, :], g0[:], r=[g0], wa=['hy_g0s'])
            P.dma('pool', g.hy_vss[tok0 + tt * 128:tok0 + (tt + 1) * 128, :], vs[:], r=[vs], wa=['hy_vss'])
    yhat = S.sb("hy_yhat", [128, 2 * ntt, 512], BF16)
    with P.scope() as S2:
        fmt = [S2.sb("hy_fmt%d" % i, [128, ntt, 128], BF16) for i in range(4)]
        kk = [S2.sb("hy_kk%d" % i, [128, 512], F32) for i in range(4)]
        vh = [S2.sb("hy_vh%d" % i, [128, 512], F32) for i in range(4)]
        tq = [S2.sb("hy_tq%d" % i, [128, 512], F32) for i in range(4)]
        it = 0
        for i in range(ntt):
            i2 = (i % 2) * 2
            fre, fim, kre, kim, vre, vim = fmt[i2], fmt[i2 + 1], kk[i2], kk[i2 + 1], vh[i2], vh[i2 + 1]
            P.dma('sp', fre[:], g.hy_FmT[li][i], w=[fre])
            P.dma('sp', fim[:], g.hy_FmT[li][i + ntt], w=[fim])
            P.dma('sp', kre[:], g.khat[li][i * 128:(i + 1) * 128, :], r=[('khat', li)], w=[kre])
            P.dma('sp', kim[:], g.khat[li][(i + ntt) * 128:(i + ntt + 1) * 128, :], r=[('khat', li)], w=[kim])
            for (fm, vdst) in ((fre, vre), (fim, vim)):
                ps = g.psA[it % 4]
                it += 1
                for tt in range(ntt):
                    P.pe('matmul', out=ps[:, :], lhsT=fm[:, tt, :], rhs=vvb[:, tt, :], start=(tt == 0),
                         stop=(tt == ntt - 1), r=[fm, vvb], w=[ps])
                P.act('activation', out=vdst[:], in_=ps[:, :], func=AF.Copy, r=[ps], w=[vdst])
            t1, t2, t3, t4 = tq
            P.dve('tensor_tensor', out=t1[:], in0=vre[:], in1=kre[:], op=ALU.mult, r=[vre, kre], w=[t1])
            P.pool('tensor_tensor', out=t2[:], in0=vim[:], in1=kim[:], op=ALU.mult, r=[vim, kim], w=[t2])
            P.dve('tensor_tensor', out=t3[:], in0=vre[:], in1=kim[:], op=ALU.mult, r=[vre, kim], w=[t3])
            P.pool('tensor_tensor', out=t4[:], in0=vim[:], in1=kre[:], op=ALU.mult, r=[vim, kre], w=[t4])
            P.dve('tensor_tensor', out=yhat[:, i, :], in0=t1[:], in1=t2[:], op=ALU.subtract, r=[t1, t2], wa=[yhat])
            P.pool('tensor_tensor', out=yhat[:, i + ntt, :], in0=t3[:], in1=t4[:], op=ALU.add, r=[t3, t4], wa=[yhat])
            if i == 0:
                P.dve('tensor_tensor', out=yhat[0:1, 0, :], in0=vre[0:1, :], in1=kre[0:1, :], op=ALU.mult,
                      r=[vre, kre], w=[yhat])
                P.dve('tensor_tensor', out=yhat[0:1, ntt, :], in0=vim[0:1, :], in1=kim[0:1, :], op=ALU.mult,
                      r=[vim, kim], w=[yhat])
    yst = S.sb("hy_yst", [128, 4, L], BF16)
    with P.scope() as S3:
        gt = [S3.sb("hy_gt%d" % i, [128, 2 * ntt, 128], BF16) for i in range(2)]
        g0 = [S3.sb("hy_g0b%d" % i, [128, 512], F32) for i in range(2)]
        vs = [S3.sb("hy_vsb%d" % i, [128, 512], F32) for i in range(2)]
        o = [S3.sb("hy_o%d" % i, [128, 512], F32) for i in range(2)]
        for j in range(ntt):
            j2 = j % 2
            P.dma('sp', gt[j2][:], g.hy_GT[li][j], w=[gt[j2]])
            P.dma('sp', g0[j2][:], g.hy_g0s[tok0 + j * 128:tok0 + (j + 1) * 128, :], r=['hy_g0s'], w=[g0[j2]])
            P.dma('sp', vs[j2][:], g.hy_vss[tok0 + j * 128:tok0 + (j + 1) * 128, :], r=['hy_vss'], w=[vs[j2]])
            ps = g.psA[j2]
            for rt in range(2 * ntt):
                P.pe('matmul', out=ps[:, :], lhsT=gt[j2][:, rt, :], rhs=yhat[:, rt, :], start=(rt == 0),
                     stop=(rt == 2 * ntt - 1), r=[gt[j2], yhat], w=[ps])
            P.dve('tensor_tensor', out=o[j2][:], in0=ps[:, :], in1=g0[j2][:], op=ALU.mult, r=[ps, g0[j2]], w=[o[j2]])
            P.pool('tensor_tensor', out=o[j2][:], in0=o[j2][:], in1=vs[j2][:], op=ALU.add, r=[vs[j2], o[j2]],
                   w=[o[j2]])
            pt = g.psB[j2]
            for k in range(4):
                P.pe('transpose', out=pt[:, k * 128:(k + 1) * 128], in_=o[j2][:, k * 128:(k + 1) * 128],
                     identity=g.ident[:], r=[o[j2], g.ident], w=[pt])
            P.act('activation', out=yst[:, :, j * 128:(j + 1) * 128], in_=pt[:, :].rearrange("p (k t) -> p k t", k=4),
                  func=AF.Copy, r=[pt], wa=[yst])
        for k in range(4):
            P.dma('sp', g.yfm[b, 1536 + k * 128:1536 + (k + 1) * 128, tok0:tok0 + L], yst[:, k, :], r=[yst],
                  wa=[('yfm', b)])


def gla_consts():
    s = np.arange(128)[:, None]
    t = np.arange(128)[None, :]
    Lm = np.stack([(s <= t), (s >= t)]).astype(np.float32)
    pos = np.arange(TLAT)
    row = (pos // 64).astype(np.float32)
    col = (pos % 64).astype(np.float32)
    n = 16
    freqs = (np.float32(10000.0) ** (-np.arange(n, dtype=np.float32) / np.float32(n))).astype(np.float32)
    ang = np.concatenate([row[:, None] * freqs, col[:, None] * freqs], axis=-1).astype(np.float32)
    cos, sin = np.cos(ang).astype(np.float32), np.sin(ang).astype(np.float32)
    c64 = np.concatenate([cos, cos], axis=-1)
    s64 = np.concatenate([-sin, sin], axis=-1)
    return {
        "gla_Lm": Lm,
        "rope_c_tm": np.ascontiguousarray(c64.reshape(16, 128, 64).transpose(1, 0, 2)),
        "rope_s_tm": np.ascontiguousarray(s64.reshape(16, 128, 64).transpose(1, 0, 2)),
        "rope_c_fm": np.ascontiguousarray(np.concatenate([c64.T, c64.T], axis=0)),
        "rope_s_fm": np.ascontiguousarray(np.concatenate([s64.T, s64.T], axis=0)),
    }


def phase_gla(g, l, b):
    with g.P.scope() as S:
        _phase_gla(g, l, b, S)


def _phase_gla(g, l, b, S):
    P = g.P
    with_ctx = l < g.NL_total - 1
    oB = S.sb("gl_oB", [128, NTT, 512], F32)
    S0, S = S, Scope(P)
    cfm = S.sb("gl_cfm", [128, TLAT], F32)
    sfm = S.sb("gl_sfm", [128, TLAT], F32)
    ctm = S.sb("gl_ctm", [128, 16, 64], F32)
    stm = S.sb("gl_stm", [128, 16, 64], F32)
    Lm = S.sb("gl_Lm", [128, 2, 128], F32)
    wg2 = S.sb("gl_wg2", [16, 2, 128], F32)
    bg = S.sb("gl_bg", [1, 2, 128], F32)
    ones = S.sb("gl_ones", [1, 128], F32)
    Sp = [S.sb("gl_S%d" % i, [128, 128], F32) for i in range(2)]
    Sbf = [S.sb("gl_Sbf%d" % i, [128, 128], BF16) for i in range(2)]
    NR = 2
    fmq = [[S.sb("gl_fm%d_%d" % (k, r), [128, 128], F32) for k in range(8)] for r in range(NR)]
    lrt = [S.sb("gl_lrt%d" % r, [16, 128], F32) for r in range(NR)]
    ktm = [S.sb("gl_ktm%d" % r, [128, 512], F32) for r in range(NR)]
    vtm = [S.sb("gl_vtm%d" % r, [128, 512], F32) for r in range(NR)]
    vbf = [S.sb("gl_vbf%d" % r, [128, 512], BF16) for r in range(NR)]
    ee = [S.sb("gl_ee%d" % r, [128, 128], F32) for r in range(NR)]
    gdup = [S.sb("gl_gdup%d" % r, [128, 256], F32) for r in range(NR)]
    enb_tm = [S.sb("gl_enbtm%d" % r, [128, 256], F32) for r in range(NR)]
    eb_fm = [[S.sb("gl_ebfm%d_%d" % (i, r), [128, 128], F32) for i in range(2)] for r in range(NR)]
    enb_fm = [[S.sb("gl_enbfm%d_%d" % (i, r), [128, 128], F32) for i in range(2)] for r in range(NR)]
    t1 = [S.sb("gl_t1_%d" % r, [128, 256], F32) for r in range(NR)]
    t2 = [S.sb("gl_t2_%d" % r, [128, 256], F32) for r in range(NR)]
    qeT = [[S.sb("gl_qeT%d_%d" % (h, r), [128, 128], BF16) for h in range(4)] for r in range(NR)]
    keT = [[S.sb("gl_keT%d_%d" % (h, r), [128, 128], BF16) for h in range(4)] for r in range(NR)]
    for r in range(NR):
        for h in range(4):
            P.pool('memset', ap=qeT[r][h][:], constant=0.0, w=[qeT[r][h]])
            P.pool('memset', ap=keT[r][h][:], constant=0.0, w=[keT[r][h]])
    ke_tm = [S.sb("gl_ketm%d" % r, [128, 256], BF16) for r in range(NR)]
    attT = [S.sb("gl_attT%d" % r, [128, 512], BF16) for r in range(NR)]
    stmp = S.sb("gl_stmp", [128, 128], F32)
    P.dma('sp', cfm[:], g.rope_c_fm, w=[cfm])
    P.dma('sp', sfm[:], g.rope_s_fm, w=[sfm])
    P.dma('sp', ctm[:], g.rope_c_tm, w=[ctm])
    P.dma('sp', stm[:], g.rope_s_tm, w=[stm])
    P.dma('sp', Lm[:], g.gla_Lm.rearrange("d s t -> s d t"), w=[Lm])
    P.dma('sp', wg2[:], g.gla_wg2[l].rearrange("d k n -> k d n"), w=[wg2])
    P.dma('sp', bg[:], g.gla_bg_row[l], w=[bg])
    P.dve('memset', ap=ones[:], constant=1.0, w=[ones])
    pg, pbt, pbf, patt, po, pss = g.psA[0], g.psA[1], g.psA[2:4], g.psB[0], g.psB[1], g.psB[2:4]
    it = 0
    for d in range(2):
        order = list(range(NTT)) if d == 0 else [1, 0] + list(range(NTT - 1, 1, -1))
        tend = 127 if d == 0 else 0
        for i in range(2):
            P.dve('memset', ap=Sp[i][:], constant=0.0, w=[Sp[i]])
            P.pool('memset', ap=Sbf[i][:], constant=0.0, w=[Sbf[i]])
        import os as _os
        LVL = int(_os.environ.get("GLA_LVL", "99"))
        if _os.environ.get("GLA_SHORT"):
            order = [int(v) for v in _os.environ["GLA_SHORT"].split(",")] if d == 0 else [int(v) for v in _os.environ.get("GLA_SHORT1", "").split(",") if v]
        for tt in order:
            r = it % NR
            it += 1
            tok = tt * 128
            lat = tt >= 2
            lt = tt - 2
            P.dma('sp', lrt[r][:], g.pfm[b, FM_GLG + 16 * d:FM_GLG + 16 * d + 16, tok:tok + 128], r=[('pfm', b)],
                  w=[lrt[r]])
            for k, f0 in enumerate((FM_GLQ, FM_GLQ + 128, FM_GLQS, FM_GLQS + 128, FM_GLK, FM_GLK + 128, FM_GLKS,
                                    FM_GLKS + 128)):
                if not lat and k in (2, 3, 6, 7):
                    continue
                P.dma('sp', fmq[r][k][:], g.pfm[b, f0:f0 + 128, tok:tok + 128], r=[('pfm', b)], w=[fmq[r][k]])
            P.dma('sp', ktm[r][:], g.ptm[b, prow(tt):prow(tt) + 128, TM_GLK:TM_GLK + 512], r=[('ptm', b)], w=[ktm[r]])
            P.dma('sp', vtm[r][:], g.ptm[b, prow(tt):prow(tt) + 128, TM_GLV:TM_GLV + 512], r=[('ptm', b)], w=[vtm[r]])
            P.pool('tensor_copy', out=vbf[r][:], in_=vtm[r][:], r=[vtm[r]], w=[vbf[r]])
            if LVL <= 1:
                continue
            P.pe('matmul', out=pg[:, 0:128], lhsT=lrt[r][:], rhs=wg2[:, d, :], start=True, stop=False,
                 r=[lrt[r], wg2], w=[pg])
            P.pe('matmul', out=pg[:, 0:128], lhsT=ones[:], rhs=bg[:, d, :], start=False, stop=True,
                 r=[ones, bg], w=[pg])
            if LVL <= 2:
                continue
            P.act('activation', out=ee[r][:], in_=pg[:, 0:128], func=AF.Exp, scale=-1.0, r=[pg], w=[ee[r]])
            P.act('activation', out=ee[r][:], in_=ee[r][:], func=AF.Ln, bias=1.0, r=[ee[r]], w=[ee[r]])
            gv = gdup[r][:].rearrange("p (h two c) -> p h two c", h=4, two=2)
            ev = ee[r][:].rearrange("p (h c) -> p h c", h=4)
            for two in range(2):
                P.dve('tensor_scalar', out=gv[:, :, two, :], in0=ev, scalar1=-1.0 / 16.0, scalar2=None, op0=ALU.mult,
                      r=[ee[r]], wa=[gdup[r]])
            if LVL <= 3:
                continue
            P.pe('matmul', out=pbt[:, 0:256], lhsT=Lm[:, d, :], rhs=gdup[r][:], start=True, stop=True,
                 r=[Lm, gdup[r]], w=[pbt])
            P.act('activation', out=enb_tm[r][:], in_=pbt[:, 0:256], func=AF.Exp, scale=-1.0, r=[pbt], w=[enb_tm[r]])
            for i in range(2):
                P.pe('matmul', out=pbf[i][:, 0:128], lhsT=gdup[r][:, 128 * i:128 * (i + 1)], rhs=Lm[:, d, :],
                     start=True, stop=True, r=[Lm, gdup[r]], w=[pbf[i]])
                P.act('activation', out=eb_fm[r][i][:], in_=pbf[i][:, 0:128], func=AF.Exp, r=[pbf[i]],
                      w=[eb_fm[r][i]])
                P.act('activation', out=enb_fm[r][i][:], in_=pbf[i][:, 0:128], func=AF.Exp, scale=-1.0, r=[pbf[i]],
                      w=[enb_fm[r][i]])
            if LVL <= 4:
                continue
            for i in range(2):
                qf, qs, kf, ks = fmq[r][i], fmq[r][2 + i], fmq[r][4 + i], fmq[r][6 + i]
                if lat:
                    cs = cfm[:, lt * 128:(lt + 1) * 128]
                    sn = sfm[:, lt * 128:(lt + 1) * 128]
                    P.dve('tensor_tensor', out=qf[:], in0=qf[:], in1=cs, op=ALU.mult, r=[qf, cfm], w=[qf])
                    P.pool('tensor_tensor', out=qs[:], in0=qs[:], in1=sn, op=ALU.mult, r=[qs, sfm], w=[qs])
                    P.dve('tensor_tensor', out=qf[:], in0=qf[:], in1=qs[:], op=ALU.add, r=[qf, qs], w=[qf])
                    P.dve('tensor_tensor', out=kf[:], in0=kf[:], in1=cs, op=ALU.mult, r=[kf, cfm], w=[kf])
                    P.pool('tensor_tensor', out=ks[:], in0=ks[:], in1=sn, op=ALU.mult, r=[ks, sfm], w=[ks])
                    P.dve('tensor_tensor', out=kf[:], in0=kf[:], in1=ks[:], op=ALU.add, r=[kf, ks], w=[kf])
                for hh in range(2):
                    h = 2 * i + hh
                    ps_ = slice(64 * hh, 64 * hh + 64)
                    P.dve('scalar_tensor_tensor', out=qeT[r][h][ps_, :], in0=qf[ps_, :], scalar=0.125,
                          in1=eb_fm[r][i][ps_, :], op0=ALU.mult, op1=ALU.mult, r=[qf, eb_fm[r][i]], w=[qeT[r][h]])
                    P.dve('tensor_tensor', out=keT[r][h][ps_, :], in0=kf[ps_, :], in1=enb_fm[r][i][ps_, :], op=ALU.mult,
                          r=[kf, enb_fm[r][i]], w=[keT[r][h]])
            if lat:
                for h in range(4):
                    P.dve('tensor_tensor', out=t1[r][:, 64 * h:64 * h + 64], in0=ktm[r][:, 64 * h:64 * h + 64],
                          in1=ctm[:, lt, :], op=ALU.mult, r=[ktm[r], ctm], wa=[t1[r]])
                    P.pool('tensor_tensor', out=t2[r][:, 64 * h:64 * h + 64], in0=ktm[r][:, 256 + 64 * h:256 + 64 * h + 64],
                           in1=stm[:, lt, :], op=ALU.mult, r=[ktm[r], stm], wa=[t2[r]])
                P.dve('tensor_tensor', out=t1[r][:], in0=t1[r][:], in1=t2[r][:], op=ALU.add, r=[t1[r], t2[r]], w=[t1[r]])
                ksrc, kkey = t1[r][:], t1[r]
            else:
                ksrc, kkey = ktm[r][:, 0:256], ktm[r]
            P.dve('tensor_tensor', out=ke_tm[r][:], in0=ksrc, in1=enb_tm[r][:], op=ALU.mult, r=[kkey, enb_tm[r]],
                  w=[ke_tm[r]])
            if LVL <= 5:
                continue
            for h in range(4):
                i, hb = h // 2, 64 * (h % 2)
                P.pe('matmul', out=patt[:, 128 * h:128 * (h + 1)], lhsT=keT[r][h][:],
                     rhs=qeT[r][h][:], start=True, stop=True, r=[keT[r][h], qeT[r][h]], w=[patt])
            for h in range(4):
                P.dve('tensor_tensor', out=attT[r][:, 128 * h:128 * (h + 1)], in0=patt[:, 128 * h:128 * (h + 1)],
                      in1=Lm[:, d, :], op=ALU.mult, r=[patt, Lm], wa=[attT[r]])
            if LVL <= 6:
                continue
            if lat or with_ctx:
                for h in range(4):
                    i, hb = h // 2, 64 * (h % 2)
                    P.pe('matmul', out=po[:, 128 * h:128 * (h + 1)], lhsT=attT[r][:, 128 * h:128 * (h + 1)],
                         rhs=vbf[r][:, 128 * h:128 * (h + 1)], start=True, stop=False, r=[attT[r], vbf[r]], w=[po])
                    P.pe('matmul', out=po[:, 128 * h:128 * (h + 1)], lhsT=qeT[r][h][:],
                         rhs=Sbf[i][:], start=False, stop=True, r=[qeT[r][h], Sbf[i]], w=[po])
                if d == 0:
                    P.act('activation', out=oB[:, tt, :], in_=po[:, :], func=AF.Copy, r=[po], w=[('oB', tt)])
                else:
                    P.dve('tensor_tensor', out=oB[:, tt, :], in0=po[:, :], in1=oB[:, tt, :], op=ALU.add,
                          r=[po, ('oB', tt)], w=[('oB', tt)])
            if LVL <= 7:
                continue
            for i in range(2):
                P.pe('matmul', out=pss[i][:, 0:256], lhsT=ke_tm[r][:, 128 * i:128 * (i + 1)],
                     rhs=vbf[r][:, 256 * i:256 * (i + 1)], start=True, stop=True, r=[ke_tm[r], vbf[r]], w=[pss[i]])
                for hh in range(2):
                    ps_ = slice(64 * hh, 64 * hh + 64)
                    P.dve('tensor_tensor', out=stmp[ps_, :], in0=pss[i][ps_, 128 * hh:128 * (hh + 1)], in1=Sp[i][ps_, :],
                          op=ALU.add, r=[pss[i], Sp[i]], w=[(stmp.name, hh)])
                    P.dve('tensor_scalar', out=Sp[i][ps_, :], in0=stmp[ps_, :], scalar1=eb_fm[r][i][ps_, tend:tend + 1],
                          scalar2=None, op0=ALU.mult, r=[(stmp.name, hh), eb_fm[r][i]], wa=[Sp[i]])
                P.pool('tensor_copy', out=Sbf[i][:], in_=Sp[i][:], r=[Sp[i]], w=[Sbf[i]])
    S.__exit__(None, None, None)
    S = S0
    gn = S.sb("gl_gn", [128, 128], F32)
    zt = [S.sb("gl_zt%d" % i, [128, 512], F32) for i in range(2)]
    sq = [S.sb("gl_sq%d" % i, [128, 512], F32) for i in range(2)]
    ssq = [S.sb("gl_ssq%d" % i, [128, 4], F32) for i in range(2)]
    yst = S.sb("gl_yst", [128, 4, TT], BF16)
    P.dma('sp', gn[:], g.gla_norm_row[l].partition_broadcast(128), w=[gn])
    tt0 = 0 if with_ctx else 2
    if LVL <= 8:
        tt0 = NTT
    for tt in range(tt0, NTT):
        r = tt % 2
        o = oB[:, tt, :]
        P.dma('sp', zt[r][:], g.ptm[b, prow(tt):prow(tt) + 128, TM_GLZ:TM_GLZ + 512], r=[('ptm', b)], w=[zt[r]])
        P.act('activation', out=zt[r][:], in_=zt[r][:], func=AF.Silu, r=[zt[r]], w=[zt[r]])
        P.pool('tensor_tensor', out=sq[r][:], in0=o, in1=o, op=ALU.mult, r=[('oB', tt)], w=[sq[r]])
        P.dve('tensor_reduce', out=ssq[r][:], in_=sq[r][:].rearrange("p (h c) -> p h c", h=4), axis=AX.X, op=ALU.add,
              r=[sq[r]], w=[ssq[r]])
        P.dve('tensor_scalar', out=ssq[r][:], in0=ssq[r][:], scalar1=1.0 / 128.0, scalar2=EPS, op0=ALU.mult,
              op1=ALU.add, r=[ssq[r]], w=[ssq[r]])
        P.act('activation', out=ssq[r][:], in_=ssq[r][:], func=AF.Sqrt, r=[ssq[r]], w=[ssq[r]])
        P.dve('reciprocal', out=ssq[r][:], in_=ssq[r][:], r=[ssq[r]], w=[ssq[r]])
        for h in range(4):
            P.dve('scalar_tensor_tensor', out=sq[r][:, 128 * h:128 * (h + 1)], in0=oB[:, tt, 128 * h:128 * (h + 1)],
                  scalar=ssq[r][:, h:h + 1], in1=gn[:], op0=ALU.mult, op1=ALU.mult, r=[('oB', tt), ssq[r], gn],
                  wa=[sq[r]])
        P.dve('tensor_tensor', out=sq[r][:], in0=sq[r][:], in1=zt[r][:], op=ALU.mult, r=[sq[r], zt[r]], w=[sq[r]])
        pt = g.psB[r]
        for k in range(4):
            P.pe('transpose', out=pt[:, k * 128:(k + 1) * 128], in_=sq[r][:, k * 128:(k + 1) * 128],
                 identity=g.ident[:], r=[sq[r], g.ident], w=[pt])
        P.act('activation', out=yst[:, :, tt * 128:(tt + 1) * 128], in_=pt[:, :].rearrange("p (k t) -> p k t", k=4),
              func=AF.Copy, r=[pt], wa=[yst])
    for k in range(4):
        if tt0 >= NTT:
            break
        P.dma('sp', g.yfm[b, 512 + k * 128:512 + (k + 1) * 128, tt0 * 128:TT], yst[:, k, tt0 * 128:TT], r=[yst],
              wa=[('yfm', b)])


GQW = 1536 + 16


def gdn_consts():
    s = np.arange(128)[:, None]
    t = np.arange(128)[None, :]
    big = np.float32(1e5)
    mbs = np.stack([np.where(t < s, 0.0, big), np.where(t > s, 0.0, big)]).astype(np.float32)
    mbi = np.stack([np.where(s <= t, 0.0, -big), np.where(s >= t, 0.0, -big)]).astype(np.float32)
    return {"gdn_mbs": np.ascontiguousarray(mbs), "gdn_mbi": np.ascontiguousarray(mbi),
            "ones128": np.ones((128, 128), np.float32)}


def phase_gdn_pre(g, l, b):
    with g.P.scope() as S:
        _phase_gdn_pre(g, l, b, S)


def _phase_gdn_pre(g, l, b, S):
    P = g.P
    wrow = S.sb("gd_wrow", [128, 5, 1536], F32)
    xs = [S.sb("gd_xs%d" % i, [128, 1536], F32) for i in range(5)]
    acc = S.sb("gd_acc", [128, GQW], F32)
    tmp = S.sb("gd_tmp", [128, 1536], F32)
    tmp2 = S.sb("gd_tmp2", [128, 1536], F32)
    ab = S.sb("gd_ab", [128, 16], F32)
    dtb = S.sb("gd_dtb", [128, 8], F32)
    eal = S.sb("gd_eal", [128, 8], F32)
    ssq = S.sb("gd_ssq", [128, 8], F32)
    sp = S.sb("gd_sp", [128, 8], F32)
    for k in range(5):
        P.dma('sp', wrow[:, k, :], g.gdn_conv_row[l, k:k + 1, :].partition_broadcast(128), wa=[wrow])
    P.dma('sp', dtb[:], g.gdn_dtb_row[l].partition_broadcast(128), w=[dtb])
    P.dma('sp', eal[:], g.gdn_alog_row[l].partition_broadcast(128), w=[eal])
    P.act('activation', out=eal[:], in_=eal[:], func=AF.Exp, r=[eal], w=[eal])
    for tt in range(NTT):
        base = prow(tt)
        for k in range(5):
            P.dma('sp', xs[k][:], g.ptm[b, base + k - 2:base + k - 2 + 128, TM_GDQKV:TM_GDQKV + 1536],
                  r=[('ptm', b)], w=[xs[k]])
        P.dma('sp', ab[:], g.ptm[b, base:base + 128, TM_GDAB:TM_GDAB + 16], r=[('ptm', b)], w=[ab])
        A = acc[:, 0:1536]
        P.dve('tensor_tensor', out=A, in0=xs[0][:], in1=wrow[:, 0, :], op=ALU.mult, r=[xs[0], wrow], w=[acc])
        for k in range(1, 5):
            eng = 'pool' if k % 2 == 1 else 'dve'
            tk = tmp if k % 2 == 1 else tmp2
            P.op(eng, 'tensor_tensor', out=tk[:], in0=xs[k][:], in1=wrow[:, k, :], op=ALU.mult, r=[xs[k], wrow],
                 w=[tk])
            P.dve('tensor_tensor', out=A, in0=A, in1=tk[:], op=ALU.add, r=[tk, acc], w=[acc])
        P.act('activation', out=A, in_=A, func=AF.Silu, r=[acc], w=[acc])
        P.pool('tensor_tensor', out=tmp[:, 0:1024], in0=acc[:, 0:1024], in1=acc[:, 0:1024], op=ALU.mult, r=[acc],
               w=[tmp])
        P.dve('tensor_reduce', out=ssq[:], in_=tmp[:, 0:1024].rearrange("p (h c) -> p h c", h=8), axis=AX.X,
              op=ALU.add, r=[tmp], w=[ssq])
        P.dve('tensor_scalar', out=ssq[:], in0=ssq[:], scalar1=EPS, scalar2=None, op0=ALU.add, r=[ssq], w=[ssq])
        P.act('activation', out=ssq[:], in_=ssq[:], func=AF.Sqrt, r=[ssq], w=[ssq])
        P.dve('reciprocal', out=ssq[:], in_=ssq[:], r=[ssq], w=[ssq])
        P.dve('tensor_scalar', out=ssq[:, 0:4], in0=ssq[:, 0:4], scalar1=128.0 ** -0.5, scalar2=None, op0=ALU.mult,
              r=[ssq], w=[ssq])
        for h8 in range(8):
            eng = 'dve' if h8 % 2 == 0 else 'pool'
            P.op(eng, 'tensor_scalar', out=acc[:, 128 * h8:128 * (h8 + 1)], in0=acc[:, 128 * h8:128 * (h8 + 1)],
                 scalar1=ssq[:, h8:h8 + 1], scalar2=None, op0=ALU.mult, r=[acc, ssq], w=[acc])
        P.act('activation', out=acc[:, 1536:1544], in_=ab[:, 8:16], func=AF.Sigmoid, r=[ab], w=[acc])
        P.dve('tensor_tensor', out=sp[:], in0=ab[:, 0:8], in1=dtb[:], op=ALU.add, r=[ab, dtb], w=[sp])
        P.act('activation', out=sp[:], in_=sp[:], func=AF.Exp, r=[sp], w=[sp])
        P.act('activation', out=sp[:], in_=sp[:], func=AF.Ln, bias=1.0, r=[sp], w=[sp])
        P.dve('scalar_tensor_tensor', out=acc[:, 1544:1552], in0=sp[:], scalar=-1.0, in1=eal[:], op0=ALU.mult,
              op1=ALU.mult, r=[sp, eal, acc], w=[acc])
        P.dma('sp', g.gq[tt * 128:(tt + 1) * 128, :], acc[:], r=[acc], wa=['gq'])


def phase_gdn(g, l, b):
    with g.P.scope() as S:
        _phase_gdn(g, l, b, S)


def _phase_gdn(g, l, b, S):
    P = g.P
    with_ctx = l < g.NL_total - 1
    oC = S.sb("gd_oC", [128, NTT, 512], F32)
    S0, S = S, Scope(P)
    Lm = S.sb("gd_Lm", [128, 2, 128], F32)
    mbs = S.sb("gd_mbs", [128, 2, 128], F32)
    mbi = S.sb("gd_mbi", [128, 2, 128], F32)
    ones = S.sb("gd_ones", [128, 128], F32)
    St = [S.sb("gd_S%d" % h, [128, 128], F32) for h in range(4)]
    qkv = [S.sb("gd_qkv%d" % i, [128, GQW], F32) for i in range(2)]
    sm = lambda n, w: S.sb("gd_" + n, [128, w], F32)
    gam, egam, bexp, nbeta, gend, kdsc, dend = (sm("gam", 4), sm("egam", 4), sm("bexp", 4), sm("nbeta", 4),
                                                sm("gend", 4), sm("kdsc", 4), sm("dend", 4))
    Lg = [sm("Lg%d" % i, 128) for i in range(4)]
    def smb(n, w):
        solve = n[:2] in ("PQ", "Y0", "Y1", "Y2", "Y3", "RH")
        return S.sb("gd_" + n, [128, w], F32 if solve else BF16)
    KQ = [smb("KQ%d" % h, 256) for h in range(4)]
    Nf = [sm("Nf%d" % i, 128) for i in range(4)]
    Sbf = [smb("Sbf%d" % h, 128) for h in range(4)]
    xx, EE, x2, E2 = ([sm("xx%d" % i, 128) for i in range(4)], [sm("EE%d" % i, 128) for i in range(4)],
                      [sm("x2%d" % i, 128) for i in range(4)], [sm("E2%d" % i, 128) for i in range(4)])
    aqk = [smb("aqk%d" % h, 128) for h in range(4)]
    PQh = [[smb("PQ%d_%d" % (h, i), 256) for i in range(2)] for h in range(4)]
    Yh = [[smb("Y%d_%d" % (h, i), 128) for i in range(2)] for h in range(4)]
    RHSu = [smb("RHSu%d" % h, 128) for h in range(4)]
    RHSw = [smb("RHSw%d" % h, 128) for h in range(4)]
    kdh = [smb("kd%d" % h, 128) for h in range(4)]
    wTn = [smb("wTn%d" % h, 128) for h in range(4)]
    esb = [smb("esb%d" % h, 128) for h in range(4)]
    o1 = [sm("o1%d" % h, 128) for h in range(4)]
    P.dma('sp', Lm[:], g.gla_Lm.rearrange("d s t -> s d t"), w=[Lm])
    P.dma('sp', mbs[:], g.gdn_mbs.rearrange("d s t -> s d t"), w=[mbs])
    P.dma('sp', mbi[:], g.gdn_mbi.rearrange("d s t -> s d t"), w=[mbi])
    P.dma('sp', ones[:], g.ones128, w=[ones])
    pA0, pG = g.psA[0], g.psA[1]
    it = 0
    import os as _os
    SHORT = _os.environ.get("GDN_SHORT")
    for d in range(2):
        order = list(range(NTT)) if d == 0 else [1, 0] + list(range(NTT - 1, 1, -1))
        if SHORT:
            order = order[:int(SHORT)]
        send = 127 if d == 0 else 0
        for h in range(4):
            P.dve('memset', ap=St[h][:], constant=0.0, w=[St[h]])
            P.pool('memset', ap=Sbf[h][:], constant=0.0, w=[Sbf[h]])
        for tt in order:
            r = it % 2
            it += 1
            X = qkv[r]
            P.dma('sp', X[:], g.gq[tt * 128:(tt + 1) * 128, :], r=['gq'], w=[X])
            bet = X[:, 1536 + 4 * d:1536 + 4 * d + 4]
            gg = X[:, 1544 + 4 * d:1544 + 4 * d + 4]
            P.pe('matmul', out=pA0[:, 0:4], lhsT=Lm[:, d, :], rhs=gg, start=True, stop=True, r=[Lm, X], w=[pA0])
            P.act('activation', out=gam[:], in_=pA0[:, 0:4], func=AF.Copy, r=[pA0], w=[gam])
            P.act('activation', out=egam[:], in_=pA0[:, 0:4], func=AF.Exp, r=[pA0], w=[egam])
            P.dve('tensor_tensor', out=bexp[:], in0=egam[:], in1=bet, op=ALU.mult, r=[egam, X], w=[bexp])
            P.dve('tensor_scalar', out=nbeta[:], in0=bet, scalar1=-1.0, scalar2=None, op0=ALU.mult, r=[X], w=[nbeta])
            for h in range(4):
                P.dve('tensor_scalar', out=Lg[h][:], in0=Lm[:, d, :], scalar1=gg[:, h:h + 1], scalar2=None,
                      op0=ALU.mult, r=[Lm, X], w=[Lg[h]])
            for h in range(4):
                P.pe('matmul', out=pG[:, 128 * h:128 * (h + 1)], lhsT=ones[:], rhs=Lg[h][:], start=True, stop=True,
                     r=[ones, Lg[h]], w=[pG])
            gendv = pG[:, :].rearrange("p (h s) -> p h s", h=4)[:, :, send]
            P.act('activation', out=gend[:], in_=gendv, func=AF.Copy, r=[pG], w=[gend])
            P.act('activation', out=dend[:], in_=gend[:], func=AF.Exp, r=[gend], w=[dend])
            P.dve('tensor_tensor', out=kdsc[:], in0=gend[:], in1=gam[:], op=ALU.subtract, r=[gend, gam], w=[kdsc])
            P.act('activation', out=kdsc[:], in_=kdsc[:], func=AF.Exp, r=[kdsc], w=[kdsc])
            qs_ = lambda h: X[:, 128 * h:128 * (h + 1)]
            ks_ = lambda h: X[:, 512 + 128 * h:512 + 128 * (h + 1)]
            vs_ = lambda h: X[:, 1024 + 128 * h:1024 + 128 * (h + 1)]
            H4 = range(4)
            for h in H4:
                pb = g.psB[h]
                P.pe('transpose', out=pb[:, 0:128], in_=ks_(h), identity=g.ident[:], r=[X, g.ident], w=[pb])
                P.pe('transpose', out=pb[:, 128:256], in_=qs_(h), identity=g.ident[:], r=[X, g.ident], w=[pb])
            for h in H4:
                P.act('activation', out=KQ[h][:], in_=g.psB[h][:, 0:256], func=AF.Copy, r=[g.psB[h]], w=[KQ[h]])
            for h in H4:
                pb = g.psB[h]
                kTh, qTh = KQ[h][:, 0:128], KQ[h][:, 128:256]
                P.pe('matmul', out=pb[:, 256:384], lhsT=kTh, rhs=kTh, start=True, stop=True, r=[KQ[h]], w=[pb])
                P.pe('matmul', out=pb[:, 384:512], lhsT=kTh, rhs=qTh, start=True, stop=True, r=[KQ[h]], w=[pb])
            for h in H4:
                Gam = pG[:, 128 * h:128 * (h + 1)]
                P.dve('scalar_tensor_tensor', out=xx[h][:], in0=Gam, scalar=gam[:, h:h + 1], in1=mbs[:, d, :],
                      op0=ALU.subtract, op1=ALU.max, r=[pG, gam, mbs], w=[xx[h]])
                P.dve('scalar_tensor_tensor', out=x2[h][:], in0=Gam, scalar=gam[:, h:h + 1], in1=mbi[:, d, :],
                      op0=ALU.subtract, op1=ALU.min, r=[pG, gam, mbi], w=[x2[h]])
            for h in H4:
                P.act('activation', out=EE[h][:], in_=xx[h][:], func=AF.Exp, scale=-1.0, r=[xx[h]], w=[EE[h]])
                P.act('activation', out=E2[h][:], in_=x2[h][:], func=AF.Exp, r=[x2[h]], w=[E2[h]])
            for h in H4:
                pb = g.psB[h]
                P.dve('scalar_tensor_tensor', out=Nf[h][:], in0=pb[:, 256:384], scalar=nbeta[:, h:h + 1],
                      in1=EE[h][:], op0=ALU.mult, op1=ALU.mult, r=[pb, nbeta, EE[h]], w=[Nf[h]])
                P.dve('tensor_tensor', out=aqk[h][:], in0=pb[:, 384:512], in1=E2[h][:], op=ALU.mult,
                      r=[pb, E2[h]], w=[aqk[h]])
            for h in H4:
                P.pool('tensor_copy', out=PQh[h][0][:, 0:128], in_=Nf[h][:], r=[Nf[h]], w=[PQh[h][0]])
                P.pe('transpose', out=g.psB[h][:, 128:256], in_=Nf[h][:], identity=g.ident[:],
                     r=[Nf[h], g.ident], w=[g.psB[h]])
            for h in range(4):
                P.act('activation', out=PQh[h][0][:, 128:256], in_=g.psB[h][:, 128:256], func=AF.Copy,
                      r=[g.psB[h]], w=[PQh[h][0]])
            for h in range(4):
                P.op('dve' if h % 2 == 0 else 'pool', 'tensor_tensor', out=Yh[h][0][:], in0=PQh[h][0][:, 128:256],
                     in1=g.ident[:], op=ALU.add, r=[PQh[h][0], g.ident], w=[Yh[h][0]])
            yi = 0
            for j in range(6):
                for h in range(4):
                    cur = PQh[h][j % 2]
                    P.pe('matmul', out=g.psB[h][:, 0:128], lhsT=cur[:, 128:256], rhs=cur[:, 0:128], start=True,
                         stop=True, r=[cur], w=[g.psB[h]])
                    if j < 5:
                        P.pe('matmul', out=g.psB[h][:, 128:256], lhsT=cur[:, 0:128], rhs=cur[:, 128:256], start=True,
                             stop=True, r=[cur], w=[g.psB[h]])
                for h in range(4):
                    nxt = PQh[h][(j + 1) % 2]
                    wid = 256 if j < 5 else 128
                    P.act('activation', out=nxt[:, 0:wid], in_=g.psB[h][:, 0:wid], func=AF.Copy, r=[g.psB[h]],
                          w=[nxt])
                for h in range(4):
                    nxt = PQh[h][(j + 1) % 2]
                    P.pe('matmul', out=g.psA[h][:, 0:128], lhsT=nxt[:, 0:128], rhs=Yh[h][yi][:], start=True, stop=True,
                         r=[nxt, Yh[h][yi]], w=[g.psA[h]])
                for h in range(4):
                    P.dve('tensor_tensor', out=Yh[h][1 - yi][:], in0=g.psA[h][:, 0:128], in1=Yh[h][yi][:], op=ALU.add,
                          r=[g.psA[h], Yh[h][yi]], w=[Yh[h][1 - yi]])
                yi = 1 - yi
            need_o = (tt >= 2 or with_ctx)
            for h in range(4):
                P.dve('tensor_scalar', out=RHSu[h][:], in0=vs_(h), scalar1=bet[:, h:h + 1], scalar2=None,
                      op0=ALU.mult, r=[X], w=[RHSu[h]])
                P.pool('tensor_scalar', out=RHSw[h][:], in0=ks_(h), scalar1=bexp[:, h:h + 1], scalar2=None,
                       op0=ALU.mult, r=[X, bexp], w=[RHSw[h]])
                P.pool('tensor_scalar', out=kdh[h][:], in0=ks_(h), scalar1=kdsc[:, h:h + 1], scalar2=None,
                       op0=ALU.mult, r=[X, kdsc], w=[kdh[h]])
            for h in range(4):
                P.pe('matmul', out=g.psA[h][:, 128:256], lhsT=RHSw[h][:], rhs=Yh[h][yi][:], start=True, stop=True,
                     r=[RHSw[h], Yh[h][yi]], w=[g.psA[h]])
            for h in range(4):
                P.act('activation', out=wTn[h][:], in_=g.psA[h][:, 128:256], func=AF.Copy, scale=-1.0,
                      r=[g.psA[h]], w=[wTn[h]])
            for h in range(4):
                P.pe('matmul', out=g.psA[h][:, 0:128], lhsT=Yh[h][yi][:], rhs=RHSu[h][:], start=True, stop=False,
                     r=[Yh[h][yi], RHSu[h]], w=[g.psA[h]])
                P.pe('matmul', out=g.psA[h][:, 0:128], lhsT=wTn[h][:], rhs=Sbf[h][:], start=False, stop=True,
                     r=[wTn[h], Sbf[h]], w=[g.psA[h]])
            for h in range(4):
                P.act('activation', out=esb[h][:], in_=g.psA[h][:, 0:128], func=AF.Copy, r=[g.psA[h]], w=[esb[h]])
            for h in range(4):
                P.pe('matmul', out=g.psB[h][:, 256:384], lhsT=kdh[h][:], rhs=esb[h][:], start=True, stop=True,
                     r=[kdh[h], esb[h]], w=[g.psB[h]])
                if need_o:
                    P.pe('matmul', out=g.psB[h][:, 0:128], lhsT=KQ[h][:, 128:256], rhs=Sbf[h][:], start=True, stop=True,
                         r=[KQ[h], Sbf[h]], w=[g.psB[h]])
                    P.pe('matmul', out=g.psB[h][:, 128:256], lhsT=aqk[h][:], rhs=esb[h][:], start=True, stop=True,
                         r=[aqk[h], esb[h]], w=[g.psB[h]])
            if need_o:
                for h in range(4):
                    P.act('activation', out=o1[h][:], in_=g.psB[h][:, 0:128], func=AF.Copy, scale=egam[:, h:h + 1],
                          r=[g.psB[h], egam], w=[o1[h]])
                for h in range(4):
                    oslc = oC[:, tt, 128 * h:128 * (h + 1)]
                    if d == 0:
                        P.dve('tensor_tensor', out=oslc, in0=g.psB[h][:, 128:256], in1=o1[h][:], op=ALU.add,
                              r=[g.psB[h], o1[h]], w=[('oC', tt, h)])
                    else:
                        P.dve('tensor_tensor', out=o1[h][:], in0=g.psB[h][:, 128:256], in1=o1[h][:], op=ALU.add,
                              r=[g.psB[h], o1[h]], w=[o1[h]])
                        P.pool('tensor_tensor', out=oslc, in0=oslc, in1=o1[h][:], op=ALU.add,
                               r=[o1[h], ('oC', tt, h)], w=[('oC', tt, h)])
            for h in range(4):
                P.dve('scalar_tensor_tensor', out=St[h][:], in0=St[h][:], scalar=dend[:, h:h + 1],
                      in1=g.psB[h][:, 256:384], op0=ALU.mult, op1=ALU.add, r=[St[h], dend, g.psB[h]], w=[St[h]])
                P.pool('tensor_copy', out=Sbf[h][:], in_=St[h][:], r=[St[h]], w=[Sbf[h]])
    S.__exit__(None, None, None)
    S = S0
    okeys = lambda tt: [('oC', tt, h) for h in range(4)]
    branch_finish(g, l, b, S, oC, okeys, g.gdn_norm_row[l], TM_GDZ, 1024, with_ctx)


def branch_finish(g, l, b, S, oB, okeys, norm_row, zcol, yrow0, with_ctx):
    P = g.P
    gn = S.sb("bf_gn", [128, 128], F32)
    zt = [S.sb("bf_zt%d" % i, [128, 512], F32) for i in range(2)]
    sq = [S.sb("bf_sq%d" % i, [128, 512], F32) for i in range(2)]
    ssq = [S.sb("bf_ssq%d" % i, [128, 4], F32) for i in range(2)]
    yst = S.sb("bf_yst", [128, 4, TT], BF16)
    P.dma('sp', gn[:], norm_row.partition_broadcast(128), w=[gn])
    tt0 = 0 if with_ctx else 2
    for tt in range(tt0, NTT):
        r = tt % 2
        o = oB[:, tt, :]
        P.dma('sp', zt[r][:], g.ptm[b, prow(tt):prow(tt) + 128, zcol:zcol + 512], r=[('ptm', b)], w=[zt[r]])
        P.act('activation', out=zt[r][:], in_=zt[r][:], func=AF.Silu, r=[zt[r]], w=[zt[r]])
        P.pool('tensor_tensor', out=sq[r][:], in0=o, in1=o, op=ALU.mult, r=okeys(tt), w=[sq[r]])
        P.dve('tensor_reduce', out=ssq[r][:], in_=sq[r][:].rearrange("p (h c) -> p h c", h=4), axis=AX.X, op=ALU.add,
              r=[sq[r]], w=[ssq[r]])
        P.dve('tensor_scalar', out=ssq[r][:], in0=ssq[r][:], scalar1=1.0 / 128.0, scalar2=EPS, op0=ALU.mult,
              op1=ALU.add, r=[ssq[r]], w=[ssq[r]])
        P.act('activation', out=ssq[r][:], in_=ssq[r][:], func=AF.Sqrt, r=[ssq[r]], w=[ssq[r]])
        P.dve('reciprocal', out=ssq[r][:], in_=ssq[r][:], r=[ssq[r]], w=[ssq[r]])
        for h in range(4):
            P.dve('scalar_tensor_tensor', out=sq[r][:, 128 * h:128 * (h + 1)], in0=oB[:, tt, 128 * h:128 * (h + 1)],
                  scalar=ssq[r][:, h:h + 1], in1=gn[:], op0=ALU.mult, op1=ALU.mult, r=okeys(tt) + [ssq[r], gn],
                  wa=[sq[r]])
        P.dve('tensor_tensor', out=sq[r][:], in0=sq[r][:], in1=zt[r][:], op=ALU.mult, r=[sq[r], zt[r]], w=[sq[r]])
        pt = g.psB[r]
        for k in range(4):
            P.pe('transpose', out=pt[:, k * 128:(k + 1) * 128], in_=sq[r][:, k * 128:(k + 1) * 128],
                 identity=g.ident[:], r=[sq[r], g.ident], w=[pt])
        P.act('activation', out=yst[:, :, tt * 128:(tt + 1) * 128], in_=pt[:, :].rearrange("p (k t) -> p k t", k=4),
              func=AF.Copy, r=[pt], wa=[yst])
    for k in range(4):
        P.dma('sp', g.yfm[b, yrow0 + k * 128:yrow0 + (k + 1) * 128, tt0 * 128:TT], yst[:, k, tt0 * 128:TT], r=[yst],
              wa=[('yfm', b)])


def phase_merge(g, l, b):
    with g.P.scope() as S:
        _phase_merge(g, l, b, S)


def _phase_merge(g, l, b, S):
    P = g.P
    with_ctx = l < g.NL_total - 1
    last = not with_ctx
    if with_ctx:
        halves = [(0, 1280), (1280, 1024)]
    else:
        halves = [(256, 1024), (1280, 1024)]
    ss = S.sb("mg_ss", [128, NTT, 4], F32)
    bgc = S.sb("mg_bgc", [128, 4, 16], F32)
    mT = S.sb("mg_mT", [128, 16, 1280], BF16)
    P.dma('sp', bgc[:], g.b_gate_col[l], w=[bgc])
    for (t0, n) in halves:
        chunks = [(c0, min(512, n - c0)) for c0 in range(0, n, 512)]
        nch = len(chunks)
        with P.scope() as SA:
            yT = SA.sb("mg_yT", [128, 16, 1280], BF16)
            wgs = [SA.sb("mg_wgs%d" % i, [128, 16, 128], F32) for i in range(2)]
            wgb = [SA.sb("mg_wgb%d" % i, [128, 16, 128], BF16) for i in range(2)]
            wbs = [SA.sb("mg_wbs%d" % i, [128, 4, 128], F32) for i in range(2)]
            wbb = [SA.sb("mg_wbb%d" % i, [128, 4, 128], BF16) for i in range(2)]
            acc = SA.sb("mg_acc", [128, 1280], F32)
            sig = [SA.sb("mg_sig%d" % i, [128, 512], F32) for i in range(2)]
            tmp = [SA.sb("mg_tmp%d" % i, [128, 512], F32) for i in range(2)]
            for kt in range(16):
                P.dma('sp', yT[:, kt, 0:n], g.yfm[b, kt * 128:(kt + 1) * 128, t0:t0 + n], r=[('yfm', b)], wa=[yT])
            it = 0
            for ft in range(16):
                for i in range(4):
                    w2 = (ft * 4 + i) % 2
                    P.dma('sp', wgs[w2][:], g.w_gate_t[l, i, ft], w=[wgs[w2]])
                    P.act('activation', out=wgb[w2][:], in_=wgs[w2][:], func=AF.Copy, r=[wgs[w2]], w=[wgb[w2]])
                    P.dma('sp', wbs[w2][:], g.w_branch_t[l, i, ft], w=[wbs[w2]])
                    P.act('activation', out=wbb[w2][:], in_=wbs[w2][:], func=AF.Copy, r=[wbs[w2]], w=[wbb[w2]])
                    for c, (c0, cw) in enumerate(chunks):
                        pg_, pb_ = g.psA[it % 4], g.psB[it % 4]
                        i2 = it % 2
                        it += 1
                        for kt in range(16):
                            P.pe('matmul', out=pg_[:, 0:cw], lhsT=wgb[w2][:, kt, :],
                                 rhs=g.uT[:, kt, t0 + c0:t0 + c0 + cw], start=(kt == 0), stop=(kt == 15),
                                 r=[wgb[w2], g.uT], w=[pg_])
                        for kt in range(4):
                            P.pe('matmul', out=pb_[:, 0:cw], lhsT=wbb[w2][:, kt, :],
                                 rhs=yT[:, 4 * i + kt, c0:c0 + cw], start=(kt == 0), stop=(kt == 3),
                                 r=[wbb[w2], yT], w=[pb_])
                        P.act('activation', out=sig[i2][:, 0:cw], in_=pg_[:, 0:cw], func=AF.Sigmoid,
                              bias=bgc[:, i, ft:ft + 1], r=[pg_, bgc], w=[sig[i2]])
                        asl = acc[:, c0:c0 + cw]
                        if i == 0:
                            P.dve('tensor_tensor', out=asl, in0=pb_[:, 0:cw], in1=sig[i2][:, 0:cw], op=ALU.mult,
                                  r=[pb_, sig[i2]], w=[('mg_acc', c)])
                        else:
                            P.dve('tensor_tensor', out=tmp[i2][:, 0:cw], in0=pb_[:, 0:cw], in1=sig[i2][:, 0:cw],
                                  op=ALU.mult, r=[pb_, sig[i2]], w=[tmp[i2]])
                            P.pool('tensor_tensor', out=asl, in0=asl, in1=tmp[i2][:, 0:cw], op=ALU.add,
                                   r=[tmp[i2], ('mg_acc', c)], w=[('mg_acc', c)])
                P.act('activation', out=mT[:, ft, 0:n], in_=acc[:, 0:n], func=AF.Copy,
                      r=[('mg_acc', c) for c in range(nch)], wa=[mT])
        with P.scope() as SB:
            wos = [SB.sb("mg_wos%d" % i, [128, 16, 512], F32) for i in range(1)]
            wob = [SB.sb("mg_wob%d" % i, [128, 16, 512], BF16) for i in range(2)]
            yst = [SB.sb("mg_yst%d" % i, [128, 512], F32) for i in range(3)]
            junk = SB.sb("mg_junk", [128, 512], F32)
            it = 0
            for cc in range(4):
                P.dma('sp', wos[0][:], g.w_out[l, :, cc * 512:(cc + 1) * 512].rearrange("(kt p) f -> p kt f", p=128),
                      w=[wos[0]])
                for q4 in range(4):
                    if q4 % 2 == 0:
                        P.dve('tensor_copy', out=wob[cc % 2][:, 4 * q4:4 * q4 + 4, :],
                              in_=wos[0][:, 4 * q4:4 * q4 + 4, :], r=[wos[0]], w=[(wob[cc % 2].name, q4)])
                    else:
                        P.act('activation', out=wob[cc % 2][:, 4 * q4:4 * q4 + 4, :],
                              in_=wos[0][:, 4 * q4:4 * q4 + 4, :], func=AF.Copy, r=[wos[0]],
                              w=[(wob[cc % 2].name, q4)])
                for ti in range(n // 128):
                    tt = (t0 + ti * 128) // 128
                    ps = g.psA[it % 4]
                    st = yst[it % 3]
                    it += 1
                    for kt in range(16):
                        P.pe('matmul', out=ps[:, :], lhsT=mT[:, kt, ti * 128:(ti + 1) * 128], rhs=wob[cc % 2][:, kt, :],
                             start=(kt == 0), stop=(kt == 15), r=[mT, (wob[cc % 2].name, kt // 4)], w=[ps])
                    P.act('activation', out=st[:], in_=ps[:, :], func=AF.Copy, r=[ps], w=[st])
                    P.dve('tensor_tensor', out=junk[:], in0=st[:], in1=st[:], op=ALU.mult, r=[st], w=[junk])
                    P.dve('tensor_reduce', out=ss[:, tt, cc:cc + 1], in_=junk[:], axis=AX.X, op=ALU.add, r=[junk],
                          wa=[('mg_ss', tt)])
                    P.dma('pool', g.ybuf[tt * 128:(tt + 1) * 128, cc * 512:(cc + 1) * 512], st[:], r=[st],
                          wa=['ybuf'])
    with P.scope() as SC:
        GG = SC.sb("mg_GG", [128, 2, D], F32)
        yt = [SC.sb("mg_yt%d" % i, [128, D], F32) for i in range(2)]
        ht = [SC.sb("mg_ht%d" % i, [128, D], F32) for i in range(2)]
        rs = [SC.sb("mg_rs%d" % i, [128, 2], F32) for i in range(2)]
        for vi, j in enumerate((b, 2)):
            for jc in range(4):
                pb = g.psA[(vi * 4 + jc) % 4]
                P.pe('matmul', out=pb[:, :], lhsT=g.sel[:, j, :], rhs=g.grow[:, jc * 512:(jc + 1) * 512],
                     start=True, stop=True, r=[g.sel, g.grow], w=[pb])
                P.act('activation', out=GG[:, vi, jc * 512:(jc + 1) * 512], in_=pb[:, :], func=AF.Copy,
                      r=[pb], wa=[GG])
        src = g.xin if l == 0 else g.hbuf
        srckey = [] if l == 0 else [('hbuf', b)]
        tt0 = 0 if with_ctx else 2
        for tt in range(tt0, NTT):
            r = tt % 2
            vi = 1 if tt < 2 else 0
            P.dma('sp', yt[r][:], g.ybuf[tt * 128:(tt + 1) * 128, :], r=['ybuf'], w=[yt[r]])
            P.dma('sp', ht[r][:], src[b, tt * 128:(tt + 1) * 128, :], r=srckey, w=[ht[r]])
            P.dve('tensor_reduce', out=rs[r][:, 0:1], in_=ss[:, tt, :], axis=AX.X, op=ALU.add, r=[('mg_ss', tt)],
                  w=[rs[r]])
            P.dve('tensor_scalar', out=rs[r][:, 1:2], in0=rs[r][:, 0:1], scalar1=1.0 / D, scalar2=EPS, op0=ALU.mult,
                  op1=ALU.add, r=[rs[r]], w=[rs[r]])
            P.act('activation', out=rs[r][:, 1:2], in_=rs[r][:, 1:2], func=AF.Sqrt, r=[rs[r]], w=[rs[r]])
            P.dve('reciprocal', out=rs[r][:, 1:2], in_=rs[r][:, 1:2], r=[rs[r]], w=[rs[r]])
            P.dve('scalar_tensor_tensor', out=yt[r][:], in0=yt[r][:], scalar=rs[r][:, 1:2], in1=GG[:, vi, :],
                  op0=ALU.mult, op1=ALU.mult, r=[yt[r], rs[r], GG], w=[yt[r]])
            P.pool('tensor_tensor', out=ht[r][:], in0=ht[r][:], in1=yt[r][:], op=ALU.add, r=[yt[r], ht[r]], w=[ht[r]])
            if last:
                P.dma('sp', g.out[b, (tt - 2) * 128:(tt - 1) * 128, :], ht[r][:], r=[ht[r]], wa=['out'])
            else:
                P.dma('sp', g.hbuf[b, tt * 128:(tt + 1) * 128, :], ht[r][:], r=[ht[r]], wa=[('hbuf', b)])


N_CORES = 8
_PROGRAM = {}


def kernel(**inputs):
    sh = prep_shared(inputs)
    if "nc" not in _PROGRAM:
        _PROGRAM["nc"] = build_program(NB=2, NL=2)[0]
    nc = _PROGRAM["nc"]
    in_maps = []
    for c in range(N_CORES):
        m = dict(sh)
        m.update(prep_core(inputs, [2 * c, 2 * c + 1]))
        in_maps.append(m)
    res = run_bass_kernel_spmd(nc, in_maps, core_ids=list(range(N_CORES)))
    out = np.concatenate([np.asarray(res.results[c]["out"]) for c in range(N_CORES)], axis=0)
    return np.ascontiguousarray(out.astype(np.float32))
```

```python
import numpy as np
import concourse.bass as bass
import concourse.mybir as mybir
from concourse.bass_utils import run_bass_kernel_spmd

F32 = mybir.dt.float32
BF16 = mybir.dt.bfloat16
AF = mybir.ActivationFunctionType
ALU = mybir.AluOpType
AX = mybir.AxisListType


class Prog:
    NHW = 16
    NDMA = NHW + 8

    def __init__(self, nc, same_engine_sync=True):
        self.nc = nc
        self.E = {'pe': nc.tensor, 'act': nc.scalar, 'dve': nc.vector, 'pool': nc.gpsimd, 'sp': nc.sync}
        self.sem = {e: nc.alloc_semaphore(name="sem_" + e) for e in ('pe', 'act', 'dve', 'pool')}
        self.cnt = {e: 0 for e in self.sem}
        self.dsem = [nc.alloc_semaphore(name="dsem%d" % i) for i in range(self.NDMA)]
        self.dval = [0] * self.NDMA
        self.dnext = 0
        self.dnext_sw = 0
        self.known = {e: {} for e in self.E}
        self.evclock = {}
        self.lastw = {}
        self.readers = {}
        self.same = same_engine_sync
        self.nwaits = 0
        self.ninst = 0
        self.pending = {}
        self._n = 0

    def sb(self, name, shape, dtype):
        return self.nc.alloc_sbuf_tensor(name, list(shape), dtype)

    def ps(self, name, shape, dtype=F32):
        return self.nc.alloc_psum_tensor(name, list(shape), dtype)

    def dram(self, name, shape, dtype, kind="Internal"):
        return self.nc.dram_tensor(name, list(shape), dtype, kind=kind)

    def _semof(self, k):
        return self.sem[k] if isinstance(k, str) else self.dsem[k[1]]

    def _wait(self, X, ev):
        k, v = ev
        kn = self.known[X]
        if kn.get(k, 0) >= v:
            return
        self.E[X].wait_ge(self._semof(k), v)
        self.nwaits += 1
        clk = self.evclock.get(ev)
        if clk:
            for kk, vv in clk.items():
                if kn.get(kk, 0) < vv:
                    kn[kk] = vv
        if kn.get(k, 0) < v:
            kn[k] = v

    @staticmethod
    def _keys(ks):
        return [k if isinstance(k, (str, tuple)) else k.name for k in ks]

    def _deps(self, X, r, w, wa=()):
        for key in r:
            for ev in self.lastw.get(key, ()):
                yield ev
        for key in w:
            for ev in self.lastw.get(key, ()):
                yield ev
            for ev in self.readers.get(key, ()):
                yield ev
        for key in wa:
            for ev in self.readers.get(key, ()):
                yield ev

    def _record(self, ev, r, w, wa=()):
        for key in r:
            lst = self.readers.setdefault(key, [])
            lst[:] = [e for e in lst if e[0] != ev[0]]
            lst.append(ev)
        for key in w:
            self.lastw[key] = [ev]
            self.readers[key] = []
        for key in wa:
            lst = self.lastw.setdefault(key, [])
            lst[:] = [e for e in lst if e[0] != ev[0]]
            lst.append(ev)

    def op(self, X, method, r=(), w=(), wa=(), defer=False, **kw):
        r = self._keys(r)
        w = self._keys(w)
        wa = self._keys(wa)
        pend = self.pending.setdefault(X, [[], [], []])
        if defer:
            assert X == 'pe'
            for ev in list(self._deps(X, r, w, wa)):
                if ev[0] == X:
                    continue
                self._wait(X, ev)
            getattr(self.E[X], method)(**kw)
            pend[0] += r
            pend[1] += w
            pend[2] += wa
            self.ninst += 1
            return None
        if pend[0] or pend[1] or pend[2]:
            r = list(dict.fromkeys(r + pend[0]))
            w = list(dict.fromkeys(w + pend[1]))
            wa = list(dict.fromkeys(wa + pend[2]))
            self.pending[X] = [[], [], []]
        for ev in list(self._deps(X, r, w, wa)):
            if ev[0] == X and (X == 'pe' or not self.same):
                continue
            self._wait(X, ev)
        inst = getattr(self.E[X], method)(**kw)
        self.cnt[X] += 1
        c = self.cnt[X]
        inst.then_inc(self.sem[X], 1)
        ev = (X, c)
        clk = dict(self.known[X])
        clk[X] = c
        self.evclock[ev] = clk
        self._record(ev, r, w, wa)
        self.ninst += 1
        return ev

    def pe(self, method, **kw):
        return self.op('pe', method, **kw)

    def act(self, method, **kw):
        return self.op('act', method, **kw)

    def dve(self, method, **kw):
        return self.op('dve', method, **kw)

    def pool(self, method, **kw):
        return self.op('pool', method, **kw)

    def dma(self, Q, out, in_, r=(), w=(), wa=(), **kw):
        r = self._keys(r)
        w = self._keys(w)
        wa = self._keys(wa)
        if Q == 'pool':
            i = self.NHW + self.dnext_sw
            self.dnext_sw = (self.dnext_sw + 1) % (self.NDMA - self.NHW)
        else:
            i = self.dnext
            self.dnext = (i + 1) % self.NHW
        k = ('d', i)
        if self.dval[i] > 0:
            self._wait(Q, (k, self.dval[i]))
        for ev in list(self._deps(Q, r, w, wa)):
            self._wait(Q, ev)
        inst = self.E[Q].dma_start(out=out, in_=in_, **kw)
        self.dval[i] += 16
        inst.then_inc(self.dsem[i], 16)
        ev = (k, self.dval[i])
        clk = dict(self.known[Q])
        clk[k] = self.dval[i]
        self.evclock[ev] = clk
        self._record(ev, r, w, wa)
        self.ninst += 1
        return ev

    def barrier(self):
        for X in ('pe', 'act', 'dve', 'pool', 'sp'):
            for e in self.sem:
                if self.cnt[e] > 0 and not (e == X and X == 'pe'):
                    self._wait(X, (e, self.cnt[e]))
            for i in range(self.NDMA):
                if self.dval[i] > 0:
                    self._wait(X, (('d', i), self.dval[i]))

    def scope(self):
        return Scope(self)

    def finish(self):
        for i in range(self.NDMA):
            if self.dval[i] > 0:
                self._wait('sp', (('d', i), self.dval[i]))
        for e in self.sem:
            if self.cnt[e] > 0:
                self._wait('sp', (e, self.cnt[e]))


D = 2048
TCTX = 256
TLAT = 2048
TT = TCTX + TLAT
NTT = TT // 128
N_TM = 6672
N_FM = 2080
PROWS = 2312
EPS = 1e-6

TM_NAV, TM_NAZ, TM_GLK, TM_GLKS, TM_GLV, TM_GLZ = 0, 512, 1024, 1280, 1536, 2048
TM_GDQKV, TM_GDZ, TM_HYXV, TM_HYZ, TM_GDAB = 2560, 4096, 4608, 6144, 6656
FM_NAQ, FM_NAK, FM_GLQ, FM_GLQS, FM_GLK, FM_GLKS, FM_GLG = 0, 512, 1024, 1280, 1536, 1792, 2048


def prow(tt):
    return 2 + tt * 128 if tt < 2 else 262 + (tt - 2) * 128


def w_in_perm():
    o = {}
    off = 0
    for name, w in (("na_q", 512), ("na_k", 512), ("na_v", 512), ("na_z", 512), ("gla_q", 256), ("gla_k", 256),
                    ("gla_v", 512), ("gla_z", 512), ("gla_g", 32), ("gdn_qkv", 1536), ("gdn_z", 512),
                    ("gdn_a", 8), ("gdn_b", 8), ("hy_xv", 1536), ("hy_z", 512)):
        o[name] = np.arange(off, off + w)
        off += w
    assert off == 7728

    def sw(ix):
        return ix.reshape(-1, 2, 32)[:, ::-1, :].reshape(-1)
    tm = np.concatenate([o["na_v"], o["na_z"], o["gla_k"], sw(o["gla_k"]), o["gla_v"], o["gla_z"], o["gdn_qkv"],
                         o["gdn_z"], o["hy_xv"], o["hy_z"], o["gdn_a"], o["gdn_b"]])
    fm = np.concatenate([o["na_q"], o["na_k"], o["gla_q"], sw(o["gla_q"]), o["gla_k"], sw(o["gla_k"]), o["gla_g"]])
    assert tm.size == N_TM and fm.size == N_FM
    return tm, fm


class Ctx:
    pass


class Scope:
    _uid = [0]

    def __init__(self, P):
        self.P = P
        self.cms = []

    def __enter__(self):
        return self

    def sb(self, name, shape, dtype):
        Scope._uid[0] += 1
        cm = self.P.nc.sbuf_tensor("%s_%d" % (name, Scope._uid[0]), list(shape), dtype)
        t = cm.__enter__()
        self.cms.append(cm)
        return t

    def ps(self, name, shape, dtype=F32):
        Scope._uid[0] += 1
        cm = self.P.nc.psum_tensor("%s_%d" % (name, Scope._uid[0]), list(shape), dtype)
        t = cm.__enter__()
        self.cms.append(cm)
        return t

    def __exit__(self, *a):
        self.P.barrier()
        for cm in reversed(self.cms):
            cm.__exit__(None, None, None)
        return False


def build_program(NB=2, NL=2, dbg=(), stop_after=None, branches="ABCDM"):
    nc = bass.Bass("TRN2", target_bir_lowering=False)
    P = Prog(nc)
    g = Ctx()
    g.nc, g.P, g.NB, g.NL = nc, P, NB, NL
    g.NL_total = 2
    g.branches = branches

    def din(name, shape, dt=F32):
        return nc.dram_tensor(name, list(shape), dt, kind="ExternalInput").ap()

    def dscr(name, shape, dt=F32):
        kind = "ExternalOutput" if name in dbg else "Internal"
        return nc.dram_tensor(name, list(shape), dt, kind=kind).ap()

    g.xin = din("xin", [NB, TT, D])
    g.cT = din("cT", [128, 16, 3])
    g.w_mod_t = din("w_mod_t", [2, 32, 128, 16, 128])
    g.w_mod_g = din("w_mod_g", [2, D, D])
    g.b_mod_col = din("b_mod_col", [2, 128, 32])
    g.b_mod_gate = din("b_mod_gate", [2, 1, D])
    g.g_pre_col = din("g_pre_col", [2, 128, 16])
    g.g_post_row = din("g_post_row", [2, 1, D])
    g.w_tm = din("w_tm", [2, D, N_TM])
    g.w_fm_t = din("w_fm_t", [2, 17, 128, 16, 128])
    g.w_gate_t = din("w_gate_t", [2, 4, 16, 128, 16, 128])
    g.b_gate_col = din("b_gate_col", [2, 128, 4, 16])
    g.w_branch_t = din("w_branch_t", [2, 4, 16, 128, 4, 128])
    g.w_out = din("w_out", [2, D, D])
    g.ident_in = din("ident", [128, 128])
    g.sel_in = din("sel", [3, 3, 128])
    g.na_tab2 = din("na_tab2", [2, 128, 8, 2, 7, 64])
    g.gla_Lm = din("gla_Lm", [2, 128, 128])
    g.rope_c_tm = din("rope_c_tm", [128, 16, 64])
    g.rope_s_tm = din("rope_s_tm", [128, 16, 64])
    g.rope_c_fm = din("rope_c_fm", [128, TLAT])
    g.rope_s_fm = din("rope_s_fm", [128, TLAT])
    g.gla_wg2 = din("gla_wg2", [2, 2, 16, 128])
    g.gla_bg_row = din("gla_bg_row", [2, 1, 2, 128])
    g.gla_norm_row = din("gla_norm_row", [2, 1, 128])
    g.gdn_mbs = din("gdn_mbs", [2, 128, 128])
    g.gdn_mbi = din("gdn_mbi", [2, 128, 128])
    g.ones128 = din("ones128", [128, 128])
    g.gdn_conv_row = din("gdn_conv_row", [2, 5, 1536])
    g.gdn_dtb_row = din("gdn_dtb_row", [2, 1, 8])
    g.gdn_alog_row = din("gdn_alog_row", [2, 1, 8])
    g.gdn_norm_row = din("gdn_norm_row", [2, 1, 128])
    g.hy_w_in = din("hy_w_in", [2, 33, 64])
    g.hy_w_mid = din("hy_w_mid", [2, 2, 64, 64])
    g.hy_w_out = din("hy_w_out", [2, 64, 1024])
    g.hy_freq_col = din("hy_freq_col", [2, 64, 3])
    g.hy_b_col = din("hy_b_col", [2, 64, 3])
    g.hy_conv_row = din("hy_conv_row", [2, 3, 1536])
    g.hy_conv_b_row = din("hy_conv_b_row", [2, 1, 1536])
    g.hy_skip_row = din("hy_skip_row", [2, 1, 512])
    g.hy_zT = [din("hy_zT0", [33, TLAT]), din("hy_zT1", [33, TCTX])]
    g.hy_decay = [din("hy_decay0", [128, 16, 512]), din("hy_decay1", [128, 2, 512])]
    g.hy_FmT = [din("hy_FmT0", [32, 128, 16, 128], BF16), din("hy_FmT1", [4, 128, 2, 128], BF16)]
    g.hy_GT = [din("hy_GT0", [16, 128, 32, 128], BF16), din("hy_GT1", [2, 128, 4, 128], BF16)]

    g.ptm = dscr("ptm", [NB, PROWS, N_TM])
    g.pfm = dscr("pfm", [NB, N_FM, TT])
    g.hbuf = dscr("hbuf", [NB, TT, D])
    g.yfm = dscr("yfm", [NB, 4 * 512, TT], BF16)
    g.khat = [dscr("khat0", [2 * TLAT, 512]), dscr("khat1", [2 * TCTX, 512])]
    g.hy_g0s = dscr("hy_g0s", [TT, 512])
    g.gq = dscr("gq", [TT, GQW])
    g.ybuf = dscr("ybuf", [TT, D])
    g.hy_vss = dscr("hy_vss", [TT, 512])
    g.out = nc.dram_tensor("out", [NB, TLAT, D], F32, kind="ExternalOutput").ap()

    g.ident = P.sb("ident_sb", [128, 128], F32)
    g.sel = P.sb("sel_sb", [3, 3, 128], F32)
    g.uT = P.sb("uT", [128, 16, TT], BF16)
    P.dma('sp', g.ident[:], g.ident_in, w=[g.ident])
    P.dma('sp', g.sel[:], g.sel_in, w=[g.sel])
    g.psA = [P.ps("psA%d" % i, [128, 512], F32) for i in range(4)]
    g.psB = [P.ps("psB%d" % i, [128, 512], F32) for i in range(4)]
    g.modcol = P.sb("modcol", [128, 32, 3], F32)
    g.Acol = P.sb("Acol", [128, 16, 3], F32)
    g.grow = P.sb("grow", [3, D], F32)
    with P.scope() as S:
        zero = S.sb("zero", [8, N_TM], F32)
        P.dve('memset', ap=zero[:], constant=0.0, w=[zero])
        for b in range(NB):
            for r0, n in ((0, 2), (258, 4), (2310, 2)):
                P.dma('sp', g.ptm[b, r0:r0 + n, :], zero[0:n, :], r=[zero], w=[('ptm', b)])

    for l in range(NL):
        phase_mod(g, l)
        if 'D' in g.branches:
            phase_hyfilt(g, l, 0)
            if l < g.NL_total - 1:
                phase_hyfilt(g, l, 1)
        for b in range(NB):
            phase_norm(g, l, b)
            if stop_after == 'norm':
                continue
            phase_proj(g, l, b)
            if stop_after == 'proj':
                continue
            if 'A' in g.branches:
                phase_na(g, l, b)
            if 'B' in g.branches:
                phase_gla(g, l, b)
            if 'C' in g.branches:
                phase_gdn_pre(g, l, b)
                phase_gdn(g, l, b)
            if 'D' in g.branches:
                phase_hy(g, l, b, 0)
                if l < g.NL_total - 1:
                    phase_hy(g, l, b, 1)
            if 'M' in g.branches:
                phase_merge(g, l, b)
    P.finish()
    return nc, g


def phase_mod(g, l):
    P, NB = g.P, g.NB
    with P.scope() as S:
        _phase_mod(g, l, S)


def _phase_mod(g, l, S):
    P = g.P
    g.cT_sb = S.sb("cT_sb", [128, 16, 3], F32)
    g.scT = S.sb("scT", [128, 16, 3], F32)
    g.mslab = [S.sb("mslab%d" % i, [128, 16, 128], F32) for i in range(2)]
    g.gslab = [S.sb("gslab%d" % i, [128, 4, 512], F32) for i in range(2)]
    g.bmc = S.sb("bmc", [128, 32], F32)
    g.gpc = S.sb("gpc", [128, 16], F32)
    g.brow = S.sb("brow", [3, D], F32)
    P.dma('sp', g.cT_sb[:], g.cT, w=[g.cT_sb])
    P.act('activation', out=g.scT[:], in_=g.cT_sb[:], func=AF.Silu, r=[g.cT_sb], w=[g.scT])
    ps = g.psA[0]
    for ft in range(32):
        slab = g.mslab[ft % 2]
        P.dma('sp', slab[:], g.w_mod_t[l, ft], w=[slab])
        for kt in range(16):
            P.pe('matmul', out=ps[:, ft * 3:(ft + 1) * 3], lhsT=slab[:, kt, :], rhs=g.scT[:, kt, :],
                 start=(kt == 0), stop=(kt == 15), r=[slab, g.scT], w=[ps])
    P.dma('sp', g.bmc[:], g.b_mod_col[l], w=[g.bmc])
    P.dma('sp', g.gpc[:], g.g_pre_col[l], w=[g.gpc])
    for j in range(3):
        P.dve('tensor_tensor', out=g.modcol[:, :, j], in0=ps[:, 0:96].rearrange("p (f j) -> p f j", j=3)[:, :, j],
              in1=g.bmc[:], op=ALU.add, r=[ps, g.bmc], w=[g.modcol])
        P.dve('scalar_tensor_tensor', out=g.Acol[:, :, j], in0=g.modcol[:, 16:32, j], scalar=1.0, in1=g.gpc[:],
              op0=ALU.add, op1=ALU.mult, r=[g.modcol, g.gpc], w=[g.Acol])
    psg = g.psA[1:3]
    for jc in range(4):
        for kq in range(4):
            slab = g.gslab[(jc * 4 + kq) % 2]
            P.dma('sp', slab[:], g.w_mod_g[l, kq * 512:(kq + 1) * 512, jc * 512:(jc + 1) * 512]
                  .rearrange("(kt p) f -> p kt f", p=128), w=[slab])
            for k4 in range(4):
                kt = kq * 4 + k4
                P.pe('matmul', out=psg[jc % 2][0:3, :], lhsT=g.scT[:, kt, :], rhs=slab[:, k4, :],
                     start=(kt == 0), stop=(kt == 15), r=[slab, g.scT], w=[psg[jc % 2]])
        P.act('activation', out=g.grow[:, jc * 512:(jc + 1) * 512], in_=psg[jc % 2][0:3, :], func=AF.Copy,
              r=[psg[jc % 2]], w=[g.grow])
    P.dma('sp', g.brow[:], g.b_mod_gate[l].partition_broadcast(3), w=[g.brow])
    P.dve('tensor_tensor', out=g.grow[:], in0=g.grow[:], in1=g.brow[:], op=ALU.add, r=[g.brow, g.grow], w=[g.grow])
    P.dma('sp', g.brow[:], g.g_post_row[l].partition_broadcast(3), r=[], w=[g.brow])
    P.dve('tensor_tensor', out=g.grow[:], in0=g.grow[:], in1=g.brow[:], op=ALU.mult, r=[g.brow, g.grow], w=[g.grow])


def phase_norm(g, l, b):
    with g.P.scope() as S:
        _phase_norm(g, l, b, S)


def _phase_norm(g, l, b, S):
    P = g.P
    g.hx = [S.sb("hx%d" % i, [128, D], F32) for i in range(2)]
    g.xr = [S.sb("xr%d" % i, [128, D], F32) for i in range(2)]
    g.ss = [S.sb("ss%d" % i, [128, 2], F32) for i in range(2)]
    src = g.xin if l == 0 else g.hbuf
    srckey = () if l == 0 else [('hbuf', b)]
    for tt in range(NTT):
        hx, xr, ss = g.hx[tt % 2], g.xr[tt % 2], g.ss[tt % 2]
        j = 2 if tt < 2 else b
        P.dma('sp', hx[:], src[b, tt * 128:(tt + 1) * 128, :], r=srckey, w=[hx])
        P.act('activation', out=xr[:], in_=hx[:], func=AF.Square, accum_out=ss[:, 0:1], r=[hx], w=[xr, ss])
        P.dve('tensor_scalar', out=ss[:, 1:2], in0=ss[:, 0:1], scalar1=1.0 / D, scalar2=EPS, op0=ALU.mult,
              op1=ALU.add, r=[ss], w=[ss])
        P.act('activation', out=ss[:, 1:2], in_=ss[:, 1:2], func=AF.Sqrt, r=[ss], w=[ss])
        P.dve('reciprocal', out=ss[:, 1:2], in_=ss[:, 1:2], r=[ss], w=[ss])
        P.dve('tensor_scalar', out=xr[:], in0=hx[:], scalar1=ss[:, 1:2], scalar2=None, op0=ALU.mult,
              r=[hx, ss], w=[xr])
        for kt in range(16):
            pt = g.psA[kt % 4]
            P.pe('transpose', out=pt[:, 0:128], in_=xr[:, kt * 128:(kt + 1) * 128], identity=g.ident[:],
                 r=[xr, g.ident], w=[pt])
            P.act('activation', out=g.uT[:, kt, tt * 128:(tt + 1) * 128], in_=pt[:, 0:128], func=AF.Identity,
                  scale=g.Acol[:, kt, j:j + 1], bias=g.modcol[:, kt, j:j + 1], r=[pt, g.Acol, g.modcol],
                  wa=[g.uT])


def phase_proj(g, l, b):
    with g.P.scope() as S:
        _phase_proj(g, l, b, S)


def _phase_proj(g, l, b, S):
    P = g.P
    g.wst = [S.sb("wst%d" % i, [128, 16, 512], F32) for i in range(2)]
    g.wbf = [S.sb("wbf%d" % i, [128, 16, 512], BF16) for i in range(2)]
    g.stg = [S.sb("stg%d" % i, [128, 512], F32) for i in range(3)]
    nchunk = (N_TM + 511) // 512
    it = 0
    for cc in range(nchunk):
        c0 = cc * 512
        cw = min(512, N_TM - c0)
        wst, wbf = g.wst[cc % 2], g.wbf[cc % 2]
        for h4 in range(4):
            P.dma('sp', wst[:, h4 * 4:(h4 + 1) * 4, 0:cw],
                  g.w_tm[l, h4 * 512:(h4 + 1) * 512, c0:c0 + cw].rearrange("(kt p) f -> p kt f", p=128),
                  w=[(wst.name, h4)])
            if h4 % 2 == 0:
                P.dve('tensor_copy', out=wbf[:, h4 * 4:(h4 + 1) * 4, 0:cw], in_=wst[:, h4 * 4:(h4 + 1) * 4, 0:cw],
                      r=[(wst.name, h4)], w=[(wbf.name, h4)])
            else:
                P.act('activation', out=wbf[:, h4 * 4:(h4 + 1) * 4, 0:cw], in_=wst[:, h4 * 4:(h4 + 1) * 4, 0:cw],
                      func=AF.Copy, r=[(wst.name, h4)], w=[(wbf.name, h4)])
        for tt in range(NTT):
            ps = g.psA[it % 4]
            stg = g.stg[it % 3]
            it += 1
            for kt in range(16):
                P.pe('matmul', out=ps[:, 0:cw], lhsT=g.uT[:, kt, tt * 128:(tt + 1) * 128], rhs=wbf[:, kt, 0:cw],
                     start=(kt == 0), stop=(kt == 15), r=[g.uT, (wbf.name, kt // 4)], w=[ps], defer=(kt < 15))
            P.act('activation', out=stg[:, 0:cw], in_=ps[:, 0:cw], func=AF.Copy, r=[ps], w=[stg])
            P.dma('pool', g.ptm[b, prow(tt):prow(tt) + 128, c0:c0 + cw], stg[:, 0:cw], r=[stg], wa=[('ptm', b)])
    nft = (N_FM + 127) // 128
    chunks = [(0, 256)] + [(256 + i * 512, 512) for i in range(4)]
    for ft in range(nft):
        f0 = ft * 128
        fw = min(128, N_FM - f0)
        wst, wbf = g.wst[ft % 2], g.wbf[ft % 2]
        P.dma('sp', wst[:, :, 0:128], g.w_fm_t[l, ft], w=[(wst.name, i) for i in range(4)])
        P.dve('tensor_copy', out=wbf[:, :, 0:fw], in_=wst[:, :, 0:fw], r=[(wst.name, i) for i in range(4)],
              w=[(wbf.name, i) for i in range(4)])
        for (t0, n) in chunks:
            ps = g.psA[it % 4]
            stg = g.stg[it % 3]
            it += 1
            for kt in range(16):
                P.pe('matmul', out=ps[0:fw, 0:n], lhsT=wbf[:, kt, 0:fw], rhs=g.uT[:, kt, t0:t0 + n],
                     start=(kt == 0), stop=(kt == 15), r=[g.uT, (wbf.name, kt // 4)], w=[ps], defer=(kt < 15))
            P.act('activation', out=stg[0:fw, 0:n], in_=ps[0:fw, 0:n], func=AF.Copy, r=[ps], w=[stg])
            P.dma('pool', g.pfm[b, f0:f0 + fw, t0:t0 + n], stg[0:fw, 0:n], r=[stg], wa=[('pfm', b)])


def prep_shared(inp):
    f = lambda a: np.ascontiguousarray(np.asarray(a, dtype=np.float32))
    tm, fm = w_in_perm()
    sh = {}
    w_mod = np.asarray(inp["w_mod"], dtype=np.float32)
    sh["w_mod_t"] = f(w_mod[:, :, :2 * D].reshape(2, 16, 128, 32, 128).transpose(0, 3, 2, 1, 4))
    sh["w_mod_g"] = f(w_mod[:, :, 2 * D:])
    b_mod = f(inp["b_mod"])
    sh["b_mod_col"] = f(b_mod[:, :2 * D].reshape(2, 32, 128).transpose(0, 2, 1))
    sh["b_mod_gate"] = f(b_mod[:, 2 * D:].reshape(2, 1, D))
    sh["g_pre_col"] = f(f(inp["g_pre"]).reshape(2, 16, 128).transpose(0, 2, 1))
    sh["g_post_row"] = f(f(inp["g_post"]).reshape(2, 1, D))
    sh["w_gate_t"] = f(np.asarray(inp["w_gate"], dtype=np.float32).reshape(2, 4, 16, 128, 16, 128)
                       .transpose(0, 1, 4, 3, 2, 5))
    sh["b_gate_col"] = f(f(inp["b_gate"]).reshape(2, 4, 16, 128).transpose(0, 3, 1, 2))
    sh["w_branch_t"] = f(np.asarray(inp["w_branch"], dtype=np.float32).reshape(2, 4, 4, 128, 16, 128)
                         .transpose(0, 1, 4, 3, 2, 5))
    sh["w_out"] = f(inp["w_out"])
    w_in = np.asarray(inp["w_in"], dtype=np.float32)
    sh["w_tm"] = f(w_in[:, :, tm])
    wfm = np.zeros((2, D, 17 * 128), np.float32)
    wfm[:, :, :N_FM] = w_in[:, :, fm]
    sh["w_fm_t"] = f(wfm.reshape(2, 16, 128, 17, 128).transpose(0, 3, 2, 1, 4))
    sh["ident"] = np.eye(128, dtype=np.float32)
    sel = np.zeros((3, 3, 128), np.float32)
    for j in range(3):
        sel[j, j, :] = 1.0
    sh["sel"] = sel
    tab5 = na_table(inp["na_rpb"])
    tab2 = np.empty((2, 2, 64, 8, 2, 7, 64), np.float32)
    for jj in range(2):
        for par in range(2):
            for m_ in range(7):
                tab2[:, jj, :, :, par, m_, :] = tab5[:, :, :, 2 * m_ + par + jj, :]
    sh["na_tab2"] = np.ascontiguousarray(tab2.reshape(2, 128, 8, 2, 7, 64))
    sh.update(gla_consts())
    sh["gla_wg2"] = f(inp["gla_wg2"])
    sh["gla_bg_row"] = f(f(inp["gla_bg"]).reshape(2, 1, 2, 128))
    sh["gla_norm_row"] = f(f(inp["gla_norm"]).reshape(2, 1, 128))
    sh.update(gdn_consts())
    sh["gdn_conv_row"] = f(f(inp["gdn_conv"]).transpose(0, 2, 1))
    sh["gdn_dtb_row"] = f(f(inp["gdn_dt_bias"]).reshape(2, 1, 8))
    sh["gdn_alog_row"] = f(f(inp["gdn_a_log"]).reshape(2, 1, 8))
    sh["gdn_norm_row"] = f(f(inp["gdn_norm"]).reshape(2, 1, 128))
    sh["hy_w_in"] = f(inp["hy_w_in"])
    sh["hy_w_mid"] = f(inp["hy_w_mid"])
    sh["hy_w_out"] = f(inp["hy_w_out"])
    sh["hy_freq_col"] = f(f(inp["hy_freq"]).transpose(0, 2, 1))
    sh["hy_b_col"] = f(np.concatenate([f(inp["hy_b_in"])[:, None, :], f(inp["hy_b_mid"])], axis=1).transpose(0, 2, 1))
    sh["hy_conv_row"] = f(f(inp["hy_conv"]).transpose(0, 2, 1))
    sh["hy_conv_b_row"] = f(f(inp["hy_conv_b"]).reshape(2, 1, 1536))
    sh["hy_skip_row"] = f(f(inp["hy_skip"]).reshape(2, 1, 512))
    for li, L in enumerate((TLAT, TCTX)):
        hc = hy_consts(L)
        sh["hy_zT%d" % li] = hc["zT"]
        sh["hy_decay%d" % li] = hc["decay"]
        sh["hy_FmT%d" % li] = hc["FmT"]
        sh["hy_GT%d" % li] = hc["GT"]
    return sh


def prep_core(inp, bs):
    x = np.asarray(inp["x"], dtype=np.float32)
    ctx = np.asarray(inp["ctx"], dtype=np.float32)
    c = np.asarray(inp["c"], dtype=np.float32)
    c_ctx = np.asarray(inp["c_ctx"], dtype=np.float32)
    d = {}
    d["xin"] = np.ascontiguousarray(np.stack([np.concatenate([ctx[b], x[b]], axis=0) for b in bs]))
    cb = [c[b] for b in bs]
    while len(cb) < 2:
        cb.append(cb[0])
    cm = np.stack(cb[:2] + [c_ctx], axis=1)
    d["cT"] = np.ascontiguousarray(cm.reshape(16, 128, 3).transpose(1, 0, 2))
    return d


def na_table(rpb):
    rpb = np.asarray(rpb, dtype=np.float32)
    kc = np.arange(64)[:, None]
    qc = np.arange(64)[None, :]
    cstart = np.clip(qc - 8, 0, 48)
    valid = (kc >= cstart) & (kc < cstart + 16)
    dc = np.clip(kc - qc + 15, 0, 30)
    tab = rpb[:, :, :, dc]
    tab = np.where(valid[None, None, None], tab, np.float32(-30000.0))
    return np.ascontiguousarray(tab.transpose(0, 3, 1, 2, 4).astype(np.float32))


def rowtok(row):
    return 64 * row


def phase_na(g, l, b):
    with g.P.scope() as S:
        _phase_na(g, l, b, S)


def _phase_na(g, l, b, S):
    P = g.P
    with_ctx = l < g.NL_total - 1
    stage = S.sb("na_stage", [128, TT], F32)
    qT = S.sb("na_qT", [128, TT], BF16)
    kTh = [S.sb("na_kT%d" % i, [128, TT], BF16) for i in range(2)]
    for i in range(2):
        P.pool('memset', ap=kTh[i][:], constant=0.0, w=[kTh[i]])
    stE = S.sb("na_stE", [128, 18, 128], F32)
    stO = S.sb("na_stO", [128, 15, 128], F32)
    vE = S.sb("na_vE", [128, 18, 2, 65], BF16)
    vO = S.sb("na_vO", [128, 15, 2, 65], BF16)
    st2 = S.sb("na_st2", [64, 36, 128], F32)
    oA = S.sb("na_oA", [64, 36, 128], F32)
    Tb = S.sb("na_Tb", [128, 2, 2, 7, 64], F32)
    sw = [S.sb("na_sw%d" % i, [128, 256], F32) for i in range(2)]
    pall = [S.sb("na_pall%d" % i, [128, 384], BF16) for i in range(2)]
    rec = [S.sb("na_rec%d" % i, [64, 1], F32) for i in range(2)]
    yst = S.sb("na_yst", [128, TT], BF16)
    psS, psO, psT = g.psA[0:2], g.psB[0:2], g.psB[2]
    row0 = 0 if with_ctx else 4
    for hp in range(4):
        P.dma('sp', stage[:], g.pfm[b, FM_NAQ + hp * 128:FM_NAQ + (hp + 1) * 128, :], r=[('pfm', b)], w=[stage])
        P.dve('tensor_copy', out=qT[:], in_=stage[:], r=[stage], w=[qT])
        P.dma('sp', stage[:], g.pfm[b, FM_NAK + hp * 128:FM_NAK + (hp + 1) * 128, :], r=[('pfm', b)], w=[stage])
        for i in range(2):
            P.dve('tensor_copy', out=kTh[i][64 * i:64 * i + 64, :], in_=stage[64 * i:64 * i + 64, :], r=[stage],
                  w=[kTh[i]])
        c0 = TM_NAV + hp * 128
        P.dma('sp', stE[:, 0:2, :], g.ptm[b, 2:258, c0:c0 + 128].rearrange("(m p) c -> p m c", p=128),
              r=[('ptm', b)], w=[stE])
        P.dma('sp', stE[:, 2:18, :], g.ptm[b, 262:2310, c0:c0 + 128].rearrange("(m p) c -> p m c", p=128),
              r=[('ptm', b)], wa=[stE])
        P.dma('sp', stO[:], g.ptm[b, 326:326 + 15 * 128, c0:c0 + 128].rearrange("(m p) c -> p m c", p=128),
              r=[('ptm', b)], w=[stO])
        P.pool('memset', ap=vE[:], constant=1.0, w=[vE])
        P.pool('memset', ap=vO[:], constant=1.0, w=[vO])
        P.dve('tensor_copy', out=vE[:, :, :, 0:64], in_=stE[:].rearrange("p r (h d) -> p r h d", h=2), r=[stE], w=[vE])
        P.dve('tensor_copy', out=vO[:, :, :, 0:64], in_=stO[:].rearrange("p r (h d) -> p r h d", h=2), r=[stO], w=[vO])
        c0 = TM_NAZ + hp * 128
        P.dma('sp', st2[:, 0:4, :], g.ptm[b, 2:258, c0:c0 + 128].rearrange("(r p) c -> p r c", p=64),
              r=[('ptm', b)], w=[st2])
        P.dma('sp', st2[:, 4:36, :], g.ptm[b, 262:2310, c0:c0 + 128].rearrange("(r p) c -> p r c", p=64),
              r=[('ptm', b)], wa=[st2])
        P.act('activation', out=st2[:], in_=st2[:], func=AF.Silu, r=[st2], w=[st2])
        P.dma('sp', Tb[:], g.na_tab2[l, :, 2 * hp:2 * hp + 2], w=[Tb])
        it = 0
        for h2 in range(2):
            hb = 64 * h2
            kT = kTh[h2]
            for row in range(row0, 36):
                i2 = it % 2
                it += 1
                tq = 64 * row
                qv = qT[:, tq:tq + 64]
                lat = row >= 4
                ps = psS[i2]
                vts = []
                if lat:
                    r_ = row - 4
                    rs = min(max(r_ - 4, 0), 24)
                    dr0 = rs - r_ + 7
                    for j in range(4):
                        tk = 256 + 64 * (rs + 2 * j)
                        P.pe('matmul', out=ps[:, j * 64:(j + 1) * 64], lhsT=kT[:, tk:tk + 128], rhs=qv, start=True,
                             stop=True, r=[kT, qT], w=[ps])
                        vts.append(vE[:, tk // 128, h2, :] if tk % 128 == 0 else vO[:, (tk - 64) // 128 - 2, h2, :])
                for i in range(2):
                    P.pe('matmul', out=ps[:, 256 + i * 64:256 + (i + 1) * 64], lhsT=kT[:, 128 * i:128 * i + 128],
                         rhs=qv, start=True, stop=True, r=[kT, qT], w=[ps])
                if lat:
                    P.dve('scalar_tensor_tensor', out=sw[i2][:], in0=ps[:, 0:256], scalar=0.125,
                          in1=Tb[:, h2, dr0 % 2, dr0 // 2:dr0 // 2 + 4, :].rearrange("p a b -> p (a b)"),
                          op0=ALU.mult, op1=ALU.add, r=[ps, Tb], w=[sw[i2]])
                    P.act('activation', out=pall[i2][:, 0:256], in_=sw[i2][:], func=AF.Exp, r=[sw[i2]],
                          w=[(pall[i2].name, 0)])
                P.act('activation', out=pall[i2][:, 256:384], in_=ps[:, 256:384], func=AF.Exp, scale=0.125,
                      r=[ps], w=[(pall[i2].name, 1)])
                if lat:
                    for j in range(4):
                        P.pe('matmul', out=psO[i2][0:64, 0:65], lhsT=pall[i2][:, j * 64:(j + 1) * 64], rhs=vts[j],
                             start=(j == 0), stop=False, r=[(pall[i2].name, 0), vE, vO], w=[psO[i2]])
                for i in range(2):
                    P.pe('matmul', out=psO[i2][0:64, 0:65], lhsT=pall[i2][:, 256 + i * 64:256 + (i + 1) * 64],
                         rhs=vE[:, i, h2, :], start=(i == 0 and not lat), stop=(i == 1),
                         r=[(pall[i2].name, 1), vE], w=[psO[i2]])
                P.dve('reciprocal', out=rec[i2][:], in_=psO[i2][0:64, 64:65], r=[psO[i2]], w=[rec[i2]])
                P.dve('tensor_scalar', out=oA[:, row, hb:hb + 64], in0=psO[i2][0:64, 0:64], scalar1=rec[i2][:, 0:1],
                      scalar2=None, op0=ALU.mult, r=[psO[i2], rec[i2]], wa=[oA])
        P.dve('tensor_tensor', out=oA[:], in0=oA[:], in1=st2[:], op=ALU.mult, r=[st2, oA], w=[oA])
        for row in range(row0, 36):
            P.pe('transpose', out=psT[:, 0:64], in_=oA[:, row, :], identity=g.ident[0:64, 0:64],
                 r=[oA, g.ident], w=[psT])
            P.act('activation', out=yst[:, 64 * row:64 * row + 64], in_=psT[:, 0:64], func=AF.Copy,
                  r=[psT], wa=[yst])
        t0 = 64 * row0
        P.dma('sp', g.yfm[b, hp * 128:(hp + 1) * 128, t0:TT], yst[:, t0:TT], r=[yst], wa=[('yfm', b)])


TWO_PI = 2.0 * np.pi


def hy_consts(L):
    import ml_dtypes
    N = 2 * L
    ntt = L // 128
    t = np.arange(L, dtype=np.float64)
    f = np.arange(L, dtype=np.float64)
    ang = 2.0 * np.pi * np.outer(t, f) / N
    Fm = np.empty((L, N), np.float64)
    Fm[:, :L] = np.cos(ang)
    Fm[:, L:] = -np.sin(ang)
    Fm[:, L] = np.cos(np.pi * t)
    G = np.empty((N, L), np.float64)
    G[:L, :] = 2.0 * np.cos(ang.T) / N
    G[0, :] = 1.0 / N
    G[L:, :] = -2.0 * np.sin(ang.T) / N
    G[L, :] = np.cos(np.pi * t) / N
    FmT = Fm.reshape(ntt, 128, 2 * ntt, 128).transpose(2, 1, 0, 3)
    GT = G.reshape(2 * ntt, 128, ntt, 128).transpose(2, 1, 0, 3)
    tt_ = np.linspace(0.0, 1.0, L, dtype=np.float32)[:, None]
    bands = 16
    wpos = (np.float32(2.0 * np.pi) * np.arange(L, dtype=np.float32)[:, None] / np.float32(L)).astype(np.float32)
    fb = np.linspace(1e-4, bands - 1, bands, dtype=np.float32)[None]
    z = np.concatenate([tt_, np.cos(fb * wpos), -np.sin(fb * wpos)], axis=-1).astype(np.float32)
    deltas = np.abs(np.linspace(np.log(1e-2) / 1.5, np.log(1e-2) / 0.3, 512, dtype=np.float32))
    decay = np.exp(-tt_ * deltas).astype(np.float32)
    return {
        "FmT": np.ascontiguousarray(FmT).astype(ml_dtypes.bfloat16),
        "GT": np.ascontiguousarray(GT).astype(ml_dtypes.bfloat16),
        "zT": np.ascontiguousarray(z.T),
        "decay": np.ascontiguousarray(decay.reshape(ntt, 128, 512).transpose(1, 0, 2)),
    }


def phase_hyfilt(g, l, li):
    with g.P.scope() as S:
        _phase_hyfilt(g, l, li, S)


def _phase_hyfilt(g, l, li, S):
    P = g.P
    L = (TLAT, TCTX)[li]
    ntt = L // 128
    wi = S.sb("hf_wi", [33, 64], F32)
    wm = S.sb("hf_wm", [64, 2, 64], F32)
    wo = S.sb("hf_wo", [64, 1024], F32)
    fcol = S.sb("hf_fcol", [64, 3], F32)
    bcol = S.sb("hf_bcol", [64, 3], F32)
    fs = S.sb("hf_fs", [64, 3], F32)
    fb = S.sb("hf_fb", [64, 3], F32)
    zT = S.sb("hf_zT", [33, L], F32)
    hT = [S.sb("hf_hT%d" % i, [64, L], F32) for i in range(2)]
    ua = [S.sb("hf_ua%d" % i, [64, 512], F32) for i in range(2)]
    ub = [S.sb("hf_ub%d" % i, [64, 512], F32) for i in range(2)]
    dec = S.sb("hf_dec", [128, ntt, 512], F32)
    hfb = [S.sb("hf_hfb%d" % i, [128, 512], F32) for i in range(2)]
    hsb = S.sb("hf_hsb", [128, ntt, 512], BF16)
    hdb = S.sb("hf_hdb", [128, ntt, 512], BF16)
    fmt = [S.sb("hf_fmt%d" % i, [128, ntt, 128], BF16) for i in range(2)]
    kst = [S.sb("hf_kst%d" % i, [128, 512], F32) for i in range(2)]
    P.dma('sp', wi[:], g.hy_w_in[l], w=[wi])
    P.dma('sp', wm[:], g.hy_w_mid[l].rearrange("i k n -> k i n"), w=[wm])
    P.dma('sp', wo[:], g.hy_w_out[l], w=[wo])
    P.dma('sp', fcol[:], g.hy_freq_col[l], w=[fcol])
    P.dma('sp', bcol[:], g.hy_b_col[l], w=[bcol])
    P.dma('sp', zT[:], g.hy_zT[li], w=[zT])
    P.dma('sp', dec[:], g.hy_decay[li], w=[dec])
    P.dve('tensor_scalar', out=fs[:], in0=fcol[:], scalar1=1.0 / TWO_PI, scalar2=None, op0=ALU.mult,
          r=[fcol], w=[fs])
    P.dve('tensor_tensor', out=fb[:], in0=fs[:], in1=bcol[:], op=ALU.mult, r=[fs, bcol], w=[fb])
    nch = (L + 511) // 512
    cwid = min(512, L)
    it = 0
    for i in range(3):
        dst = hT[i % 2]
        for c in range(nch):
            ps = g.psA[it % 4]
            i2 = it % 2
            it += 1
            if i == 0:
                P.pe('matmul', out=ps[0:64, 0:cwid], lhsT=wi[:], rhs=zT[:, c * cwid:(c + 1) * cwid], start=True,
                     stop=True, r=[wi, zT], w=[ps])
            else:
                src = hT[(i - 1) % 2]
                P.pe('matmul', out=ps[0:64, 0:cwid], lhsT=wm[:, i - 1, :], rhs=src[:, c * cwid:(c + 1) * cwid],
                     start=True, stop=True, r=[wm, src], w=[ps])
            P.act('activation', out=ua[i2][:, 0:cwid], in_=ps[0:64, 0:cwid], func=AF.Identity,
                  scale=fs[:, i:i + 1], bias=fb[:, i:i + 1], r=[ps, fs, fb], w=[ua[i2]])
            P.dve('scalar_tensor_tensor', out=ub[i2][:, 0:cwid], in0=ua[i2][:, 0:cwid], scalar=0.5,
                  in1=ua[i2][:, 0:cwid], op0=ALU.is_gt, op1=ALU.subtract, r=[ua[i2]], w=[ub[i2]])
            P.dve('scalar_tensor_tensor', out=ub[i2][:, 0:cwid], in0=ua[i2][:, 0:cwid], scalar=-0.5,
                  in1=ub[i2][:, 0:cwid], op0=ALU.is_lt, op1=ALU.subtract, r=[ua[i2], ub[i2]], w=[ub[i2]])
            P.act('activation', out=dst[:, c * cwid:(c + 1) * cwid], in_=ub[i2][:, 0:cwid], func=AF.Sin,
                  scale=TWO_PI, r=[ub[i2]], wa=[dst])
    h3 = hT[0]
    for tt in range(ntt):
        for half in range(2):
            ps = g.psA[it % 4]
            it += 1
            P.pe('matmul', out=ps[:, :], lhsT=h3[:, tt * 128:(tt + 1) * 128], rhs=wo[:, half * 512:(half + 1) * 512],
                 start=True, stop=True, r=[h3, wo], w=[ps])
            P.dve('tensor_tensor', out=hfb[half][:], in0=ps[:, :], in1=dec[:, tt, :], op=ALU.mult,
                  r=[ps, dec], w=[hfb[half]])
        P.dve('tensor_tensor', out=hsb[:, tt, :], in0=hfb[0][:], in1=hfb[1][:], op=ALU.add,
              r=[hfb[0], hfb[1]], wa=[hsb])
        P.pool('tensor_tensor', out=hdb[:, tt, :], in0=hfb[0][:], in1=hfb[1][:], op=ALU.subtract,
               r=[hfb[0], hfb[1]], wa=[hdb])
    for rt in range(2 * ntt):
        fm = fmt[rt % 2]
        ks = kst[rt % 2]
        P.dma('sp', fm[:], g.hy_FmT[li][rt], w=[fm])
        ps = g.psA[it % 4]
        it += 1
        src = hsb if rt < ntt else hdb
        for tt in range(ntt):
            P.pe('matmul', out=ps[:, :], lhsT=fm[:, tt, :], rhs=src[:, tt, :], start=(tt == 0), stop=(tt == ntt - 1),
                 r=[fm, src], w=[ps])
        P.act('activation', out=ks[:], in_=ps[:, :], func=AF.Copy, r=[ps], w=[ks])
        if rt == ntt:
            ps2 = g.psA[it % 4]
            it += 1
            for tt in range(ntt):
                P.pe('matmul', out=ps2[:, :], lhsT=fm[:, tt, :], rhs=hsb[:, tt, :], start=(tt == 0),
                     stop=(tt == ntt - 1), r=[fm, hsb], w=[ps2])
            P.act('activation', out=ks[0:1, :], in_=ps2[0:1, :], func=AF.Copy, r=[ps2, ks], w=[ks])
        P.dma('sp', g.khat[li][rt * 128:(rt + 1) * 128, :], ks[:], r=[ks], wa=[('khat', li)])


def phase_hy(g, l, b, li):
    with g.P.scope() as S:
        _phase_hy(g, l, b, li, S)


def _phase_hy(g, l, b, li, S):
    P = g.P
    L = (TLAT, TCTX)[li]
    ntt = L // 128
    tile0 = 2 if li == 0 else 0
    tok0 = 256 if li == 0 else 0
    vvb = S.sb("hy_vvb", [128, ntt, 512], BF16)
    with P.scope() as S1:
        wrow = S1.sb("hy_wrow", [128, 3, 1536], F32)
        brow = S1.sb("hy_brow", [128, 1536], F32)
        srow = S1.sb("hy_srow", [128, 512], F32)
        xs = [S1.sb("hy_xs%d" % i, [128, 1536], F32) for i in range(3)]
        acc = S1.sb("hy_acc", [128, 1536], F32)
        tmp = S1.sb("hy_tmp", [128, 1536], F32)
        zt = S1.sb("hy_zt", [128, 512], F32)
        vv = S1.sb("hy_vv", [128, 512], F32)
        g0 = S1.sb("hy_g0", [128, 512], F32)
        vs = S1.sb("hy_vs", [128, 512], F32)
        for k in range(3):
            P.dma('sp', wrow[:, k, :], g.hy_conv_row[l, k:k + 1, :].partition_broadcast(128), wa=[wrow])
        P.dma('sp', brow[:], g.hy_conv_b_row[l].partition_broadcast(128), w=[brow])
        P.dma('sp', srow[:], g.hy_skip_row[l].partition_broadcast(128), w=[srow])
        for tt in range(ntt):
            base = prow(tile0 + tt)
            for k in range(3):
                P.dma('sp', xs[k][:], g.ptm[b, base + k - 1:base + k - 1 + 128, TM_HYXV:TM_HYXV + 1536],
                      r=[('ptm', b)], w=[xs[k]])
            P.dma('sp', zt[:], g.ptm[b, base:base + 128, TM_HYZ:TM_HYZ + 512], r=[('ptm', b)], w=[zt])
            P.dve('tensor_tensor', out=acc[:], in0=xs[0][:], in1=wrow[:, 0, :], op=ALU.mult, r=[xs[0], wrow], w=[acc])
            P.pool('tensor_tensor', out=tmp[:], in0=xs[1][:], in1=wrow[:, 1, :], op=ALU.mult, r=[xs[1], wrow],
                   w=[tmp])
            P.dve('tensor_tensor', out=acc[:], in0=acc[:], in1=tmp[:], op=ALU.add, r=[tmp, acc], w=[acc])
            P.pool('tensor_tensor', out=tmp[:], in0=xs[2][:], in1=wrow[:, 2, :], op=ALU.mult, r=[xs[2], wrow],
                   w=[tmp])
            P.dve('tensor_tensor', out=acc[:], in0=acc[:], in1=tmp[:], op=ALU.add, r=[tmp, acc], w=[acc])
            P.dve('tensor_tensor', out=acc[:], in0=acc[:], in1=brow[:], op=ALU.add, r=[brow, acc], w=[acc])
            P.dve('tensor_tensor', out=vv[:], in0=acc[:, 1024:1536], in1=acc[:, 512:1024], op=ALU.mult,
                  r=[acc], w=[vv])
            P.pool('tensor_copy', out=vvb[:, tt, :], in_=vv[:], r=[vv], wa=[vvb])
            P.act('activation', out=zt[:], in_=zt[:], func=AF.Silu, r=[zt], w=[zt])
            P.dve('tensor_tensor', out=g0[:], in0=acc[:, 0:512], in1=zt[:], op=ALU.mult, r=[acc, zt], w=[g0])
            P.pool('tensor_tensor', out=vs[:], in0=vv[:], in1=srow[:], op=ALU.mult, r=[vv, srow], w=[vs])
            P.dve('tensor_tensor', out=vs[:], in0=vs[:], in1=g0[:], op=ALU.mult, r=[vs, g0], w=[vs])
            P.dma('pool', g.hy_g0s[tok0 + tt * 128:tok0 + (tt + 1) * 128, :], g0[:], r=[g0], wa=['hy_g0s'])
            P.dma('pool', g.hy_vss[tok0 + tt * 128:tok0 + (tt + 1) * 128, :], vs[:], r=[vs], wa=['hy_vss'])
    yhat = S.sb("hy_yhat", [128, 2 * ntt, 512], BF16)
    with P.scope() as S2:
        fmt = [S2.sb("hy_fmt%d" % i, [128, ntt, 128], BF16) for i in range(4)]
        kk = [S2.sb("hy_kk%d" % i, [128, 512], F32) for i in range(4)]
        vh = [S2.sb("hy_vh%d" % i, [128, 512], F32) for i in range(4)]
        tq = [S2.sb("hy_tq%d" % i, [128, 512], F32) for i in range(4)]
        it = 0
        for i in range(ntt):
            i2 = (i % 2) * 2
            fre, fim, kre, kim, vre, vim = fmt[i2], fmt[i2 + 1], kk[i2], kk[i2 + 1], vh[i2], vh[i2 + 1]
            P.dma('sp', fre[:], g.hy_FmT[li][i], w=[fre])
            P.dma('sp', fim[:], g.hy_FmT[li][i + ntt], w=[fim])
            P.dma('sp', kre[:], g.khat[li][i * 128:(i + 1) * 128, :], r=[('khat', li)], w=[kre])
            P.dma('sp', kim[:], g.khat[li][(i + ntt) * 128:(i + ntt + 1) * 128, :], r=[('khat', li)], w=[kim])
            for (fm, vdst) in ((fre, vre), (fim, vim)):
                ps = g.psA[it % 4]
                it += 1
                for tt in range(ntt):
                    P.pe('matmul', out=ps[:, :], lhsT=fm[:, tt, :], rhs=vvb[:, tt, :], start=(tt == 0),
                         stop=(tt == ntt - 1), r=[fm, vvb], w=[ps])
                P.act('activation', out=vdst[:], in_=ps[:, :], func=AF.Copy, r=[ps], w=[vdst])
            t1, t2, t3, t4 = tq
            P.dve('tensor_tensor', out=t1[:], in0=vre[:], in1=kre[:], op=ALU.mult, r=[vre, kre], w=[t1])
            P.pool('tensor_tensor', out=t2[:], in0=vim[:], in1=kim[:], op=ALU.mult, r=[vim, kim], w=[t2])
            P.dve('tensor_tensor', out=t3[:], in0=vre[:], in1=kim[:], op=ALU.mult, r=[vre, kim], w=[t3])
            P.pool('tensor_tensor', out=t4[:], in0=vim[:], in1=kre[:], op=ALU.mult, r=[vim, kre], w=[t4])
            P.dve('tensor_tensor', out=yhat[:, i, :], in0=t1[:], in1=t2[:], op=ALU.subtract, r=[t1, t2], wa=[yhat])
            P.pool('tensor_tensor', out=yhat[:, i + ntt, :], in0=t3[:], in1=t4[:], op=ALU.add, r=[t3, t4], wa=[yhat])
            if i == 0:
                P.dve('tensor_tensor', out=yhat[0:1, 0, :], in0=vre[0:1, :], in1=kre[0:1, :], op=ALU.mult,
                      r=[vre, kre], w=[yhat])
                P.dve('tensor_tensor', out=yhat[0:1, ntt, :], in0=vim[0:1, :], in1=kim[0:1, :], op=ALU.mult,
                      r=[vim, kim], w=[yhat])
    yst = S.sb("hy_yst", [128, 4, L], BF16)
    with P.scope() as S3:
        gt = [S3.sb("hy_gt%d" % i, [128, 2 * ntt, 128], BF16) for i in range(2)]
        g0 = [S3.sb("hy_g0b%d" % i, [128, 512], F32) for i in range(2)]
        vs = [S3.sb("hy_vsb%d" % i, [128, 512], F32) for i in range(2)]
        o = [S3.sb("hy_o%d" % i, [128, 512], F32) for i in range(2)]
        for j in range(ntt):
            j2 = j % 2
            P.dma('sp', gt[j2][:], g.hy_GT[li][j], w=[gt[j2]])
            P.dma('sp', g0[j2][:], g.hy_g0s[tok0 + j * 128:tok0 + (j + 1) * 128, :], r=['hy_g0s'], w=[g0[j2]])
            P.dma('sp', vs[j2][:], g.hy_vss[tok0 + j * 128:tok0 + (j + 1) * 128, :], r=['hy_vss'], w=[vs[j2]])
            ps = g.psA[j2]
            for rt in range(2 * ntt):
                P.pe('matmul', out=ps[:, :], lhsT=gt[j2][:, rt, :], rhs=yhat[:, rt, :], start=(rt == 0),
                     stop=(rt == 2 * ntt - 1), r=[gt[j2], yhat], w=[ps])
            P.dve('tensor_tensor', out=o[j2][:], in0=ps[:, :], in1=g0[j2][:], op=ALU.mult, r=[ps, g0[j2]], w=[o[j2]])
            P.pool('tensor_tensor', out=o[j2][:], in0=o[j2][:], in1=vs[j2][:], op=ALU.add, r=[vs[j2], o[j2]],
                   w=[o[j2]])
            pt = g.psB[j2]
            for k in range(4):
                P.pe('transpose', out=pt[:, k * 128:(k + 1) * 128], in_=o[j2][:, k * 128:(k + 1) * 128],
                     identity=g.ident[:], r=[o[j2], g.ident], w=[pt])
            P.act('activation', out=yst[:, :, j * 128:(j + 1) * 128], in_=pt[:, :].rearrange("p (k t) -> p k t", k=4),
                  func=AF.Copy, r=[pt], wa=[yst])
        for k in range(4):
            P.dma('sp', g.yfm[b, 1536 + k * 128:1536 + (k + 1) * 128, tok0:tok0 + L], yst[:, k, :], r=[yst],
                  wa=[('yfm', b)])


def gla_consts():
    s = np.arange(128)[:, None]
    t = np.arange(128)[None, :]
    Lm = np.stack([(s <= t), (s >= t)]).astype(np.float32)
    pos = np.arange(TLAT)
    row = (pos // 64).astype(np.float32)
    col = (pos % 64).astype(np.float32)
    n = 16
    freqs = (np.float32(10000.0) ** (-np.arange(n, dtype=np.float32) / np.float32(n))).astype(np.float32)
    ang = np.concatenate([row[:, None] * freqs, col[:, None] * freqs], axis=-1).astype(np.float32)
    cos, sin = np.cos(ang).astype(np.float32), np.sin(ang).astype(np.float32)
    c64 = np.concatenate([cos, cos], axis=-1)
    s64 = np.concatenate([-sin, sin], axis=-1)
    return {
        "gla_Lm": Lm,
        "rope_c_tm": np.ascontiguousarray(c64.reshape(16, 128, 64).transpose(1, 0, 2)),
        "rope_s_tm": np.ascontiguousarray(s64.reshape(16, 128, 64).transpose(1, 0, 2)),
        "rope_c_fm": np.ascontiguousarray(np.concatenate([c64.T, c64.T], axis=0)),
        "rope_s_fm": np.ascontiguousarray(np.concatenate([s64.T, s64.T], axis=0)),
    }


def phase_gla(g, l, b):
    with g.P.scope() as S:
        _phase_gla(g, l, b, S)


def _phase_gla(g, l, b, S):
    P = g.P
    with_ctx = l < g.NL_total - 1
    oB = S.sb("gl_oB", [128, NTT, 512], F32)
    S0, S = S, Scope(P)
    cfm = S.sb("gl_cfm", [128, TLAT], F32)
    sfm = S.sb("gl_sfm", [128, TLAT], F32)
    ctm = S.sb("gl_ctm", [128, 16, 64], F32)
    stm = S.sb("gl_stm", [128, 16, 64], F32)
    Lm = S.sb("gl_Lm", [128, 2, 128], F32)
    wg2 = S.sb("gl_wg2", [16, 2, 128], F32)
    bg = S.sb("gl_bg", [1, 2, 128], F32)
    ones = S.sb("gl_ones", [1, 128], F32)
    Sp = [S.sb("gl_S%d" % i, [128, 128], F32) for i in range(2)]
    Sbf = [S.sb("gl_Sbf%d" % i, [128, 128], BF16) for i in range(2)]
    NR = 2
    fmq = [[S.sb("gl_fm%d_%d" % (k, r), [128, 128], F32) for k in range(8)] for r in range(NR)]
    lrt = [S.sb("gl_lrt%d" % r, [16, 128], F32) for r in range(NR)]
    ktm = [S.sb("gl_ktm%d" % r, [128, 512], F32) for r in range(NR)]
    vtm = [S.sb("gl_vtm%d" % r, [128, 512], F32) for r in range(NR)]
    vbf = [S.sb("gl_vbf%d" % r, [128, 512], BF16) for r in range(NR)]
    ee = [S.sb("gl_ee%d" % r, [128, 128], F32) for r in range(NR)]
    gdup = [S.sb("gl_gdup%d" % r, [128, 256], F32) for r in range(NR)]
    enb_tm = [S.sb("gl_enbtm%d" % r, [128, 256], F32) for r in range(NR)]
    eb_fm = [[S.sb("gl_ebfm%d_%d" % (i, r), [128, 128], F32) for i in range(2)] for r in range(NR)]
    enb_fm = [[S.sb("gl_enbfm%d_%d" % (i, r), [128, 128], F32) for i in range(2)] for r in range(NR)]
    t1 = [S.sb("gl_t1_%d" % r, [128, 256], F32) for r in range(NR)]
    t2 = [S.sb("gl_t2_%d" % r, [128, 256], F32) for r in range(NR)]
    qeT = [[S.sb("gl_qeT%d_%d" % (h, r), [128, 128], BF16) for h in range(4)] for r in range(NR)]
    keT = [[S.sb("gl_keT%d_%d" % (h, r), [128, 128], BF16) for h in range(4)] for r in range(NR)]
    for r in range(NR):
        for h in range(4):
            P.pool('memset', ap=qeT[r][h][:], constant=0.0, w=[qeT[r][h]])
            P.pool('memset', ap=keT[r][h][:], constant=0.0, w=[keT[r][h]])
    ke_tm = [S.sb("gl_ketm%d" % r, [128, 256], BF16) for r in range(NR)]
    attT = [S.sb("gl_attT%d" % r, [128, 512], BF16) for r in range(NR)]
    stmp = S.sb("gl_stmp", [128, 128], F32)
    P.dma('sp', cfm[:], g.rope_c_fm, w=[cfm])
    P.dma('sp', sfm[:], g.rope_s_fm, w=[sfm])
    P.dma('sp', ctm[:], g.rope_c_tm, w=[ctm])
    P.dma('sp', stm[:], g.rope_s_tm, w=[stm])
    P.dma('sp', Lm[:], g.gla_Lm.rearrange("d s t -> s d t"), w=[Lm])
    P.dma('sp', wg2[:], g.gla_wg2[l].rearrange("d k n -> k d n"), w=[wg2])
    P.dma('sp', bg[:], g.gla_bg_row[l], w=[bg])
    P.dve('memset', ap=ones[:], constant=1.0, w=[ones])
    pg, pbt, pbf, patt, po, pss = g.psA[0], g.psA[1], g.psA[2:4], g.psB[0], g.psB[1], g.psB[2:4]
    it = 0
    for d in range(2):
        order = list(range(NTT)) if d == 0 else [1, 0] + list(range(NTT - 1, 1, -1))
        tend = 127 if d == 0 else 0
        for i in range(2):
            P.dve('memset', ap=Sp[i][:], constant=0.0, w=[Sp[i]])
            P.pool('memset', ap=Sbf[i][:], constant=0.0, w=[Sbf[i]])
        for tt in order:
            r = it % NR
            it += 1
            tok = tt * 128
            lat = tt >= 2
            lt = tt - 2
            P.dma('sp', lrt[r][:], g.pfm[b, FM_GLG + 16 * d:FM_GLG + 16 * d + 16, tok:tok + 128], r=[('pfm', b)],
                  w=[lrt[r]])
            for k, f0 in enumerate((FM_GLQ, FM_GLQ + 128, FM_GLQS, FM_GLQS + 128, FM_GLK, FM_GLK + 128, FM_GLKS,
                                    FM_GLKS + 128)):
                if not lat and k in (2, 3, 6, 7):
                    continue
                P.dma('sp', fmq[r][k][:], g.pfm[b, f0:f0 + 128, tok:tok + 128], r=[('pfm', b)], w=[fmq[r][k]])
            P.dma('sp', ktm[r][:], g.ptm[b, prow(tt):prow(tt) + 128, TM_GLK:TM_GLK + 512], r=[('ptm', b)], w=[ktm[r]])
            P.dma('sp', vtm[r][:], g.ptm[b, prow(tt):prow(tt) + 128, TM_GLV:TM_GLV + 512], r=[('ptm', b)], w=[vtm[r]])
            P.pool('tensor_copy', out=vbf[r][:], in_=vtm[r][:], r=[vtm[r]], w=[vbf[r]])
            P.pe('matmul', out=pg[:, 0:128], lhsT=lrt[r][:], rhs=wg2[:, d, :], start=True, stop=False,
                 r=[lrt[r], wg2], w=[pg])
            P.pe('matmul', out=pg[:, 0:128], lhsT=ones[:], rhs=bg[:, d, :], start=False, stop=True,
                 r=[ones, bg], w=[pg])
            P.act('activation', out=ee[r][:], in_=pg[:, 0:128], func=AF.Exp, scale=-1.0, r=[pg], w=[ee[r]])
            P.act('activation', out=ee[r][:], in_=ee[r][:], func=AF.Ln, bias=1.0, r=[ee[r]], w=[ee[r]])
            gv = gdup[r][:].rearrange("p (h two c) -> p h two c", h=4, two=2)
            ev = ee[r][:].rearrange("p (h c) -> p h c", h=4)
            for two in range(2):
                P.dve('tensor_scalar', out=gv[:, :, two, :], in0=ev, scalar1=-1.0 / 16.0, scalar2=None, op0=ALU.mult,
                      r=[ee[r]], wa=[gdup[r]])
            P.pe('matmul', out=pbt[:, 0:256], lhsT=Lm[:, d, :], rhs=gdup[r][:], start=True, stop=True,
                 r=[Lm, gdup[r]], w=[pbt])
            P.act('activation', out=enb_tm[r][:], in_=pbt[:, 0:256], func=AF.Exp, scale=-1.0, r=[pbt], w=[enb_tm[r]])
            for i in range(2):
                P.pe('matmul', out=pbf[i][:, 0:128], lhsT=gdup[r][:, 128 * i:128 * (i + 1)], rhs=Lm[:, d, :],
                     start=True, stop=True, r=[Lm, gdup[r]], w=[pbf[i]])
                P.act('activation', out=eb_fm[r][i][:], in_=pbf[i][:, 0:128], func=AF.Exp, r=[pbf[i]],
                      w=[eb_fm[r][i]])
                P.act('activation', out=enb_fm[r][i][:], in_=pbf[i][:, 0:128], func=AF.Exp, scale=-1.0, r=[pbf[i]],
                      w=[enb_fm[r][i]])
            for i in range(2):
                qf, qs, kf, ks = fmq[r][i], fmq[r][2 + i], fmq[r][4 + i], fmq[r][6 + i]
                if lat:
                    cs = cfm[:, lt * 128:(lt + 1) * 128]
                    sn = sfm[:, lt * 128:(lt + 1) * 128]
                    P.dve('tensor_tensor', out=qf[:], in0=qf[:], in1=cs, op=ALU.mult, r=[qf, cfm], w=[qf])
                    P.pool('tensor_tensor', out=qs[:], in0=qs[:], in1=sn, op=ALU.mult, r=[qs, sfm], w=[qs])
                    P.dve('tensor_tensor', out=qf[:], in0=qf[:], in1=qs[:], op=ALU.add, r=[qf, qs], w=[qf])
                    P.dve('tensor_tensor', out=kf[:], in0=kf[:], in1=cs, op=ALU.mult, r=[kf, cfm], w=[kf])
                    P.pool('tensor_tensor', out=ks[:], in0=ks[:], in1=sn, op=ALU.mult, r=[ks, sfm], w=[ks])
                    P.dve('tensor_tensor', out=kf[:], in0=kf[:], in1=ks[:], op=ALU.add, r=[kf, ks], w=[kf])
                for hh in range(2):
                    h = 2 * i + hh
                    ps_ = slice(64 * hh, 64 * hh + 64)
                    P.dve('scalar_tensor_tensor', out=qeT[r][h][ps_, :], in0=qf[ps_, :], scalar=0.125,
                          in1=eb_fm[r][i][ps_, :], op0=ALU.mult, op1=ALU.mult, r=[qf, eb_fm[r][i]], w=[qeT[r][h]])
                    P.dve('tensor_tensor', out=keT[r][h][ps_, :], in0=kf[ps_, :], in1=enb_fm[r][i][ps_, :], op=ALU.mult,
                          r=[kf, enb_fm[r][i]], w=[keT[r][h]])
            if lat:
                for h in range(4):
                    P.dve('tensor_tensor', out=t1[r][:, 64 * h:64 * h + 64], in0=ktm[r][:, 64 * h:64 * h + 64],
                          in1=ctm[:, lt, :], op=ALU.mult, r=[ktm[r], ctm], wa=[t1[r]])
                    P.pool('tensor_tensor', out=t2[r][:, 64 * h:64 * h + 64], in0=ktm[r][:, 256 + 64 * h:256 + 64 * h + 64],
                           in1=stm[:, lt, :], op=ALU.mult, r=[ktm[r], stm], wa=[t2[r]])
                P.dve('tensor_tensor', out=t1[r][:], in0=t1[r][:], in1=t2[r][:], op=ALU.add, r=[t1[r], t2[r]], w=[t1[r]])
                ksrc, kkey = t1[r][:], t1[r]
            else:
                ksrc, kkey = ktm[r][:, 0:256], ktm[r]
            P.dve('tensor_tensor', out=ke_tm[r][:], in0=ksrc, in1=enb_tm[r][:], op=ALU.mult, r=[kkey, enb_tm[r]],
                  w=[ke_tm[r]])
            for h in range(4):
                i, hb = h // 2, 64 * (h % 2)
                P.pe('matmul', out=patt[:, 128 * h:128 * (h + 1)], lhsT=keT[r][h][:],
                     rhs=qeT[r][h][:], start=True, stop=True, r=[keT[r][h], qeT[r][h]], w=[patt])
            for h in range(4):
                P.dve('tensor_tensor', out=attT[r][:, 128 * h:128 * (h + 1)], in0=patt[:, 128 * h:128 * (h + 1)],
                      in1=Lm[:, d, :], op=ALU.mult, r=[patt, Lm], wa=[attT[r]])
            if lat or with_ctx:
                for h in range(4):
                    i, hb = h // 2, 64 * (h % 2)
                    P.pe('matmul', out=po[:, 128 * h:128 * (h + 1)], lhsT=attT[r][:, 128 * h:128 * (h + 1)],
                         rhs=vbf[r][:, 128 * h:128 * (h + 1)], start=True, stop=False, r=[attT[r], vbf[r]], w=[po])
                    P.pe('matmul', out=po[:, 128 * h:128 * (h + 1)], lhsT=qeT[r][h][:],
                         rhs=Sbf[i][:], start=False, stop=True, r=[qeT[r][h], Sbf[i]], w=[po])
                if d == 0:
                    P.act('activation', out=oB[:, tt, :], in_=po[:, :], func=AF.Copy, r=[po], w=[('oB', tt)])
                else:
                    P.dve('tensor_tensor', out=oB[:, tt, :], in0=po[:, :], in1=oB[:, tt, :], op=ALU.add,
                          r=[po, ('oB', tt)], w=[('oB', tt)])
            for i in range(2):
                P.pe('matmul', out=pss[i][:, 0:256], lhsT=ke_tm[r][:, 128 * i:128 * (i + 1)],
                     rhs=vbf[r][:, 256 * i:256 * (i + 1)], start=True, stop=True, r=[ke_tm[r], vbf[r]], w=[pss[i]])
                for hh in range(2):
                    ps_ = slice(64 * hh, 64 * hh + 64)
                    P.dve('tensor_tensor', out=stmp[ps_, :], in0=pss[i][ps_, 128 * hh:128 * (hh + 1)], in1=Sp[i][ps_, :],
                          op=ALU.add, r=[pss[i], Sp[i]], w=[(stmp.name, hh)])
                    P.dve('tensor_scalar', out=Sp[i][ps_, :], in0=stmp[ps_, :], scalar1=eb_fm[r][i][ps_, tend:tend + 1],
                          scalar2=None, op0=ALU.mult, r=[(stmp.name, hh), eb_fm[r][i]], wa=[Sp[i]])
                P.pool('tensor_copy', out=Sbf[i][:], in_=Sp[i][:], r=[Sp[i]], w=[Sbf[i]])
    S.__exit__(None, None, None)
    S = S0
    gn = S.sb("gl_gn", [128, 128], F32)
    zt = [S.sb("gl_zt%d" % i, [128, 512], F32) for i in range(2)]
    sq = [S.sb("gl_sq%d" % i, [128, 512], F32) for i in range(2)]
    ssq = [S.sb("gl_ssq%d" % i, [128, 4], F32) for i in range(2)]
    yst = S.sb("gl_yst", [128, 4, TT], BF16)
    P.dma('sp', gn[:], g.gla_norm_row[l].partition_broadcast(128), w=[gn])
    tt0 = 0 if with_ctx else 2
    for tt in range(tt0, NTT):
        r = tt % 2
        o = oB[:, tt, :]
        P.dma('sp', zt[r][:], g.ptm[b, prow(tt):prow(tt) + 128, TM_GLZ:TM_GLZ + 512], r=[('ptm', b)], w=[zt[r]])
        P.act('activation', out=zt[r][:], in_=zt[r][:], func=AF.Silu, r=[zt[r]], w=[zt[r]])
        P.pool('tensor_tensor', out=sq[r][:], in0=o, in1=o, op=ALU.mult, r=[('oB', tt)], w=[sq[r]])
        P.dve('tensor_reduce', out=ssq[r][:], in_=sq[r][:].rearrange("p (h c) -> p h c", h=4), axis=AX.X, op=ALU.add,
              r=[sq[r]], w=[ssq[r]])
        P.dve('tensor_scalar', out=ssq[r][:], in0=ssq[r][:], scalar1=1.0 / 128.0, scalar2=EPS, op0=ALU.mult,
              op1=ALU.add, r=[ssq[r]], w=[ssq[r]])
        P.act('activation', out=ssq[r][:], in_=ssq[r][:], func=AF.Sqrt, r=[ssq[r]], w=[ssq[r]])
        P.dve('reciprocal', out=ssq[r][:], in_=ssq[r][:], r=[ssq[r]], w=[ssq[r]])
        for h in range(4):
            P.dve('scalar_tensor_tensor', out=sq[r][:, 128 * h:128 * (h + 1)], in0=oB[:, tt, 128 * h:128 * (h + 1)],
                  scalar=ssq[r][:, h:h + 1], in1=gn[:], op0=ALU.mult, op1=ALU.mult, r=[('oB', tt), ssq[r], gn],
                  wa=[sq[r]])
        P.dve('tensor_tensor', out=sq[r][:], in0=sq[r][:], in1=zt[r][:], op=ALU.mult, r=[sq[r], zt[r]], w=[sq[r]])
        pt = g.psB[r]
        for k in range(4):
            P.pe('transpose', out=pt[:, k * 128:(k + 1) * 128], in_=sq[r][:, k * 128:(k + 1) * 128],
                 identity=g.ident[:], r=[sq[r], g.ident], w=[pt])
        P.act('activation', out=yst[:, :, tt * 128:(tt + 1) * 128], in_=pt[:, :].rearrange("p (k t) -> p k t", k=4),
              func=AF.Copy, r=[pt], wa=[yst])
    for k in range(4):
        if tt0 >= NTT:
            break
        P.dma('sp', g.yfm[b, 512 + k * 128:512 + (k + 1) * 128, tt0 * 128:TT], yst[:, k, tt0 * 128:TT], r=[yst],
              wa=[('yfm', b)])


GQW = 1536 + 16


def gdn_consts():
    s = np.arange(128)[:, None]
    t = np.arange(128)[None, :]
    big = np.float32(1e5)
    mbs = np.stack([np.where(t < s, 0.0, big), np.where(t > s, 0.0, big)]).astype(np.float32)
    mbi = np.stack([np.where(s <= t, 0.0, -big), np.where(s >= t, 0.0, -big)]).astype(np.float32)
    return {"gdn_mbs": np.ascontiguousarray(mbs), "gdn_mbi": np.ascontiguousarray(mbi),
            "ones128": np.ones((128, 128), np.float32)}


def phase_gdn_pre(g, l, b):
    with g.P.scope() as S:
        _phase_gdn_pre(g, l, b, S)


def _phase_gdn_pre(g, l, b, S):
    P = g.P
    wrow = S.sb("gd_wrow", [128, 5, 1536], F32)
    xs = [S.sb("gd_xs%d" % i, [128, 1536], F32) for i in range(5)]
    acc = S.sb("gd_acc", [128, GQW], F32)
    tmp = S.sb("gd_tmp", [128, 1536], F32)
    tmp2 = S.sb("gd_tmp2", [128, 1536], F32)
    ab = S.sb("gd_ab", [128, 16], F32)
    dtb = S.sb("gd_dtb", [128, 8], F32)
    eal = S.sb("gd_eal", [128, 8], F32)
    ssq = S.sb("gd_ssq", [128, 8], F32)
    sp = S.sb("gd_sp", [128, 8], F32)
    for k in range(5):
        P.dma('sp', wrow[:, k, :], g.gdn_conv_row[l, k:k + 1, :].partition_broadcast(128), wa=[wrow])
    P.dma('sp', dtb[:], g.gdn_dtb_row[l].partition_broadcast(128), w=[dtb])
    P.dma('sp', eal[:], g.gdn_alog_row[l].partition_broadcast(128), w=[eal])
    P.act('activation', out=eal[:], in_=eal[:], func=AF.Exp, r=[eal], w=[eal])
    for tt in range(NTT):
        base = prow(tt)
        for k in range(5):
            P.dma('sp', xs[k][:], g.ptm[b, base + k - 2:base + k - 2 + 128, TM_GDQKV:TM_GDQKV + 1536],
                  r=[('ptm', b)], w=[xs[k]])
        P.dma('sp', ab[:], g.ptm[b, base:base + 128, TM_GDAB:TM_GDAB + 16], r=[('ptm', b)], w=[ab])
        A = acc[:, 0:1536]
        P.dve('tensor_tensor', out=A, in0=xs[0][:], in1=wrow[:, 0, :], op=ALU.mult, r=[xs[0], wrow], w=[acc])
        for k in range(1, 5):
            eng = 'pool' if k % 2 == 1 else 'dve'
            tk = tmp if k % 2 == 1 else tmp2
            P.op(eng, 'tensor_tensor', out=tk[:], in0=xs[k][:], in1=wrow[:, k, :], op=ALU.mult, r=[xs[k], wrow],
                 w=[tk])
            P.dve('tensor_tensor', out=A, in0=A, in1=tk[:], op=ALU.add, r=[tk, acc], w=[acc])
        P.act('activation', out=A, in_=A, func=AF.Silu, r=[acc], w=[acc])
        P.pool('tensor_tensor', out=tmp[:, 0:1024], in0=acc[:, 0:1024], in1=acc[:, 0:1024], op=ALU.mult, r=[acc],
               w=[tmp])
        P.dve('tensor_reduce', out=ssq[:], in_=tmp[:, 0:1024].rearrange("p (h c) -> p h c", h=8), axis=AX.X,
              op=ALU.add, r=[tmp], w=[ssq])
        P.dve('tensor_scalar', out=ssq[:], in0=ssq[:], scalar1=EPS, scalar2=None, op0=ALU.add, r=[ssq], w=[ssq])
        P.act('activation', out=ssq[:], in_=ssq[:], func=AF.Sqrt, r=[ssq], w=[ssq])
        P.dve('reciprocal', out=ssq[:], in_=ssq[:], r=[ssq], w=[ssq])
        P.dve('tensor_scalar', out=ssq[:, 0:4], in0=ssq[:, 0:4], scalar1=128.0 ** -0.5, scalar2=None, op0=ALU.mult,
              r=[ssq], w=[ssq])
        for h8 in range(8):
            eng = 'dve' if h8 % 2 == 0 else 'pool'
            P.op(eng, 'tensor_scalar', out=acc[:, 128 * h8:128 * (h8 + 1)], in0=acc[:, 128 * h8:128 * (h8 + 1)],
                 scalar1=ssq[:, h8:h8 + 1], scalar2=None, op0=ALU.mult, r=[acc, ssq], w=[acc])
        P.act('activation', out=acc[:, 1536:1544], in_=ab[:, 8:16], func=AF.Sigmoid, r=[ab], w=[acc])
        P.dve('tensor_tensor', out=sp[:], in0=ab[:, 0:8], in1=dtb[:], op=ALU.add, r=[ab, dtb], w=[sp])
        P.act('activation', out=sp[:], in_=sp[:], func=AF.Exp, r=[sp], w=[sp])
        P.act('activation', out=sp[:], in_=sp[:], func=AF.Ln, bias=1.0, r=[sp], w=[sp])
        P.dve('scalar_tensor_tensor', out=acc[:, 1544:1552], in0=sp[:], scalar=-1.0, in1=eal[:], op0=ALU.mult,
              op1=ALU.mult, r=[sp, eal, acc], w=[acc])
        P.dma('sp', g.gq[tt * 128:(tt + 1) * 128, :], acc[:], r=[acc], wa=['gq'])


def phase_gdn(g, l, b):
    with g.P.scope() as S:
        _phase_gdn(g, l, b, S)


def _phase_gdn(g, l, b, S):
    P = g.P
    with_ctx = l < g.NL_total - 1
    oC = S.sb("gd_oC", [128, NTT, 512], F32)
    S0, S = S, Scope(P)
    Lm = S.sb("gd_Lm", [128, 2, 128], F32)
    mbs = S.sb("gd_mbs", [128, 2, 128], F32)
    mbi = S.sb("gd_mbi", [128, 2, 128], F32)
    ones = S.sb("gd_ones", [128, 128], F32)
    St = [S.sb("gd_S%d" % h, [128, 128], F32) for h in range(4)]
    qkv = [S.sb("gd_qkv%d" % i, [128, GQW], F32) for i in range(2)]
    sm = lambda n, w: S.sb("gd_" + n, [128, w], F32)
    gam, egam, bexp, nbeta, gend, kdsc, dend = (sm("gam", 4), sm("egam", 4), sm("bexp", 4), sm("nbeta", 4),
                                                sm("gend", 4), sm("kdsc", 4), sm("dend", 4))
    Lg = [sm("Lg%d" % i, 128) for i in range(4)]
    def smb(n, w):
        solve = n[:2] in ("PQ", "Y0", "Y1", "Y2", "Y3", "RH")
        return S.sb("gd_" + n, [128, w], F32 if solve else BF16)
    KQ = [smb("KQ%d" % h, 256) for h in range(4)]
    Nf = [sm("Nf%d" % i, 128) for i in range(4)]
    Sbf = [smb("Sbf%d" % h, 128) for h in range(4)]
    xx, EE, x2, E2 = ([sm("xx%d" % i, 128) for i in range(4)], [sm("EE%d" % i, 128) for i in range(4)],
                      [sm("x2%d" % i, 128) for i in range(4)], [sm("E2%d" % i, 128) for i in range(4)])
    aqk = [smb("aqk%d" % h, 128) for h in range(4)]
    PQh = [[smb("PQ%d_%d" % (h, i), 256) for i in range(2)] for h in range(4)]
    Yh = [[smb("Y%d_%d" % (h, i), 128) for i in range(2)] for h in range(4)]
    RHSu = [smb("RHSu%d" % h, 128) for h in range(4)]
    RHSw = [smb("RHSw%d" % h, 128) for h in range(4)]
    kdh = [smb("kd%d" % h, 128) for h in range(4)]
    wTn = [smb("wTn%d" % h, 128) for h in range(4)]
    esb = [smb("esb%d" % h, 128) for h in range(4)]
    o1 = [sm("o1%d" % h, 128) for h in range(4)]
    P.dma('sp', Lm[:], g.gla_Lm.rearrange("d s t -> s d t"), w=[Lm])
    P.dma('sp', mbs[:], g.gdn_mbs.rearrange("d s t -> s d t"), w=[mbs])
    P.dma('sp', mbi[:], g.gdn_mbi.rearrange("d s t -> s d t"), w=[mbi])
    P.dma('sp', ones[:], g.ones128, w=[ones])
    pA0, pG = g.psA[0], g.psA[1]
    it = 0
    for d in range(2):
        order = list(range(NTT)) if d == 0 else [1, 0] + list(range(NTT - 1, 1, -1))
        send = 127 if d == 0 else 0
        for h in range(4):
            P.dve('memset', ap=St[h][:], constant=0.0, w=[St[h]])
            P.pool('memset', ap=Sbf[h][:], constant=0.0, w=[Sbf[h]])
        for tt in order:
            r = it % 2
            it += 1
            X = qkv[r]
            P.dma('sp', X[:], g.gq[tt * 128:(tt + 1) * 128, :], r=['gq'], w=[X])
            bet = X[:, 1536 + 4 * d:1536 + 4 * d + 4]
            gg = X[:, 1544 + 4 * d:1544 + 4 * d + 4]
            P.pe('matmul', out=pA0[:, 0:4], lhsT=Lm[:, d, :], rhs=gg, start=True, stop=True, r=[Lm, X], w=[pA0])
            P.act('activation', out=gam[:], in_=pA0[:, 0:4], func=AF.Copy, r=[pA0], w=[gam])
            P.act('activation', out=egam[:], in_=pA0[:, 0:4], func=AF.Exp, r=[pA0], w=[egam])
            P.dve('tensor_tensor', out=bexp[:], in0=egam[:], in1=bet, op=ALU.mult, r=[egam, X], w=[bexp])
            P.dve('tensor_scalar', out=nbeta[:], in0=bet, scalar1=-1.0, scalar2=None, op0=ALU.mult, r=[X], w=[nbeta])
            for h in range(4):
                P.dve('tensor_scalar', out=Lg[h][:], in0=Lm[:, d, :], scalar1=gg[:, h:h + 1], scalar2=None,
                      op0=ALU.mult, r=[Lm, X], w=[Lg[h]])
            for h in range(4):
                P.pe('matmul', out=pG[:, 128 * h:128 * (h + 1)], lhsT=ones[:], rhs=Lg[h][:], start=True, stop=True,
                     r=[ones, Lg[h]], w=[pG])
            gendv = pG[:, :].rearrange("p (h s) -> p h s", h=4)[:, :, send]
            P.act('activation', out=gend[:], in_=gendv, func=AF.Copy, r=[pG], w=[gend])
            P.act('activation', out=dend[:], in_=gend[:], func=AF.Exp, r=[gend], w=[dend])
            P.dve('tensor_tensor', out=kdsc[:], in0=gend[:], in1=gam[:], op=ALU.subtract, r=[gend, gam], w=[kdsc])
            P.act('activation', out=kdsc[:], in_=kdsc[:], func=AF.Exp, r=[kdsc], w=[kdsc])
            qs_ = lambda h: X[:, 128 * h:128 * (h + 1)]
            ks_ = lambda h: X[:, 512 + 128 * h:512 + 128 * (h + 1)]
            vs_ = lambda h: X[:, 1024 + 128 * h:1024 + 128 * (h + 1)]
            H4 = range(4)
            for h in H4:
                pb = g.psB[h]
                P.pe('transpose', out=pb[:, 0:128], in_=ks_(h), identity=g.ident[:], r=[X, g.ident], w=[pb])
                P.pe('transpose', out=pb[:, 128:256], in_=qs_(h), identity=g.ident[:], r=[X, g.ident], w=[pb])
            for h in H4:
                P.act('activation', out=KQ[h][:], in_=g.psB[h][:, 0:256], func=AF.Copy, r=[g.psB[h]], w=[KQ[h]])
            for h in H4:
                pb = g.psB[h]
                kTh, qTh = KQ[h][:, 0:128], KQ[h][:, 128:256]
                P.pe('matmul', out=pb[:, 256:384], lhsT=kTh, rhs=kTh, start=True, stop=True, r=[KQ[h]], w=[pb])
                P.pe('matmul', out=pb[:, 384:512], lhsT=kTh, rhs=qTh, start=True, stop=True, r=[KQ[h]], w=[pb])
            for h in H4:
                Gam = pG[:, 128 * h:128 * (h + 1)]
                P.dve('scalar_tensor_tensor', out=xx[h][:], in0=Gam, scalar=gam[:, h:h + 1], in1=mbs[:, d, :],
                      op0=ALU.subtract, op1=ALU.max, r=[pG, gam, mbs], w=[xx[h]])
                P.dve('scalar_tensor_tensor', out=x2[h][:], in0=Gam, scalar=gam[:, h:h + 1], in1=mbi[:, d, :],
                      op0=ALU.subtract, op1=ALU.min, r=[pG, gam, mbi], w=[x2[h]])
            for h in H4:
                P.act('activation', out=EE[h][:], in_=xx[h][:], func=AF.Exp, scale=-1.0, r=[xx[h]], w=[EE[h]])
                P.act('activation', out=E2[h][:], in_=x2[h][:], func=AF.Exp, r=[x2[h]], w=[E2[h]])
            for h in H4:
                pb = g.psB[h]
                P.dve('scalar_tensor_tensor', out=Nf[h][:], in0=pb[:, 256:384], scalar=nbeta[:, h:h + 1],
                      in1=EE[h][:], op0=ALU.mult, op1=ALU.mult, r=[pb, nbeta, EE[h]], w=[Nf[h]])
                P.dve('tensor_tensor', out=aqk[h][:], in0=pb[:, 384:512], in1=E2[h][:], op=ALU.mult,
                      r=[pb, E2[h]], w=[aqk[h]])
            for h in H4:
                P.pool('tensor_copy', out=PQh[h][0][:, 0:128], in_=Nf[h][:], r=[Nf[h]], w=[PQh[h][0]])
                P.pe('transpose', out=g.psB[h][:, 128:256], in_=Nf[h][:], identity=g.ident[:],
                     r=[Nf[h], g.ident], w=[g.psB[h]])
            for h in range(4):
                P.act('activation', out=PQh[h][0][:, 128:256], in_=g.psB[h][:, 128:256], func=AF.Copy,
                      r=[g.psB[h]], w=[PQh[h][0]])
            for h in range(4):
                P.op('dve' if h % 2 == 0 else 'pool', 'tensor_tensor', out=Yh[h][0][:], in0=PQh[h][0][:, 128:256],
                     in1=g.ident[:], op=ALU.add, r=[PQh[h][0], g.ident], w=[Yh[h][0]])
            yi = 0
            for j in range(6):
                for h in range(4):
                    cur = PQh[h][j % 2]
                    P.pe('matmul', out=g.psB[h][:, 0:128], lhsT=cur[:, 128:256], rhs=cur[:, 0:128], start=True,
                         stop=True, r=[cur], w=[g.psB[h]])
                    if j < 5:
                        P.pe('matmul', out=g.psB[h][:, 128:256], lhsT=cur[:, 0:128], rhs=cur[:, 128:256], start=True,
                             stop=True, r=[cur], w=[g.psB[h]])
                for h in range(4):
                    nxt = PQh[h][(j + 1) % 2]
                    wid = 256 if j < 5 else 128
                    P.act('activation', out=nxt[:, 0:wid], in_=g.psB[h][:, 0:wid], func=AF.Copy, r=[g.psB[h]],
                          w=[nxt])
                for h in range(4):
                    nxt = PQh[h][(j + 1) % 2]
                    P.pe('matmul', out=g.psA[h][:, 0:128], lhsT=nxt[:, 0:128], rhs=Yh[h][yi][:], start=True, stop=True,
                         r=[nxt, Yh[h][yi]], w=[g.psA[h]])
                for h in range(4):
                    P.dve('tensor_tensor', out=Yh[h][1 - yi][:], in0=g.psA[h][:, 0:128], in1=Yh[h][yi][:], op=ALU.add,
                          r=[g.psA[h], Yh[h][yi]], w=[Yh[h][1 - yi]])
                yi = 1 - yi
            need_o = (tt >= 2 or with_ctx)
            for h in range(4):
                P.dve('tensor_scalar', out=RHSu[h][:], in0=vs_(h), scalar1=bet[:, h:h + 1], scalar2=None,
                      op0=ALU.mult, r=[X], w=[RHSu[h]])
                P.pool('tensor_scalar', out=RHSw[h][:], in0=ks_(h), scalar1=bexp[:, h:h + 1], scalar2=None,
                       op0=ALU.mult, r=[X, bexp], w=[RHSw[h]])
                P.pool('tensor_scalar', out=kdh[h][:], in0=ks_(h), scalar1=kdsc[:, h:h + 1], scalar2=None,
                       op0=ALU.mult, r=[X, kdsc], w=[kdh[h]])
            for h in range(4):
                P.pe('matmul', out=g.psA[h][:, 128:256], lhsT=RHSw[h][:], rhs=Yh[h][yi][:], start=True, stop=True,
                     r=[RHSw[h], Yh[h][yi]], w=[g.psA[h]])
            for h in range(4):
                P.act('activation', out=wTn[h][:], in_=g.psA[h][:, 128:256], func=AF.Copy, scale=-1.0,
                      r=[g.psA[h]], w=[wTn[h]])
            for h in range(4):
                P.pe('matmul', out=g.psA[h][:, 0:128], lhsT=Yh[h][yi][:], rhs=RHSu[h][:], start=True, stop=False,
                     r=[Yh[h][yi], RHSu[h]], w=[g.psA[h]])
                P.pe('matmul', out=g.psA[h][:, 0:128], lhsT=wTn[h][:], rhs=Sbf[h][:], start=False, stop=True,
                     r=[wTn[h], Sbf[h]], w=[g.psA[h]])
            for h in range(4):
                P.act('activation', out=esb[h][:], in_=g.psA[h][:, 0:128], func=AF.Copy, r=[g.psA[h]], w=[esb[h]])
            for h in range(4):
                P.pe('matmul', out=g.psB[h][:, 256:384], lhsT=kdh[h][:], rhs=esb[h][:], start=True, stop=True,
                     r=[kdh[h], esb[h]], w=[g.psB[h]])
                if need_o:
                    P.pe('matmul', out=g.psB[h][:, 0:128], lhsT=KQ[h][:, 128:256], rhs=Sbf[h][:], start=True, stop=True,
                         r=[KQ[h], Sbf[h]], w=[g.psB[h]])
                    P.pe('matmul', out=g.psB[h][:, 128:256], lhsT=aqk[h][:], rhs=esb[h][:], start=True, stop=True,
                         r=[aqk[h], esb[h]], w=[g.psB[h]])
            if need_o:
                for h in range(4):
                    P.act('activation', out=o1[h][:], in_=g.psB[h][:, 0:128], func=AF.Copy, scale=egam[:, h:h + 1],
                          r=[g.psB[h], egam], w=[o1[h]])
                for h in range(4):
                    oslc = oC[:, tt, 128 * h:128 * (h + 1)]
                    if d == 0:
                        P.dve('tensor_tensor', out=oslc, in0=g.psB[h][:, 128:256], in1=o1[h][:], op=ALU.add,
                              r=[g.psB[h], o1[h]], w=[('oC', tt, h)])
                    else:
                        P.dve('tensor_tensor', out=o1[h][:], in0=g.psB[h][:, 128:256], in1=o1[h][:], op=ALU.add,
                              r=[g.psB[h], o1[h]], w=[o1[h]])
                        P.pool('tensor_tensor', out=oslc, in0=oslc, in1=o1[h][:], op=ALU.add,
                               r=[o1[h], ('oC', tt, h)], w=[('oC', tt, h)])
            for h in range(4):
                P.dve('scalar_tensor_tensor', out=St[h][:], in0=St[h][:], scalar=dend[:, h:h + 1],
                      in1=g.psB[h][:, 256:384], op0=ALU.mult, op1=ALU.add, r=[St[h], dend, g.psB[h]], w=[St[h]])
                P.pool('tensor_copy', out=Sbf[h][:], in_=St[h][:], r=[St[h]], w=[Sbf[h]])
    S.__exit__(None, None, None)
    S = S0
    okeys = lambda tt: [('oC', tt, h) for h in range(4)]
    branch_finish(g, l, b, S, oC, okeys, g.gdn_norm_row[l], TM_GDZ, 1024, with_ctx)


def branch_finish(g, l, b, S, oB, okeys, norm_row, zcol, yrow0, with_ctx):
    P = g.P
    gn = S.sb("bf_gn", [128, 128], F32)
    zt = [S.sb("bf_zt%d" % i, [128, 512], F32) for i in range(2)]
    sq = [S.sb("bf_sq%d" % i, [128, 512], F32) for i in range(2)]
    ssq = [S.sb("bf_ssq%d" % i, [128, 4], F32) for i in range(2)]
    yst = S.sb("bf_yst", [128, 4, TT], BF16)
    P.dma('sp', gn[:], norm_row.partition_broadcast(128), w=[gn])
    tt0 = 0 if with_ctx else 2
    for tt in range(tt0, NTT):
        r = tt % 2
        o = oB[:, tt, :]
        P.dma('sp', zt[r][:], g.ptm[b, prow(tt):prow(tt) + 128, zcol:zcol + 512], r=[('ptm', b)], w=[zt[r]])
        P.act('activation', out=zt[r][:], in_=zt[r][:], func=AF.Silu, r=[zt[r]], w=[zt[r]])
        P.pool('tensor_tensor', out=sq[r][:], in0=o, in1=o, op=ALU.mult, r=okeys(tt), w=[sq[r]])
        P.dve('tensor_reduce', out=ssq[r][:], in_=sq[r][:].rearrange("p (h c) -> p h c", h=4), axis=AX.X, op=ALU.add,
              r=[sq[r]], w=[ssq[r]])
        P.dve('tensor_scalar', out=ssq[r][:], in0=ssq[r][:], scalar1=1.0 / 128.0, scalar2=EPS, op0=ALU.mult,
              op1=ALU.add, r=[ssq[r]], w=[ssq[r]])
        P.act('activation', out=ssq[r][:], in_=ssq[r][:], func=AF.Sqrt, r=[ssq[r]], w=[ssq[r]])
        P.dve('reciprocal', out=ssq[r][:], in_=ssq[r][:], r=[ssq[r]], w=[ssq[r]])
        for h in range(4):
            P.dve('scalar_tensor_tensor', out=sq[r][:, 128 * h:128 * (h + 1)], in0=oB[:, tt, 128 * h:128 * (h + 1)],
                  scalar=ssq[r][:, h:h + 1], in1=gn[:], op0=ALU.mult, op1=ALU.mult, r=okeys(tt) + [ssq[r], gn],
                  wa=[sq[r]])
        P.dve('tensor_tensor', out=sq[r][:], in0=sq[r][:], in1=zt[r][:], op=ALU.mult, r=[sq[r], zt[r]], w=[sq[r]])
        pt = g.psB[r]
        for k in range(4):
            P.pe('transpose', out=pt[:, k * 128:(k + 1) * 128], in_=sq[r][:, k * 128:(k + 1) * 128],
                 identity=g.ident[:], r=[sq[r], g.ident], w=[pt])
        P.act('activation', out=yst[:, :, tt * 128:(tt + 1) * 128], in_=pt[:, :].rearrange("p (k t) -> p k t", k=4),
              func=AF.Copy, r=[pt], wa=[yst])
    for k in range(4):
        P.dma('sp', g.yfm[b, yrow0 + k * 128:yrow0 + (k + 1) * 128, tt0 * 128:TT], yst[:, k, tt0 * 128:TT], r=[yst],
              wa=[('yfm', b)])


def phase_merge(g, l, b):
    with g.P.scope() as S:
        _phase_merge(g, l, b, S)


def _phase_merge(g, l, b, S):
    P = g.P
    with_ctx = l < g.NL_total - 1
    last = not with_ctx
    if with_ctx:
        halves = [(0, 1280), (1280, 1024)]
    else:
        halves = [(256, 1024), (1280, 1024)]
    ss = S.sb("mg_ss", [128, NTT, 4], F32)
    bgc = S.sb("mg_bgc", [128, 4, 16], F32)
    mT = S.sb("mg_mT", [128, 16, 1280], BF16)
    P.dma('sp', bgc[:], g.b_gate_col[l], w=[bgc])
    for (t0, n) in halves:
        chunks = [(c0, min(512, n - c0)) for c0 in range(0, n, 512)]
        nch = len(chunks)
        with P.scope() as SA:
            yT = SA.sb("mg_yT", [128, 16, 1280], BF16)
            wgs = [SA.sb("mg_wgs%d" % i, [128, 16, 128], F32) for i in range(2)]
            wgb = [SA.sb("mg_wgb%d" % i, [128, 16, 128], BF16) for i in range(2)]
            wbs = [SA.sb("mg_wbs%d" % i, [128, 4, 128], F32) for i in range(2)]
            wbb = [SA.sb("mg_wbb%d" % i, [128, 4, 128], BF16) for i in range(2)]
            acc = SA.sb("mg_acc", [128, 1280], F32)
            sig = [SA.sb("mg_sig%d" % i, [128, 512], F32) for i in range(2)]
            tmp = [SA.sb("mg_tmp%d" % i, [128, 512], F32) for i in range(2)]
            for kt in range(16):
                P.dma('sp', yT[:, kt, 0:n], g.yfm[b, kt * 128:(kt + 1) * 128, t0:t0 + n], r=[('yfm', b)], wa=[yT])
            it = 0
            for ft in range(16):
                for i in range(4):
                    w2 = (ft * 4 + i) % 2
                    P.dma('sp', wgs[w2][:], g.w_gate_t[l, i, ft], w=[wgs[w2]])
                    P.act('activation', out=wgb[w2][:], in_=wgs[w2][:], func=AF.Copy, r=[wgs[w2]], w=[wgb[w2]])
                    P.dma('sp', wbs[w2][:], g.w_branch_t[l, i, ft], w=[wbs[w2]])
                    P.act('activation', out=wbb[w2][:], in_=wbs[w2][:], func=AF.Copy, r=[wbs[w2]], w=[wbb[w2]])
                    for c, (c0, cw) in enumerate(chunks):
                        pg_, pb_ = g.psA[it % 4], g.psB[it % 4]
                        i2 = it % 2
                        it += 1
                        for kt in range(16):
                            P.pe('matmul', out=pg_[:, 0:cw], lhsT=wgb[w2][:, kt, :],
                                 rhs=g.uT[:, kt, t0 + c0:t0 + c0 + cw], start=(kt == 0), stop=(kt == 15),
                                 r=[wgb[w2], g.uT], w=[pg_])
                        for kt in range(4):
                            P.pe('matmul', out=pb_[:, 0:cw], lhsT=wbb[w2][:, kt, :],
                                 rhs=yT[:, 4 * i + kt, c0:c0 + cw], start=(kt == 0), stop=(kt == 3),
                                 r=[wbb[w2], yT], w=[pb_])
                        P.act('activation', out=sig[i2][:, 0:cw], in_=pg_[:, 0:cw], func=AF.Sigmoid,
                              bias=bgc[:, i, ft:ft + 1], r=[pg_, bgc], w=[sig[i2]])
                        asl = acc[:, c0:c0 + cw]
                        if i == 0:
                            P.dve('tensor_tensor', out=asl, in0=pb_[:, 0:cw], in1=sig[i2][:, 0:cw], op=ALU.mult,
                                  r=[pb_, sig[i2]], w=[('mg_acc', c)])
                        else:
                            P.dve('tensor_tensor', out=tmp[i2][:, 0:cw], in0=pb_[:, 0:cw], in1=sig[i2][:, 0:cw],
                                  op=ALU.mult, r=[pb_, sig[i2]], w=[tmp[i2]])
                            P.pool('tensor_tensor', out=asl, in0=asl, in1=tmp[i2][:, 0:cw], op=ALU.add,
                                   r=[tmp[i2], ('mg_acc', c)], w=[('mg_acc', c)])
                P.act('activation', out=mT[:, ft, 0:n], in_=acc[:, 0:n], func=AF.Copy,
                      r=[('mg_acc', c) for c in range(nch)], wa=[mT])
        with P.scope() as SB:
            wos = [SB.sb("mg_wos%d" % i, [128, 16, 512], F32) for i in range(1)]
            wob = [SB.sb("mg_wob%d" % i, [128, 16, 512], BF16) for i in range(2)]
            yst = [SB.sb("mg_yst%d" % i, [128, 512], F32) for i in range(3)]
            junk = SB.sb("mg_junk", [128, 512], F32)
            it = 0
            for cc in range(4):
                P.dma('sp', wos[0][:], g.w_out[l, :, cc * 512:(cc + 1) * 512].rearrange("(kt p) f -> p kt f", p=128),
                      w=[wos[0]])
                for q4 in range(4):
                    if q4 % 2 == 0:
                        P.dve('tensor_copy', out=wob[cc % 2][:, 4 * q4:4 * q4 + 4, :],
                              in_=wos[0][:, 4 * q4:4 * q4 + 4, :], r=[wos[0]], w=[(wob[cc % 2].name, q4)])
                    else:
                        P.act('activation', out=wob[cc % 2][:, 4 * q4:4 * q4 + 4, :],
                              in_=wos[0][:, 4 * q4:4 * q4 + 4, :], func=AF.Copy, r=[wos[0]],
                              w=[(wob[cc % 2].name, q4)])
                for ti in range(n // 128):
                    tt = (t0 + ti * 128) // 128
                    ps = g.psA[it % 4]
                    st = yst[it % 3]
                    it += 1
                    for kt in range(16):
                        P.pe('matmul', out=ps[:, :], lhsT=mT[:, kt, ti * 128:(ti + 1) * 128], rhs=wob[cc % 2][:, kt, :],
                             start=(kt == 0), stop=(kt == 15), r=[mT, (wob[cc % 2].name, kt // 4)], w=[ps])
                    P.act('activation', out=st[:], in_=ps[:, :], func=AF.Copy, r=[ps], w=[st])
                    P.dve('tensor_tensor', out=junk[:], in0=st[:], in1=st[:], op=ALU.mult, r=[st], w=[junk])
                    P.dve('tensor_reduce', out=ss[:, tt, cc:cc + 1], in_=junk[:], axis=AX.X, op=ALU.add, r=[junk],
                          wa=[('mg_ss', tt)])
                    P.dma('pool', g.ybuf[tt * 128:(tt + 1) * 128, cc * 512:(cc + 1) * 512], st[:], r=[st],
                          wa=['ybuf'])
    with P.scope() as SC:
        GG = SC.sb("mg_GG", [128, 2, D], F32)
        yt = [SC.sb("mg_yt%d" % i, [128, D], F32) for i in range(2)]
        ht = [SC.sb("mg_ht%d" % i, [128, D], F32) for i in range(2)]
        rs = [SC.sb("mg_rs%d" % i, [128, 2], F32) for i in range(2)]
        for vi, j in enumerate((b, 2)):
            for jc in range(4):
                pb = g.psA[(vi * 4 + jc) % 4]
                P.pe('matmul', out=pb[:, :], lhsT=g.sel[:, j, :], rhs=g.grow[:, jc * 512:(jc + 1) * 512],
                     start=True, stop=True, r=[g.sel, g.grow], w=[pb])
                P.act('activation', out=GG[:, vi, jc * 512:(jc + 1) * 512], in_=pb[:, :], func=AF.Copy,
                      r=[pb], wa=[GG])
        src = g.xin if l == 0 else g.hbuf
        srckey = [] if l == 0 else [('hbuf', b)]
        tt0 = 0 if with_ctx else 2
        for tt in range(tt0, NTT):
            r = tt % 2
            vi = 1 if tt < 2 else 0
            P.dma('sp', yt[r][:], g.ybuf[tt * 128:(tt + 1) * 128, :], r=['ybuf'], w=[yt[r]])
            P.dma('sp', ht[r][:], src[b, tt * 128:(tt + 1) * 128, :], r=srckey, w=[ht[r]])
            P.dve('tensor_reduce', out=rs[r][:, 0:1], in_=ss[:, tt, :], axis=AX.X, op=ALU.add, r=[('mg_ss', tt)],
                  w=[rs[r]])
            P.dve('tensor_scalar', out=rs[r][:, 1:2], in0=rs[r][:, 0:1], scalar1=1.0 / D, scalar2=EPS, op0=ALU.mult,
                  op1=ALU.add, r=[rs[r]], w=[rs[r]])
            P.act('activation', out=rs[r][:, 1:2], in_=rs[r][:, 1:2], func=AF.Sqrt, r=[rs[r]], w=[rs[r]])
            P.dve('reciprocal', out=rs[r][:, 1:2], in_=rs[r][:, 1:2], r=[rs[r]], w=[rs[r]])
            P.dve('scalar_tensor_tensor', out=yt[r][:], in0=yt[r][:], scalar=rs[r][:, 1:2], in1=GG[:, vi, :],
                  op0=ALU.mult, op1=ALU.mult, r=[yt[r], rs[r], GG], w=[yt[r]])
            P.pool('tensor_tensor', out=ht[r][:], in0=ht[r][:], in1=yt[r][:], op=ALU.add, r=[yt[r], ht[r]], w=[ht[r]])
            if last:
                P.dma('sp', g.out[b, (tt - 2) * 128:(tt - 1) * 128, :], ht[r][:], r=[ht[r]], wa=['out'])
            else:
                P.dma('sp', g.hbuf[b, tt * 128:(tt + 1) * 128, :], ht[r][:], r=[ht[r]], wa=[('hbuf', b)])


N_CORES = 8
_PROGRAM = {}


def kernel(**inputs):
    sh = prep_shared(inputs)
    if "nc" not in _PROGRAM:
        _PROGRAM["nc"] = build_program(NB=2, NL=2)[0]
    nc = _PROGRAM["nc"]
    in_maps = []
    for c in range(N_CORES):
        m = dict(sh)
        m.update(prep_core(inputs, [2 * c, 2 * c + 1]))
        in_maps.append(m)
    res = run_bass_kernel_spmd(nc, in_maps, core_ids=list(range(N_CORES)))
    out = np.concatenate([np.asarray(res.results[c]["out"]) for c in range(N_CORES)], axis=0)
    return np.ascontiguousarray(out.astype(np.float32))
```

```python
import numpy as np
import concourse.bass as bass
import concourse.mybir as mybir
from concourse.bass_utils import run_bass_kernel_spmd

F32 = mybir.dt.float32
BF16 = mybir.dt.bfloat16
AF = mybir.ActivationFunctionType
ALU = mybir.AluOpType
AX = mybir.AxisListType


class Prog:
    NHW = 16
    NDMA = NHW + 8

    def __init__(self, nc, same_engine_sync=True):
        self.nc = nc
        self.E = {'pe': nc.tensor, 'act': nc.scalar, 'dve': nc.vector, 'pool': nc.gpsimd, 'sp': nc.sync}
        self.sem = {e: nc.alloc_semaphore(name="sem_" + e) for e in ('pe', 'act', 'dve', 'pool')}
        self.cnt = {e: 0 for e in self.sem}
        self.dsem = [nc.alloc_semaphore(name="dsem%d" % i) for i in range(self.NDMA)]
        self.dval = [0] * self.NDMA
        self.dnext = 0
        self.dnext_sw = 0
        self.known = {e: {} for e in self.E}
        self.evclock = {}
        self.lastw = {}
        self.readers = {}
        self.same = same_engine_sync
        self.nwaits = 0
        self.ninst = 0
        self.pending = {}
        self._n = 0

    def sb(self, name, shape, dtype):
        return self.nc.alloc_sbuf_tensor(name, list(shape), dtype)

    def ps(self, name, shape, dtype=F32):
        return self.nc.alloc_psum_tensor(name, list(shape), dtype)

    def dram(self, name, shape, dtype, kind="Internal"):
        return self.nc.dram_tensor(name, list(shape), dtype, kind=kind)

    def _semof(self, k):
        return self.sem[k] if isinstance(k, str) else self.dsem[k[1]]

    def _wait(self, X, ev):
        k, v = ev
        kn = self.known[X]
        if kn.get(k, 0) >= v:
            return
        self.E[X].wait_ge(self._semof(k), v)
        self.nwaits += 1
        clk = self.evclock.get(ev)
        if clk:
            for kk, vv in clk.items():
                if kn.get(kk, 0) < vv:
                    kn[kk] = vv
        if kn.get(k, 0) < v:
            kn[k] = v

    @staticmethod
    def _keys(ks):
        return [k if isinstance(k, (str, tuple)) else k.name for k in ks]

    def _deps(self, X, r, w, wa=()):
        for key in r:
            for ev in self.lastw.get(key, ()):
                yield ev
        for key in w:
            for ev in self.lastw.get(key, ()):
                yield ev
            for ev in self.readers.get(key, ()):
                yield ev
        for key in wa:
            for ev in self.readers.get(key, ()):
                yield ev

    def _record(self, ev, r, w, wa=()):
        for key in r:
            lst = self.readers.setdefault(key, [])
            lst[:] = [e for e in lst if e[0] != ev[0]]
            lst.append(ev)
        for key in w:
            self.lastw[key] = [ev]
            self.readers[key] = []
        for key in wa:
            lst = self.lastw.setdefault(key, [])
            lst[:] = [e for e in lst if e[0] != ev[0]]
            lst.append(ev)

    def op(self, X, method, r=(), w=(), wa=(), defer=False, **kw):
        r = self._keys(r)
        w = self._keys(w)
        wa = self._keys(wa)
        pend = self.pending.setdefault(X, [[], [], []])
        if defer:
            assert X == 'pe'
            for ev in list(self._deps(X, r, w, wa)):
                if ev[0] == X:
                    continue
                self._wait(X, ev)
            getattr(self.E[X], method)(**kw)
            pend[0] += r
            pend[1] += w
            pend[2] += wa
            self.ninst += 1
            return None
        if pend[0] or pend[1] or pend[2]:
            r = list(dict.fromkeys(r + pend[0]))
            w = list(dict.fromkeys(w + pend[1]))
            wa = list(dict.fromkeys(wa + pend[2]))
            self.pending[X] = [[], [], []]
        for ev in list(self._deps(X, r, w, wa)):
            if ev[0] == X and (X == 'pe' or not self.same):
                continue
            self._wait(X, ev)
        inst = getattr(self.E[X], method)(**kw)
        self.cnt[X] += 1
        c = self.cnt[X]
        inst.then_inc(self.sem[X], 1)
        ev = (X, c)
        clk = dict(self.known[X])
        clk[X] = c
        self.evclock[ev] = clk
        self._record(ev, r, w, wa)
        self.ninst += 1
        return ev

    def pe(self, method, **kw):
        return self.op('pe', method, **kw)

    def act(self, method, **kw):
        return self.op('act', method, **kw)

    def dve(self, method, **kw):
        return self.op('dve', method, **kw)

    def pool(self, method, **kw):
        return self.op('pool', method, **kw)

    def dma(self, Q, out, in_, r=(), w=(), wa=(), **kw):
        r = self._keys(r)
        w = self._keys(w)
        wa = self._keys(wa)
        if Q == 'pool':
            i = self.NHW + self.dnext_sw
            self.dnext_sw = (self.dnext_sw + 1) % (self.NDMA - self.NHW)
        else:
            i = self.dnext
            self.dnext = (i + 1) % self.NHW
        k = ('d', i)
        if self.dval[i] > 0:
            self._wait(Q, (k, self.dval[i]))
        for ev in list(self._deps(Q, r, w, wa)):
            self._wait(Q, ev)
        inst = self.E[Q].dma_start(out=out, in_=in_, **kw)
        self.dval[i] += 16
        inst.then_inc(self.dsem[i], 16)
        ev = (k, self.dval[i])
        clk = dict(self.known[Q])
        clk[k] = self.dval[i]
        self.evclock[ev] = clk
        self._record(ev, r, w, wa)
        self.ninst += 1
        return ev

    def barrier(self):
        for X in ('pe', 'act', 'dve', 'pool', 'sp'):
            for e in self.sem:
                if self.cnt[e] > 0 and not (e == X and X == 'pe'):
                    self._wait(X, (e, self.cnt[e]))
            for i in range(self.NDMA):
                if self.dval[i] > 0:
                    self._wait(X, (('d', i), self.dval[i]))

    def scope(self):
        return Scope(self)

    def finish(self):
        for i in range(self.NDMA):
            if self.dval[i] > 0:
                self._wait('sp', (('d', i), self.dval[i]))
        for e in self.sem:
            if self.cnt[e] > 0:
                self._wait('sp', (e, self.cnt[e]))


D = 2048
TCTX = 256
TLAT = 2048
TT = TCTX + TLAT
NTT = TT // 128
N_TM = 6672
N_FM = 2080
PROWS = 2312
EPS = 1e-6

TM_NAV, TM_NAZ, TM_GLK, TM_GLKS, TM_GLV, TM_GLZ = 0, 512, 1024, 1280, 1536, 2048
TM_GDQKV, TM_GDZ, TM_HYXV, TM_HYZ, TM_GDAB = 2560, 4096, 4608, 6144, 6656
FM_NAQ, FM_NAK, FM_GLQ, FM_GLQS, FM_GLK, FM_GLKS, FM_GLG = 0, 512, 1024, 1280, 1536, 1792, 2048


def prow(tt):
    return 2 + tt * 128 if tt < 2 else 262 + (tt - 2) * 128


def w_in_perm():
    o = {}
    off = 0
    for name, w in (("na_q", 512), ("na_k", 512), ("na_v", 512), ("na_z", 512), ("gla_q", 256), ("gla_k", 256),
                    ("gla_v", 512), ("gla_z", 512), ("gla_g", 32), ("gdn_qkv", 1536), ("gdn_z", 512),
                    ("gdn_a", 8), ("gdn_b", 8), ("hy_xv", 1536), ("hy_z", 512)):
        o[name] = np.arange(off, off + w)
        off += w
    assert off == 7728

    def sw(ix):
        return ix.reshape(-1, 2, 32)[:, ::-1, :].reshape(-1)
    tm = np.concatenate([o["na_v"], o["na_z"], o["gla_k"], sw(o["gla_k"]), o["gla_v"], o["gla_z"], o["gdn_qkv"],
                         o["gdn_z"], o["hy_xv"], o["hy_z"], o["gdn_a"], o["gdn_b"]])
    fm = np.concatenate([o["na_q"], o["na_k"], o["gla_q"], sw(o["gla_q"]), o["gla_k"], sw(o["gla_k"]), o["gla_g"]])
    assert tm.size == N_TM and fm.size == N_FM
    return tm, fm


class Ctx:
    pass


class Scope:
    _uid = [0]

    def __init__(self, P):
        self.P = P
        self.cms = []

    def __enter__(self):
        return self

    def sb(self, name, shape, dtype):
        Scope._uid[0] += 1
        cm = self.P.nc.sbuf_tensor("%s_%d" % (name, Scope._uid[0]), list(shape), dtype)
        t = cm.__enter__()
        self.cms.append(cm)
        return t

    def ps(self, name, shape, dtype=F32):
        Scope._uid[0] += 1
        cm = self.P.nc.psum_tensor("%s_%d" % (name, Scope._uid[0]), list(shape), dtype)
        t = cm.__enter__()
        self.cms.append(cm)
        return t

    def __exit__(self, *a):
        self.P.barrier()
        for cm in reversed(self.cms):
            cm.__exit__(None, None, None)
        return False


def build_program(NB=2, NL=2, dbg=(), stop_after=None, branches="ABCDM"):
    nc = bass.Bass("TRN2", target_bir_lowering=False)
    P = Prog(nc)
    g = Ctx()
    g.nc, g.P, g.NB, g.NL = nc, P, NB, NL
    g.NL_total = 2
    g.branches = branches

    def din(name, shape, dt=F32):
        return nc.dram_tensor(name, list(shape), dt, kind="ExternalInput").ap()

    def dscr(name, shape, dt=F32):
        kind = "ExternalOutput" if name in dbg else "Internal"
        return nc.dram_tensor(name, list(shape), dt, kind=kind).ap()

    g.xin = din("xin", [NB, TT, D])
    g.cT = din("cT", [128, 16, 3])
    g.w_mod_t = din("w_mod_t", [2, 32, 128, 16, 128])
    g.w_mod_g = din("w_mod_g", [2, D, D])
    g.b_mod_col = din("b_mod_col", [2, 128, 32])
    g.b_mod_gate = din("b_mod_gate", [2, 1, D])
    g.g_pre_col = din("g_pre_col", [2, 128, 16])
    g.g_post_row = din("g_post_row", [2, 1, D])
    g.w_tm = din("w_tm", [2, D, N_TM])
    g.w_fm_t = din("w_fm_t", [2, 17, 128, 16, 128])
    g.w_gate_t = din("w_gate_t", [2, 4, 16, 128, 16, 128])
    g.b_gate_col = din("b_gate_col", [2, 128, 4, 16])
    g.w_branch_t = din("w_branch_t", [2, 4, 16, 128, 4, 128])
    g.w_out = din("w_out", [2, D, D])
    g.ident_in = din("ident", [128, 128])
    g.sel_in = din("sel", [3, 3, 128])
    g.na_tab2 = din("na_tab2", [2, 128, 8, 2, 7, 64])
    g.gla_Lm = din("gla_Lm", [2, 128, 128])
    g.rope_c_tm = din("rope_c_tm", [128, 16, 64])
    g.rope_s_tm = din("rope_s_tm", [128, 16, 64])
    g.rope_c_fm = din("rope_c_fm", [128, TLAT])
    g.rope_s_fm = din("rope_s_fm", [128, TLAT])
    g.gla_wg2 = din("gla_wg2", [2, 2, 16, 128])
    g.gla_bg_row = din("gla_bg_row", [2, 1, 2, 128])
    g.gla_norm_row = din("gla_norm_row", [2, 1, 128])
    g.gdn_mbs = din("gdn_mbs", [2, 128, 128])
    g.gdn_mbi = din("gdn_mbi", [2, 128, 128])
    g.ones128 = din("ones128", [128, 128])
    g.gdn_conv_row = din("gdn_conv_row", [2, 5, 1536])
    g.gdn_dtb_row = din("gdn_dtb_row", [2, 1, 8])
    g.gdn_alog_row = din("gdn_alog_row", [2, 1, 8])
    g.gdn_norm_row = din("gdn_norm_row", [2, 1, 128])
    g.hy_w_in = din("hy_w_in", [2, 33, 64])
    g.hy_w_mid = din("hy_w_mid", [2, 2, 64, 64])
    g.hy_w_out = din("hy_w_out", [2, 64, 1024])
    g.hy_freq_col = din("hy_freq_col", [2, 64, 3])
    g.hy_b_col = din("hy_b_col", [2, 64, 3])
    g.hy_conv_row = din("hy_conv_row", [2, 3, 1536])
    g.hy_conv_b_row = din("hy_conv_b_row", [2, 1, 1536])
    g.hy_skip_row = din("hy_skip_row", [2, 1, 512])
    g.hy_zT = [din("hy_zT0", [33, TLAT]), din("hy_zT1", [33, TCTX])]
    g.hy_decay = [din("hy_decay0", [128, 16, 512]), din("hy_decay1", [128, 2, 512])]
    g.hy_FmT = [din("hy_FmT0", [32, 128, 16, 128], BF16), din("hy_FmT1", [4, 128, 2, 128], BF16)]
    g.hy_GT = [din("hy_GT0", [16, 128, 32, 128], BF16), din("hy_GT1", [2, 128, 4, 128], BF16)]

    g.ptm = dscr("ptm", [NB, PROWS, N_TM])
    g.pfm = dscr("pfm", [NB, N_FM, TT])
    g.hbuf = dscr("hbuf", [NB, TT, D])
    g.yfm = dscr("yfm", [NB, 4 * 512, TT], BF16)
    g.khat = [dscr("khat0", [2 * TLAT, 512]), dscr("khat1", [2 * TCTX, 512])]
    g.hy_g0s = dscr("hy_g0s", [TT, 512])
    g.gq = dscr("gq", [TT, GQW])
    g.ybuf = dscr("ybuf", [TT, D])
    g.hy_vss = dscr("hy_vss", [TT, 512])
    g.out = nc.dram_tensor("out", [NB, TLAT, D], F32, kind="ExternalOutput").ap()

    g.ident = P.sb("ident_sb", [128, 128], F32)
    g.sel = P.sb("sel_sb", [3, 3, 128], F32)
    g.uT = P.sb("uT", [128, 16, TT], BF16)
    P.dma('sp', g.ident[:], g.ident_in, w=[g.ident])
    P.dma('sp', g.sel[:], g.sel_in, w=[g.sel])
    g.psA = [P.ps("psA%d" % i, [128, 512], F32) for i in range(4)]
    g.psB = [P.ps("psB%d" % i, [128, 512], F32) for i in range(4)]
    g.modcol = P.sb("modcol", [128, 32, 3], F32)
    g.Acol = P.sb("Acol", [128, 16, 3], F32)
    g.grow = P.sb("grow", [3, D], F32)
    with P.scope() as S:
        zero = S.sb("zero", [8, N_TM], F32)
        P.dve('memset', ap=zero[:], constant=0.0, w=[zero])
        for b in range(NB):
            for r0, n in ((0, 2), (258, 4), (2310, 2)):
                P.dma('sp', g.ptm[b, r0:r0 + n, :], zero[0:n, :], r=[zero], w=[('ptm', b)])

    for l in range(NL):
        phase_mod(g, l)
        if 'D' in g.branches:
            phase_hyfilt(g, l, 0)
            if l < g.NL_total - 1:
                phase_hyfilt(g, l, 1)
        for b in range(NB):
            phase_norm(g, l, b)
            if stop_after == 'norm':
                continue
            phase_proj(g, l, b)
            if stop_after == 'proj':
                continue
            if 'A' in g.branches:
                phase_na(g, l, b)
            if 'B' in g.branches:
                phase_gla(g, l, b)
            if 'C' in g.branches:
                phase_gdn_pre(g, l, b)
                phase_gdn(g, l, b)
            if 'D' in g.branches:
                phase_hy(g, l, b, 0)
                if l < g.NL_total - 1:
                    phase_hy(g, l, b, 1)
            if 'M' in g.branches:
                phase_merge(g, l, b)
    P.finish()
    return nc, g


def phase_mod(g, l):
    P, NB = g.P, g.NB
    with P.scope() as S:
        _phase_mod(g, l, S)


def _phase_mod(g, l, S):
    P = g.P
    g.cT_sb = S.sb("cT_sb", [128, 16, 3], F32)
    g.scT = S.sb("scT", [128, 16, 3], F32)
    g.mslab = [S.sb("mslab%d" % i, [128, 16, 128], F32) for i in range(2)]
    g.gslab = [S.sb("gslab%d" % i, [128, 4, 512], F32) for i in range(2)]
    g.bmc = S.sb("bmc", [128, 32], F32)
    g.gpc = S.sb("gpc", [128, 16], F32)
    g.brow = S.sb("brow", [3, D], F32)
    P.dma('sp', g.cT_sb[:], g.cT, w=[g.cT_sb])
    P.act('activation', out=g.scT[:], in_=g.cT_sb[:], func=AF.Silu, r=[g.cT_sb], w=[g.scT])
    ps = g.psA[0]
    for ft in range(32):
        slab = g.mslab[ft % 2]
        P.dma('sp', slab[:], g.w_mod_t[l, ft], w=[slab])
        for kt in range(16):
            P.pe('matmul', out=ps[:, ft * 3:(ft + 1) * 3], lhsT=slab[:, kt, :], rhs=g.scT[:, kt, :],
                 start=(kt == 0), stop=(kt == 15), r=[slab, g.scT], w=[ps])
    P.dma('sp', g.bmc[:], g.b_mod_col[l], w=[g.bmc])
    P.dma('sp', g.gpc[:], g.g_pre_col[l], w=[g.gpc])
    for j in range(3):
        P.dve('tensor_tensor', out=g.modcol[:, :, j], in0=ps[:, 0:96].rearrange("p (f j) -> p f j", j=3)[:, :, j],
              in1=g.bmc[:], op=ALU.add, r=[ps, g.bmc], w=[g.modcol])
        P.dve('scalar_tensor_tensor', out=g.Acol[:, :, j], in0=g.modcol[:, 16:32, j], scalar=1.0, in1=g.gpc[:],
              op0=ALU.add, op1=ALU.mult, r=[g.modcol, g.gpc], w=[g.Acol])
    psg = g.psA[1:3]
    for jc in range(4):
        for kq in range(4):
            slab = g.gslab[(jc * 4 + kq) % 2]
            P.dma('sp', slab[:], g.w_mod_g[l, kq * 512:(kq + 1) * 512, jc * 512:(jc + 1) * 512]
                  .rearrange("(kt p) f -> p kt f", p=128), w=[slab])
            for k4 in range(4):
                kt = kq * 4 + k4
                P.pe('matmul', out=psg[jc % 2][0:3, :], lhsT=g.scT[:, kt, :], rhs=slab[:, k4, :],
                     start=(kt == 0), stop=(kt == 15), r=[slab, g.scT], w=[psg[jc % 2]])
        P.act('activation', out=g.grow[:, jc * 512:(jc + 1) * 512], in_=psg[jc % 2][0:3, :], func=AF.Copy,
              r=[psg[jc % 2]], w=[g.grow])
    P.dma('sp', g.brow[:], g.b_mod_gate[l].partition_broadcast(3), w=[g.brow])
    P.dve('tensor_tensor', out=g.grow[:], in0=g.grow[:], in1=g.brow[:], op=ALU.add, r=[g.brow, g.grow], w=[g.grow])
    P.dma('sp', g.brow[:], g.g_post_row[l].partition_broadcast(3), r=[], w=[g.brow])
    P.dve('tensor_tensor', out=g.grow[:], in0=g.grow[:], in1=g.brow[:], op=ALU.mult, r=[g.brow, g.grow], w=[g.grow])


def phase_norm(g, l, b):
    with g.P.scope() as S:
        _phase_norm(g, l, b, S)


def _phase_norm(g, l, b, S):
    P = g.P
    g.hx = [S.sb("hx%d" % i, [128, D], F32) for i in range(2)]
    g.xr = [S.sb("xr%d" % i, [128, D], F32) for i in range(2)]
    g.ss = [S.sb("ss%d" % i, [128, 2], F32) for i in range(2)]
    src = g.xin if l == 0 else g.hbuf
    srckey = () if l == 0 else [('hbuf', b)]
    for tt in range(NTT):
        hx, xr, ss = g.hx[tt % 2], g.xr[tt % 2], g.ss[tt % 2]
        j = 2 if tt < 2 else b
        P.dma('sp', hx[:], src[b, tt * 128:(tt + 1) * 128, :], r=srckey, w=[hx])
        P.act('activation', out=xr[:], in_=hx[:], func=AF.Square, accum_out=ss[:, 0:1], r=[hx], w=[xr, ss])
        P.dve('tensor_scalar', out=ss[:, 1:2], in0=ss[:, 0:1], scalar1=1.0 / D, scalar2=EPS, op0=ALU.mult,
              op1=ALU.add, r=[ss], w=[ss])
        P.act('activation', out=ss[:, 1:2], in_=ss[:, 1:2], func=AF.Sqrt, r=[ss], w=[ss])
        P.dve('reciprocal', out=ss[:, 1:2], in_=ss[:, 1:2], r=[ss], w=[ss])
        P.dve('tensor_scalar', out=xr[:], in0=hx[:], scalar1=ss[:, 1:2], scalar2=None, op0=ALU.mult,
              r=[hx, ss], w=[xr])
        for kt in range(16):
            pt = g.psA[kt % 4]
            P.pe('transpose', out=pt[:, 0:128], in_=xr[:, kt * 128:(kt + 1) * 128], identity=g.ident[:],
                 r=[xr, g.ident], w=[pt])
            P.act('activation', out=g.uT[:, kt, tt * 128:(tt + 1) * 128], in_=pt[:, 0:128], func=AF.Identity,
                  scale=g.Acol[:, kt, j:j + 1], bias=g.modcol[:, kt, j:j + 1], r=[pt, g.Acol, g.modcol],
                  wa=[g.uT])


def phase_proj(g, l, b):
    with g.P.scope() as S:
        _phase_proj(g, l, b, S)


def _phase_proj(g, l, b, S):
    P = g.P
    g.wst = [S.sb("wst%d" % i, [128, 16, 512], F32) for i in range(2)]
    g.wbf = [S.sb("wbf%d" % i, [128, 16, 512], BF16) for i in range(2)]
    g.stg = [S.sb("stg%d" % i, [128, 512], F32) for i in range(3)]
    nchunk = (N_TM + 511) // 512
    it = 0

    def prep_tm(cc):
        c0 = cc * 512
        cw = min(512, N_TM - c0)
        wst, wbf = g.wst[cc % 2], g.wbf[cc % 2]
        for h4 in range(4):
            P.dma('sp', wst[:, h4 * 4:(h4 + 1) * 4, 0:cw],
                  g.w_tm[l, h4 * 512:(h4 + 1) * 512, c0:c0 + cw].rearrange("(kt p) f -> p kt f", p=128),
                  w=[(wst.name, h4)])
            if h4 % 2 == 0:
                P.dve('tensor_copy', out=wbf[:, h4 * 4:(h4 + 1) * 4, 0:cw], in_=wst[:, h4 * 4:(h4 + 1) * 4, 0:cw],
                      r=[(wst.name, h4)], w=[(wbf.name, h4)])
            else:
                P.act('activation', out=wbf[:, h4 * 4:(h4 + 1) * 4, 0:cw], in_=wst[:, h4 * 4:(h4 + 1) * 4, 0:cw],
                      func=AF.Copy, r=[(wst.name, h4)], w=[(wbf.name, h4)])

    nft = (N_FM + 127) // 128

    def prep_fm(ft):
        j = nchunk + ft
        wst, wbf = g.wst[j % 2], g.wbf[j % 2]
        P.dma('sp', wst[:, :, 0:128], g.w_fm_t[l, ft], w=[(wst.name, i) for i in range(4)])
        P.dve('tensor_copy', out=wbf[:, :, 0:128], in_=wst[:, :, 0:128], r=[(wst.name, i) for i in range(4)],
              w=[(wbf.name, i) for i in range(4)])

    prep_tm(0)
    for cc in range(nchunk):
        c0 = cc * 512
        cw = min(512, N_TM - c0)
        wst, wbf = g.wst[cc % 2], g.wbf[cc % 2]
        if cc + 1 < nchunk:
            prep_tm(cc + 1)
        else:
            prep_fm(0)
        for tt in range(NTT):
            ps = g.psA[it % 4]
            stg = g.stg[it % 3]
            it += 1
            for kt in range(16):
                P.pe('matmul', out=ps[:, 0:cw], lhsT=g.uT[:, kt, tt * 128:(tt + 1) * 128], rhs=wbf[:, kt, 0:cw],
                     start=(kt == 0), stop=(kt == 15), r=[g.uT, (wbf.name, kt // 4)], w=[ps], defer=(kt < 15))
            P.act('activation', out=stg[:, 0:cw], in_=ps[:, 0:cw], func=AF.Copy, r=[ps], w=[stg])
            P.dma('pool', g.ptm[b, prow(tt):prow(tt) + 128, c0:c0 + cw], stg[:, 0:cw], r=[stg], wa=[('ptm', b)])
    chunks = [(0, 256)] + [(256 + i * 512, 512) for i in range(4)]
    for ft in range(nft):
        f0 = ft * 128
        fw = min(128, N_FM - f0)
        j = nchunk + ft
        wst, wbf = g.wst[j % 2], g.wbf[j % 2]
        if ft + 1 < nft:
            prep_fm(ft + 1)
        for (t0, n) in chunks:
            ps = g.psA[it % 4]
            stg = g.stg[it % 3]
            it += 1
            for kt in range(16):
                P.pe('matmul', out=ps[0:fw, 0:n], lhsT=wbf[:, kt, 0:fw], rhs=g.uT[:, kt, t0:t0 + n],
                     start=(kt == 0), stop=(kt == 15), r=[g.uT, (wbf.name, kt // 4)], w=[ps], defer=(kt < 15))
            P.act('activation', out=stg[0:fw, 0:n], in_=ps[0:fw, 0:n], func=AF.Copy, r=[ps], w=[stg])
            P.dma('pool', g.pfm[b, f0:f0 + fw, t0:t0 + n], stg[0:fw, 0:n], r=[stg], wa=[('pfm', b)])


def prep_shared(inp):
    f = lambda a: np.ascontiguousarray(np.asarray(a, dtype=np.float32))
    tm, fm = w_in_perm()
    sh = {}
    w_mod = np.asarray(inp["w_mod"], dtype=np.float32)
    sh["w_mod_t"] = f(w_mod[:, :, :2 * D].reshape(2, 16, 128, 32, 128).transpose(0, 3, 2, 1, 4))
    sh["w_mod_g"] = f(w_mod[:, :, 2 * D:])
    b_mod = f(inp["b_mod"])
    sh["b_mod_col"] = f(b_mod[:, :2 * D].reshape(2, 32, 128).transpose(0, 2, 1))
    sh["b_mod_gate"] = f(b_mod[:, 2 * D:].reshape(2, 1, D))
    sh["g_pre_col"] = f(f(inp["g_pre"]).reshape(2, 16, 128).transpose(0, 2, 1))
    sh["g_post_row"] = f(f(inp["g_post"]).reshape(2, 1, D))
    sh["w_gate_t"] = f(np.asarray(inp["w_gate"], dtype=np.float32).reshape(2, 4, 16, 128, 16, 128)
                       .transpose(0, 1, 4, 3, 2, 5))
    sh["b_gate_col"] = f(f(inp["b_gate"]).reshape(2, 4, 16, 128).transpose(0, 3, 1, 2))
    sh["w_branch_t"] = f(np.asarray(inp["w_branch"], dtype=np.float32).reshape(2, 4, 4, 128, 16, 128)
                         .transpose(0, 1, 4, 3, 2, 5))
    sh["w_out"] = f(inp["w_out"])
    w_in = np.asarray(inp["w_in"], dtype=np.float32)
    sh["w_tm"] = f(w_in[:, :, tm])
    wfm = np.zeros((2, D, 17 * 128), np.float32)
    wfm[:, :, :N_FM] = w_in[:, :, fm]
    sh["w_fm_t"] = f(wfm.reshape(2, 16, 128, 17, 128).transpose(0, 3, 2, 1, 4))
    sh["ident"] = np.eye(128, dtype=np.float32)
    sel = np.zeros((3, 3, 128), np.float32)
    for j in range(3):
        sel[j, j, :] = 1.0
    sh["sel"] = sel
    tab5 = na_table(inp["na_rpb"])
    tab2 = np.empty((2, 2, 64, 8, 2, 7, 64), np.float32)
    for jj in range(2):
        for par in range(2):
            for m_ in range(7):
                tab2[:, jj, :, :, par, m_, :] = tab5[:, :, :, 2 * m_ + par + jj, :]
    sh["na_tab2"] = np.ascontiguousarray(tab2.reshape(2, 128, 8, 2, 7, 64))
    sh.update(gla_consts())
    sh["gla_wg2"] = f(inp["gla_wg2"])
    sh["gla_bg_row"] = f(f(inp["gla_bg"]).reshape(2, 1, 2, 128))
    sh["gla_norm_row"] = f(f(inp["gla_norm"]).reshape(2, 1, 128))
    sh.update(gdn_consts())
    sh["gdn_conv_row"] = f(f(inp["gdn_conv"]).transpose(0, 2, 1))
    sh["gdn_dtb_row"] = f(f(inp["gdn_dt_bias"]).reshape(2, 1, 8))
    sh["gdn_alog_row"] = f(f(inp["gdn_a_log"]).reshape(2, 1, 8))
    sh["gdn_norm_row"] = f(f(inp["gdn_norm"]).reshape(2, 1, 128))
    sh["hy_w_in"] = f(inp["hy_w_in"])
    sh["hy_w_mid"] = f(inp["hy_w_mid"])
    sh["hy_w_out"] = f(inp["hy_w_out"])
    sh["hy_freq_col"] = f(f(inp["hy_freq"]).transpose(0, 2, 1))
    sh["hy_b_col"] = f(np.concatenate([f(inp["hy_b_in"])[:, None, :], f(inp["hy_b_mid"])], axis=1).transpose(0, 2, 1))
    sh["hy_conv_row"] = f(f(inp["hy_conv"]).transpose(0, 2, 1))
    sh["hy_conv_b_row"] = f(f(inp["hy_conv_b"]).reshape(2, 1, 1536))
    sh["hy_skip_row"] = f(f(inp["hy_skip"]).reshape(2, 1, 512))
    for li, L in enumerate((TLAT, TCTX)):
        hc = hy_consts(L)
        sh["hy_zT%d" % li] = hc["zT"]
        sh["hy_decay%d" % li] = hc["decay"]
        sh["hy_FmT%d" % li] = hc["FmT"]
        sh["hy_GT%d" % li] = hc["GT"]
    return sh


def prep_core(inp, bs):
    x = np.asarray(inp["x"], dtype=np.float32)
    ctx = np.asarray(inp["ctx"], dtype=np.float32)
    c = np.asarray(inp["c"], dtype=np.float32)
    c_ctx = np.asarray(inp["c_ctx"], dtype=np.float32)
    d = {}
    d["xin"] = np.ascontiguousarray(np.stack([np.concatenate([ctx[b], x[b]], axis=0) for b in bs]))
    cb = [c[b] for b in bs]
    while len(cb) < 2:
        cb.append(cb[0])
    cm = np.stack(cb[:2] + [c_ctx], axis=1)
    d["cT"] = np.ascontiguousarray(cm.reshape(16, 128, 3).transpose(1, 0, 2))
    return d


def na_table(rpb):
    rpb = np.asarray(rpb, dtype=np.float32)
    kc = np.arange(64)[:, None]
    qc = np.arange(64)[None, :]
    cstart = np.clip(qc - 8, 0, 48)
    valid = (kc >= cstart) & (kc < cstart + 16)
    dc = np.clip(kc - qc + 15, 0, 30)
    tab = rpb[:, :, :, dc]
    tab = np.where(valid[None, None, None], tab, np.float32(-30000.0))
    return np.ascontiguousarray(tab.transpose(0, 3, 1, 2, 4).astype(np.float32))


def rowtok(row):
    return 64 * row


def phase_na(g, l, b):
    with g.P.scope() as S:
        _phase_na(g, l, b, S)


def _phase_na(g, l, b, S):
    P = g.P
    with_ctx = l < g.NL_total - 1
    stage = S.sb("na_stage", [128, TT], F32)
    qT = S.sb("na_qT", [128, TT], BF16)
    kTh = [S.sb("na_kT%d" % i, [128, TT], BF16) for i in range(2)]
    for i in range(2):
        P.pool('memset', ap=kTh[i][:], constant=0.0, w=[kTh[i]])
    stE = S.sb("na_stE", [128, 18, 128], F32)
    stO = S.sb("na_stO", [128, 15, 128], F32)
    vE = S.sb("na_vE", [128, 18, 2, 65], BF16)
    vO = S.sb("na_vO", [128, 15, 2, 65], BF16)
    st2 = S.sb("na_st2", [64, 36, 128], F32)
    oA = S.sb("na_oA", [64, 36, 128], F32)
    Tb = S.sb("na_Tb", [128, 2, 2, 7, 64], F32)
    sw = [S.sb("na_sw%d" % i, [128, 256], F32) for i in range(2)]
    pall = [S.sb("na_pall%d" % i, [128, 384], BF16) for i in range(2)]
    rec = [S.sb("na_rec%d" % i, [64, 1], F32) for i in range(2)]
    yst = S.sb("na_yst", [128, TT], BF16)
    psS, psO, psT = g.psA[0:2], g.psB[0:2], g.psB[2]
    row0 = 0 if with_ctx else 4
    for hp in range(4):
        P.dma('sp', stage[:], g.pfm[b, FM_NAQ + hp * 128:FM_NAQ + (hp + 1) * 128, :], r=[('pfm', b)], w=[stage])
        P.dve('tensor_copy', out=qT[:], in_=stage[:], r=[stage], w=[qT])
        P.dma('sp', stage[:], g.pfm[b, FM_NAK + hp * 128:FM_NAK + (hp + 1) * 128, :], r=[('pfm', b)], w=[stage])
        for i in range(2):
            P.dve('tensor_copy', out=kTh[i][64 * i:64 * i + 64, :], in_=stage[64 * i:64 * i + 64, :], r=[stage],
                  w=[kTh[i]])
        c0 = TM_NAV + hp * 128
        P.dma('sp', stE[:, 0:2, :], g.ptm[b, 2:258, c0:c0 + 128].rearrange("(m p) c -> p m c", p=128),
              r=[('ptm', b)], w=[stE])
        P.dma('sp', stE[:, 2:18, :], g.ptm[b, 262:2310, c0:c0 + 128].rearrange("(m p) c -> p m c", p=128),
              r=[('ptm', b)], wa=[stE])
        P.dma('sp', stO[:], g.ptm[b, 326:326 + 15 * 128, c0:c0 + 128].rearrange("(m p) c -> p m c", p=128),
              r=[('ptm', b)], w=[stO])
        P.pool('memset', ap=vE[:], constant=1.0, w=[vE])
        P.pool('memset', ap=vO[:], constant=1.0, w=[vO])
        P.dve('tensor_copy', out=vE[:, :, :, 0:64], in_=stE[:].rearrange("p r (h d) -> p r h d", h=2), r=[stE], w=[vE])
        P.dve('tensor_copy', out=vO[:, :, :, 0:64], in_=stO[:].rearrange("p r (h d) -> p r h d", h=2), r=[stO], w=[vO])
        c0 = TM_NAZ + hp * 128
        P.dma('sp', st2[:, 0:4, :], g.ptm[b, 2:258, c0:c0 + 128].rearrange("(r p) c -> p r c", p=64),
              r=[('ptm', b)], w=[st2])
        P.dma('sp', st2[:, 4:36, :], g.ptm[b, 262:2310, c0:c0 + 128].rearrange("(r p) c -> p r c", p=64),
              r=[('ptm', b)], wa=[st2])
        P.act('activation', out=st2[:], in_=st2[:], func=AF.Silu, r=[st2], w=[st2])
        P.dma('sp', Tb[:], g.na_tab2[l, :, 2 * hp:2 * hp + 2], w=[Tb])
        it = 0
        for h2 in range(2):
            hb = 64 * h2
            kT = kTh[h2]
            for row in range(row0, 36):
                i2 = it % 2
                it += 1
                tq = 64 * row
                qv = qT[:, tq:tq + 64]
                lat = row >= 4
                ps = psS[i2]
                vts = []
                if lat:
                    r_ = row - 4
                    rs = min(max(r_ - 4, 0), 24)
                    dr0 = rs - r_ + 7
                    for j in range(4):
                        tk = 256 + 64 * (rs + 2 * j)
                        P.pe('matmul', out=ps[:, j * 64:(j + 1) * 64], lhsT=kT[:, tk:tk + 128], rhs=qv, start=True,
                             stop=True, r=[kT, qT], w=[ps])
                        vts.append(vE[:, tk // 128, h2, :] if tk % 128 == 0 else vO[:, (tk - 64) // 128 - 2, h2, :])
                for i in range(2):
                    P.pe('matmul', out=ps[:, 256 + i * 64:256 + (i + 1) * 64], lhsT=kT[:, 128 * i:128 * i + 128],
                         rhs=qv, start=True, stop=True, r=[kT, qT], w=[ps])
                if lat:
                    P.dve('scalar_tensor_tensor', out=sw[i2][:], in0=ps[:, 0:256], scalar=0.125,
                          in1=Tb[:, h2, dr0 % 2, dr0 // 2:dr0 // 2 + 4, :].rearrange("p a b -> p (a b)"),
                          op0=ALU.mult, op1=ALU.add, r=[ps, Tb], w=[sw[i2]])
                    P.act('activation', out=pall[i2][:, 0:256], in_=sw[i2][:], func=AF.Exp, r=[sw[i2]],
                          w=[(pall[i2].name, 0)])
                P.act('activation', out=pall[i2][:, 256:384], in_=ps[:, 256:384], func=AF.Exp, scale=0.125,
                      r=[ps], w=[(pall[i2].name, 1)])
                if lat:
                    for j in range(4):
                        P.pe('matmul', out=psO[i2][0:64, 0:65], lhsT=pall[i2][:, j * 64:(j + 1) * 64], rhs=vts[j],
                             start=(j == 0), stop=False, r=[(pall[i2].name, 0), vE, vO], w=[psO[i2]])
                for i in range(2):
                    P.pe('matmul', out=psO[i2][0:64, 0:65], lhsT=pall[i2][:, 256 + i * 64:256 + (i + 1) * 64],
                         rhs=vE[:, i, h2, :], start=(i == 0 and not lat), stop=(i == 1),
                         r=[(pall[i2].name, 1), vE], w=[psO[i2]])
                P.dve('reciprocal', out=rec[i2][:], in_=psO[i2][0:64, 64:65], r=[psO[i2]], w=[rec[i2]])
                P.dve('tensor_scalar', out=oA[:, row, hb:hb + 64], in0=psO[i2][0:64, 0:64], scalar1=rec[i2][:, 0:1],
                      scalar2=None, op0=ALU.mult, r=[psO[i2], rec[i2]], wa=[oA])
        P.dve('tensor_tensor', out=oA[:], in0=oA[:], in1=st2[:], op=ALU.mult, r=[st2, oA], w=[oA])
        for row in range(row0, 36):
            P.pe('transpose', out=psT[:, 0:64], in_=oA[:, row, :], identity=g.ident[0:64, 0:64],
                 r=[oA, g.ident], w=[psT])
            P.act('activation', out=yst[:, 64 * row:64 * row + 64], in_=psT[:, 0:64], func=AF.Copy,
                  r=[psT], wa=[yst])
        t0 = 64 * row0
        P.dma('sp', g.yfm[b, hp * 128:(hp + 1) * 128, t0:TT], yst[:, t0:TT], r=[yst], wa=[('yfm', b)])


TWO_PI = 2.0 * np.pi


def hy_consts(L):
    import ml_dtypes
    N = 2 * L
    ntt = L // 128
    t = np.arange(L, dtype=np.float64)
    f = np.arange(L, dtype=np.float64)
    ang = 2.0 * np.pi * np.outer(t, f) / N
    Fm = np.empty((L, N), np.float64)
    Fm[:, :L] = np.cos(ang)
    Fm[:, L:] = -np.sin(ang)
    Fm[:, L] = np.cos(np.pi * t)
    G = np.empty((N, L), np.float64)
    G[:L, :] = 2.0 * np.cos(ang.T) / N
    G[0, :] = 1.0 / N
    G[L:, :] = -2.0 * np.sin(ang.T) / N
    G[L, :] = np.cos(np.pi * t) / N
    FmT = Fm.reshape(ntt, 128, 2 * ntt, 128).transpose(2, 1, 0, 3)
    GT = G.reshape(2 * ntt, 128, ntt, 128).transpose(2, 1, 0, 3)
    tt_ = np.linspace(0.0, 1.0, L, dtype=np.float32)[:, None]
    bands = 16
    wpos = (np.float32(2.0 * np.pi) * np.arange(L, dtype=np.float32)[:, None] / np.float32(L)).astype(np.float32)
    fb = np.linspace(1e-4, bands - 1, bands, dtype=np.float32)[None]
    z = np.concatenate([tt_, np.cos(fb * wpos), -np.sin(fb * wpos)], axis=-1).astype(np.float32)
    deltas = np.abs(np.linspace(np.log(1e-2) / 1.5, np.log(1e-2) / 0.3, 512, dtype=np.float32))
    decay = np.exp(-tt_ * deltas).astype(np.float32)
    return {
        "FmT": np.ascontiguousarray(FmT).astype(ml_dtypes.bfloat16),
        "GT": np.ascontiguousarray(GT).astype(ml_dtypes.bfloat16),
        "zT": np.ascontiguousarray(z.T),
        "decay": np.ascontiguousarray(decay.reshape(ntt, 128, 512).transpose(1, 0, 2)),
    }


def phase_hyfilt(g, l, li):
    with g.P.scope() as S:
        _phase_hyfilt(g, l, li, S)


def _phase_hyfilt(g, l, li, S):
    P = g.P
    L = (TLAT, TCTX)[li]
    ntt = L // 128
    wi = S.sb("hf_wi", [33, 64], F32)
    wm = S.sb("hf_wm", [64, 2, 64], F32)
    wo = S.sb("hf_wo", [64, 1024], F32)
    fcol = S.sb("hf_fcol", [64, 3], F32)
    bcol = S.sb("hf_bcol", [64, 3], F32)
    fs = S.sb("hf_fs", [64, 3], F32)
    fb = S.sb("hf_fb", [64, 3], F32)
    zT = S.sb("hf_zT", [33, L], F32)
    hT = [S.sb("hf_hT%d" % i, [64, L], F32) for i in range(2)]
    ua = [S.sb("hf_ua%d" % i, [64, 512], F32) for i in range(2)]
    ub = [S.sb("hf_ub%d" % i, [64, 512], F32) for i in range(2)]
    dec = S.sb("hf_dec", [128, ntt, 512], F32)
    hfb = [S.sb("hf_hfb%d" % i, [128, 512], F32) for i in range(2)]
    hsb = S.sb("hf_hsb", [128, ntt, 512], BF16)
    hdb = S.sb("hf_hdb", [128, ntt, 512], BF16)
    fmt = [S.sb("hf_fmt%d" % i, [128, ntt, 128], BF16) for i in range(2)]
    kst = [S.sb("hf_kst%d" % i, [128, 512], F32) for i in range(2)]
    P.dma('sp', wi[:], g.hy_w_in[l], w=[wi])
    P.dma('sp', wm[:], g.hy_w_mid[l].rearrange("i k n -> k i n"), w=[wm])
    P.dma('sp', wo[:], g.hy_w_out[l], w=[wo])
    P.dma('sp', fcol[:], g.hy_freq_col[l], w=[fcol])
    P.dma('sp', bcol[:], g.hy_b_col[l], w=[bcol])
    P.dma('sp', zT[:], g.hy_zT[li], w=[zT])
    P.dma('sp', dec[:], g.hy_decay[li], w=[dec])
    P.dve('tensor_scalar', out=fs[:], in0=fcol[:], scalar1=1.0 / TWO_PI, scalar2=None, op0=ALU.mult,
          r=[fcol], w=[fs])
    P.dve('tensor_tensor', out=fb[:], in0=fs[:], in1=bcol[:], op=ALU.mult, r=[fs, bcol], w=[fb])
    nch = (L + 511) // 512
    cwid = min(512, L)
    it = 0
    for i in range(3):
        dst = hT[i % 2]
        for c in range(nch):
            ps = g.psA[it % 4]
            i2 = it % 2
            it += 1
            if i == 0:
                P.pe('matmul', out=ps[0:64, 0:cwid], lhsT=wi[:], rhs=zT[:, c * cwid:(c + 1) * cwid], start=True,
                     stop=True, r=[wi, zT], w=[ps])
            else:
                src = hT[(i - 1) % 2]
                P.pe('matmul', out=ps[0:64, 0:cwid], lhsT=wm[:, i - 1, :], rhs=src[:, c * cwid:(c + 1) * cwid],
                     start=True, stop=True, r=[wm, src], w=[ps])
            P.act('activation', out=ua[i2][:, 0:cwid], in_=ps[0:64, 0:cwid], func=AF.Identity,
                  scale=fs[:, i:i + 1], bias=fb[:, i:i + 1], r=[ps, fs, fb], w=[ua[i2]])
            P.dve('scalar_tensor_tensor', out=ub[i2][:, 0:cwid], in0=ua[i2][:, 0:cwid], scalar=0.5,
                  in1=ua[i2][:, 0:cwid], op0=ALU.is_gt, op1=ALU.subtract, r=[ua[i2]], w=[ub[i2]])
            P.dve('scalar_tensor_tensor', out=ub[i2][:, 0:cwid], in0=ua[i2][:, 0:cwid], scalar=-0.5,
                  in1=ub[i2][:, 0:cwid], op0=ALU.is_lt, op1=ALU.subtract, r=[ua[i2], ub[i2]], w=[ub[i2]])
            P.act('activation', out=dst[:, c * cwid:(c + 1) * cwid], in_=ub[i2][:, 0:cwid], func=AF.Sin,
                  scale=TWO_PI, r=[ub[i2]], wa=[dst])
    h3 = hT[0]
    for tt in range(ntt):
        for half in range(2):
            ps = g.psA[it % 4]
            it += 1
            P.pe('matmul', out=ps[:, :], lhsT=h3[:, tt * 128:(tt + 1) * 128], rhs=wo[:, half * 512:(half + 1) * 512],
                 start=True, stop=True, r=[h3, wo], w=[ps])
            P.dve('tensor_tensor', out=hfb[half][:], in0=ps[:, :], in1=dec[:, tt, :], op=ALU.mult,
                  r=[ps, dec], w=[hfb[half]])
        P.dve('tensor_tensor', out=hsb[:, tt, :], in0=hfb[0][:], in1=hfb[1][:], op=ALU.add,
              r=[hfb[0], hfb[1]], wa=[hsb])
        P.pool('tensor_tensor', out=hdb[:, tt, :], in0=hfb[0][:], in1=hfb[1][:], op=ALU.subtract,
               r=[hfb[0], hfb[1]], wa=[hdb])
    for rt in range(2 * ntt):
        fm = fmt[rt % 2]
        ks = kst[rt % 2]
        P.dma('sp', fm[:], g.hy_FmT[li][rt], w=[fm])
        ps = g.psA[it % 4]
        it += 1
        src = hsb if rt < ntt else hdb
        for tt in range(ntt):
            P.pe('matmul', out=ps[:, :], lhsT=fm[:, tt, :], rhs=src[:, tt, :], start=(tt == 0), stop=(tt == ntt - 1),
                 r=[fm, src], w=[ps])
        P.act('activation', out=ks[:], in_=ps[:, :], func=AF.Copy, r=[ps], w=[ks])
        if rt == ntt:
            ps2 = g.psA[it % 4]
            it += 1
            for tt in range(ntt):
                P.pe('matmul', out=ps2[:, :], lhsT=fm[:, tt, :], rhs=hsb[:, tt, :], start=(tt == 0),
                     stop=(tt == ntt - 1), r=[fm, hsb], w=[ps2])
            P.act('activation', out=ks[0:1, :], in_=ps2[0:1, :], func=AF.Copy, r=[ps2, ks], w=[ks])
        P.dma('sp', g.khat[li][rt * 128:(rt + 1) * 128, :], ks[:], r=[ks], wa=[('khat', li)])


def phase_hy(g, l, b, li):
    with g.P.scope() as S:
        _phase_hy(g, l, b, li, S)


def _phase_hy(g, l, b, li, S):
    P = g.P
    L = (TLAT, TCTX)[li]
    ntt = L // 128
    tile0 = 2 if li == 0 else 0
    tok0 = 256 if li == 0 else 0
    vvb = S.sb("hy_vvb", [128, ntt, 512], BF16)
    with P.scope() as S1:
        wrow = S1.sb("hy_wrow", [128, 3, 1536], F32)
        brow = S1.sb("hy_brow", [128, 1536], F32)
        srow = S1.sb("hy_srow", [128, 512], F32)
        xs = [S1.sb("hy_xs%d" % i, [128, 1536], F32) for i in range(3)]
        acc = S1.sb("hy_acc", [128, 1536], F32)
        tmp = S1.sb("hy_tmp", [128, 1536], F32)
        zt = S1.sb("hy_zt", [128, 512], F32)
        vv = S1.sb("hy_vv", [128, 512], F32)
        g0 = S1.sb("hy_g0", [128, 512], F32)
        vs = S1.sb("hy_vs", [128, 512], F32)
        for k in range(3):
            P.dma('sp', wrow[:, k, :], g.hy_conv_row[l, k:k + 1, :].partition_broadcast(128), wa=[wrow])
        P.dma('sp', brow[:], g.hy_conv_b_row[l].partition_broadcast(128), w=[brow])
        P.dma('sp', srow[:], g.hy_skip_row[l].partition_broadcast(128), w=[srow])
        for tt in range(ntt):
            base = prow(tile0 + tt)
            for k in range(3):
                P.dma('sp', xs[k][:], g.ptm[b, base + k - 1:base + k - 1 + 128, TM_HYXV:TM_HYXV + 1536],
                      r=[('ptm', b)], w=[xs[k]])
            P.dma('sp', zt[:], g.ptm[b, base:base + 128, TM_HYZ:TM_HYZ + 512], r=[('ptm', b)], w=[zt])
            P.dve('tensor_tensor', out=acc[:], in0=xs[0][:], in1=wrow[:, 0, :], op=ALU.mult, r=[xs[0], wrow], w=[acc])
            P.pool('tensor_tensor', out=tmp[:], in0=xs[1][:], in1=wrow[:, 1, :], op=ALU.mult, r=[xs[1], wrow],
                   w=[tmp])
            P.dve('tensor_tensor', out=acc[:], in0=acc[:], in1=tmp[:], op=ALU.add, r=[tmp, acc], w=[acc])
            P.pool('tensor_tensor', out=tmp[:], in0=xs[2][:], in1=wrow[:, 2, :], op=ALU.mult, r=[xs[2], wrow],
                   w=[tmp])
            P.dve('tensor_tensor', out=acc[:], in0=acc[:], in1=tmp[:], op=ALU.add, r=[tmp, acc], w=[acc])
            P.dve('tensor_tensor', out=acc[:], in0=acc[:], in1=brow[:], op=ALU.add, r=[brow, acc], w=[acc])
            P.dve('tensor_tensor', out=vv[:], in0=acc[:, 1024:1536], in1=acc[:, 512:1024], op=ALU.mult,
                  r=[acc], w=[vv])
            P.pool('tensor_copy', out=vvb[:, tt, :], in_=vv[:], r=[vv], wa=[vvb])
            P.act('activation', out=zt[:], in_=zt[:], func=AF.Silu, r=[zt], w=[zt])
            P.dve('tensor_tensor', out=g0[:], in0=acc[:, 0:512], in1=zt[:], op=ALU.mult, r=[acc, zt], w=[g0])
            P.pool('tensor_tensor', out=vs[:], in0=vv[:], in1=srow[:], op=ALU.mult, r=[vv, srow], w=[vs])
            P.dve('tensor_tensor', out=vs[:], in0=vs[:], in1=g0[:], op=ALU.mult, r=[vs, g0], w=[vs])
            P.dma('pool', g.hy_g0s[tok0 + tt * 128:tok0 + (tt + 1) * 128, :], g0[:], r=[g0], wa=['hy_g0s'])
            P.dma('pool', g.hy_vss[tok0 + tt * 128:tok0 + (tt + 1) * 128, :], vs[:], r=[vs], wa=['hy_vss'])
    yhat = S.sb("hy_yhat", [128, 2 * ntt, 512], BF16)
    with P.scope() as S2:
        fmt = [S2.sb("hy_fmt%d" % i, [128, ntt, 128], BF16) for i in range(4)]
        kk = [S2.sb("hy_kk%d" % i, [128, 512], F32) for i in range(4)]
        vh = [S2.sb("hy_vh%d" % i, [128, 512], F32) for i in range(4)]
        tq = [S2.sb("hy_tq%d" % i, [128, 512], F32) for i in range(4)]
        it = 0
        for i in range(ntt):
            i2 = (i % 2) * 2
            fre, fim, kre, kim, vre, vim = fmt[i2], fmt[i2 + 1], kk[i2], kk[i2 + 1], vh[i2], vh[i2 + 1]
            P.dma('sp', fre[:], g.hy_FmT[li][i], w=[fre])
            P.dma('sp', fim[:], g.hy_FmT[li][i + ntt], w=[fim])
            P.dma('sp', kre[:], g.khat[li][i * 128:(i + 1) * 128, :], r=[('khat', li)], w=[kre])
            P.dma('sp', kim[:], g.khat[li][(i + ntt) * 128:(i + ntt + 1) * 128, :], r=[('khat', li)], w=[kim])
            for (fm, vdst) in ((fre, vre), (fim, vim)):
                ps = g.psA[it % 4]
                it += 1
                for tt in range(ntt):
                    P.pe('matmul', out=ps[:, :], lhsT=fm[:, tt, :], rhs=vvb[:, tt, :], start=(tt == 0),
                         stop=(tt == ntt - 1), r=[fm, vvb], w=[ps])
                P.act('activation', out=vdst[:], in_=ps[:, :], func=AF.Copy, r=[ps], w=[vdst])
            t1, t2, t3, t4 = tq
            P.dve('tensor_tensor', out=t1[:], in0=vre[:], in1=kre[:], op=ALU.mult, r=[vre, kre], w=[t1])
            P.pool('tensor_tensor', out=t2[:], in0=vim[:], in1=kim[:], op=ALU.mult, r=[vim, kim], w=[t2])
            P.dve('tensor_tensor', out=t3[:], in0=vre[:], in1=kim[:], op=ALU.mult, r=[vre, kim], w=[t3])
            P.pool('tensor_tensor', out=t4[:], in0=vim[:], in1=kre[:], op=ALU.mult, r=[vim, kre], w=[t4])
            P.dve('tensor_tensor', out=yhat[:, i, :], in0=t1[:], in1=t2[:], op=ALU.subtract, r=[t1, t2], wa=[yhat])
            P.pool('tensor_tensor', out=yhat[:, i + ntt, :], in0=t3[:], in1=t4[:], op=ALU.add, r=[t3, t4], wa=[yhat])
            if i == 0:
                P.dve('tensor_tensor', out=yhat[0:1, 0, :], in0=vre[0:1, :], in1=kre[0:1, :], op=ALU.mult,
                      r=[vre, kre], w=[yhat])
                P.dve('tensor_tensor', out=yhat[0:1, ntt, :], in0=vim[0:1, :], in1=kim[0:1, :], op=ALU.mult,
                      r=[vim, kim], w=[yhat])
    yst = S.sb("hy_yst", [128, 4, L], BF16)
    with P.scope() as S3:
        gt = [S3.sb("hy_gt%d" % i, [128, 2 * ntt, 128], BF16) for i in range(2)]
        g0 = [S3.sb("hy_g0b%d" % i, [128, 512], F32) for i in range(2)]
        vs = [S3.sb("hy_vsb%d" % i, [128, 512], F32) for i in range(2)]
        o = [S3.sb("hy_o%d" % i, [128, 512], F32) for i in range(2)]
        for j in range(ntt):
            j2 = j % 2
            P.dma('sp', gt[j2][:], g.hy_GT[li][j], w=[gt[j2]])
            P.dma('sp', g0[j2][:], g.hy_g0s[tok0 + j * 128:tok0 + (j + 1) * 128, :], r=['hy_g0s'], w=[g0[j2]])
            P.dma('sp', vs[j2][:], g.hy_vss[tok0 + j * 128:tok0 + (j + 1) * 128, :], r=['hy_vss'], w=[vs[j2]])
            ps = g.psA[j2]
            for rt in range(2 * ntt):
                P.pe('matmul', out=ps[:, :], lhsT=gt[j2][:, rt, :], rhs=yhat[:, rt, :], start=(rt == 0),
                     stop=(rt == 2 * ntt - 1), r=[gt[j2], yhat], w=[ps])
            P.dve('tensor_tensor', out=o[j2][:], in0=ps[:, :], in1=g0[j2][:], op=ALU.mult, r=[ps, g0[j2]], w=[o[j2]])
            P.pool('tensor_tensor', out=o[j2][:], in0=o[j2][:], in1=vs[j2][:], op=ALU.add, r=[vs[j2], o[j2]],
                   w=[o[j2]])
            pt = g.psB[j2]
            for k in range(4):
                P.pe('transpose', out=pt[:, k * 128:(k + 1) * 128], in_=o[j2][:, k * 128:(k + 1) * 128],
                     identity=g.ident[:], r=[o[j2], g.ident], w=[pt])
            P.act('activation', out=yst[:, :, j * 128:(j + 1) * 128], in_=pt[:, :].rearrange("p (k t) -> p k t", k=4),
                  func=AF.Copy, r=[pt], wa=[yst])
        for k in range(4):
            P.dma('sp', g.yfm[b, 1536 + k * 128:1536 + (k + 1) * 128, tok0:tok0 + L], yst[:, k, :], r=[yst],
                  wa=[('yfm', b)])


def gla_consts():
    s = np.arange(128)[:, None]
    t = np.arange(128)[None, :]
    Lm = np.stack([(s <= t), (s >= t)]).astype(np.float32)
    pos = np.arange(TLAT)
    row = (pos // 64).astype(np.float32)
    col = (pos % 64).astype(np.float32)
    n = 16
    freqs = (np.float32(10000.0) ** (-np.arange(n, dtype=np.float32) / np.float32(n))).astype(np.float32)
    ang = np.concatenate([row[:, None] * freqs, col[:, None] * freqs], axis=-1).astype(np.float32)
    cos, sin = np.cos(ang).astype(np.float32), np.sin(ang).astype(np.float32)
    c64 = np.concatenate([cos, cos], axis=-1)
    s64 = np.concatenate([-sin, sin], axis=-1)
    return {
        "gla_Lm": Lm,
        "rope_c_tm": np.ascontiguousarray(c64.reshape(16, 128, 64).transpose(1, 0, 2)),
        "rope_s_tm": np.ascontiguousarray(s64.reshape(16, 128, 64).transpose(1, 0, 2)),
        "rope_c_fm": np.ascontiguousarray(np.concatenate([c64.T, c64.T], axis=0)),
        "rope_s_fm": np.ascontiguousarray(np.concatenate([s64.T, s64.T], axis=0)),
    }


def phase_gla(g, l, b):
    with g.P.scope() as S:
        _phase_gla(g, l, b, S)


def _phase_gla(g, l, b, S):
    P = g.P
    with_ctx = l < g.NL_total - 1
    oB = S.sb("gl_oB", [128, NTT, 512], F32)
    S0, S = S, Scope(P)
    cfm = S.sb("gl_cfm", [128, TLAT], F32)
    sfm = S.sb("gl_sfm", [128, TLAT], F32)
    ctm = S.sb("gl_ctm", [128, 16, 64], F32)
    stm = S.sb("gl_stm", [128, 16, 64], F32)
    Lm = S.sb("gl_Lm", [128, 2, 128], F32)
    wg2 = S.sb("gl_wg2", [16, 2, 128], F32)
    bg = S.sb("gl_bg", [1, 2, 128], F32)
    ones = S.sb("gl_ones", [1, 128], F32)
    Sp = [S.sb("gl_S%d" % i, [128, 128], F32) for i in range(2)]
    Sbf = [S.sb("gl_Sbf%d" % i, [128, 128], BF16) for i in range(2)]
    NR = 2
    fmq = [[S.sb("gl_fm%d_%d" % (k, r), [128, 128], F32) for k in range(8)] for r in range(NR)]
    lrt = [S.sb("gl_lrt%d" % r, [16, 128], F32) for r in range(NR)]
    ktm = [S.sb("gl_ktm%d" % r, [128, 512], F32) for r in range(NR)]
    vtm = [S.sb("gl_vtm%d" % r, [128, 512], F32) for r in range(NR)]
    vbf = [S.sb("gl_vbf%d" % r, [128, 512], BF16) for r in range(NR)]
    ee = [S.sb("gl_ee%d" % r, [128, 128], F32) for r in range(NR)]
    gdup = [S.sb("gl_gdup%d" % r, [128, 256], F32) for r in range(NR)]
    enb_tm = [S.sb("gl_enbtm%d" % r, [128, 256], F32) for r in range(NR)]
    eb_fm = [[S.sb("gl_ebfm%d_%d" % (i, r), [128, 128], F32) for i in range(2)] for r in range(NR)]
    enb_fm = [[S.sb("gl_enbfm%d_%d" % (i, r), [128, 128], F32) for i in range(2)] for r in range(NR)]
    t1 = [S.sb("gl_t1_%d" % r, [128, 256], F32) for r in range(NR)]
    t2 = [S.sb("gl_t2_%d" % r, [128, 256], F32) for r in range(NR)]
    qeT = [[S.sb("gl_qeT%d_%d" % (h, r), [128, 128], BF16) for h in range(4)] for r in range(NR)]
    keT = [[S.sb("gl_keT%d_%d" % (h, r), [128, 128], BF16) for h in range(4)] for r in range(NR)]
    for r in range(NR):
        for h in range(4):
            P.pool('memset', ap=qeT[r][h][:], constant=0.0, w=[qeT[r][h]])
            P.pool('memset', ap=keT[r][h][:], constant=0.0, w=[keT[r][h]])
    ke_tm = [S.sb("gl_ketm%d" % r, [128, 256], BF16) for r in range(NR)]
    attT = [S.sb("gl_attT%d" % r, [128, 512], BF16) for r in range(NR)]
    stmp = S.sb("gl_stmp", [128, 128], F32)
    P.dma('sp', cfm[:], g.rope_c_fm, w=[cfm])
    P.dma('sp', sfm[:], g.rope_s_fm, w=[sfm])
    P.dma('sp', ctm[:], g.rope_c_tm, w=[ctm])
    P.dma('sp', stm[:], g.rope_s_tm, w=[stm])
    P.dma('sp', Lm[:], g.gla_Lm.rearrange("d s t -> s d t"), w=[Lm])
    P.dma('sp', wg2[:], g.gla_wg2[l].rearrange("d k n -> k d n"), w=[wg2])
    P.dma('sp', bg[:], g.gla_bg_row[l], w=[bg])
    P.dve('memset', ap=ones[:], constant=1.0, w=[ones])
    pg, pbt, pbf, patt, po, pss = g.psA[0], g.psA[1], g.psA[2:4], g.psB[0], g.psB[1], g.psB[2:4]
    it = 0
    for d in range(2):
        order = list(range(NTT)) if d == 0 else [1, 0] + list(range(NTT - 1, 1, -1))
        tend = 127 if d == 0 else 0
        for i in range(2):
            P.dve('memset', ap=Sp[i][:], constant=0.0, w=[Sp[i]])
            P.pool('memset', ap=Sbf[i][:], constant=0.0, w=[Sbf[i]])
        for tt in order:
            r = it % NR
            it += 1
            tok = tt * 128
            lat = tt >= 2
            lt = tt - 2
            P.dma('sp', lrt[r][:], g.pfm[b, FM_GLG + 16 * d:FM_GLG + 16 * d + 16, tok:tok + 128], r=[('pfm', b)],
                  w=[lrt[r]])
            for k, f0 in enumerate((FM_GLQ, FM_GLQ + 128, FM_GLQS, FM_GLQS + 128, FM_GLK, FM_GLK + 128, FM_GLKS,
                                    FM_GLKS + 128)):
                if not lat and k in (2, 3, 6, 7):
                    continue
                P.dma('sp', fmq[r][k][:], g.pfm[b, f0:f0 + 128, tok:tok + 128], r=[('pfm', b)], w=[fmq[r][k]])
            P.dma('sp', ktm[r][:], g.ptm[b, prow(tt):prow(tt) + 128, TM_GLK:TM_GLK + 512], r=[('ptm', b)], w=[ktm[r]])
            P.dma('sp', vtm[r][:], g.ptm[b, prow(tt):prow(tt) + 128, TM_GLV:TM_GLV + 512], r=[('ptm', b)], w=[vtm[r]])
            P.pool('tensor_copy', out=vbf[r][:], in_=vtm[r][:], r=[vtm[r]], w=[vbf[r]])
            P.pe('matmul', out=pg[:, 0:128], lhsT=lrt[r][:], rhs=wg2[:, d, :], start=True, stop=False,
                 r=[lrt[r], wg2], w=[pg])
            P.pe('matmul', out=pg[:, 0:128], lhsT=ones[:], rhs=bg[:, d, :], start=False, stop=True,
                 r=[ones, bg], w=[pg])
            P.act('activation', out=ee[r][:], in_=pg[:, 0:128], func=AF.Exp, scale=-1.0, r=[pg], w=[ee[r]])
            P.act('activation', out=ee[r][:], in_=ee[r][:], func=AF.Ln, bias=1.0, r=[ee[r]], w=[ee[r]])
            gv = gdup[r][:].rearrange("p (h two c) -> p h two c", h=4, two=2)
            ev = ee[r][:].rearrange("p (h c) -> p h c", h=4)
            for two in range(2):
                P.dve('tensor_scalar', out=gv[:, :, two, :], in0=ev, scalar1=-1.0 / 16.0, scalar2=None, op0=ALU.mult,
                      r=[ee[r]], wa=[gdup[r]])
            P.pe('matmul', out=pbt[:, 0:256], lhsT=Lm[:, d, :], rhs=gdup[r][:], start=True, stop=True,
                 r=[Lm, gdup[r]], w=[pbt])
            P.act('activation', out=enb_tm[r][:], in_=pbt[:, 0:256], func=AF.Exp, scale=-1.0, r=[pbt], w=[enb_tm[r]])
            for i in range(2):
                P.pe('matmul', out=pbf[i][:, 0:128], lhsT=gdup[r][:, 128 * i:128 * (i + 1)], rhs=Lm[:, d, :],
                     start=True, stop=True, r=[Lm, gdup[r]], w=[pbf[i]])
                P.act('activation', out=eb_fm[r][i][:], in_=pbf[i][:, 0:128], func=AF.Exp, r=[pbf[i]],
                      w=[eb_fm[r][i]])
                P.act('activation', out=enb_fm[r][i][:], in_=pbf[i][:, 0:128], func=AF.Exp, scale=-1.0, r=[pbf[i]],
                      w=[enb_fm[r][i]])
            for i in range(2):
                qf, qs, kf, ks = fmq[r][i], fmq[r][2 + i], fmq[r][4 + i], fmq[r][6 + i]
                if lat:
                    cs = cfm[:, lt * 128:(lt + 1) * 128]
                    sn = sfm[:, lt * 128:(lt + 1) * 128]
                    P.dve('tensor_tensor', out=qf[:], in0=qf[:], in1=cs, op=ALU.mult, r=[qf, cfm], w=[qf])
                    P.pool('tensor_tensor', out=qs[:], in0=qs[:], in1=sn, op=ALU.mult, r=[qs, sfm], w=[qs])
                    P.dve('tensor_tensor', out=qf[:], in0=qf[:], in1=qs[:], op=ALU.add, r=[qf, qs], w=[qf])
                    P.dve('tensor_tensor', out=kf[:], in0=kf[:], in1=cs, op=ALU.mult, r=[kf, cfm], w=[kf])
                    P.pool('tensor_tensor', out=ks[:], in0=ks[:], in1=sn, op=ALU.mult, r=[ks, sfm], w=[ks])
                    P.dve('tensor_tensor', out=kf[:], in0=kf[:], in1=ks[:], op=ALU.add, r=[kf, ks], w=[kf])
                for hh in range(2):
                    h = 2 * i + hh
                    ps_ = slice(64 * hh, 64 * hh + 64)
                    P.dve('scalar_tensor_tensor', out=qeT[r][h][ps_, :], in0=qf[ps_, :], scalar=0.125,
                          in1=eb_fm[r][i][ps_, :], op0=ALU.mult, op1=ALU.mult, r=[qf, eb_fm[r][i]], w=[qeT[r][h]])
                    P.dve('tensor_tensor', out=keT[r][h][ps_, :], in0=kf[ps_, :], in1=enb_fm[r][i][ps_, :], op=ALU.mult,
                          r=[kf, enb_fm[r][i]], w=[keT[r][h]])
            if lat:
                for h in range(4):
                    P.dve('tensor_tensor', out=t1[r][:, 64 * h:64 * h + 64], in0=ktm[r][:, 64 * h:64 * h + 64],
                          in1=ctm[:, lt, :], op=ALU.mult, r=[ktm[r], ctm], wa=[t1[r]])
                    P.pool('tensor_tensor', out=t2[r][:, 64 * h:64 * h + 64], in0=ktm[r][:, 256 + 64 * h:256 + 64 * h + 64],
                           in1=stm[:, lt, :], op=ALU.mult, r=[ktm[r], stm], wa=[t2[r]])
                P.dve('tensor_tensor', out=t1[r][:], in0=t1[r][:], in1=t2[r][:], op=ALU.add, r=[t1[r], t2[r]], w=[t1[r]])
                ksrc, kkey = t1[r][:], t1[r]
            else:
                ksrc, kkey = ktm[r][:, 0:256], ktm[r]
            P.dve('tensor_tensor', out=ke_tm[r][:], in0=ksrc, in1=enb_tm[r][:], op=ALU.mult, r=[kkey, enb_tm[r]],
                  w=[ke_tm[r]])
            for h in range(4):
                i, hb = h // 2, 64 * (h % 2)
                P.pe('matmul', out=patt[:, 128 * h:128 * (h + 1)], lhsT=keT[r][h][:],
                     rhs=qeT[r][h][:], start=True, stop=True, r=[keT[r][h], qeT[r][h]], w=[patt])
            for h in range(4):
                P.dve('tensor_tensor', out=attT[r][:, 128 * h:128 * (h + 1)], in0=patt[:, 128 * h:128 * (h + 1)],
                      in1=Lm[:, d, :], op=ALU.mult, r=[patt, Lm], wa=[attT[r]])
            if lat or with_ctx:
                for h in range(4):
                    i, hb = h // 2, 64 * (h % 2)
                    P.pe('matmul', out=po[:, 128 * h:128 * (h + 1)], lhsT=attT[r][:, 128 * h:128 * (h + 1)],
                         rhs=vbf[r][:, 128 * h:128 * (h + 1)], start=True, stop=False, r=[attT[r], vbf[r]], w=[po])
                    P.pe('matmul', out=po[:, 128 * h:128 * (h + 1)], lhsT=qeT[r][h][:],
                         rhs=Sbf[i][:], start=False, stop=True, r=[qeT[r][h], Sbf[i]], w=[po])
                if d == 0:
                    P.act('activation', out=oB[:, tt, :], in_=po[:, :], func=AF.Copy, r=[po], w=[('oB', tt)])
                else:
                    P.dve('tensor_tensor', out=oB[:, tt, :], in0=po[:, :], in1=oB[:, tt, :], op=ALU.add,
                          r=[po, ('oB', tt)], w=[('oB', tt)])
            for i in range(2):
                P.pe('matmul', out=pss[i][:, 0:256], lhsT=ke_tm[r][:, 128 * i:128 * (i + 1)],
                     rhs=vbf[r][:, 256 * i:256 * (i + 1)], start=True, stop=True, r=[ke_tm[r], vbf[r]], w=[pss[i]])
                for hh in range(2):
                    ps_ = slice(64 * hh, 64 * hh + 64)
                    P.dve('tensor_tensor', out=stmp[ps_, :], in0=pss[i][ps_, 128 * hh:128 * (hh + 1)], in1=Sp[i][ps_, :],
                          op=ALU.add, r=[pss[i], Sp[i]], w=[(stmp.name, hh)])
                    P.dve('tensor_scalar', out=Sp[i][ps_, :], in0=stmp[ps_, :], scalar1=eb_fm[r][i][ps_, tend:tend + 1],
                          scalar2=None, op0=ALU.mult, r=[(stmp.name, hh), eb_fm[r][i]], wa=[Sp[i]])
                P.pool('tensor_copy', out=Sbf[i][:], in_=Sp[i][:], r=[Sp[i]], w=[Sbf[i]])
    S.__exit__(None, None, None)
    S = S0
    gn = S.sb("gl_gn", [128, 128], F32)
    zt = [S.sb("gl_zt%d" % i, [128, 512], F32) for i in range(2)]
    sq = [S.sb("gl_sq%d" % i, [128, 512], F32) for i in range(2)]
    ssq = [S.sb("gl_ssq%d" % i, [128, 4], F32) for i in range(2)]
    yst = S.sb("gl_yst", [128, 4, TT], BF16)
    P.dma('sp', gn[:], g.gla_norm_row[l].partition_broadcast(128), w=[gn])
    tt0 = 0 if with_ctx else 2
    for tt in range(tt0, NTT):
        r = tt % 2
        o = oB[:, tt, :]
        P.dma('sp', zt[r][:], g.ptm[b, prow(tt):prow(tt) + 128, TM_GLZ:TM_GLZ + 512], r=[('ptm', b)], w=[zt[r]])
        P.act('activation', out=zt[r][:], in_=zt[r][:], func=AF.Silu, r=[zt[r]], w=[zt[r]])
        P.pool('tensor_tensor', out=sq[r][:], in0=o, in1=o, op=ALU.mult, r=[('oB', tt)], w=[sq[r]])
        P.dve('tensor_reduce', out=ssq[r][:], in_=sq[r][:].rearrange("p (h c) -> p h c", h=4), axis=AX.X, op=ALU.add,
              r=[sq[r]], w=[ssq[r]])
        P.dve('tensor_scalar', out=ssq[r][:], in0=ssq[r][:], scalar1=1.0 / 128.0, scalar2=EPS, op0=ALU.mult,
              op1=ALU.add, r=[ssq[r]], w=[ssq[r]])
        P.act('activation', out=ssq[r][:], in_=ssq[r][:], func=AF.Sqrt, r=[ssq[r]], w=[ssq[r]])
        P.dve('reciprocal', out=ssq[r][:], in_=ssq[r][:], r=[ssq[r]], w=[ssq[r]])
        for h in range(4):
            P.dve('scalar_tensor_tensor', out=sq[r][:, 128 * h:128 * (h + 1)], in0=oB[:, tt, 128 * h:128 * (h + 1)],
                  scalar=ssq[r][:, h:h + 1], in1=gn[:], op0=ALU.mult, op1=ALU.mult, r=[('oB', tt), ssq[r], gn],
                  wa=[sq[r]])
        P.dve('tensor_tensor', out=sq[r][:], in0=sq[r][:], in1=zt[r][:], op=ALU.mult, r=[sq[r], zt[r]], w=[sq[r]])
        pt = g.psB[r]
        for k in range(4):
            P.pe('transpose', out=pt[:, k * 128:(k + 1) * 128], in_=sq[r][:, k * 128:(k + 1) * 128],
                 identity=g.ident[:], r=[sq[r], g.ident], w=[pt])
        P.act('activation', out=yst[:, :, tt * 128:(tt + 1) * 128], in_=pt[:, :].rearrange("p (k t) -> p k t", k=4),
              func=AF.Copy, r=[pt], wa=[yst])
    for k in range(4):
        if tt0 >= NTT:
            break
        P.dma('sp', g.yfm[b, 512 + k * 128:512 + (k + 1) * 128, tt0 * 128:TT], yst[:, k, tt0 * 128:TT], r=[yst],
              wa=[('yfm', b)])


GQW = 1536 + 16


def gdn_consts():
    s = np.arange(128)[:, None]
    t = np.arange(128)[None, :]
    big = np.float32(1e5)
    mbs = np.stack([np.where(t < s, 0.0, big), np.where(t > s, 0.0, big)]).astype(np.float32)
    mbi = np.stack([np.where(s <= t, 0.0, -big), np.where(s >= t, 0.0, -big)]).astype(np.float32)
    return {"gdn_mbs": np.ascontiguousarray(mbs), "gdn_mbi": np.ascontiguousarray(mbi),
            "ones128": np.ones((128, 128), np.float32)}


def phase_gdn_pre(g, l, b):
    with g.P.scope() as S:
        _phase_gdn_pre(g, l, b, S)


def _phase_gdn_pre(g, l, b, S):
    P = g.P
    wrow = S.sb("gd_wrow", [128, 5, 1536], F32)
    xs = [S.sb("gd_xs%d" % i, [128, 1536], F32) for i in range(5)]
    acc = S.sb("gd_acc", [128, GQW], F32)
    tmp = S.sb("gd_tmp", [128, 1536], F32)
    tmp2 = S.sb("gd_tmp2", [128, 1536], F32)
    ab = S.sb("gd_ab", [128, 16], F32)
    dtb = S.sb("gd_dtb", [128, 8], F32)
    eal = S.sb("gd_eal", [128, 8], F32)
    ssq = S.sb("gd_ssq", [128, 8], F32)
    sp = S.sb("gd_sp", [128, 8], F32)
    for k in range(5):
        P.dma('sp', wrow[:, k, :], g.gdn_conv_row[l, k:k + 1, :].partition_broadcast(128), wa=[wrow])
    P.dma('sp', dtb[:], g.gdn_dtb_row[l].partition_broadcast(128), w=[dtb])
    P.dma('sp', eal[:], g.gdn_alog_row[l].partition_broadcast(128), w=[eal])
    P.act('activation', out=eal[:], in_=eal[:], func=AF.Exp, r=[eal], w=[eal])
    for tt in range(NTT):
        base = prow(tt)
        for k in range(5):
            P.dma('sp', xs[k][:], g.ptm[b, base + k - 2:base + k - 2 + 128, TM_GDQKV:TM_GDQKV + 1536],
                  r=[('ptm', b)], w=[xs[k]])
        P.dma('sp', ab[:], g.ptm[b, base:base + 128, TM_GDAB:TM_GDAB + 16], r=[('ptm', b)], w=[ab])
        A = acc[:, 0:1536]
        P.dve('tensor_tensor', out=A, in0=xs[0][:], in1=wrow[:, 0, :], op=ALU.mult, r=[xs[0], wrow], w=[acc])
        for k in range(1, 5):
            eng = 'pool' if k % 2 == 1 else 'dve'
            tk = tmp if k % 2 == 1 else tmp2
            P.op(eng, 'tensor_tensor', out=tk[:], in0=xs[k][:], in1=wrow[:, k, :], op=ALU.mult, r=[xs[k], wrow],
                 w=[tk])
            P.dve('tensor_tensor', out=A, in0=A, in1=tk[:], op=ALU.add, r=[tk, acc], w=[acc])
        P.act('activation', out=A, in_=A, func=AF.Silu, r=[acc], w=[acc])
        P.pool('tensor_tensor', out=tmp[:, 0:1024], in0=acc[:, 0:1024], in1=acc[:, 0:1024], op=ALU.mult, r=[acc],
               w=[tmp])
        P.dve('tensor_reduce', out=ssq[:], in_=tmp[:, 0:1024].rearrange("p (h c) -> p h c", h=8), axis=AX.X,
              op=ALU.add, r=[tmp], w=[ssq])
        P.dve('tensor_scalar', out=ssq[:], in0=ssq[:], scalar1=EPS, scalar2=None, op0=ALU.add, r=[ssq], w=[ssq])
        P.act('activation', out=ssq[:], in_=ssq[:], func=AF.Sqrt, r=[ssq], w=[ssq])
        P.dve('reciprocal', out=ssq[:], in_=ssq[:], r=[ssq], w=[ssq])
        P.dve('tensor_scalar', out=ssq[:, 0:4], in0=ssq[:, 0:4], scalar1=128.0 ** -0.5, scalar2=None, op0=ALU.mult,
              r=[ssq], w=[ssq])
        for h8 in range(8):
            eng = 'dve' if h8 % 2 == 0 else 'pool'
            P.op(eng, 'tensor_scalar', out=acc[:, 128 * h8:128 * (h8 + 1)], in0=acc[:, 128 * h8:128 * (h8 + 1)],
                 scalar1=ssq[:, h8:h8 + 1], scalar2=None, op0=ALU.mult, r=[acc, ssq], w=[acc])
        P.act('activation', out=acc[:, 1536:1544], in_=ab[:, 8:16], func=AF.Sigmoid, r=[ab], w=[acc])
        P.dve('tensor_tensor', out=sp[:], in0=ab[:, 0:8], in1=dtb[:], op=ALU.add, r=[ab, dtb], w=[sp])
        P.act('activation', out=sp[:], in_=sp[:], func=AF.Exp, r=[sp], w=[sp])
        P.act('activation', out=sp[:], in_=sp[:], func=AF.Ln, bias=1.0, r=[sp], w=[sp])
        P.dve('scalar_tensor_tensor', out=acc[:, 1544:1552], in0=sp[:], scalar=-1.0, in1=eal[:], op0=ALU.mult,
              op1=ALU.mult, r=[sp, eal, acc], w=[acc])
        P.dma('sp', g.gq[tt * 128:(tt + 1) * 128, :], acc[:], r=[acc], wa=['gq'])


def phase_gdn(g, l, b):
    with g.P.scope() as S:
        _phase_gdn(g, l, b, S)


def _phase_gdn(g, l, b, S):
    P = g.P
    with_ctx = l < g.NL_total - 1
    oC = S.sb("gd_oC", [128, NTT, 512], F32)
    S0, S = S, Scope(P)
    Lm = S.sb("gd_Lm", [128, 2, 128], F32)
    mbs = S.sb("gd_mbs", [128, 2, 128], F32)
    mbi = S.sb("gd_mbi", [128, 2, 128], F32)
    ones = S.sb("gd_ones", [128, 128], F32)
    St = [S.sb("gd_S%d" % h, [128, 128], F32) for h in range(4)]
    qkv = [S.sb("gd_qkv%d" % i, [128, GQW], F32) for i in range(2)]
    sm = lambda n, w: S.sb("gd_" + n, [128, w], F32)
    gam, egam, bexp, nbeta, gend, kdsc, dend = (sm("gam", 4), sm("egam", 4), sm("bexp", 4), sm("nbeta", 4),
                                                sm("gend", 4), sm("kdsc", 4), sm("dend", 4))
    Lg = [sm("Lg%d" % i, 128) for i in range(4)]
    def smb(n, w):
        solve = n[:2] in ("PQ", "Y0", "Y1", "Y2", "Y3", "RH")
        return S.sb("gd_" + n, [128, w], F32 if solve else BF16)
    KQ = [smb("KQ%d" % h, 256) for h in range(4)]
    Nf = [sm("Nf%d" % i, 128) for i in range(4)]
    Sbf = [smb("Sbf%d" % h, 128) for h in range(4)]
    xx, EE, x2, E2 = ([sm("xx%d" % i, 128) for i in range(4)], [sm("EE%d" % i, 128) for i in range(4)],
                      [sm("x2%d" % i, 128) for i in range(4)], [sm("E2%d" % i, 128) for i in range(4)])
    aqk = [smb("aqk%d" % h, 128) for h in range(4)]
    PQh = [[smb("PQ%d_%d" % (h, i), 256) for i in range(2)] for h in range(4)]
    Yh = [[smb("Y%d_%d" % (h, i), 128) for i in range(2)] for h in range(4)]
    RHSu = [smb("RHSu%d" % h, 128) for h in range(4)]
    RHSw = [smb("RHSw%d" % h, 128) for h in range(4)]
    kdh = [smb("kd%d" % h, 128) for h in range(4)]
    wTn = [smb("wTn%d" % h, 128) for h in range(4)]
    esb = [smb("esb%d" % h, 128) for h in range(4)]
    o1 = [sm("o1%d" % h, 128) for h in range(4)]
    P.dma('sp', Lm[:], g.gla_Lm.rearrange("d s t -> s d t"), w=[Lm])
    P.dma('sp', mbs[:], g.gdn_mbs.rearrange("d s t -> s d t"), w=[mbs])
    P.dma('sp', mbi[:], g.gdn_mbi.rearrange("d s t -> s d t"), w=[mbi])
    P.dma('sp', ones[:], g.ones128, w=[ones])
    pA0, pG = g.psA[0], g.psA[1]
    it = 0
    for d in range(2):
        order = list(range(NTT)) if d == 0 else [1, 0] + list(range(NTT - 1, 1, -1))
        send = 127 if d == 0 else 0
        for h in range(4):
            P.dve('memset', ap=St[h][:], constant=0.0, w=[St[h]])
            P.pool('memset', ap=Sbf[h][:], constant=0.0, w=[Sbf[h]])
        for tt in order:
            r = it % 2
            it += 1
            X = qkv[r]
            P.dma('sp', X[:], g.gq[tt * 128:(tt + 1) * 128, :], r=['gq'], w=[X])
            bet = X[:, 1536 + 4 * d:1536 + 4 * d + 4]
            gg = X[:, 1544 + 4 * d:1544 + 4 * d + 4]
            P.pe('matmul', out=pA0[:, 0:4], lhsT=Lm[:, d, :], rhs=gg, start=True, stop=True, r=[Lm, X], w=[pA0])
            P.act('activation', out=gam[:], in_=pA0[:, 0:4], func=AF.Copy, r=[pA0], w=[gam])
            P.act('activation', out=egam[:], in_=pA0[:, 0:4], func=AF.Exp, r=[pA0], w=[egam])
            P.dve('tensor_tensor', out=bexp[:], in0=egam[:], in1=bet, op=ALU.mult, r=[egam, X], w=[bexp])
            P.dve('tensor_scalar', out=nbeta[:], in0=bet, scalar1=-1.0, scalar2=None, op0=ALU.mult, r=[X], w=[nbeta])
            for h in range(4):
                P.dve('tensor_scalar', out=Lg[h][:], in0=Lm[:, d, :], scalar1=gg[:, h:h + 1], scalar2=None,
                      op0=ALU.mult, r=[Lm, X], w=[Lg[h]])
            for h in range(4):
                P.pe('matmul', out=pG[:, 128 * h:128 * (h + 1)], lhsT=ones[:], rhs=Lg[h][:], start=True, stop=True,
                     r=[ones, Lg[h]], w=[pG])
            gendv = pG[:, :].rearrange("p (h s) -> p h s", h=4)[:, :, send]
            P.act('activation', out=gend[:], in_=gendv, func=AF.Copy, r=[pG], w=[gend])
            P.act('activation', out=dend[:], in_=gend[:], func=AF.Exp, r=[gend], w=[dend])
            P.dve('tensor_tensor', out=kdsc[:], in0=gend[:], in1=gam[:], op=ALU.subtract, r=[gend, gam], w=[kdsc])
            P.act('activation', out=kdsc[:], in_=kdsc[:], func=AF.Exp, r=[kdsc], w=[kdsc])
            qs_ = lambda h: X[:, 128 * h:128 * (h + 1)]
            ks_ = lambda h: X[:, 512 + 128 * h:512 + 128 * (h + 1)]
            vs_ = lambda h: X[:, 1024 + 128 * h:1024 + 128 * (h + 1)]
            H4 = range(4)
            for h in H4:
                pb = g.psB[h]
                P.pe('transpose', out=pb[:, 0:128], in_=ks_(h), identity=g.ident[:], r=[X, g.ident], w=[pb])
                P.pe('transpose', out=pb[:, 128:256], in_=qs_(h), identity=g.ident[:], r=[X, g.ident], w=[pb])
            for h in H4:
                P.act('activation', out=KQ[h][:], in_=g.psB[h][:, 0:256], func=AF.Copy, r=[g.psB[h]], w=[KQ[h]])
            for h in H4:
                pb = g.psB[h]
                kTh, qTh = KQ[h][:, 0:128], KQ[h][:, 128:256]
                P.pe('matmul', out=pb[:, 256:384], lhsT=kTh, rhs=kTh, start=True, stop=True, r=[KQ[h]], w=[pb])
                P.pe('matmul', out=pb[:, 384:512], lhsT=kTh, rhs=qTh, start=True, stop=True, r=[KQ[h]], w=[pb])
            for h in H4:
                Gam = pG[:, 128 * h:128 * (h + 1)]
                P.dve('scalar_tensor_tensor', out=xx[h][:], in0=Gam, scalar=gam[:, h:h + 1], in1=mbs[:, d, :],
                      op0=ALU.subtract, op1=ALU.max, r=[pG, gam, mbs], w=[xx[h]])
                P.dve('scalar_tensor_tensor', out=x2[h][:], in0=Gam, scalar=gam[:, h:h + 1], in1=mbi[:, d, :],
                      op0=ALU.subtract, op1=ALU.min, r=[pG, gam, mbi], w=[x2[h]])
            for h in H4:
                P.act('activation', out=EE[h][:], in_=xx[h][:], func=AF.Exp, scale=-1.0, r=[xx[h]], w=[EE[h]])
                P.act('activation', out=E2[h][:], in_=x2[h][:], func=AF.Exp, r=[x2[h]], w=[E2[h]])
            for h in H4:
                pb = g.psB[h]
                P.dve('scalar_tensor_tensor', out=Nf[h][:], in0=pb[:, 256:384], scalar=nbeta[:, h:h + 1],
                      in1=EE[h][:], op0=ALU.mult, op1=ALU.mult, r=[pb, nbeta, EE[h]], w=[Nf[h]])
                P.dve('tensor_tensor', out=aqk[h][:], in0=pb[:, 384:512], in1=E2[h][:], op=ALU.mult,
                      r=[pb, E2[h]], w=[aqk[h]])
            for h in H4:
                P.pool('tensor_copy', out=PQh[h][0][:, 0:128], in_=Nf[h][:], r=[Nf[h]], w=[PQh[h][0]])
                P.pe('transpose', out=g.psB[h][:, 128:256], in_=Nf[h][:], identity=g.ident[:],
                     r=[Nf[h], g.ident], w=[g.psB[h]])
            for h in range(4):
                P.act('activation', out=PQh[h][0][:, 128:256], in_=g.psB[h][:, 128:256], func=AF.Copy,
                      r=[g.psB[h]], w=[PQh[h][0]])
            for h in range(4):
                P.op('dve' if h % 2 == 0 else 'pool', 'tensor_tensor', out=Yh[h][0][:], in0=PQh[h][0][:, 128:256],
                     in1=g.ident[:], op=ALU.add, r=[PQh[h][0], g.ident], w=[Yh[h][0]])
            yi = 0
            for j in range(6):
                for h in range(4):
                    cur = PQh[h][j % 2]
                    P.pe('matmul', out=g.psB[h][:, 0:128], lhsT=cur[:, 128:256], rhs=cur[:, 0:128], start=True,
                         stop=True, r=[cur], w=[g.psB[h]])
                    if j < 5:
                        P.pe('matmul', out=g.psB[h][:, 128:256], lhsT=cur[:, 0:128], rhs=cur[:, 128:256], start=True,
                             stop=True, r=[cur], w=[g.psB[h]])
                for h in range(4):
                    nxt = PQh[h][(j + 1) % 2]
                    wid = 256 if j < 5 else 128
                    P.act('activation', out=nxt[:, 0:wid], in_=g.psB[h][:, 0:wid], func=AF.Copy, r=[g.psB[h]],
                          w=[nxt])
                for h in range(4):
                    nxt = PQh[h][(j + 1) % 2]
                    P.pe('matmul', out=g.psA[h][:, 0:128], lhsT=nxt[:, 0:128], rhs=Yh[h][yi][:], start=True, stop=True,
                         r=[nxt, Yh[h][yi]], w=[g.psA[h]])
                for h in range(4):
                    P.dve('tensor_tensor', out=Yh[h][1 - yi][:], in0=g.psA[h][:, 0:128], in1=Yh[h][yi][:], op=ALU.add,
                          r=[g.psA[h], Yh[h][yi]], w=[Yh[h][1 - yi]])
                yi = 1 - yi
            need_o = (tt >= 2 or with_ctx)
            for h in range(4):
                P.dve('tensor_scalar', out=RHSu[h][:], in0=vs_(h), scalar1=bet[:, h:h + 1], scalar2=None,
                      op0=ALU.mult, r=[X], w=[RHSu[h]])
                P.pool('tensor_scalar', out=RHSw[h][:], in0=ks_(h), scalar1=bexp[:, h:h + 1], scalar2=None,
                       op0=ALU.mult, r=[X, bexp], w=[RHSw[h]])
                P.pool('tensor_scalar', out=kdh[h][:], in0=ks_(h), scalar1=kdsc[:, h:h + 1], scalar2=None,
                       op0=ALU.mult, r=[X, kdsc], w=[kdh[h]])
            for h in range(4):
                P.pe('matmul', out=g.psA[h][:, 128:256], lhsT=RHSw[h][:], rhs=Yh[h][yi][:], start=True, stop=True,
                     r=[RHSw[h], Yh[h][yi]], w=[g.psA[h]])
            for h in range(4):
                P.act('activation', out=wTn[h][:], in_=g.psA[h][:, 128:256], func=AF.Copy, scale=-1.0,
                      r=[g.psA[h]], w=[wTn[h]])
            for h in range(4):
                P.pe('matmul', out=g.psA[h][:, 0:128], lhsT=Yh[h][yi][:], rhs=RHSu[h][:], start=True, stop=False,
                     r=[Yh[h][yi], RHSu[h]], w=[g.psA[h]])
                P.pe('matmul', out=g.psA[h][:, 0:128], lhsT=wTn[h][:], rhs=Sbf[h][:], start=False, stop=True,
                     r=[wTn[h], Sbf[h]], w=[g.psA[h]])
            for h in range(4):
                P.act('activation', out=esb[h][:], in_=g.psA[h][:, 0:128], func=AF.Copy, r=[g.psA[h]], w=[esb[h]])
            for h in range(4):
                P.pe('matmul', out=g.psB[h][:, 256:384], lhsT=kdh[h][:], rhs=esb[h][:], start=True, stop=True,
                     r=[kdh[h], esb[h]], w=[g.psB[h]])
                if need_o:
                    P.pe('matmul', out=g.psB[h][:, 0:128], lhsT=KQ[h][:, 128:256], rhs=Sbf[h][:], start=True, stop=True,
                         r=[KQ[h], Sbf[h]], w=[g.psB[h]])
                    P.pe('matmul', out=g.psB[h][:, 128:256], lhsT=aqk[h][:], rhs=esb[h][:], start=True, stop=True,
                         r=[aqk[h], esb[h]], w=[g.psB[h]])
            if need_o:
                for h in range(4):
                    P.act('activation', out=o1[h][:], in_=g.psB[h][:, 0:128], func=AF.Copy, scale=egam[:, h:h + 1],
                          r=[g.psB[h], egam], w=[o1[h]])
                for h in range(4):
                    oslc = oC[:, tt, 128 * h:128 * (h + 1)]
                    if d == 0:
                        P.dve('tensor_tensor', out=oslc, in0=g.psB[h][:, 128:256], in1=o1[h][:], op=ALU.add,
                              r=[g.psB[h], o1[h]], w=[('oC', tt, h)])
                    else:
                        P.dve('tensor_tensor', out=o1[h][:], in0=g.psB[h][:, 128:256], in1=o1[h][:], op=ALU.add,
                              r=[g.psB[h], o1[h]], w=[o1[h]])
                        P.pool('tensor_tensor', out=oslc, in0=oslc, in1=o1[h][:], op=ALU.add,
                               r=[o1[h], ('oC', tt, h)], w=[('oC', tt, h)])
            for h in range(4):
                P.dve('scalar_tensor_tensor', out=St[h][:], in0=St[h][:], scalar=dend[:, h:h + 1],
                      in1=g.psB[h][:, 256:384], op0=ALU.mult, op1=ALU.add, r=[St[h], dend, g.psB[h]], w=[St[h]])
                P.pool('tensor_copy', out=Sbf[h][:], in_=St[h][:], r=[St[h]], w=[Sbf[h]])
    S.__exit__(None, None, None)
    S = S0
    okeys = lambda tt: [('oC', tt, h) for h in range(4)]
    branch_finish(g, l, b, S, oC, okeys, g.gdn_norm_row[l], TM_GDZ, 1024, with_ctx)


def branch_finish(g, l, b, S, oB, okeys, norm_row, zcol, yrow0, with_ctx):
    P = g.P
    gn = S.sb("bf_gn", [128, 128], F32)
    zt = [S.sb("bf_zt%d" % i, [128, 512], F32) for i in range(2)]
    sq = [S.sb("bf_sq%d" % i, [128, 512], F32) for i in range(2)]
    ssq = [S.sb("bf_ssq%d" % i, [128, 4], F32) for i in range(2)]
    yst = S.sb("bf_yst", [128, 4, TT], BF16)
    P.dma('sp', gn[:], norm_row.partition_broadcast(128), w=[gn])
    tt0 = 0 if with_ctx else 2
    for tt in range(tt0, NTT):
        r = tt % 2
        o = oB[:, tt, :]
        P.dma('sp', zt[r][:], g.ptm[b, prow(tt):prow(tt) + 128, zcol:zcol + 512], r=[('ptm', b)], w=[zt[r]])
        P.act('activation', out=zt[r][:], in_=zt[r][:], func=AF.Silu, r=[zt[r]], w=[zt[r]])
        P.pool('tensor_tensor', out=sq[r][:], in0=o, in1=o, op=ALU.mult, r=okeys(tt), w=[sq[r]])
        P.dve('tensor_reduce', out=ssq[r][:], in_=sq[r][:].rearrange("p (h c) -> p h c", h=4), axis=AX.X, op=ALU.add,
              r=[sq[r]], w=[ssq[r]])
        P.dve('tensor_scalar', out=ssq[r][:], in0=ssq[r][:], scalar1=1.0 / 128.0, scalar2=EPS, op0=ALU.mult,
              op1=ALU.add, r=[ssq[r]], w=[ssq[r]])
        P.act('activation', out=ssq[r][:], in_=ssq[r][:], func=AF.Sqrt, r=[ssq[r]], w=[ssq[r]])
        P.dve('reciprocal', out=ssq[r][:], in_=ssq[r][:], r=[ssq[r]], w=[ssq[r]])
        for h in range(4):
            P.dve('scalar_tensor_tensor', out=sq[r][:, 128 * h:128 * (h + 1)], in0=oB[:, tt, 128 * h:128 * (h + 1)],
                  scalar=ssq[r][:, h:h + 1], in1=gn[:], op0=ALU.mult, op1=ALU.mult, r=okeys(tt) + [ssq[r], gn],
                  wa=[sq[r]])
        P.dve('tensor_tensor', out=sq[r][:], in0=sq[r][:], in1=zt[r][:], op=ALU.mult, r=[sq[r], zt[r]], w=[sq[r]])
        pt = g.psB[r]
        for k in range(4):
            P.pe('transpose', out=pt[:, k * 128:(k + 1) * 128], in_=sq[r][:, k * 128:(k + 1) * 128],
                 identity=g.ident[:], r=[sq[r], g.ident], w=[pt])
        P.act('activation', out=yst[:, :, tt * 128:(tt + 1) * 128], in_=pt[:, :].rearrange("p (k t) -> p k t", k=4),
              func=AF.Copy, r=[pt], wa=[yst])
    for k in range(4):
        P.dma('sp', g.yfm[b, yrow0 + k * 128:yrow0 + (k + 1) * 128, tt0 * 128:TT], yst[:, k, tt0 * 128:TT], r=[yst],
              wa=[('yfm', b)])


def phase_merge(g, l, b):
    with g.P.scope() as S:
        _phase_merge(g, l, b, S)


def _phase_merge(g, l, b, S):
    P = g.P
    with_ctx = l < g.NL_total - 1
    last = not with_ctx
    if with_ctx:
        halves = [(0, 1280), (1280, 1024)]
    else:
        halves = [(256, 1024), (1280, 1024)]
    ss = S.sb("mg_ss", [128, NTT, 4], F32)
    bgc = S.sb("mg_bgc", [128, 4, 16], F32)
    mT = S.sb("mg_mT", [128, 16, 1280], BF16)
    P.dma('sp', bgc[:], g.b_gate_col[l], w=[bgc])
    for (t0, n) in halves:
        chunks = [(c0, min(512, n - c0)) for c0 in range(0, n, 512)]
        nch = len(chunks)
        with P.scope() as SA:
            yT = SA.sb("mg_yT", [128, 16, 1280], BF16)
            wgs = [SA.sb("mg_wgs%d" % i, [128, 16, 128], F32) for i in range(2)]
            wgb = [SA.sb("mg_wgb%d" % i, [128, 16, 128], BF16) for i in range(2)]
            wbs = [SA.sb("mg_wbs%d" % i, [128, 4, 128], F32) for i in range(2)]
            wbb = [SA.sb("mg_wbb%d" % i, [128, 4, 128], BF16) for i in range(2)]
            acc = SA.sb("mg_acc", [128, 1280], F32)
            sig = [SA.sb("mg_sig%d" % i, [128, 512], F32) for i in range(2)]
            tmp = [SA.sb("mg_tmp%d" % i, [128, 512], F32) for i in range(2)]
            for kt in range(16):
                P.dma('sp', yT[:, kt, 0:n], g.yfm[b, kt * 128:(kt + 1) * 128, t0:t0 + n], r=[('yfm', b)], wa=[yT])
            it = 0
            steps = [(ft, i) for ft in range(16) for i in range(4)]

            def prep(k):
                ft, i = steps[k]
                w2 = k % 2
                P.dma('sp', wgs[w2][:], g.w_gate_t[l, i, ft], w=[wgs[w2]])
                P.act('activation', out=wgb[w2][:], in_=wgs[w2][:], func=AF.Copy, r=[wgs[w2]], w=[wgb[w2]])
                P.dma('sp', wbs[w2][:], g.w_branch_t[l, i, ft], w=[wbs[w2]])
                P.act('activation', out=wbb[w2][:], in_=wbs[w2][:], func=AF.Copy, r=[wbs[w2]], w=[wbb[w2]])

            prep(0)
            for k, (ft, i) in enumerate(steps):
                w2 = k % 2
                if k + 1 < len(steps):
                    prep(k + 1)
                if True:
                    for c, (c0, cw) in enumerate(chunks):
                        pg_, pb_ = g.psA[it % 4], g.psB[it % 4]
                        i2 = it % 2
                        it += 1
                        for kt in range(16):
                            P.pe('matmul', out=pg_[:, 0:cw], lhsT=wgb[w2][:, kt, :],
                                 rhs=g.uT[:, kt, t0 + c0:t0 + c0 + cw], start=(kt == 0), stop=(kt == 15),
                                 r=[wgb[w2], g.uT], w=[pg_], defer=(kt < 15))
                        for kt in range(4):
                            P.pe('matmul', out=pb_[:, 0:cw], lhsT=wbb[w2][:, kt, :],
                                 rhs=yT[:, 4 * i + kt, c0:c0 + cw], start=(kt == 0), stop=(kt == 3),
                                 r=[wbb[w2], yT], w=[pb_], defer=(kt < 3))
                        P.act('activation', out=sig[i2][:, 0:cw], in_=pg_[:, 0:cw], func=AF.Sigmoid,
                              bias=bgc[:, i, ft:ft + 1], r=[pg_, bgc], w=[sig[i2]])
                        asl = acc[:, c0:c0 + cw]
                        if i == 0:
                            P.dve('tensor_tensor', out=asl, in0=pb_[:, 0:cw], in1=sig[i2][:, 0:cw], op=ALU.mult,
                                  r=[pb_, sig[i2]], w=[('mg_acc', c)])
                        else:
                            P.dve('tensor_tensor', out=tmp[i2][:, 0:cw], in0=pb_[:, 0:cw], in1=sig[i2][:, 0:cw],
                                  op=ALU.mult, r=[pb_, sig[i2]], w=[tmp[i2]])
                            P.pool('tensor_tensor', out=asl, in0=asl, in1=tmp[i2][:, 0:cw], op=ALU.add,
                                   r=[tmp[i2], ('mg_acc', c)], w=[('mg_acc', c)])
                if i < 3:
                    continue
                P.act('activation', out=mT[:, ft, 0:n], in_=acc[:, 0:n], func=AF.Copy,
                      r=[('mg_acc', c) for c in range(nch)], wa=[mT])
        with P.scope() as SB:
            wos = [SB.sb("mg_wos%d" % i, [128, 16, 512], F32) for i in range(1)]
            wob = [SB.sb("mg_wob%d" % i, [128, 16, 512], BF16) for i in range(2)]
            yst = [SB.sb("mg_yst%d" % i, [128, 512], F32) for i in range(3)]
            junk = SB.sb("mg_junk", [128, 512], F32)
            it = 0
            for cc in range(4):
                P.dma('sp', wos[0][:], g.w_out[l, :, cc * 512:(cc + 1) * 512].rearrange("(kt p) f -> p kt f", p=128),
                      w=[wos[0]])
                for q4 in range(4):
                    if q4 % 2 == 0:
                        P.dve('tensor_copy', out=wob[cc % 2][:, 4 * q4:4 * q4 + 4, :],
                              in_=wos[0][:, 4 * q4:4 * q4 + 4, :], r=[wos[0]], w=[(wob[cc % 2].name, q4)])
                    else:
                        P.act('activation', out=wob[cc % 2][:, 4 * q4:4 * q4 + 4, :],
                              in_=wos[0][:, 4 * q4:4 * q4 + 4, :], func=AF.Copy, r=[wos[0]],
                              w=[(wob[cc % 2].name, q4)])
                for ti in range(n // 128):
                    tt = (t0 + ti * 128) // 128
                    ps = g.psA[it % 4]
                    st = yst[it % 3]
                    it += 1
                    for kt in range(16):
                        P.pe('matmul', out=ps[:, :], lhsT=mT[:, kt, ti * 128:(ti + 1) * 128], rhs=wob[cc % 2][:, kt, :],
                             start=(kt == 0), stop=(kt == 15), r=[mT, (wob[cc % 2].name, kt // 4)], w=[ps],
                             defer=(kt < 15))
                    P.act('activation', out=st[:], in_=ps[:, :], func=AF.Copy, r=[ps], w=[st])
                    P.dve('tensor_tensor', out=junk[:], in0=st[:], in1=st[:], op=ALU.mult, r=[st], w=[junk])
                    P.dve('tensor_reduce', out=ss[:, tt, cc:cc + 1], in_=junk[:], axis=AX.X, op=ALU.add, r=[junk],
                          wa=[('mg_ss', tt)])
                    P.dma('pool', g.ybuf[tt * 128:(tt + 1) * 128, cc * 512:(cc + 1) * 512], st[:], r=[st],
                          wa=['ybuf'])
    with P.scope() as SC:
        GG = SC.sb("mg_GG", [128, 2, D], F32)
        yt = [SC.sb("mg_yt%d" % i, [128, D], F32) for i in range(2)]
        ht = [SC.sb("mg_ht%d" % i, [128, D], F32) for i in range(2)]
        rs = [SC.sb("mg_rs%d" % i, [128, 2], F32) for i in range(2)]
        for vi, j in enumerate((b, 2)):
            for jc in range(4):
                pb = g.psA[(vi * 4 + jc) % 4]
                P.pe('matmul', out=pb[:, :], lhsT=g.sel[:, j, :], rhs=g.grow[:, jc * 512:(jc + 1) * 512],
                     start=True, stop=True, r=[g.sel, g.grow], w=[pb])
                P.act('activation', out=GG[:, vi, jc * 512:(jc + 1) * 512], in_=pb[:, :], func=AF.Copy,
                      r=[pb], wa=[GG])
        src = g.xin if l == 0 else g.hbuf
        srckey = [] if l == 0 else [('hbuf', b)]
        tt0 = 0 if with_ctx else 2
        for tt in range(tt0, NTT):
            r = tt % 2
            vi = 1 if tt < 2 else 0
            P.dma('sp', yt[r][:], g.ybuf[tt * 128:(tt + 1) * 128, :], r=['ybuf'], w=[yt[r]])
            P.dma('sp', ht[r][:], src[b, tt * 128:(tt + 1) * 128, :], r=srckey, w=[ht[r]])
            P.dve('tensor_reduce', out=rs[r][:, 0:1], in_=ss[:, tt, :], axis=AX.X, op=ALU.add, r=[('mg_ss', tt)],
                  w=[rs[r]])
            P.dve('tensor_scalar', out=rs[r][:, 1:2], in0=rs[r][:, 0:1], scalar1=1.0 / D, scalar2=EPS, op0=ALU.mult,
                  op1=ALU.add, r=[rs[r]], w=[rs[r]])
            P.act('activation', out=rs[r][:, 1:2], in_=rs[r][:, 1:2], func=AF.Sqrt, r=[rs[r]], w=[rs[r]])
            P.dve('reciprocal', out=rs[r][:, 1:2], in_=rs[r][:, 1:2], r=[rs[r]], w=[rs[r]])
            P.dve('scalar_tensor_tensor', out=yt[r][:], in0=yt[r][:], scalar=rs[r][:, 1:2], in1=GG[:, vi, :],
                  op0=ALU.mult, op1=ALU.mult, r=[yt[r], rs[r], GG], w=[yt[r]])
            P.pool('tensor_tensor', out=ht[r][:], in0=ht[r][:], in1=yt[r][:], op=ALU.add, r=[yt[r], ht[r]], w=[ht[r]])
            if last:
                P.dma('sp', g.out[b, (tt - 2) * 128:(tt - 1) * 128, :], ht[r][:], r=[ht[r]], wa=['out'])
            else:
                P.dma('sp', g.hbuf[b, tt * 128:(tt + 1) * 128, :], ht[r][:], r=[ht[r]], wa=[('hbuf', b)])


N_CORES = 8
_PROGRAM = {}


def kernel(**inputs):
    sh = prep_shared(inputs)
    if "nc" not in _PROGRAM:
        _PROGRAM["nc"] = build_program(NB=2, NL=2)[0]
    nc = _PROGRAM["nc"]
    in_maps = []
    for c in range(N_CORES):
        m = dict(sh)
        m.update(prep_core(inputs, [2 * c, 2 * c + 1]))
        in_maps.append(m)
    res = run_bass_kernel_spmd(nc, in_maps, core_ids=list(range(N_CORES)))
    out = np.concatenate([np.asarray(res.results[c]["out"]) for c in range(N_CORES)], axis=0)
    return np.ascontiguousarray(out.astype(np.float32))
```

```python
import numpy as np
import concourse.bass as bass
import concourse.mybir as mybir
from concourse.bass_utils import run_bass_kernel_spmd

F32 = mybir.dt.float32
BF16 = mybir.dt.bfloat16
AF = mybir.ActivationFunctionType
ALU = mybir.AluOpType
AX = mybir.AxisListType


class Prog:
    NHW = 16
    NDMA = NHW + 8

    def __init__(self, nc, same_engine_sync=True):
        self.nc = nc
        self.E = {'pe': nc.tensor, 'act': nc.scalar, 'dve': nc.vector, 'pool': nc.gpsimd, 'sp': nc.sync}
        self.sem = {e: nc.alloc_semaphore(name="sem_" + e) for e in ('pe', 'act', 'dve', 'pool')}
        self.cnt = {e: 0 for e in self.sem}
        self.dsem = [nc.alloc_semaphore(name="dsem%d" % i) for i in range(self.NDMA)]
        self.dval = [0] * self.NDMA
        self.dnext = 0
        self.dnext_sw = 0
        self.known = {e: {} for e in self.E}
        self.evclock = {}
        self.lastw = {}
        self.readers = {}
        self.same = same_engine_sync
        self.nwaits = 0
        self.ninst = 0
        self.pending = {}
        self._n = 0

    def sb(self, name, shape, dtype):
        return self.nc.alloc_sbuf_tensor(name, list(shape), dtype)

    def ps(self, name, shape, dtype=F32):
        return self.nc.alloc_psum_tensor(name, list(shape), dtype)

    def dram(self, name, shape, dtype, kind="Internal"):
        return self.nc.dram_tensor(name, list(shape), dtype, kind=kind)

    def _semof(self, k):
        return self.sem[k] if isinstance(k, str) else self.dsem[k[1]]

    def _wait(self, X, ev):
        k, v = ev
        kn = self.known[X]
        if kn.get(k, 0) >= v:
            return
        self.E[X].wait_ge(self._semof(k), v)
        self.nwaits += 1
        clk = self.evclock.get(ev)
        if clk:
            for kk, vv in clk.items():
                if kn.get(kk, 0) < vv:
                    kn[kk] = vv
        if kn.get(k, 0) < v:
            kn[k] = v

    @staticmethod
    def _keys(ks):
        return [k if isinstance(k, (str, tuple)) else k.name for k in ks]

    def _deps(self, X, r, w, wa=()):
        for key in r:
            for ev in self.lastw.get(key, ()):
                yield ev
        for key in w:
            for ev in self.lastw.get(key, ()):
                yield ev
            for ev in self.readers.get(key, ()):
                yield ev
        for key in wa:
            for ev in self.readers.get(key, ()):
                yield ev

    def _record(self, ev, r, w, wa=()):
        for key in r:
            lst = self.readers.setdefault(key, [])
            lst[:] = [e for e in lst if e[0] != ev[0]]
            lst.append(ev)
        for key in w:
            self.lastw[key] = [ev]
            self.readers[key] = []
        for key in wa:
            lst = self.lastw.setdefault(key, [])
            lst[:] = [e for e in lst if e[0] != ev[0]]
            lst.append(ev)

    def op(self, X, method, r=(), w=(), wa=(), defer=False, **kw):
        r = self._keys(r)
        w = self._keys(w)
        wa = self._keys(wa)
        pend = self.pending.setdefault(X, [[], [], []])
        if defer:
            assert X == 'pe'
            for ev in list(self._deps(X, r, w, wa)):
                if ev[0] == X:
                    continue
                self._wait(X, ev)
            getattr(self.E[X], method)(**kw)
            pend[0] += r
            pend[1] += w
            pend[2] += wa
            self.ninst += 1
            return None
        if pend[0] or pend[1] or pend[2]:
            r = list(dict.fromkeys(r + pend[0]))
            w = list(dict.fromkeys(w + pend[1]))
            wa = list(dict.fromkeys(wa + pend[2]))
            self.pending[X] = [[], [], []]
        for ev in list(self._deps(X, r, w, wa)):
            if ev[0] == X and (X == 'pe' or not self.same):
                continue
            self._wait(X, ev)
        inst = getattr(self.E[X], method)(**kw)
        self.cnt[X] += 1
        c = self.cnt[X]
        inst.then_inc(self.sem[X], 1)
        ev = (X, c)
        clk = dict(self.known[X])
        clk[X] = c
        self.evclock[ev] = clk
        self._record(ev, r, w, wa)
        self.ninst += 1
        return ev

    def pe(self, method, **kw):
        return self.op('pe', method, **kw)

    def act(self, method, **kw):
        return self.op('act', method, **kw)

    def dve(self, method, **kw):
        return self.op('dve', method, **kw)

    def pool(self, method, **kw):
        return self.op('pool', method, **kw)

    def dma(self, Q, out, in_, r=(), w=(), wa=(), **kw):
        r = self._keys(r)
        w = self._keys(w)
        wa = self._keys(wa)
        if Q == 'pool':
            i = self.NHW + self.dnext_sw
            self.dnext_sw = (self.dnext_sw + 1) % (self.NDMA - self.NHW)
        else:
            i = self.dnext
            self.dnext = (i + 1) % self.NHW
        k = ('d', i)
        if self.dval[i] > 0:
            self._wait(Q, (k, self.dval[i]))
        for ev in list(self._deps(Q, r, w, wa)):
            self._wait(Q, ev)
        inst = self.E[Q].dma_start(out=out, in_=in_, **kw)
        self.dval[i] += 16
        inst.then_inc(self.dsem[i], 16)
        ev = (k, self.dval[i])
        clk = dict(self.known[Q])
        clk[k] = self.dval[i]
        self.evclock[ev] = clk
        self._record(ev, r, w, wa)
        self.ninst += 1
        return ev

    def barrier(self):
        for X in ('pe', 'act', 'dve', 'pool', 'sp'):
            for e in self.sem:
                if self.cnt[e] > 0 and not (e == X and X == 'pe'):
                    self._wait(X, (e, self.cnt[e]))
            for i in range(self.NDMA):
                if self.dval[i] > 0:
                    self._wait(X, (('d', i), self.dval[i]))

    def scope(self):
        return Scope(self)

    def finish(self):
        for i in range(self.NDMA):
            if self.dval[i] > 0:
                self._wait('sp', (('d', i), self.dval[i]))
        for e in self.sem:
            if self.cnt[e] > 0:
                self._wait('sp', (e, self.cnt[e]))


D = 2048
TCTX = 256
TLAT = 2048
TT = TCTX + TLAT
NTT = TT // 128
N_TM = 6672
N_FM = 2080
PROWS = 2312
EPS = 1e-6

TM_NAV, TM_NAZ, TM_GLK, TM_GLKS, TM_GLV, TM_GLZ = 0, 512, 1024, 1280, 1536, 2048
TM_GDQKV, TM_GDZ, TM_HYXV, TM_HYZ, TM_GDAB = 2560, 4096, 4608, 6144, 6656
FM_NAQ, FM_NAK, FM_GLQ, FM_GLQS, FM_GLK, FM_GLKS, FM_GLG = 0, 512, 1024, 1280, 1536, 1792, 2048


def prow(tt):
    return 2 + tt * 128 if tt < 2 else 262 + (tt - 2) * 128


def w_in_perm():
    o = {}
    off = 0
    for name, w in (("na_q", 512), ("na_k", 512), ("na_v", 512), ("na_z", 512), ("gla_q", 256), ("gla_k", 256),
                    ("gla_v", 512), ("gla_z", 512), ("gla_g", 32), ("gdn_qkv", 1536), ("gdn_z", 512),
                    ("gdn_a", 8), ("gdn_b", 8), ("hy_xv", 1536), ("hy_z", 512)):
        o[name] = np.arange(off, off + w)
        off += w
    assert off == 7728

    def sw(ix):
        return ix.reshape(-1, 2, 32)[:, ::-1, :].reshape(-1)
    tm = np.concatenate([o["na_v"], o["na_z"], o["gla_k"], sw(o["gla_k"]), o["gla_v"], o["gla_z"], o["gdn_qkv"],
                         o["gdn_z"], o["hy_xv"], o["hy_z"], o["gdn_a"], o["gdn_b"]])
    fm = np.concatenate([o["na_q"], o["na_k"], o["gla_q"], sw(o["gla_q"]), o["gla_k"], sw(o["gla_k"]), o["gla_g"]])
    assert tm.size == N_TM and fm.size == N_FM
    return tm, fm


class Ctx:
    pass


class Scope:
    _uid = [0]

    def __init__(self, P):
        self.P = P
        self.cms = []

    def __enter__(self):
        return self

    def sb(self, name, shape, dtype):
        Scope._uid[0] += 1
        cm = self.P.nc.sbuf_tensor("%s_%d" % (name, Scope._uid[0]), list(shape), dtype)
        t = cm.__enter__()
        self.cms.append(cm)
        return t

    def ps(self, name, shape, dtype=F32):
        Scope._uid[0] += 1
        cm = self.P.nc.psum_tensor("%s_%d" % (name, Scope._uid[0]), list(shape), dtype)
        t = cm.__enter__()
        self.cms.append(cm)
        return t

    def __exit__(self, *a):
        self.P.barrier()
        for cm in reversed(self.cms):
            cm.__exit__(None, None, None)
        return False


def build_program(NB=2, NL=2, dbg=(), stop_after=None, branches="ABCDM"):
    nc = bass.Bass("TRN2", target_bir_lowering=False)
    P = Prog(nc)
    g = Ctx()
    g.nc, g.P, g.NB, g.NL = nc, P, NB, NL
    g.NL_total = 2
    g.branches = branches

    def din(name, shape, dt=F32):
        return nc.dram_tensor(name, list(shape), dt, kind="ExternalInput").ap()

    def dscr(name, shape, dt=F32):
        kind = "ExternalOutput" if name in dbg else "Internal"
        return nc.dram_tensor(name, list(shape), dt, kind=kind).ap()

    g.xin = din("xin", [NB, TT, D])
    g.cT = din("cT", [128, 16, 3])
    g.w_mod_t = din("w_mod_t", [2, 32, 128, 16, 128])
    g.w_mod_g = din("w_mod_g", [2, D, D])
    g.b_mod_col = din("b_mod_col", [2, 128, 32])
    g.b_mod_gate = din("b_mod_gate", [2, 1, D])
    g.g_pre_col = din("g_pre_col", [2, 128, 16])
    g.g_post_row = din("g_post_row", [2, 1, D])
    g.w_tm = din("w_tm", [2, D, N_TM])
    g.w_fm_t = din("w_fm_t", [2, 17, 128, 16, 128])
    g.w_gate_t = din("w_gate_t", [2, 4, 16, 128, 16, 128])
    g.b_gate_col = din("b_gate_col", [2, 128, 4, 16])
    g.w_branch_t = din("w_branch_t", [2, 4, 16, 128, 4, 128])
    g.w_out = din("w_out", [2, D, D])
    g.ident_in = din("ident", [128, 128])
    g.sel_in = din("sel", [3, 3, 128])
    g.na_tab2 = din("na_tab2", [2, 128, 8, 2, 7, 64])
    g.gla_Lm = din("gla_Lm", [2, 128, 128])
    g.rope_c_tm = din("rope_c_tm", [128, 16, 64])
    g.rope_s_tm = din("rope_s_tm", [128, 16, 64])
    g.rope_c_fm = din("rope_c_fm", [128, TLAT])
    g.rope_s_fm = din("rope_s_fm", [128, TLAT])
    g.gla_wg2 = din("gla_wg2", [2, 2, 16, 128])
    g.gla_bg_row = din("gla_bg_row", [2, 1, 2, 128])
    g.gla_norm_row = din("gla_norm_row", [2, 1, 128])
    g.gdn_mbs = din("gdn_mbs", [2, 128, 128])
    g.gdn_mbi = din("gdn_mbi", [2, 128, 128])
    g.ones128 = din("ones128", [128, 128])
    g.gdn_conv_row = din("gdn_conv_row", [2, 5, 1536])
    g.gdn_dtb_row = din("gdn_dtb_row", [2, 1, 8])
    g.gdn_alog_row = din("gdn_alog_row", [2, 1, 8])
    g.gdn_norm_row = din("gdn_norm_row", [2, 1, 128])
    g.hy_w_in = din("hy_w_in", [2, 33, 64])
    g.hy_w_mid = din("hy_w_mid", [2, 2, 64, 64])
    g.hy_w_out = din("hy_w_out", [2, 64, 1024])
    g.hy_freq_col = din("hy_freq_col", [2, 64, 3])
    g.hy_b_col = din("hy_b_col", [2, 64, 3])
    g.hy_conv_row = din("hy_conv_row", [2, 3, 1536])
    g.hy_conv_b_row = din("hy_conv_b_row", [2, 1, 1536])
    g.hy_skip_row = din("hy_skip_row", [2, 1, 512])
    g.hy_zT = [din("hy_zT0", [33, TLAT]), din("hy_zT1", [33, TCTX])]
    g.hy_decay = [din("hy_decay0", [128, 16, 512]), din("hy_decay1", [128, 2, 512])]
    g.hy_FmT = [din("hy_FmT0", [32, 128, 16, 128], BF16), din("hy_FmT1", [4, 128, 2, 128], BF16)]
    g.hy_GT = [din("hy_GT0", [16, 128, 32, 128], BF16), din("hy_GT1", [2, 128, 4, 128], BF16)]

    g.ptm = dscr("ptm", [NB, PROWS, N_TM])
    g.pfm = dscr("pfm", [NB, N_FM, TT])
    g.hbuf = dscr("hbuf", [NB, TT, D])
    g.yfm = dscr("yfm", [NB, 4 * 512, TT], BF16)
    g.khat = [dscr("khat0", [2 * TLAT, 512]), dscr("khat1", [2 * TCTX, 512])]
    g.hy_g0s = dscr("hy_g0s", [TT, 512])
    g.gq = dscr("gq", [TT, GQW])
    g.ybuf = dscr("ybuf", [TT, D])
    g.hy_vss = dscr("hy_vss", [TT, 512])
    g.out = nc.dram_tensor("out", [NB, TLAT, D], F32, kind="ExternalOutput").ap()

    g.ident = P.sb("ident_sb", [128, 128], F32)
    g.sel = P.sb("sel_sb", [3, 3, 128], F32)
    g.uT = P.sb("uT", [128, 16, TT], BF16)
    P.dma('sp', g.ident[:], g.ident_in, w=[g.ident])
    P.dma('sp', g.sel[:], g.sel_in, w=[g.sel])
    g.psA = [P.ps("psA%d" % i, [128, 512], F32) for i in range(4)]
    g.psB = [P.ps("psB%d" % i, [128, 512], F32) for i in range(4)]
    g.modcol = P.sb("modcol", [128, 32, 3], F32)
    g.Acol = P.sb("Acol", [128, 16, 3], F32)
    g.grow = P.sb("grow", [3, D], F32)
    with P.scope() as S:
        zero = S.sb("zero", [8, N_TM], F32)
        P.dve('memset', ap=zero[:], constant=0.0, w=[zero])
        for b in range(NB):
            for r0, n in ((0, 2), (258, 4), (2310, 2)):
                P.dma('sp', g.ptm[b, r0:r0 + n, :], zero[0:n, :], r=[zero], w=[('ptm', b)])

    for l in range(NL):
        phase_mod(g, l)
        if 'D' in g.branches:
            phase_hyfilt(g, l, 0)
            if l < g.NL_total - 1:
                phase_hyfilt(g, l, 1)
        for b in range(NB):
            phase_norm(g, l, b)
            if stop_after == 'norm':
                continue
            phase_proj(g, l, b)
            if stop_after == 'proj':
                continue
            if 'A' in g.branches:
                phase_na(g, l, b)
            if 'B' in g.branches:
                phase_gla(g, l, b)
            if 'C' in g.branches:
                phase_gdn_pre(g, l, b)
                phase_gdn(g, l, b)
            if 'D' in g.branches:
                phase_hy(g, l, b, 0)
                if l < g.NL_total - 1:
                    phase_hy(g, l, b, 1)
            if 'M' in g.branches:
                phase_merge(g, l, b)
    P.finish()
    return nc, g


def phase_mod(g, l):
    P, NB = g.P, g.NB
    with P.scope() as S:
        _phase_mod(g, l, S)


def _phase_mod(g, l, S):
    P = g.P
    g.cT_sb = S.sb("cT_sb", [128, 16, 3], F32)
    g.scT = S.sb("scT", [128, 16, 3], F32)
    g.mslab = [S.sb("mslab%d" % i, [128, 16, 128], F32) for i in range(2)]
    g.gslab = [S.sb("gslab%d" % i, [128, 4, 512], F32) for i in range(2)]
    g.bmc = S.sb("bmc", [128, 32], F32)
    g.gpc = S.sb("gpc", [128, 16], F32)
    g.brow = S.sb("brow", [3, D], F32)
    P.dma('sp', g.cT_sb[:], g.cT, w=[g.cT_sb])
    P.act('activation', out=g.scT[:], in_=g.cT_sb[:], func=AF.Silu, r=[g.cT_sb], w=[g.scT])
    ps = g.psA[0]
    for ft in range(32):
        slab = g.mslab[ft % 2]
        P.dma('sp', slab[:], g.w_mod_t[l, ft], w=[slab])
        for kt in range(16):
            P.pe('matmul', out=ps[:, ft * 3:(ft + 1) * 3], lhsT=slab[:, kt, :], rhs=g.scT[:, kt, :],
                 start=(kt == 0), stop=(kt == 15), r=[slab, g.scT], w=[ps])
    P.dma('sp', g.bmc[:], g.b_mod_col[l], w=[g.bmc])
    P.dma('sp', g.gpc[:], g.g_pre_col[l], w=[g.gpc])
    for j in range(3):
        P.dve('tensor_tensor', out=g.modcol[:, :, j], in0=ps[:, 0:96].rearrange("p (f j) -> p f j", j=3)[:, :, j],
              in1=g.bmc[:], op=ALU.add, r=[ps, g.bmc], w=[g.modcol])
        P.dve('scalar_tensor_tensor', out=g.Acol[:, :, j], in0=g.modcol[:, 16:32, j], scalar=1.0, in1=g.gpc[:],
              op0=ALU.add, op1=ALU.mult, r=[g.modcol, g.gpc], w=[g.Acol])
    psg = g.psA[1:3]
    for jc in range(4):
        for kq in range(4):
            slab = g.gslab[(jc * 4 + kq) % 2]
            P.dma('sp', slab[:], g.w_mod_g[l, kq * 512:(kq + 1) * 512, jc * 512:(jc + 1) * 512]
                  .rearrange("(kt p) f -> p kt f", p=128), w=[slab])
            for k4 in range(4):
                kt = kq * 4 + k4
                P.pe('matmul', out=psg[jc % 2][0:3, :], lhsT=g.scT[:, kt, :], rhs=slab[:, k4, :],
                     start=(kt == 0), stop=(kt == 15), r=[slab, g.scT], w=[psg[jc % 2]])
        P.act('activation', out=g.grow[:, jc * 512:(jc + 1) * 512], in_=psg[jc % 2][0:3, :], func=AF.Copy,
              r=[psg[jc % 2]], w=[g.grow])
    P.dma('sp', g.brow[:], g.b_mod_gate[l].partition_broadcast(3), w=[g.brow])
    P.dve('tensor_tensor', out=g.grow[:], in0=g.grow[:], in1=g.brow[:], op=ALU.add, r=[g.brow, g.grow], w=[g.grow])
    P.dma('sp', g.brow[:], g.g_post_row[l].partition_broadcast(3), r=[], w=[g.brow])
    P.dve('tensor_tensor', out=g.grow[:], in0=g.grow[:], in1=g.brow[:], op=ALU.mult, r=[g.brow, g.grow], w=[g.grow])


def phase_norm(g, l, b):
    with g.P.scope() as S:
        _phase_norm(g, l, b, S)


def _phase_norm(g, l, b, S):
    P = g.P
    g.hx = [S.sb("hx%d" % i, [128, D], F32) for i in range(2)]
    g.xr = [S.sb("xr%d" % i, [128, D], F32) for i in range(2)]
    g.ss = [S.sb("ss%d" % i, [128, 2], F32) for i in range(2)]
    src = g.xin if l == 0 else g.hbuf
    srckey = () if l == 0 else [('hbuf', b)]
    for tt in range(NTT):
        hx, xr, ss = g.hx[tt % 2], g.xr[tt % 2], g.ss[tt % 2]
        j = 2 if tt < 2 else b
        P.dma('sp', hx[:], src[b, tt * 128:(tt + 1) * 128, :], r=srckey, w=[hx])
        P.act('activation', out=xr[:], in_=hx[:], func=AF.Square, accum_out=ss[:, 0:1], r=[hx], w=[xr, ss])
        P.dve('tensor_scalar', out=ss[:, 1:2], in0=ss[:, 0:1], scalar1=1.0 / D, scalar2=EPS, op0=ALU.mult,
              op1=ALU.add, r=[ss], w=[ss])
        P.act('activation', out=ss[:, 1:2], in_=ss[:, 1:2], func=AF.Sqrt, r=[ss], w=[ss])
        P.dve('reciprocal', out=ss[:, 1:2], in_=ss[:, 1:2], r=[ss], w=[ss])
        P.dve('tensor_scalar', out=xr[:], in0=hx[:], scalar1=ss[:, 1:2], scalar2=None, op0=ALU.mult,
              r=[hx, ss], w=[xr])
        for kt in range(16):
            pt = g.psA[kt % 4]
            P.pe('transpose', out=pt[:, 0:128], in_=xr[:, kt * 128:(kt + 1) * 128], identity=g.ident[:],
                 r=[xr, g.ident], w=[pt])
            P.act('activation', out=g.uT[:, kt, tt * 128:(tt + 1) * 128], in_=pt[:, 0:128], func=AF.Identity,
                  scale=g.Acol[:, kt, j:j + 1], bias=g.modcol[:, kt, j:j + 1], r=[pt, g.Acol, g.modcol],
                  wa=[g.uT])


def phase_proj(g, l, b):
    with g.P.scope() as S:
        _phase_proj(g, l, b, S)


def _phase_proj(g, l, b, S):
    P = g.P
    g.wst = [S.sb("wst%d" % i, [128, 16, 512], F32) for i in range(2)]
    g.wbf = [S.sb("wbf%d" % i, [128, 16, 512], BF16) for i in range(2)]
    g.stg = [S.sb("stg%d" % i, [128, 512], F32) for i in range(3)]
    nchunk = (N_TM + 511) // 512
    it = 0

    def prep_tm(cc):
        c0 = cc * 512
        cw = min(512, N_TM - c0)
        wst, wbf = g.wst[cc % 2], g.wbf[cc % 2]
        for h4 in range(4):
            P.dma('sp', wst[:, h4 * 4:(h4 + 1) * 4, 0:cw],
                  g.w_tm[l, h4 * 512:(h4 + 1) * 512, c0:c0 + cw].rearrange("(kt p) f -> p kt f", p=128),
                  w=[(wst.name, h4)])
            if h4 % 2 == 0:
                P.dve('tensor_copy', out=wbf[:, h4 * 4:(h4 + 1) * 4, 0:cw], in_=wst[:, h4 * 4:(h4 + 1) * 4, 0:cw],
                      r=[(wst.name, h4)], w=[(wbf.name, h4)])
            else:
                P.act('activation', out=wbf[:, h4 * 4:(h4 + 1) * 4, 0:cw], in_=wst[:, h4 * 4:(h4 + 1) * 4, 0:cw],
                      func=AF.Copy, r=[(wst.name, h4)], w=[(wbf.name, h4)])

    nft = (N_FM + 127) // 128

    def prep_fm(ft):
        j = nchunk + ft
        wst, wbf = g.wst[j % 2], g.wbf[j % 2]
        P.dma('sp', wst[:, :, 0:128], g.w_fm_t[l, ft], w=[(wst.name, i) for i in range(4)])
        P.dve('tensor_copy', out=wbf[:, :, 0:128], in_=wst[:, :, 0:128], r=[(wst.name, i) for i in range(4)],
              w=[(wbf.name, i) for i in range(4)])

    prep_tm(0)
    for cc in range(nchunk):
        c0 = cc * 512
        cw = min(512, N_TM - c0)
        wst, wbf = g.wst[cc % 2], g.wbf[cc % 2]
        if cc + 1 < nchunk:
            prep_tm(cc + 1)
        else:
            prep_fm(0)
        for tt in range(NTT):
            ps = g.psA[it % 4]
            stg = g.stg[it % 3]
            it += 1
            for kt in range(16):
                P.pe('matmul', out=ps[:, 0:cw], lhsT=g.uT[:, kt, tt * 128:(tt + 1) * 128], rhs=wbf[:, kt, 0:cw],
                     start=(kt == 0), stop=(kt == 15), r=[g.uT, (wbf.name, kt // 4)], w=[ps], defer=(kt < 15))
            P.act('activation', out=stg[:, 0:cw], in_=ps[:, 0:cw], func=AF.Copy, r=[ps], w=[stg])
            P.dma('pool', g.ptm[b, prow(tt):prow(tt) + 128, c0:c0 + cw], stg[:, 0:cw], r=[stg], wa=[('ptm', b)])
    chunks = [(0, 256)] + [(256 + i * 512, 512) for i in range(4)]
    for ft in range(nft):
        f0 = ft * 128
        fw = min(128, N_FM - f0)
        j = nchunk + ft
        wst, wbf = g.wst[j % 2], g.wbf[j % 2]
        if ft + 1 < nft:
            prep_fm(ft + 1)
        for (t0, n) in chunks:
            ps = g.psA[it % 4]
            stg = g.stg[it % 3]
            it += 1
            for kt in range(16):
                P.pe('matmul', out=ps[0:fw, 0:n], lhsT=wbf[:, kt, 0:fw], rhs=g.uT[:, kt, t0:t0 + n],
                     start=(kt == 0), stop=(kt == 15), r=[g.uT, (wbf.name, kt // 4)], w=[ps], defer=(kt < 15))
            P.act('activation', out=stg[0:fw, 0:n], in_=ps[0:fw, 0:n], func=AF.Copy, r=[ps], w=[stg])
            P.dma('pool', g.pfm[b, f0:f0 + fw, t0:t0 + n], stg[0:fw, 0:n], r=[stg], wa=[('pfm', b)])


def prep_shared(inp):
    f = lambda a: np.ascontiguousarray(np.asarray(a, dtype=np.float32))
    tm, fm = w_in_perm()
    sh = {}
    w_mod = np.asarray(inp["w_mod"], dtype=np.float32)
    sh["w_mod_t"] = f(w_mod[:, :, :2 * D].reshape(2, 16, 128, 32, 128).transpose(0, 3, 2, 1, 4))
    sh["w_mod_g"] = f(w_mod[:, :, 2 * D:])
    b_mod = f(inp["b_mod"])
    sh["b_mod_col"] = f(b_mod[:, :2 * D].reshape(2, 32, 128).transpose(0, 2, 1))
    sh["b_mod_gate"] = f(b_mod[:, 2 * D:].reshape(2, 1, D))
    sh["g_pre_col"] = f(f(inp["g_pre"]).reshape(2, 16, 128).transpose(0, 2, 1))
    sh["g_post_row"] = f(f(inp["g_post"]).reshape(2, 1, D))
    sh["w_gate_t"] = f(np.asarray(inp["w_gate"], dtype=np.float32).reshape(2, 4, 16, 128, 16, 128)
                       .transpose(0, 1, 4, 3, 2, 5))
    sh["b_gate_col"] = f(f(inp["b_gate"]).reshape(2, 4, 16, 128).transpose(0, 3, 1, 2))
    sh["w_branch_t"] = f(np.asarray(inp["w_branch"], dtype=np.float32).reshape(2, 4, 4, 128, 16, 128)
                         .transpose(0, 1, 4, 3, 2, 5))
    sh["w_out"] = f(inp["w_out"])
    w_in = np.asarray(inp["w_in"], dtype=np.float32)
    sh["w_tm"] = f(w_in[:, :, tm])
    wfm = np.zeros((2, D, 17 * 128), np.float32)
    wfm[:, :, :N_FM] = w_in[:, :, fm]
    sh["w_fm_t"] = f(wfm.reshape(2, 16, 128, 17, 128).transpose(0, 3, 2, 1, 4))
    sh["ident"] = np.eye(128, dtype=np.float32)
    sel = np.zeros((3, 3, 128), np.float32)
    for j in range(3):
        sel[j, j, :] = 1.0
    sh["sel"] = sel
    tab5 = na_table(inp["na_rpb"])
    tab2 = np.empty((2, 2, 64, 8, 2, 7, 64), np.float32)
    for jj in range(2):
        for par in range(2):
            for m_ in range(7):
                tab2[:, jj, :, :, par, m_, :] = tab5[:, :, :, 2 * m_ + par + jj, :]
    sh["na_tab2"] = np.ascontiguousarray(tab2.reshape(2, 128, 8, 2, 7, 64))
    sh.update(gla_consts())
    sh["gla_wg2"] = f(inp["gla_wg2"])
    sh["gla_bg_row"] = f(f(inp["gla_bg"]).reshape(2, 1, 2, 128))
    sh["gla_norm_row"] = f(f(inp["gla_norm"]).reshape(2, 1, 128))
    sh.update(gdn_consts())
    sh["gdn_conv_row"] = f(f(inp["gdn_conv"]).transpose(0, 2, 1))
    sh["gdn_dtb_row"] = f(f(inp["gdn_dt_bias"]).reshape(2, 1, 8))
    sh["gdn_alog_row"] = f(f(inp["gdn_a_log"]).reshape(2, 1, 8))
    sh["gdn_norm_row"] = f(f(inp["gdn_norm"]).reshape(2, 1, 128))
    sh["hy_w_in"] = f(inp["hy_w_in"])
    sh["hy_w_mid"] = f(inp["hy_w_mid"])
    sh["hy_w_out"] = f(inp["hy_w_out"])
    sh["hy_freq_col"] = f(f(inp["hy_freq"]).transpose(0, 2, 1))
    sh["hy_b_col"] = f(np.concatenate([f(inp["hy_b_in"])[:, None, :], f(inp["hy_b_mid"])], axis=1).transpose(0, 2, 1))
    sh["hy_conv_row"] = f(f(inp["hy_conv"]).transpose(0, 2, 1))
    sh["hy_conv_b_row"] = f(f(inp["hy_conv_b"]).reshape(2, 1, 1536))
    sh["hy_skip_row"] = f(f(inp["hy_skip"]).reshape(2, 1, 512))
    for li, L in enumerate((TLAT, TCTX)):
        hc = hy_consts(L)
        sh["hy_zT%d" % li] = hc["zT"]
        sh["hy_decay%d" % li] = hc["decay"]
        sh["hy_FmT%d" % li] = hc["FmT"]
        sh["hy_GT%d" % li] = hc["GT"]
    return sh


def prep_core(inp, bs):
    x = np.asarray(inp["x"], dtype=np.float32)
    ctx = np.asarray(inp["ctx"], dtype=np.float32)
    c = np.asarray(inp["c"], dtype=np.float32)
    c_ctx = np.asarray(inp["c_ctx"], dtype=np.float32)
    d = {}
    d["xin"] = np.ascontiguousarray(np.stack([np.concatenate([ctx[b], x[b]], axis=0) for b in bs]))
    cb = [c[b] for b in bs]
    while len(cb) < 2:
        cb.append(cb[0])
    cm = np.stack(cb[:2] + [c_ctx], axis=1)
    d["cT"] = np.ascontiguousarray(cm.reshape(16, 128, 3).transpose(1, 0, 2))
    return d


def na_table(rpb):
    rpb = np.asarray(rpb, dtype=np.float32)
    kc = np.arange(64)[:, None]
    qc = np.arange(64)[None, :]
    cstart = np.clip(qc - 8, 0, 48)
    valid = (kc >= cstart) & (kc < cstart + 16)
    dc = np.clip(kc - qc + 15, 0, 30)
    tab = rpb[:, :, :, dc]
    tab = np.where(valid[None, None, None], tab, np.float32(-30000.0))
    return np.ascontiguousarray(tab.transpose(0, 3, 1, 2, 4).astype(np.float32))


def rowtok(row):
    return 64 * row


def phase_na(g, l, b):
    with g.P.scope() as S:
        _phase_na(g, l, b, S)


def _phase_na(g, l, b, S):
    P = g.P
    with_ctx = l < g.NL_total - 1
    stage = S.sb("na_stage", [128, TT], F32)
    qT = S.sb("na_qT", [128, TT], BF16)
    kTh = [S.sb("na_kT%d" % i, [128, TT], BF16) for i in range(2)]
    for i in range(2):
        P.pool('memset', ap=kTh[i][:], constant=0.0, w=[kTh[i]])
    stE = S.sb("na_stE", [128, 18, 128], F32)
    stO = S.sb("na_stO", [128, 15, 128], F32)
    vE = S.sb("na_vE", [128, 18, 2, 65], BF16)
    vO = S.sb("na_vO", [128, 15, 2, 65], BF16)
    st2 = S.sb("na_st2", [64, 36, 128], F32)
    oA = S.sb("na_oA", [64, 36, 128], F32)
    Tb = S.sb("na_Tb", [128, 2, 2, 7, 64], F32)
    sw = [S.sb("na_sw%d" % i, [128, 256], F32) for i in range(2)]
    pall = [S.sb("na_pall%d" % i, [128, 384], BF16) for i in range(2)]
    rec = [S.sb("na_rec%d" % i, [64, 1], F32) for i in range(2)]
    yst = S.sb("na_yst", [128, TT], BF16)
    psS, psO, psT = g.psA[0:2], g.psB[0:2], g.psB[2]
    row0 = 0 if with_ctx else 4
    for hp in range(4):
        P.dma('sp', stage[:], g.pfm[b, FM_NAQ + hp * 128:FM_NAQ + (hp + 1) * 128, :], r=[('pfm', b)], w=[stage])
        P.dve('tensor_copy', out=qT[:], in_=stage[:], r=[stage], w=[qT])
        P.dma('sp', stage[:], g.pfm[b, FM_NAK + hp * 128:FM_NAK + (hp + 1) * 128, :], r=[('pfm', b)], w=[stage])
        for i in range(2):
            P.dve('tensor_copy', out=kTh[i][64 * i:64 * i + 64, :], in_=stage[64 * i:64 * i + 64, :], r=[stage],
                  w=[kTh[i]])
        c0 = TM_NAV + hp * 128
        P.dma('sp', stE[:, 0:2, :], g.ptm[b, 2:258, c0:c0 + 128].rearrange("(m p) c -> p m c", p=128),
              r=[('ptm', b)], w=[stE])
        P.dma('sp', stE[:, 2:18, :], g.ptm[b, 262:2310, c0:c0 + 128].rearrange("(m p) c -> p m c", p=128),
              r=[('ptm', b)], wa=[stE])
        P.dma('sp', stO[:], g.ptm[b, 326:326 + 15 * 128, c0:c0 + 128].rearrange("(m p) c -> p m c", p=128),
              r=[('ptm', b)], w=[stO])
        P.pool('memset', ap=vE[:], constant=1.0, w=[vE])
        P.pool('memset', ap=vO[:], constant=1.0, w=[vO])
        P.dve('tensor_copy', out=vE[:, :, :, 0:64], in_=stE[:].rearrange("p r (h d) -> p r h d", h=2), r=[stE], w=[vE])
        P.dve('tensor_copy', out=vO[:, :, :, 0:64], in_=stO[:].rearrange("p r (h d) -> p r h d", h=2), r=[stO], w=[vO])
        c0 = TM_NAZ + hp * 128
        P.dma('sp', st2[:, 0:4, :], g.ptm[b, 2:258, c0:c0 + 128].rearrange("(r p) c -> p r c", p=64),
              r=[('ptm', b)], w=[st2])
        P.dma('sp', st2[:, 4:36, :], g.ptm[b, 262:2310, c0:c0 + 128].rearrange("(r p) c -> p r c", p=64),
              r=[('ptm', b)], wa=[st2])
        P.act('activation', out=st2[:], in_=st2[:], func=AF.Silu, r=[st2], w=[st2])
        P.dma('sp', Tb[:], g.na_tab2[l, :, 2 * hp:2 * hp + 2], w=[Tb])
        it = 0
        for h2 in range(2):
            hb = 64 * h2
            kT = kTh[h2]
            for row in range(row0, 36):
                i2 = it % 2
                it += 1
                tq = 64 * row
                qv = qT[:, tq:tq + 64]
                lat = row >= 4
                ps = psS[i2]
                vts = []
                if lat:
                    r_ = row - 4
                    rs = min(max(r_ - 4, 0), 24)
                    dr0 = rs - r_ + 7
                    for j in range(4):
                        tk = 256 + 64 * (rs + 2 * j)
                        P.pe('matmul', out=ps[:, j * 64:(j + 1) * 64], lhsT=kT[:, tk:tk + 128], rhs=qv, start=True,
                             stop=True, r=[kT, qT], w=[ps])
                        vts.append(vE[:, tk // 128, h2, :] if tk % 128 == 0 else vO[:, (tk - 64) // 128 - 2, h2, :])
                for i in range(2):
                    P.pe('matmul', out=ps[:, 256 + i * 64:256 + (i + 1) * 64], lhsT=kT[:, 128 * i:128 * i + 128],
                         rhs=qv, start=True, stop=True, r=[kT, qT], w=[ps])
                if lat:
                    P.dve('scalar_tensor_tensor', out=sw[i2][:], in0=ps[:, 0:256], scalar=0.125,
                          in1=Tb[:, h2, dr0 % 2, dr0 // 2:dr0 // 2 + 4, :].rearrange("p a b -> p (a b)"),
                          op0=ALU.mult, op1=ALU.add, r=[ps, Tb], w=[sw[i2]])
                    P.act('activation', out=pall[i2][:, 0:256], in_=sw[i2][:], func=AF.Exp, r=[sw[i2]],
                          w=[(pall[i2].name, 0)])
                P.act('activation', out=pall[i2][:, 256:384], in_=ps[:, 256:384], func=AF.Exp, scale=0.125,
                      r=[ps], w=[(pall[i2].name, 1)])
                if lat:
                    for j in range(4):
                        P.pe('matmul', out=psO[i2][0:64, 0:65], lhsT=pall[i2][:, j * 64:(j + 1) * 64], rhs=vts[j],
                             start=(j == 0), stop=False, r=[(pall[i2].name, 0), vE, vO], w=[psO[i2]])
                for i in range(2):
                    P.pe('matmul', out=psO[i2][0:64, 0:65], lhsT=pall[i2][:, 256 + i * 64:256 + (i + 1) * 64],
                         rhs=vE[:, i, h2, :], start=(i == 0 and not lat), stop=(i == 1),
                         r=[(pall[i2].name, 1), vE], w=[psO[i2]])
                P.dve('reciprocal', out=rec[i2][:], in_=psO[i2][0:64, 64:65], r=[psO[i2]], w=[rec[i2]])
                P.dve('tensor_scalar', out=oA[:, row, hb:hb + 64], in0=psO[i2][0:64, 0:64], scalar1=rec[i2][:, 0:1],
                      scalar2=None, op0=ALU.mult, r=[psO[i2], rec[i2]], wa=[oA])
        P.dve('tensor_tensor', out=oA[:], in0=oA[:], in1=st2[:], op=ALU.mult, r=[st2, oA], w=[oA])
        for row in range(row0, 36):
            P.pe('transpose', out=psT[:, 0:64], in_=oA[:, row, :], identity=g.ident[0:64, 0:64],
                 r=[oA, g.ident], w=[psT])
            P.act('activation', out=yst[:, 64 * row:64 * row + 64], in_=psT[:, 0:64], func=AF.Copy,
                  r=[psT], wa=[yst])
        t0 = 64 * row0
        P.dma('sp', g.yfm[b, hp * 128:(hp + 1) * 128, t0:TT], yst[:, t0:TT], r=[yst], wa=[('yfm', b)])


TWO_PI = 2.0 * np.pi


def hy_consts(L):
    import ml_dtypes
    N = 2 * L
    ntt = L // 128
    t = np.arange(L, dtype=np.float64)
    f = np.arange(L, dtype=np.float64)
    ang = 2.0 * np.pi * np.outer(t, f) / N
    Fm = np.empty((L, N), np.float64)
    Fm[:, :L] = np.cos(ang)
    Fm[:, L:] = -np.sin(ang)
    Fm[:, L] = np.cos(np.pi * t)
    G = np.empty((N, L), np.float64)
    G[:L, :] = 2.0 * np.cos(ang.T) / N
    G[0, :] = 1.0 / N
    G[L:, :] = -2.0 * np.sin(ang.T) / N
    G[L, :] = np.cos(np.pi * t) / N
    FmT = Fm.reshape(ntt, 128, 2 * ntt, 128).transpose(2, 1, 0, 3)
    GT = G.reshape(2 * ntt, 128, ntt, 128).transpose(2, 1, 0, 3)
    tt_ = np.linspace(0.0, 1.0, L, dtype=np.float32)[:, None]
    bands = 16
    wpos = (np.float32(2.0 * np.pi) * np.arange(L, dtype=np.float32)[:, None] / np.float32(L)).astype(np.float32)
    fb = np.linspace(1e-4, bands - 1, bands, dtype=np.float32)[None]
    z = np.concatenate([tt_, np.cos(fb * wpos), -np.sin(fb * wpos)], axis=-1).astype(np.float32)
    deltas = np.abs(np.linspace(np.log(1e-2) / 1.5, np.log(1e-2) / 0.3, 512, dtype=np.float32))
    decay = np.exp(-tt_ * deltas).astype(np.float32)
    return {
        "FmT": np.ascontiguousarray(FmT).astype(ml_dtypes.bfloat16),
        "GT": np.ascontiguousarray(GT).astype(ml_dtypes.bfloat16),
        "zT": np.ascontiguousarray(z.T),
        "decay": np.ascontiguousarray(decay.reshape(ntt, 128, 512).transpose(1, 0, 2)),
    }


def phase_hyfilt(g, l, li):
    with g.P.scope() as S:
        _phase_hyfilt(g, l, li, S)


def _phase_hyfilt(g, l, li, S):
    P = g.P
    L = (TLAT, TCTX)[li]
    ntt = L // 128
    wi = S.sb("hf_wi", [33, 64], F32)
    wm = S.sb("hf_wm", [64, 2, 64], F32)
    wo = S.sb("hf_wo", [64, 1024], F32)
    fcol = S.sb("hf_fcol", [64, 3], F32)
    bcol = S.sb("hf_bcol", [64, 3], F32)
    fs = S.sb("hf_fs", [64, 3], F32)
    fb = S.sb("hf_fb", [64, 3], F32)
    zT = S.sb("hf_zT", [33, L], F32)
    hT = [S.sb("hf_hT%d" % i, [64, L], F32) for i in range(2)]
    ua = [S.sb("hf_ua%d" % i, [64, 512], F32) for i in range(2)]
    ub = [S.sb("hf_ub%d" % i, [64, 512], F32) for i in range(2)]
    dec = S.sb("hf_dec", [128, ntt, 512], F32)
    hfb = [S.sb("hf_hfb%d" % i, [128, 512], F32) for i in range(2)]
    hsb = S.sb("hf_hsb", [128, ntt, 512], BF16)
    hdb = S.sb("hf_hdb", [128, ntt, 512], BF16)
    fmt = [S.sb("hf_fmt%d" % i, [128, ntt, 128], BF16) for i in range(2)]
    kst = [S.sb("hf_kst%d" % i, [128, 512], F32) for i in range(2)]
    P.dma('sp', wi[:], g.hy_w_in[l], w=[wi])
    P.dma('sp', wm[:], g.hy_w_mid[l].rearrange("i k n -> k i n"), w=[wm])
    P.dma('sp', wo[:], g.hy_w_out[l], w=[wo])
    P.dma('sp', fcol[:], g.hy_freq_col[l], w=[fcol])
    P.dma('sp', bcol[:], g.hy_b_col[l], w=[bcol])
    P.dma('sp', zT[:], g.hy_zT[li], w=[zT])
    P.dma('sp', dec[:], g.hy_decay[li], w=[dec])
    P.dve('tensor_scalar', out=fs[:], in0=fcol[:], scalar1=1.0 / TWO_PI, scalar2=None, op0=ALU.mult,
          r=[fcol], w=[fs])
    P.dve('tensor_tensor', out=fb[:], in0=fs[:], in1=bcol[:], op=ALU.mult, r=[fs, bcol], w=[fb])
    nch = (L + 511) // 512
    cwid = min(512, L)
    it = 0
    for i in range(3):
        dst = hT[i % 2]
        for c in range(nch):
            ps = g.psA[it % 4]
            i2 = it % 2
            it += 1
            if i == 0:
                P.pe('matmul', out=ps[0:64, 0:cwid], lhsT=wi[:], rhs=zT[:, c * cwid:(c + 1) * cwid], start=True,
                     stop=True, r=[wi, zT], w=[ps])
            else:
                src = hT[(i - 1) % 2]
                P.pe('matmul', out=ps[0:64, 0:cwid], lhsT=wm[:, i - 1, :], rhs=src[:, c * cwid:(c + 1) * cwid],
                     start=True, stop=True, r=[wm, src], w=[ps])
            P.act('activation', out=ua[i2][:, 0:cwid], in_=ps[0:64, 0:cwid], func=AF.Identity,
                  scale=fs[:, i:i + 1], bias=fb[:, i:i + 1], r=[ps, fs, fb], w=[ua[i2]])
            P.dve('scalar_tensor_tensor', out=ub[i2][:, 0:cwid], in0=ua[i2][:, 0:cwid], scalar=0.5,
                  in1=ua[i2][:, 0:cwid], op0=ALU.is_gt, op1=ALU.subtract, r=[ua[i2]], w=[ub[i2]])
            P.dve('scalar_tensor_tensor', out=ub[i2][:, 0:cwid], in0=ua[i2][:, 0:cwid], scalar=-0.5,
                  in1=ub[i2][:, 0:cwid], op0=ALU.is_lt, op1=ALU.subtract, r=[ua[i2], ub[i2]], w=[ub[i2]])
            P.act('activation', out=dst[:, c * cwid:(c + 1) * cwid], in_=ub[i2][:, 0:cwid], func=AF.Sin,
                  scale=TWO_PI, r=[ub[i2]], wa=[dst])
    h3 = hT[0]
    for tt in range(ntt):
        for half in range(2):
            ps = g.psA[it % 4]
            it += 1
            P.pe('matmul', out=ps[:, :], lhsT=h3[:, tt * 128:(tt + 1) * 128], rhs=wo[:, half * 512:(half + 1) * 512],
                 start=True, stop=True, r=[h3, wo], w=[ps])
            P.dve('tensor_tensor', out=hfb[half][:], in0=ps[:, :], in1=dec[:, tt, :], op=ALU.mult,
                  r=[ps, dec], w=[hfb[half]])
        P.dve('tensor_tensor', out=hsb[:, tt, :], in0=hfb[0][:], in1=hfb[1][:], op=ALU.add,
              r=[hfb[0], hfb[1]], wa=[hsb])
        P.pool('tensor_tensor', out=hdb[:, tt, :], in0=hfb[0][:], in1=hfb[1][:], op=ALU.subtract,
               r=[hfb[0], hfb[1]], wa=[hdb])
    for rt in range(2 * ntt):
        fm = fmt[rt % 2]
        ks = kst[rt % 2]
        P.dma('sp', fm[:], g.hy_FmT[li][rt], w=[fm])
        ps = g.psA[it % 4]
        it += 1
        src = hsb if rt < ntt else hdb
        for tt in range(ntt):
            P.pe('matmul', out=ps[:, :], lhsT=fm[:, tt, :], rhs=src[:, tt, :], start=(tt == 0), stop=(tt == ntt - 1),
                 r=[fm, src], w=[ps])
        P.act('activation', out=ks[:], in_=ps[:, :], func=AF.Copy, r=[ps], w=[ks])
        if rt == ntt:
            ps2 = g.psA[it % 4]
            it += 1
            for tt in range(ntt):
                P.pe('matmul', out=ps2[:, :], lhsT=fm[:, tt, :], rhs=hsb[:, tt, :], start=(tt == 0),
                     stop=(tt == ntt - 1), r=[fm, hsb], w=[ps2])
            P.act('activation', out=ks[0:1, :], in_=ps2[0:1, :], func=AF.Copy, r=[ps2, ks], w=[ks])
        P.dma('sp', g.khat[li][rt * 128:(rt + 1) * 128, :], ks[:], r=[ks], wa=[('khat', li)])


def phase_hy(g, l, b, li):
    with g.P.scope() as S:
        _phase_hy(g, l, b, li, S)


def _phase_hy(g, l, b, li, S):
    P = g.P
    L = (TLAT, TCTX)[li]
    ntt = L // 128
    tile0 = 2 if li == 0 else 0
    tok0 = 256 if li == 0 else 0
    vvb = S.sb("hy_vvb", [128, ntt, 512], BF16)
    with P.scope() as S1:
        wrow = S1.sb("hy_wrow", [128, 3, 1536], F32)
        brow = S1.sb("hy_brow", [128, 1536], F32)
        srow = S1.sb("hy_srow", [128, 512], F32)
        xs = [S1.sb("hy_xs%d" % i, [128, 1536], F32) for i in range(3)]
        acc = S1.sb("hy_acc", [128, 1536], F32)
        tmp = S1.sb("hy_tmp", [128, 1536], F32)
        zt = S1.sb("hy_zt", [128, 512], F32)
        vv = S1.sb("hy_vv", [128, 512], F32)
        g0 = S1.sb("hy_g0", [128, 512], F32)
        vs = S1.sb("hy_vs", [128, 512], F32)
        for k in range(3):
            P.dma('sp', wrow[:, k, :], g.hy_conv_row[l, k:k + 1, :].partition_broadcast(128), wa=[wrow])
        P.dma('sp', brow[:], g.hy_conv_b_row[l].partition_broadcast(128), w=[brow])
        P.dma('sp', srow[:], g.hy_skip_row[l].partition_broadcast(128), w=[srow])
        for tt in range(ntt):
            base = prow(tile0 + tt)
            for k in range(3):
                P.dma('sp', xs[k][:], g.ptm[b, base + k - 1:base + k - 1 + 128, TM_HYXV:TM_HYXV + 1536],
                      r=[('ptm', b)], w=[xs[k]])
            P.dma('sp', zt[:], g.ptm[b, base:base + 128, TM_HYZ:TM_HYZ + 512], r=[('ptm', b)], w=[zt])
            P.dve('tensor_tensor', out=acc[:], in0=xs[0][:], in1=wrow[:, 0, :], op=ALU.mult, r=[xs[0], wrow], w=[acc])
            P.pool('tensor_tensor', out=tmp[:], in0=xs[1][:], in1=wrow[:, 1, :], op=ALU.mult, r=[xs[1], wrow],
                   w=[tmp])
            P.dve('tensor_tensor', out=acc[:], in0=acc[:], in1=tmp[:], op=ALU.add, r=[tmp, acc], w=[acc])
            P.pool('tensor_tensor', out=tmp[:], in0=xs[2][:], in1=wrow[:, 2, :], op=ALU.mult, r=[xs[2], wrow],
                   w=[tmp])
            P.dve('tensor_tensor', out=acc[:], in0=acc[:], in1=tmp[:], op=ALU.add, r=[tmp, acc], w=[acc])
            P.dve('tensor_tensor', out=acc[:], in0=acc[:], in1=brow[:], op=ALU.add, r=[brow, acc], w=[acc])
            P.dve('tensor_tensor', out=vv[:], in0=acc[:, 1024:1536], in1=acc[:, 512:1024], op=ALU.mult,
                  r=[acc], w=[vv])
            P.pool('tensor_copy', out=vvb[:, tt, :], in_=vv[:], r=[vv], wa=[vvb])
            P.act('activation', out=zt[:], in_=zt[:], func=AF.Silu, r=[zt], w=[zt])
            P.dve('tensor_tensor', out=g0[:], in0=acc[:, 0:512], in1=zt[:], op=ALU.mult, r=[acc, zt], w=[g0])
            P.pool('tensor_tensor', out=vs[:], in0=vv[:], in1=srow[:], op=ALU.mult, r=[vv, srow], w=[vs])
            P.dve('tensor_tensor', out=vs[:], in0=vs[:], in1=g0[:], op=ALU.mult, r=[vs, g0], w=[vs])
            P.dma('pool', g.hy_g0s[tok0 + tt * 128:tok0 + (tt + 1) * 128, :], g0[:], r=[g0], wa=['hy_g0s'])
            P.dma('pool', g.hy_vss[tok0 + tt * 128:tok0 + (tt + 1) * 128, :], vs[:], r=[vs], wa=['hy_vss'])
    yhat = S.sb("hy_yhat", [128, 2 * ntt, 512], BF16)
    with P.scope() as S2:
        fmt = [S2.sb("hy_fmt%d" % i, [128, ntt, 128], BF16) for i in range(4)]
        kk = [S2.sb("hy_kk%d" % i, [128, 512], F32) for i in range(4)]
        vh = [S2.sb("hy_vh%d" % i, [128, 512], F32) for i in range(4)]
        tq = [S2.sb("hy_tq%d" % i, [128, 512], F32) for i in range(4)]
        it = 0
        for i in range(ntt):
            i2 = (i % 2) * 2
            fre, fim, kre, kim, vre, vim = fmt[i2], fmt[i2 + 1], kk[i2], kk[i2 + 1], vh[i2], vh[i2 + 1]
            P.dma('sp', fre[:], g.hy_FmT[li][i], w=[fre])
            P.dma('sp', fim[:], g.hy_FmT[li][i + ntt], w=[fim])
            P.dma('sp', kre[:], g.khat[li][i * 128:(i + 1) * 128, :], r=[('khat', li)], w=[kre])
            P.dma('sp', kim[:], g.khat[li][(i + ntt) * 128:(i + ntt + 1) * 128, :], r=[('khat', li)], w=[kim])
            for (fm, vdst) in ((fre, vre), (fim, vim)):
                ps = g.psA[it % 4]
                it += 1
                for tt in range(ntt):
                    P.pe('matmul', out=ps[:, :], lhsT=fm[:, tt, :], rhs=vvb[:, tt, :], start=(tt == 0),
                         stop=(tt == ntt - 1), r=[fm, vvb], w=[ps])
                P.act('activation', out=vdst[:], in_=ps[:, :], func=AF.Copy, r=[ps], w=[vdst])
            t1, t2, t3, t4 = tq
            P.dve('tensor_tensor', out=t1[:], in0=vre[:], in1=kre[:], op=ALU.mult, r=[vre, kre], w=[t1])
            P.pool('tensor_tensor', out=t2[:], in0=vim[:], in1=kim[:], op=ALU.mult, r=[vim, kim], w=[t2])
            P.dve('tensor_tensor', out=t3[:], in0=vre[:], in1=kim[:], op=ALU.mult, r=[vre, kim], w=[t3])
            P.pool('tensor_tensor', out=t4[:], in0=vim[:], in1=kre[:], op=ALU.mult, r=[vim, kre], w=[t4])
            P.dve('tensor_tensor', out=yhat[:, i, :], in0=t1[:], in1=t2[:], op=ALU.subtract, r=[t1, t2], wa=[yhat])
            P.pool('tensor_tensor', out=yhat[:, i + ntt, :], in0=t3[:], in1=t4[:], op=ALU.add, r=[t3, t4], wa=[yhat])
            if i == 0:
                P.dve('tensor_tensor', out=yhat[0:1, 0, :], in0=vre[0:1, :], in1=kre[0:1, :], op=ALU.mult,
                      r=[vre, kre], w=[yhat])
                P.dve('tensor_tensor', out=yhat[0:1, ntt, :], in0=vim[0:1, :], in1=kim[0:1, :], op=ALU.mult,
                      r=[vim, kim], w=[yhat])
    yst = S.sb("hy_yst", [128, 4, L], BF16)
    with P.scope() as S3:
        gt = [S3.sb("hy_gt%d" % i, [128, 2 * ntt, 128], BF16) for i in range(2)]
        g0 = [S3.sb("hy_g0b%d" % i, [128, 512], F32) for i in range(2)]
        vs = [S3.sb("hy_vsb%d" % i, [128, 512], F32) for i in range(2)]
        o = [S3.sb("hy_o%d" % i, [128, 512], F32) for i in range(2)]
        for j in range(ntt):
            j2 = j % 2
            P.dma('sp', gt[j2][:], g.hy_GT[li][j], w=[gt[j2]])
            P.dma('sp', g0[j2][:], g.hy_g0s[tok0 + j * 128:tok0 + (j + 1) * 128, :], r=['hy_g0s'], w=[g0[j2]])
            P.dma('sp', vs[j2][:], g.hy_vss[tok0 + j * 128:tok0 + (j + 1) * 128, :], r=['hy_vss'], w=[vs[j2]])
            ps = g.psA[j2]
            for rt in range(2 * ntt):
                P.pe('matmul', out=ps[:, :], lhsT=gt[j2][:, rt, :], rhs=yhat[:, rt, :], start=(rt == 0),
                     stop=(rt == 2 * ntt - 1), r=[gt[j2], yhat], w=[ps])
            P.dve('tensor_tensor', out=o[j2][:], in0=ps[:, :], in1=g0[j2][:], op=ALU.mult, r=[ps, g0[j2]], w=[o[j2]])
            P.pool('tensor_tensor', out=o[j2][:], in0=o[j2][:], in1=vs[j2][:], op=ALU.add, r=[vs[j2], o[j2]],
                   w=[o[j2]])
            pt = g.psB[j2]
            for k in range(4):
                P.pe('transpose', out=pt[:, k * 128:(k + 1) * 128], in_=o[j2][:, k * 128:(k + 1) * 128],
                     identity=g.ident[:], r=[o[j2], g.ident], w=[pt])
            P.act('activation', out=yst[:, :, j * 128:(j + 1) * 128], in_=pt[:, :].rearrange("p (k t) -> p k t", k=4),
                  func=AF.Copy, r=[pt], wa=[yst])
        for k in range(4):
            P.dma('sp', g.yfm[b, 1536 + k * 128:1536 + (k + 1) * 128, tok0:tok0 + L], yst[:, k, :], r=[yst],
                  wa=[('yfm', b)])


def gla_consts():
    s = np.arange(128)[:, None]
    t = np.arange(128)[None, :]
    Lm = np.stack([(s <= t), (s >= t)]).astype(np.float32)
    pos = np.arange(TLAT)
    row = (pos // 64).astype(np.float32)
    col = (pos % 64).astype(np.float32)
    n = 16
    freqs = (np.float32(10000.0) ** (-np.arange(n, dtype=np.float32) / np.float32(n))).astype(np.float32)
    ang = np.concatenate([row[:, None] * freqs, col[:, None] * freqs], axis=-1).astype(np.float32)
    cos, sin = np.cos(ang).astype(np.float32), np.sin(ang).astype(np.float32)
    c64 = np.concatenate([cos, cos], axis=-1)
    s64 = np.concatenate([-sin, sin], axis=-1)
    return {
        "gla_Lm": Lm,
        "rope_c_tm": np.ascontiguousarray(c64.reshape(16, 128, 64).transpose(1, 0, 2)),
        "rope_s_tm": np.ascontiguousarray(s64.reshape(16, 128, 64).transpose(1, 0, 2)),
        "rope_c_fm": np.ascontiguousarray(np.concatenate([c64.T, c64.T], axis=0)),
        "rope_s_fm": np.ascontiguousarray(np.concatenate([s64.T, s64.T], axis=0)),
    }


def phase_gla(g, l, b):
    with g.P.scope() as S:
        _phase_gla(g, l, b, S)


def _phase_gla(g, l, b, S):
    P = g.P
    with_ctx = l < g.NL_total - 1
    oB = S.sb("gl_oB", [128, NTT, 512], F32)
    S0, S = S, Scope(P)
    cfm = S.sb("gl_cfm", [128, TLAT], F32)
    sfm = S.sb("gl_sfm", [128, TLAT], F32)
    ctm = S.sb("gl_ctm", [128, 16, 64], F32)
    stm = S.sb("gl_stm", [128, 16, 64], F32)
    Lm = S.sb("gl_Lm", [128, 2, 128], F32)
    wg2 = S.sb("gl_wg2", [16, 2, 128], F32)
    bg = S.sb("gl_bg", [1, 2, 128], F32)
    ones = S.sb("gl_ones", [1, 128], F32)
    Sp = [S.sb("gl_S%d" % i, [128, 128], F32) for i in range(2)]
    Sbf = [S.sb("gl_Sbf%d" % i, [128, 128], BF16) for i in range(2)]
    NR = 2
    fmq = [[S.sb("gl_fm%d_%d" % (k, r), [128, 128], F32) for k in range(8)] for r in range(NR)]
    lrt = [S.sb("gl_lrt%d" % r, [16, 128], F32) for r in range(NR)]
    ktm = [S.sb("gl_ktm%d" % r, [128, 512], F32) for r in range(NR)]
    vtm = [S.sb("gl_vtm%d" % r, [128, 512], F32) for r in range(NR)]
    vbf = [S.sb("gl_vbf%d" % r, [128, 512], BF16) for r in range(NR)]
    ee = [S.sb("gl_ee%d" % r, [128, 128], F32) for r in range(NR)]
    gdup = [S.sb("gl_gdup%d" % r, [128, 256], F32) for r in range(NR)]
    enb_tm = [S.sb("gl_enbtm%d" % r, [128, 256], F32) for r in range(NR)]
    eb_fm = [[S.sb("gl_ebfm%d_%d" % (i, r), [128, 128], F32) for i in range(2)] for r in range(NR)]
    enb_fm = [[S.sb("gl_enbfm%d_%d" % (i, r), [128, 128], F32) for i in range(2)] for r in range(NR)]
    t1 = [S.sb("gl_t1_%d" % r, [128, 256], F32) for r in range(NR)]
    t2 = [S.sb("gl_t2_%d" % r, [128, 256], F32) for r in range(NR)]
    qeT = [[S.sb("gl_qeT%d_%d" % (h, r), [128, 128], BF16) for h in range(4)] for r in range(NR)]
    keT = [[S.sb("gl_keT%d_%d" % (h, r), [128, 128], BF16) for h in range(4)] for r in range(NR)]
    for r in range(NR):
        for h in range(4):
            P.pool('memset', ap=qeT[r][h][:], constant=0.0, w=[qeT[r][h]])
            P.pool('memset', ap=keT[r][h][:], constant=0.0, w=[keT[r][h]])
    ke_tm = [S.sb("gl_ketm%d" % r, [128, 256], BF16) for r in range(NR)]
    attT = [S.sb("gl_attT%d" % r, [128, 512], BF16) for r in range(NR)]
    stmp = S.sb("gl_stmp", [128, 128], F32)
    P.dma('sp', cfm[:], g.rope_c_fm, w=[cfm])
    P.dma('sp', sfm[:], g.rope_s_fm, w=[sfm])
    P.dma('sp', ctm[:], g.rope_c_tm, w=[ctm])
    P.dma('sp', stm[:], g.rope_s_tm, w=[stm])
    P.dma('sp', Lm[:], g.gla_Lm.rearrange("d s t -> s d t"), w=[Lm])
    P.dma('sp', wg2[:], g.gla_wg2[l].rearrange("d k n -> k d n"), w=[wg2])
    P.dma('sp', bg[:], g.gla_bg_row[l], w=[bg])
    P.dve('memset', ap=ones[:], constant=1.0, w=[ones])
    pg, pbt, pbf, patt, po, pss = g.psA[0], g.psA[1], g.psA[2:4], g.psB[0], g.psB[1], g.psB[2:4]
    it = 0
    for d in range(2):
        order = list(range(NTT)) if d == 0 else [1, 0] + list(range(NTT - 1, 1, -1))
        tend = 127 if d == 0 else 0
        for i in range(2):
            P.dve('memset', ap=Sp[i][:], constant=0.0, w=[Sp[i]])
            P.pool('memset', ap=Sbf[i][:], constant=0.0, w=[Sbf[i]])
        for tt in order:
            r = it % NR
            it += 1
            tok = tt * 128
            lat = tt >= 2
            lt = tt - 2
            P.dma('sp', lrt[r][:], g.pfm[b, FM_GLG + 16 * d:FM_GLG + 16 * d + 16, tok:tok + 128], r=[('pfm', b)],
                  w=[lrt[r]])
            for k, f0 in enumerate((FM_GLQ, FM_GLQ + 128, FM_GLQS, FM_GLQS + 128, FM_GLK, FM_GLK + 128, FM_GLKS,
                                    FM_GLKS + 128)):
                if not lat and k in (2, 3, 6, 7):
                    continue
                P.dma('sp', fmq[r][k][:], g.pfm[b, f0:f0 + 128, tok:tok + 128], r=[('pfm', b)], w=[fmq[r][k]])
            P.dma('sp', ktm[r][:], g.ptm[b, prow(tt):prow(tt) + 128, TM_GLK:TM_GLK + 512], r=[('ptm', b)], w=[ktm[r]])
            P.dma('sp', vtm[r][:], g.ptm[b, prow(tt):prow(tt) + 128, TM_GLV:TM_GLV + 512], r=[('ptm', b)], w=[vtm[r]])
            P.pool('tensor_copy', out=vbf[r][:], in_=vtm[r][:], r=[vtm[r]], w=[vbf[r]])
            P.pe('matmul', out=pg[:, 0:128], lhsT=lrt[r][:], rhs=wg2[:, d, :], start=True, stop=False,
                 r=[lrt[r], wg2], w=[pg])
            P.pe('matmul', out=pg[:, 0:128], lhsT=ones[:], rhs=bg[:, d, :], start=False, stop=True,
                 r=[ones, bg], w=[pg])
            P.act('activation', out=ee[r][:], in_=pg[:, 0:128], func=AF.Exp, scale=-1.0, r=[pg], w=[ee[r]])
            P.act('activation', out=ee[r][:], in_=ee[r][:], func=AF.Ln, bias=1.0, r=[ee[r]], w=[ee[r]])
            gv = gdup[r][:].rearrange("p (h two c) -> p h two c", h=4, two=2)
            ev = ee[r][:].rearrange("p (h c) -> p h c", h=4)
            for two in range(2):
                P.dve('tensor_scalar', out=gv[:, :, two, :], in0=ev, scalar1=-1.0 / 16.0, scalar2=None, op0=ALU.mult,
                      r=[ee[r]], wa=[gdup[r]])
            P.pe('matmul', out=pbt[:, 0:256], lhsT=Lm[:, d, :], rhs=gdup[r][:], start=True, stop=True,
                 r=[Lm, gdup[r]], w=[pbt])
            P.act('activation', out=enb_tm[r][:], in_=pbt[:, 0:256], func=AF.Exp, scale=-1.0, r=[pbt], w=[enb_tm[r]])
            for i in range(2):
                P.pe('matmul', out=pbf[i][:, 0:128], lhsT=gdup[r][:, 128 * i:128 * (i + 1)], rhs=Lm[:, d, :],
                     start=True, stop=True, r=[Lm, gdup[r]], w=[pbf[i]])
                P.act('activation', out=eb_fm[r][i][:], in_=pbf[i][:, 0:128], func=AF.Exp, r=[pbf[i]],
                      w=[eb_fm[r][i]])
                P.act('activation', out=enb_fm[r][i][:], in_=pbf[i][:, 0:128], func=AF.Exp, scale=-1.0, r=[pbf[i]],
                      w=[enb_fm[r][i]])
            for i in range(2):
                qf, qs, kf, ks = fmq[r][i], fmq[r][2 + i], fmq[r][4 + i], fmq[r][6 + i]
                if lat:
                    cs = cfm[:, lt * 128:(lt + 1) * 128]
                    sn = sfm[:, lt * 128:(lt + 1) * 128]
                    P.dve('tensor_tensor', out=qf[:], in0=qf[:], in1=cs, op=ALU.mult, r=[qf, cfm], w=[qf])
                    P.pool('tensor_tensor', out=qs[:], in0=qs[:], in1=sn, op=ALU.mult, r=[qs, sfm], w=[qs])
                    P.dve('tensor_tensor', out=qf[:], in0=qf[:], in1=qs[:], op=ALU.add, r=[qf, qs], w=[qf])
                    P.dve('tensor_tensor', out=kf[:], in0=kf[:], in1=cs, op=ALU.mult, r=[kf, cfm], w=[kf])
                    P.pool('tensor_tensor', out=ks[:], in0=ks[:], in1=sn, op=ALU.mult, r=[ks, sfm], w=[ks])
                    P.dve('tensor_tensor', out=kf[:], in0=kf[:], in1=ks[:], op=ALU.add, r=[kf, ks], w=[kf])
                for hh in range(2):
                    h = 2 * i + hh
                    ps_ = slice(64 * hh, 64 * hh + 64)
                    P.dve('scalar_tensor_tensor', out=qeT[r][h][ps_, :], in0=qf[ps_, :], scalar=0.125,
                          in1=eb_fm[r][i][ps_, :], op0=ALU.mult, op1=ALU.mult, r=[qf, eb_fm[r][i]], w=[qeT[r][h]])
                    P.dve('tensor_tensor', out=keT[r][h][ps_, :], in0=kf[ps_, :], in1=enb_fm[r][i][ps_, :], op=ALU.mult,
                          r=[kf, enb_fm[r][i]], w=[keT[r][h]])
            if lat:
                for h in range(4):
                    P.dve('tensor_tensor', out=t1[r][:, 64 * h:64 * h + 64], in0=ktm[r][:, 64 * h:64 * h + 64],
                          in1=ctm[:, lt, :], op=ALU.mult, r=[ktm[r], ctm], wa=[t1[r]])
                    P.pool('tensor_tensor', out=t2[r][:, 64 * h:64 * h + 64], in0=ktm[r][:, 256 + 64 * h:256 + 64 * h + 64],
                           in1=stm[:, lt, :], op=ALU.mult, r=[ktm[r], stm], wa=[t2[r]])
                P.dve('tensor_tensor', out=t1[r][:], in0=t1[r][:], in1=t2[r][:], op=ALU.add, r=[t1[r], t2[r]], w=[t1[r]])
                ksrc, kkey = t1[r][:], t1[r]
            else:
                ksrc, kkey = ktm[r][:, 0:256], ktm[r]
            P.dve('tensor_tensor', out=ke_tm[r][:], in0=ksrc, in1=enb_tm[r][:], op=ALU.mult, r=[kkey, enb_tm[r]],
                  w=[ke_tm[r]])
            for h in range(4):
                i, hb = h // 2, 64 * (h % 2)
                P.pe('matmul', out=patt[:, 128 * h:128 * (h + 1)], lhsT=keT[r][h][:],
                     rhs=qeT[r][h][:], start=True, stop=True, r=[keT[r][h], qeT[r][h]], w=[patt])
            for h in range(4):
                P.dve('tensor_tensor', out=attT[r][:, 128 * h:128 * (h + 1)], in0=patt[:, 128 * h:128 * (h + 1)],
                      in1=Lm[:, d, :], op=ALU.mult, r=[patt, Lm], wa=[attT[r]])
            if lat or with_ctx:
                for h in range(4):
                    i, hb = h // 2, 64 * (h % 2)
                    P.pe('matmul', out=po[:, 128 * h:128 * (h + 1)], lhsT=attT[r][:, 128 * h:128 * (h + 1)],
                         rhs=vbf[r][:, 128 * h:128 * (h + 1)], start=True, stop=False, r=[attT[r], vbf[r]], w=[po])
                    P.pe('matmul', out=po[:, 128 * h:128 * (h + 1)], lhsT=qeT[r][h][:],
                         rhs=Sbf[i][:], start=False, stop=True, r=[qeT[r][h], Sbf[i]], w=[po])
                if d == 0:
                    P.act('activation', out=oB[:, tt, :], in_=po[:, :], func=AF.Copy, r=[po], w=[('oB', tt)])
                else:
                    P.dve('tensor_tensor', out=oB[:, tt, :], in0=po[:, :], in1=oB[:, tt, :], op=ALU.add,
                          r=[po, ('oB', tt)], w=[('oB', tt)])
            for i in range(2):
                P.pe('matmul', out=pss[i][:, 0:256], lhsT=ke_tm[r][:, 128 * i:128 * (i + 1)],
                     rhs=vbf[r][:, 256 * i:256 * (i + 1)], start=True, stop=True, r=[ke_tm[r], vbf[r]], w=[pss[i]])
                for hh in range(2):
                    ps_ = slice(64 * hh, 64 * hh + 64)
                    P.dve('tensor_tensor', out=stmp[ps_, :], in0=pss[i][ps_, 128 * hh:128 * (hh + 1)], in1=Sp[i][ps_, :],
                          op=ALU.add, r=[pss[i], Sp[i]], w=[(stmp.name, hh)])
                    P.dve('tensor_scalar', out=Sp[i][ps_, :], in0=stmp[ps_, :], scalar1=eb_fm[r][i][ps_, tend:tend + 1],
                          scalar2=None, op0=ALU.mult, r=[(stmp.name, hh), eb_fm[r][i]], wa=[Sp[i]])
                P.pool('tensor_copy', out=Sbf[i][:], in_=Sp[i][:], r=[Sp[i]], w=[Sbf[i]])
    S.__exit__(None, None, None)
    S = S0
    gn = S.sb("gl_gn", [128, 128], F32)
    zt = [S.sb("gl_zt%d" % i, [128, 512], F32) for i in range(2)]
    sq = [S.sb("gl_sq%d" % i, [128, 512], F32) for i in range(2)]
    ssq = [S.sb("gl_ssq%d" % i, [128, 4], F32) for i in range(2)]
    yst = S.sb("gl_yst", [128, 4, TT], BF16)
    P.dma('sp', gn[:], g.gla_norm_row[l].partition_broadcast(128), w=[gn])
    tt0 = 0 if with_ctx else 2
    for tt in range(tt0, NTT):
        r = tt % 2
        o = oB[:, tt, :]
        P.dma('sp', zt[r][:], g.ptm[b, prow(tt):prow(tt) + 128, TM_GLZ:TM_GLZ + 512], r=[('ptm', b)], w=[zt[r]])
        P.act('activation', out=zt[r][:], in_=zt[r][:], func=AF.Silu, r=[zt[r]], w=[zt[r]])
        P.pool('tensor_tensor', out=sq[r][:], in0=o, in1=o, op=ALU.mult, r=[('oB', tt)], w=[sq[r]])
        P.dve('tensor_reduce', out=ssq[r][:], in_=sq[r][:].rearrange("p (h c) -> p h c", h=4), axis=AX.X, op=ALU.add,
              r=[sq[r]], w=[ssq[r]])
        P.dve('tensor_scalar', out=ssq[r][:], in0=ssq[r][:], scalar1=1.0 / 128.0, scalar2=EPS, op0=ALU.mult,
              op1=ALU.add, r=[ssq[r]], w=[ssq[r]])
        P.act('activation', out=ssq[r][:], in_=ssq[r][:], func=AF.Sqrt, r=[ssq[r]], w=[ssq[r]])
        P.dve('reciprocal', out=ssq[r][:], in_=ssq[r][:], r=[ssq[r]], w=[ssq[r]])
        for h in range(4):
            P.dve('scalar_tensor_tensor', out=sq[r][:, 128 * h:128 * (h + 1)], in0=oB[:, tt, 128 * h:128 * (h + 1)],
                  scalar=ssq[r][:, h:h + 1], in1=gn[:], op0=ALU.mult, op1=ALU.mult, r=[('oB', tt), ssq[r], gn],
                  wa=[sq[r]])
        P.dve('tensor_tensor', out=sq[r][:], in0=sq[r][:], in1=zt[r][:], op=ALU.mult, r=[sq[r], zt[r]], w=[sq[r]])
        pt = g.psB[r]
        for k in range(4):
            P.pe('transpose', out=pt[:, k * 128:(k + 1) * 128], in_=sq[r][:, k * 128:(k + 1) * 128],
                 identity=g.ident[:], r=[sq[r], g.ident], w=[pt])
        P.act('activation', out=yst[:, :, tt * 128:(tt + 1) * 128], in_=pt[:, :].rearrange("p (k t) -> p k t", k=4),
              func=AF.Copy, r=[pt], wa=[yst])
    for k in range(4):
        if tt0 >= NTT:
            break
        P.dma('sp', g.yfm[b, 512 + k * 128:512 + (k + 1) * 128, tt0 * 128:TT], yst[:, k, tt0 * 128:TT], r=[yst],
              wa=[('yfm', b)])


GQW = 1536 + 16


def gdn_consts():
    s = np.arange(128)[:, None]
    t = np.arange(128)[None, :]
    big = np.float32(1e5)
    mbs = np.stack([np.where(t < s, 0.0, big), np.where(t > s, 0.0, big)]).astype(np.float32)
    mbi = np.stack([np.where(s <= t, 0.0, -big), np.where(s >= t, 0.0, -big)]).astype(np.float32)
    return {"gdn_mbs": np.ascontiguousarray(mbs), "gdn_mbi": np.ascontiguousarray(mbi),
            "ones128": np.ones((128, 128), np.float32)}


def phase_gdn_pre(g, l, b):
    with g.P.scope() as S:
        _phase_gdn_pre(g, l, b, S)


def _phase_gdn_pre(g, l, b, S):
    P = g.P
    wrow = S.sb("gd_wrow", [128, 5, 1536], F32)
    xs = [S.sb("gd_xs%d" % i, [128, 1536], F32) for i in range(5)]
    acc = S.sb("gd_acc", [128, GQW], F32)
    tmp = S.sb("gd_tmp", [128, 1536], F32)
    tmp2 = S.sb("gd_tmp2", [128, 1536], F32)
    ab = S.sb("gd_ab", [128, 16], F32)
    dtb = S.sb("gd_dtb", [128, 8], F32)
    eal = S.sb("gd_eal", [128, 8], F32)
    ssq = S.sb("gd_ssq", [128, 8], F32)
    sp = S.sb("gd_sp", [128, 8], F32)
    for k in range(5):
        P.dma('sp', wrow[:, k, :], g.gdn_conv_row[l, k:k + 1, :].partition_broadcast(128), wa=[wrow])
    P.dma('sp', dtb[:], g.gdn_dtb_row[l].partition_broadcast(128), w=[dtb])
    P.dma('sp', eal[:], g.gdn_alog_row[l].partition_broadcast(128), w=[eal])
    P.act('activation', out=eal[:], in_=eal[:], func=AF.Exp, r=[eal], w=[eal])
    for tt in range(NTT):
        base = prow(tt)
        for k in range(5):
            P.dma('sp', xs[k][:], g.ptm[b, base + k - 2:base + k - 2 + 128, TM_GDQKV:TM_GDQKV + 1536],
                  r=[('ptm', b)], w=[xs[k]])
        P.dma('sp', ab[:], g.ptm[b, base:base + 128, TM_GDAB:TM_GDAB + 16], r=[('ptm', b)], w=[ab])
        A = acc[:, 0:1536]
        P.dve('tensor_tensor', out=A, in0=xs[0][:], in1=wrow[:, 0, :], op=ALU.mult, r=[xs[0], wrow], w=[acc])
        for k in range(1, 5):
            eng = 'pool' if k % 2 == 1 else 'dve'
            tk = tmp if k % 2 == 1 else tmp2
            P.op(eng, 'tensor_tensor', out=tk[:], in0=xs[k][:], in1=wrow[:, k, :], op=ALU.mult, r=[xs[k], wrow],
                 w=[tk])
            P.dve('tensor_tensor', out=A, in0=A, in1=tk[:], op=ALU.add, r=[tk, acc], w=[acc])
        P.act('activation', out=A, in_=A, func=AF.Silu, r=[acc], w=[acc])
        P.pool('tensor_tensor', out=tmp[:, 0:1024], in0=acc[:, 0:1024], in1=acc[:, 0:1024], op=ALU.mult, r=[acc],
               w=[tmp])
        P.dve('tensor_reduce', out=ssq[:], in_=tmp[:, 0:1024].rearrange("p (h c) -> p h c", h=8), axis=AX.X,
              op=ALU.add, r=[tmp], w=[ssq])
        P.dve('tensor_scalar', out=ssq[:], in0=ssq[:], scalar1=EPS, scalar2=None, op0=ALU.add, r=[ssq], w=[ssq])
        P.act('activation', out=ssq[:], in_=ssq[:], func=AF.Sqrt, r=[ssq], w=[ssq])
        P.dve('reciprocal', out=ssq[:], in_=ssq[:], r=[ssq], w=[ssq])
        P.dve('tensor_scalar', out=ssq[:, 0:4], in0=ssq[:, 0:4], scalar1=128.0 ** -0.5, scalar2=None, op0=ALU.mult,
              r=[ssq], w=[ssq])
        for h8 in range(8):
            eng = 'dve' if h8 % 2 == 0 else 'pool'
            P.op(eng, 'tensor_scalar', out=acc[:, 128 * h8:128 * (h8 + 1)], in0=acc[:, 128 * h8:128 * (h8 + 1)],
                 scalar1=ssq[:, h8:h8 + 1], scalar2=None, op0=ALU.mult, r=[acc, ssq], w=[acc])
        P.act('activation', out=acc[:, 1536:1544], in_=ab[:, 8:16], func=AF.Sigmoid, r=[ab], w=[acc])
        P.dve('tensor_tensor', out=sp[:], in0=ab[:, 0:8], in1=dtb[:], op=ALU.add, r=[ab, dtb], w=[sp])
        P.act('activation', out=sp[:], in_=sp[:], func=AF.Exp, r=[sp], w=[sp])
        P.act('activation', out=sp[:], in_=sp[:], func=AF.Ln, bias=1.0, r=[sp], w=[sp])
        P.dve('scalar_tensor_tensor', out=acc[:, 1544:1552], in0=sp[:], scalar=-1.0, in1=eal[:], op0=ALU.mult,
              op1=ALU.mult, r=[sp, eal, acc], w=[acc])
        P.dma('sp', g.gq[tt * 128:(tt + 1) * 128, :], acc[:], r=[acc], wa=['gq'])


def phase_gdn(g, l, b):
    with g.P.scope() as S:
        _phase_gdn(g, l, b, S)


def _phase_gdn(g, l, b, S):
    P = g.P
    with_ctx = l < g.NL_total - 1
    oC = S.sb("gd_oC", [128, NTT, 512], F32)
    S0, S = S, Scope(P)
    Lm = S.sb("gd_Lm", [128, 2, 128], F32)
    mbs = S.sb("gd_mbs", [128, 2, 128], F32)
    mbi = S.sb("gd_mbi", [128, 2, 128], F32)
    ones = S.sb("gd_ones", [128, 128], F32)
    St = [S.sb("gd_S%d" % h, [128, 128], F32) for h in range(4)]
    qkv = [S.sb("gd_qkv%d" % i, [128, GQW], F32) for i in range(2)]
    sm = lambda n, w: S.sb("gd_" + n, [128, w], F32)
    gam, egam, bexp, nbeta, gend, kdsc, dend = (sm("gam", 4), sm("egam", 4), sm("bexp", 4), sm("nbeta", 4),
                                                sm("gend", 4), sm("kdsc", 4), sm("dend", 4))
    Lg = [sm("Lg%d" % i, 128) for i in range(4)]
    def smb(n, w):
        solve = n[:2] in ("PQ", "Y0", "Y1", "Y2", "Y3", "RH")
        return S.sb("gd_" + n, [128, w], F32 if solve else BF16)
    KQ = [smb("KQ%d" % h, 256) for h in range(4)]
    Nf = [sm("Nf%d" % i, 128) for i in range(4)]
    Sbf = [smb("Sbf%d" % h, 128) for h in range(4)]
    xx, EE, x2, E2 = ([sm("xx%d" % i, 128) for i in range(4)], [sm("EE%d" % i, 128) for i in range(4)],
                      [sm("x2%d" % i, 128) for i in range(4)], [sm("E2%d" % i, 128) for i in range(4)])
    aqk = [smb("aqk%d" % h, 128) for h in range(4)]
    PQh = [[smb("PQ%d_%d" % (h, i), 256) for i in range(2)] for h in range(4)]
    Yh = [[smb("Y%d_%d" % (h, i), 128) for i in range(2)] for h in range(4)]
    RHSu = [smb("RHSu%d" % h, 128) for h in range(4)]
    RHSw = [smb("RHSw%d" % h, 128) for h in range(4)]
    kdh = [smb("kd%d" % h, 128) for h in range(4)]
    wTn = [smb("wTn%d" % h, 128) for h in range(4)]
    esb = [smb("esb%d" % h, 128) for h in range(4)]
    o1 = [sm("o1%d" % h, 128) for h in range(4)]
    P.dma('sp', Lm[:], g.gla_Lm.rearrange("d s t -> s d t"), w=[Lm])
    P.dma('sp', mbs[:], g.gdn_mbs.rearrange("d s t -> s d t"), w=[mbs])
    P.dma('sp', mbi[:], g.gdn_mbi.rearrange("d s t -> s d t"), w=[mbi])
    P.dma('sp', ones[:], g.ones128, w=[ones])
    pA0, pG = g.psA[0], g.psA[1]
    it = 0
    for d in range(2):
        order = list(range(NTT)) if d == 0 else [1, 0] + list(range(NTT - 1, 1, -1))
        send = 127 if d == 0 else 0
        for h in range(4):
            P.dve('memset', ap=St[h][:], constant=0.0, w=[St[h]])
            P.pool('memset', ap=Sbf[h][:], constant=0.0, w=[Sbf[h]])
        for tt in order:
            r = it % 2
            it += 1
            X = qkv[r]
            P.dma('sp', X[:], g.gq[tt * 128:(tt + 1) * 128, :], r=['gq'], w=[X])
            bet = X[:, 1536 + 4 * d:1536 + 4 * d + 4]
            gg = X[:, 1544 + 4 * d:1544 + 4 * d + 4]
            P.pe('matmul', out=pA0[:, 0:4], lhsT=Lm[:, d, :], rhs=gg, start=True, stop=True, r=[Lm, X], w=[pA0])
            P.act('activation', out=gam[:], in_=pA0[:, 0:4], func=AF.Copy, r=[pA0], w=[gam])
            P.act('activation', out=egam[:], in_=pA0[:, 0:4], func=AF.Exp, r=[pA0], w=[egam])
            P.dve('tensor_tensor', out=bexp[:], in0=egam[:], in1=bet, op=ALU.mult, r=[egam, X], w=[bexp])
            P.dve('tensor_scalar', out=nbeta[:], in0=bet, scalar1=-1.0, scalar2=None, op0=ALU.mult, r=[X], w=[nbeta])
            for h in range(4):
                P.dve('tensor_scalar', out=Lg[h][:], in0=Lm[:, d, :], scalar1=gg[:, h:h + 1], scalar2=None,
                      op0=ALU.mult, r=[Lm, X], w=[Lg[h]])
            for h in range(4):
                P.pe('matmul', out=pG[:, 128 * h:128 * (h + 1)], lhsT=ones[:], rhs=Lg[h][:], start=True, stop=True,
                     r=[ones, Lg[h]], w=[pG])
            gendv = pG[:, :].rearrange("p (h s) -> p h s", h=4)[:, :, send]
            P.act('activation', out=gend[:], in_=gendv, func=AF.Copy, r=[pG], w=[gend])
            P.act('activation', out=dend[:], in_=gend[:], func=AF.Exp, r=[gend], w=[dend])
            P.dve('tensor_tensor', out=kdsc[:], in0=gend[:], in1=gam[:], op=ALU.subtract, r=[gend, gam], w=[kdsc])
            P.act('activation', out=kdsc[:], in_=kdsc[:], func=AF.Exp, r=[kdsc], w=[kdsc])
            qs_ = lambda h: X[:, 128 * h:128 * (h + 1)]
            ks_ = lambda h: X[:, 512 + 128 * h:512 + 128 * (h + 1)]
            vs_ = lambda h: X[:, 1024 + 128 * h:1024 + 128 * (h + 1)]
            H4 = range(4)
            for h in H4:
                pb = g.psB[h]
                P.pe('transpose', out=pb[:, 0:128], in_=ks_(h), identity=g.ident[:], r=[X, g.ident], w=[pb])
                P.pe('transpose', out=pb[:, 128:256], in_=qs_(h), identity=g.ident[:], r=[X, g.ident], w=[pb])
            for h in H4:
                P.act('activation', out=KQ[h][:], in_=g.psB[h][:, 0:256], func=AF.Copy, r=[g.psB[h]], w=[KQ[h]])
            for h in H4:
                pb = g.psB[h]
                kTh, qTh = KQ[h][:, 0:128], KQ[h][:, 128:256]
                P.pe('matmul', out=pb[:, 256:384], lhsT=kTh, rhs=kTh, start=True, stop=True, r=[KQ[h]], w=[pb])
                P.pe('matmul', out=pb[:, 384:512], lhsT=kTh, rhs=qTh, start=True, stop=True, r=[KQ[h]], w=[pb])
            for h in H4:
                Gam = pG[:, 128 * h:128 * (h + 1)]
                P.dve('scalar_tensor_tensor', out=xx[h][:], in0=Gam, scalar=gam[:, h:h + 1], in1=mbs[:, d, :],
                      op0=ALU.subtract, op1=ALU.max, r=[pG, gam, mbs], w=[xx[h]])
                P.dve('scalar_tensor_tensor', out=x2[h][:], in0=Gam, scalar=gam[:, h:h + 1], in1=mbi[:, d, :],
                      op0=ALU.subtract, op1=ALU.min, r=[pG, gam, mbi], w=[x2[h]])
            for h in H4:
                P.act('activation', out=EE[h][:], in_=xx[h][:], func=AF.Exp, scale=-1.0, r=[xx[h]], w=[EE[h]])
                P.act('activation', out=E2[h][:], in_=x2[h][:], func=AF.Exp, r=[x2[h]], w=[E2[h]])
            for h in H4:
                pb = g.psB[h]
                P.dve('scalar_tensor_tensor', out=Nf[h][:], in0=pb[:, 256:384], scalar=nbeta[:, h:h + 1],
                      in1=EE[h][:], op0=ALU.mult, op1=ALU.mult, r=[pb, nbeta, EE[h]], w=[Nf[h]])
                P.dve('tensor_tensor', out=aqk[h][:], in0=pb[:, 384:512], in1=E2[h][:], op=ALU.mult,
                      r=[pb, E2[h]], w=[aqk[h]])
            for h in H4:
                P.pool('tensor_copy', out=PQh[h][0][:, 0:128], in_=Nf[h][:], r=[Nf[h]], w=[PQh[h][0]])
                P.pe('transpose', out=g.psB[h][:, 128:256], in_=Nf[h][:], identity=g.ident[:],
                     r=[Nf[h], g.ident], w=[g.psB[h]])
            for h in range(4):
                P.act('activation', out=PQh[h][0][:, 128:256], in_=g.psB[h][:, 128:256], func=AF.Copy,
                      r=[g.psB[h]], w=[PQh[h][0]])
            for h in range(4):
                P.op('dve' if h % 2 == 0 else 'pool', 'tensor_tensor', out=Yh[h][0][:], in0=PQh[h][0][:, 128:256],
                     in1=g.ident[:], op=ALU.add, r=[PQh[h][0], g.ident], w=[Yh[h][0]])
            yi = 0
            for j in range(6):
                for h in range(4):
                    cur = PQh[h][j % 2]
                    P.pe('matmul', out=g.psB[h][:, 0:128], lhsT=cur[:, 128:256], rhs=cur[:, 0:128], start=True,
                         stop=True, r=[cur], w=[g.psB[h]])
                for h in range(4):
                    nxt = PQh[h][(j + 1) % 2]
                    P.act('activation', out=nxt[:, 0:128], in_=g.psB[h][:, 0:128], func=AF.Copy, r=[g.psB[h]],
                          w=[nxt])
                for h in range(4):
                    nxt = PQh[h][(j + 1) % 2]
                    P.pe('matmul', out=g.psA[h][:, 0:128], lhsT=nxt[:, 0:128], rhs=Yh[h][yi][:], start=True, stop=True,
                         r=[nxt, Yh[h][yi]], w=[g.psA[h]])
                    if j < 5:
                        P.pe('transpose', out=g.psB[h][:, 128:256], in_=nxt[:, 0:128], identity=g.ident[:],
                             r=[nxt, g.ident], w=[g.psB[h]])
                for h in range(4):
                    nxt = PQh[h][(j + 1) % 2]
                    P.dve('tensor_tensor', out=Yh[h][1 - yi][:], in0=g.psA[h][:, 0:128], in1=Yh[h][yi][:], op=ALU.add,
                          r=[g.psA[h], Yh[h][yi]], w=[Yh[h][1 - yi]])
                    if j < 5:
                        P.act('activation', out=nxt[:, 128:256], in_=g.psB[h][:, 128:256], func=AF.Copy,
                              r=[g.psB[h]], w=[nxt])
                yi = 1 - yi
            need_o = (tt >= 2 or with_ctx)
            for h in range(4):
                P.dve('tensor_scalar', out=RHSu[h][:], in0=vs_(h), scalar1=bet[:, h:h + 1], scalar2=None,
                      op0=ALU.mult, r=[X], w=[RHSu[h]])
                P.pool('tensor_scalar', out=RHSw[h][:], in0=ks_(h), scalar1=bexp[:, h:h + 1], scalar2=None,
                       op0=ALU.mult, r=[X, bexp], w=[RHSw[h]])
                P.pool('tensor_scalar', out=kdh[h][:], in0=ks_(h), scalar1=kdsc[:, h:h + 1], scalar2=None,
                       op0=ALU.mult, r=[X, kdsc], w=[kdh[h]])
            for h in range(4):
                P.pe('matmul', out=g.psA[h][:, 128:256], lhsT=RHSw[h][:], rhs=Yh[h][yi][:], start=True, stop=True,
                     r=[RHSw[h], Yh[h][yi]], w=[g.psA[h]])
            for h in range(4):
                P.act('activation', out=wTn[h][:], in_=g.psA[h][:, 128:256], func=AF.Copy, scale=-1.0,
                      r=[g.psA[h]], w=[wTn[h]])
            for h in range(4):
                P.pe('matmul', out=g.psA[h][:, 0:128], lhsT=Yh[h][yi][:], rhs=RHSu[h][:], start=True, stop=False,
                     r=[Yh[h][yi], RHSu[h]], w=[g.psA[h]])
                P.pe('matmul', out=g.psA[h][:, 0:128], lhsT=wTn[h][:], rhs=Sbf[h][:], start=False, stop=True,
                     r=[wTn[h], Sbf[h]], w=[g.psA[h]])
            for h in range(4):
                P.act('activation', out=esb[h][:], in_=g.psA[h][:, 0:128], func=AF.Copy, r=[g.psA[h]], w=[esb[h]])
            for h in range(4):
                P.pe('matmul', out=g.psB[h][:, 256:384], lhsT=kdh[h][:], rhs=esb[h][:], start=True, stop=True,
                     r=[kdh[h], esb[h]], w=[g.psB[h]])
                if need_o:
                    P.pe('matmul', out=g.psB[h][:, 0:128], lhsT=KQ[h][:, 128:256], rhs=Sbf[h][:], start=True, stop=True,
                         r=[KQ[h], Sbf[h]], w=[g.psB[h]])
                    P.pe('matmul', out=g.psB[h][:, 128:256], lhsT=aqk[h][:], rhs=esb[h][:], start=True, stop=True,
                         r=[aqk[h], esb[h]], w=[g.psB[h]])
            if need_o:
                for h in range(4):
                    P.act('activation', out=o1[h][:], in_=g.psB[h][:, 0:128], func=AF.Copy, scale=egam[:, h:h + 1],
                          r=[g.psB[h], egam], w=[o1[h]])
                for h in range(4):
                    oslc = oC[:, tt, 128 * h:128 * (h + 1)]
                    if d == 0:
                        P.dve('tensor_tensor', out=oslc, in0=g.psB[h][:, 128:256], in1=o1[h][:], op=ALU.add,
                              r=[g.psB[h], o1[h]], w=[('oC', tt, h)])
                    else:
                        P.dve('tensor_tensor', out=o1[h][:], in0=g.psB[h][:, 128:256], in1=o1[h][:], op=ALU.add,
                              r=[g.psB[h], o1[h]], w=[o1[h]])
                        P.pool('tensor_tensor', out=oslc, in0=oslc, in1=o1[h][:], op=ALU.add,
                               r=[o1[h], ('oC', tt, h)], w=[('oC', tt, h)])
            for h in range(4):
                P.dve('scalar_tensor_tensor', out=St[h][:], in0=St[h][:], scalar=dend[:, h:h + 1],
                      in1=g.psB[h][:, 256:384], op0=ALU.mult, op1=ALU.add, r=[St[h], dend, g.psB[h]], w=[St[h]])
                P.pool('tensor_copy', out=Sbf[h][:], in_=St[h][:], r=[St[h]], w=[Sbf[h]])
    S.__exit__(None, None, None)
    S = S0
    okeys = lambda tt: [('oC', tt, h) for h in range(4)]
    branch_finish(g, l, b, S, oC, okeys, g.gdn_norm_row[l], TM_GDZ, 1024, with_ctx)


def branch_finish(g, l, b, S, oB, okeys, norm_row, zcol, yrow0, with_ctx):
    P = g.P
    gn = S.sb("bf_gn", [128, 128], F32)
    zt = [S.sb("bf_zt%d" % i, [128, 512], F32) for i in range(2)]
    sq = [S.sb("bf_sq%d" % i, [128, 512], F32) for i in range(2)]
    ssq = [S.sb("bf_ssq%d" % i, [128, 4], F32) for i in range(2)]
    yst = S.sb("bf_yst", [128, 4, TT], BF16)
    P.dma('sp', gn[:], norm_row.partition_broadcast(128), w=[gn])
    tt0 = 0 if with_ctx else 2
    for tt in range(tt0, NTT):
        r = tt % 2
        o = oB[:, tt, :]
        P.dma('sp', zt[r][:], g.ptm[b, prow(tt):prow(tt) + 128, zcol:zcol + 512], r=[('ptm', b)], w=[zt[r]])
        P.act('activation', out=zt[r][:], in_=zt[r][:], func=AF.Silu, r=[zt[r]], w=[zt[r]])
        P.pool('tensor_tensor', out=sq[r][:], in0=o, in1=o, op=ALU.mult, r=okeys(tt), w=[sq[r]])
        P.dve('tensor_reduce', out=ssq[r][:], in_=sq[r][:].rearrange("p (h c) -> p h c", h=4), axis=AX.X, op=ALU.add,
              r=[sq[r]], w=[ssq[r]])
        P.dve('tensor_scalar', out=ssq[r][:], in0=ssq[r][:], scalar1=1.0 / 128.0, scalar2=EPS, op0=ALU.mult,
              op1=ALU.add, r=[ssq[r]], w=[ssq[r]])
        P.act('activation', out=ssq[r][:], in_=ssq[r][:], func=AF.Sqrt, r=[ssq[r]], w=[ssq[r]])
        P.dve('reciprocal', out=ssq[r][:], in_=ssq[r][:], r=[ssq[r]], w=[ssq[r]])
        for h in range(4):
            P.dve('scalar_tensor_tensor', out=sq[r][:, 128 * h:128 * (h + 1)], in0=oB[:, tt, 128 * h:128 * (h + 1)],
                  scalar=ssq[r][:, h:h + 1], in1=gn[:], op0=ALU.mult, op1=ALU.mult, r=okeys(tt) + [ssq[r], gn],
                  wa=[sq[r]])
        P.dve('tensor_tensor', out=sq[r][:], in0=sq[r][:], in1=zt[r][:], op=ALU.mult, r=[sq[r], zt[r]], w=[sq[r]])
        pt = g.psB[r]
        for k in range(4):
            P.pe('transpose', out=pt[:, k * 128:(k + 1) * 128], in_=sq[r][:, k * 128:(k + 1) * 128],
                 identity=g.ident[:], r=[sq[r], g.ident], w=[pt])
        P.act('activation', out=yst[:, :, tt * 128:(tt + 1) * 128], in_=pt[:, :].rearrange("p (k t) -> p k t", k=4),
              func=AF.Copy, r=[pt], wa=[yst])
    for k in range(4):
        P.dma('sp', g.yfm[b, yrow0 + k * 128:yrow0 + (k + 1) * 128, tt0 * 128:TT], yst[:, k, tt0 * 128:TT], r=[yst],
              wa=[('yfm', b)])


def phase_merge(g, l, b):
    with g.P.scope() as S:
        _phase_merge(g, l, b, S)


def _phase_merge(g, l, b, S):
    P = g.P
    with_ctx = l < g.NL_total - 1
    last = not with_ctx
    if with_ctx:
        halves = [(0, 1280), (1280, 1024)]
    else:
        halves = [(256, 1024), (1280, 1024)]
    ss = S.sb("mg_ss", [128, NTT, 4], F32)
    bgc = S.sb("mg_bgc", [128, 4, 16], F32)
    mT = S.sb("mg_mT", [128, 16, 1280], BF16)
    P.dma('sp', bgc[:], g.b_gate_col[l], w=[bgc])
    for (t0, n) in halves:
        chunks = [(c0, min(512, n - c0)) for c0 in range(0, n, 512)]
        nch = len(chunks)
        with P.scope() as SA:
            yT = SA.sb("mg_yT", [128, 16, 1280], BF16)
            wgs = [SA.sb("mg_wgs%d" % i, [128, 16, 128], F32) for i in range(2)]
            wgb = [SA.sb("mg_wgb%d" % i, [128, 16, 128], BF16) for i in range(2)]
            wbs = [SA.sb("mg_wbs%d" % i, [128, 4, 128], F32) for i in range(2)]
            wbb = [SA.sb("mg_wbb%d" % i, [128, 4, 128], BF16) for i in range(2)]
            acc = SA.sb("mg_acc", [128, 1280], F32)
            sig = [SA.sb("mg_sig%d" % i, [128, 512], F32) for i in range(2)]
            tmp = [SA.sb("mg_tmp%d" % i, [128, 512], F32) for i in range(2)]
            for kt in range(16):
                P.dma('sp', yT[:, kt, 0:n], g.yfm[b, kt * 128:(kt + 1) * 128, t0:t0 + n], r=[('yfm', b)], wa=[yT])
            it = 0
            steps = [(ft, i) for ft in range(16) for i in range(4)]

            def prep(k):
                ft, i = steps[k]
                w2 = k % 2
                P.dma('sp', wgs[w2][:], g.w_gate_t[l, i, ft], w=[wgs[w2]])
                P.act('activation', out=wgb[w2][:], in_=wgs[w2][:], func=AF.Copy, r=[wgs[w2]], w=[wgb[w2]])
                P.dma('sp', wbs[w2][:], g.w_branch_t[l, i, ft], w=[wbs[w2]])
                P.act('activation', out=wbb[w2][:], in_=wbs[w2][:], func=AF.Copy, r=[wbs[w2]], w=[wbb[w2]])

            prep(0)
            for k, (ft, i) in enumerate(steps):
                w2 = k % 2
                if k + 1 < len(steps):
                    prep(k + 1)
                if True:
                    for c, (c0, cw) in enumerate(chunks):
                        pg_, pb_ = g.psA[it % 4], g.psB[it % 4]
                        i2 = it % 2
                        it += 1
                        for kt in range(16):
                            P.pe('matmul', out=pg_[:, 0:cw], lhsT=wgb[w2][:, kt, :],
                                 rhs=g.uT[:, kt, t0 + c0:t0 + c0 + cw], start=(kt == 0), stop=(kt == 15),
                                 r=[wgb[w2], g.uT], w=[pg_], defer=(kt < 15))
                        for kt in range(4):
                            P.pe('matmul', out=pb_[:, 0:cw], lhsT=wbb[w2][:, kt, :],
                                 rhs=yT[:, 4 * i + kt, c0:c0 + cw], start=(kt == 0), stop=(kt == 3),
                                 r=[wbb[w2], yT], w=[pb_], defer=(kt < 3))
                        P.act('activation', out=sig[i2][:, 0:cw], in_=pg_[:, 0:cw], func=AF.Sigmoid,
                              bias=bgc[:, i, ft:ft + 1], r=[pg_, bgc], w=[sig[i2]])
                        asl = acc[:, c0:c0 + cw]
                        if i == 0:
                            P.dve('tensor_tensor', out=asl, in0=pb_[:, 0:cw], in1=sig[i2][:, 0:cw], op=ALU.mult,
                                  r=[pb_, sig[i2]], w=[('mg_acc', c)])
                        else:
                            P.dve('tensor_tensor', out=tmp[i2][:, 0:cw], in0=pb_[:, 0:cw], in1=sig[i2][:, 0:cw],
                                  op=ALU.mult, r=[pb_, sig[i2]], w=[tmp[i2]])
                            P.pool('tensor_tensor', out=asl, in0=asl, in1=tmp[i2][:, 0:cw], op=ALU.add,
                                   r=[tmp[i2], ('mg_acc', c)], w=[('mg_acc', c)])
                if i < 3:
                    continue
                P.act('activation', out=mT[:, ft, 0:n], in_=acc[:, 0:n], func=AF.Copy,
                      r=[('mg_acc', c) for c in range(nch)], wa=[mT])
        with P.scope() as SB:
            wos = [SB.sb("mg_wos%d" % i, [128, 16, 512], F32) for i in range(1)]
            wob = [SB.sb("mg_wob%d" % i, [128, 16, 512], BF16) for i in range(2)]
            yst = [SB.sb("mg_yst%d" % i, [128, 512], F32) for i in range(3)]
            junk = SB.sb("mg_junk", [128, 512], F32)
            it = 0
            for cc in range(4):
                P.dma('sp', wos[0][:], g.w_out[l, :, cc * 512:(cc + 1) * 512].rearrange("(kt p) f -> p kt f", p=128),
                      w=[wos[0]])
                for q4 in range(4):
                    if q4 % 2 == 0:
                        P.dve('tensor_copy', out=wob[cc % 2][:, 4 * q4:4 * q4 + 4, :],
                              in_=wos[0][:, 4 * q4:4 * q4 + 4, :], r=[wos[0]], w=[(wob[cc % 2].name, q4)])
                    else:
                        P.act('activation', out=wob[cc % 2][:, 4 * q4:4 * q4 + 4, :],
                              in_=wos[0][:, 4 * q4:4 * q4 + 4, :], func=AF.Copy, r=[wos[0]],
                              w=[(wob[cc % 2].name, q4)])
                for ti in range(n // 128):
                    tt = (t0 + ti * 128) // 128
                    ps = g.psA[it % 4]
                    st = yst[it % 3]
                    it += 1
                    for kt in range(16):
                        P.pe('matmul', out=ps[:, :], lhsT=mT[:, kt, ti * 128:(ti + 1) * 128], rhs=wob[cc % 2][:, kt, :],
                             start=(kt == 0), stop=(kt == 15), r=[mT, (wob[cc % 2].name, kt // 4)], w=[ps],
                             defer=(kt < 15))
                    P.act('activation', out=st[:], in_=ps[:, :], func=AF.Copy, r=[ps], w=[st])
                    P.dve('tensor_tensor', out=junk[:], in0=st[:], in1=st[:], op=ALU.mult, r=[st], w=[junk])
                    P.dve('tensor_reduce', out=ss[:, tt, cc:cc + 1], in_=junk[:], axis=AX.X, op=ALU.add, r=[junk],
                          wa=[('mg_ss', tt)])
                    P.dma('pool', g.ybuf[tt * 128:(tt + 1) * 128, cc * 512:(cc + 1) * 512], st[:], r=[st],
                          wa=['ybuf'])
    with P.scope() as SC:
        GG = SC.sb("mg_GG", [128, 2, D], F32)
        yt = [SC.sb("mg_yt%d" % i, [128, D], F32) for i in range(2)]
        ht = [SC.sb("mg_ht%d" % i, [128, D], F32) for i in range(2)]
        rs = [SC.sb("mg_rs%d" % i, [128, 2], F32) for i in range(2)]
        for vi, j in enumerate((b, 2)):
            for jc in range(4):
                pb = g.psA[(vi * 4 + jc) % 4]
                P.pe('matmul', out=pb[:, :], lhsT=g.sel[:, j, :], rhs=g.grow[:, jc * 512:(jc + 1) * 512],
                     start=True, stop=True, r=[g.sel, g.grow], w=[pb])
                P.act('activation', out=GG[:, vi, jc * 512:(jc + 1) * 512], in_=pb[:, :], func=AF.Copy,
                      r=[pb], wa=[GG])
        src = g.xin if l == 0 else g.hbuf
        srckey = [] if l == 0 else [('hbuf', b)]
        tt0 = 0 if with_ctx else 2
        for tt in range(tt0, NTT):
            r = tt % 2
            vi = 1 if tt < 2 else 0
            P.dma('sp', yt[r][:], g.ybuf[tt * 128:(tt + 1) * 128, :], r=['ybuf'], w=[yt[r]])
            P.dma('sp', ht[r][:], src[b, tt * 128:(tt + 1) * 128, :], r=srckey, w=[ht[r]])
            P.dve('tensor_reduce', out=rs[r][:, 0:1], in_=ss[:, tt, :], axis=AX.X, op=ALU.add, r=[('mg_ss', tt)],
                  w=[rs[r]])
            P.dve('tensor_scalar', out=rs[r][:, 1:2], in0=rs[r][:, 0:1], scalar1=1.0 / D, scalar2=EPS, op0=ALU.mult,
                  op1=ALU.add, r=[rs[r]], w=[rs[r]])
            P.act('activation', out=rs[r][:, 1:2], in_=rs[r][:, 1:2], func=AF.Sqrt, r=[rs[r]], w=[rs[r]])
            P.dve('reciprocal', out=rs[r][:, 1:2], in_=rs[r][:, 1:2], r=[rs[r]], w=[rs[r]])
            P.dve('scalar_tensor_tensor', out=yt[r][:], in0=yt[r][:], scalar=rs[r][:, 1:2], in1=GG[:, vi, :],
                  op0=ALU.mult, op1=ALU.mult, r=[yt[r], rs[r], GG], w=[yt[r]])
            P.pool('tensor_tensor', out=ht[r][:], in0=ht[r][:], in1=yt[r][:], op=ALU.add, r=[yt[r], ht[r]], w=[ht[r]])
            if last:
                P.dma('sp', g.out[b, (tt - 2) * 128:(tt - 1) * 128, :], ht[r][:], r=[ht[r]], wa=['out'])
            else:
                P.dma('sp', g.hbuf[b, tt * 128:(tt + 1) * 128, :], ht[r][:], r=[ht[r]], wa=[('hbuf', b)])


N_CORES = 8
_PROGRAM = {}


def kernel(**inputs):
    sh = prep_shared(inputs)
    if "nc" not in _PROGRAM:
        _PROGRAM["nc"] = build_program(NB=2, NL=2)[0]
    nc = _PROGRAM["nc"]
    in_maps = []
    for c in range(N_CORES):
        m = dict(sh)
        m.update(prep_core(inputs, [2 * c, 2 * c + 1]))
        in_maps.append(m)
    res = run_bass_kernel_spmd(nc, in_maps, core_ids=list(range(N_CORES)))
    out = np.concatenate([np.asarray(res.results[c]["out"]) for c in range(N_CORES)], axis=0)
    return np.ascontiguousarray(out.astype(np.float32))
```
